# Optimizing a Trainium2 kernel written in Bass

```python
import math
import jax, jax.numpy as jnp
from jax import lax
import numpy as np

D_MODEL = 1024
BATCH = 4
SEQ = 4096
DEPTH = 1
DEC_BATCH = 32
DEC_SEQ = 1
PAST_LEN = 8192
PAGE_SIZE = 128

N_HEADS = 8
HEAD_DIM = 64
ATTN_WIDTH = N_HEADS * HEAD_DIM
MOBA_BLOCK = 256
MOBA_TOPK = 3
Q_CHUNK = 16
ROPE_THETA = 10000.0
SSM_WIDTH = D_MODEL // 2
SSM_GROUP = 16
SSM_GROUPS = SSM_WIDTH // SSM_GROUP
SSM_STATE = 64
DT_MIN = 1e-3
DT_MAX = 1e-1
FFN_HIDDEN = ((8 * D_MODEL + 3 * 256 - 1) // (3 * 256)) * 256
IN_WIDTH = 3 * ATTN_WIDTH + SSM_WIDTH + 2 * D_MODEL
RMS_EPS = 1e-6

kernel_name = 'moba_s5_gated_hybrid_step'

F32 = jnp.float32


def rmsnorm(x, g):
    xf = x.astype(F32)
    y = xf * lax.rsqrt(jnp.mean(xf * xf, axis=-1, keepdims=True) + RMS_EPS)
    return (y * g.astype(F32)).astype(x.dtype)


def rotary(x, pos):
    half = HEAD_DIM // 2
    inv = jnp.power(jnp.float32(ROPE_THETA), -2.0 * jnp.arange(half, dtype=F32) / HEAD_DIM)
    ang = pos.astype(F32)[:, None] * inv[None, :]
    cos = jnp.cos(ang)[None, :, None, :]
    sin = jnp.sin(ang)[None, :, None, :]
    xf = x.astype(F32)
    x1, x2 = xf[..., :half], xf[..., half:]
    return jnp.concatenate([x1 * cos - x2 * sin, x2 * cos + x1 * sin], axis=-1).astype(x.dtype)


def to_blocks(k, v):
    b, L = k.shape[:2]
    nb = -(-L // MOBA_BLOCK)
    pad = nb * MOBA_BLOCK - L
    kb = jnp.pad(k, ((0, 0), (0, pad), (0, 0), (0, 0))).reshape(b, nb, MOBA_BLOCK, N_HEADS, HEAD_DIM)
    vb = jnp.pad(v, ((0, 0), (0, pad), (0, 0), (0, 0))).reshape(b, nb, MOBA_BLOCK, N_HEADS, HEAD_DIM)
    k_mean = jnp.mean(kb, axis=2, dtype=F32)
    return kb, vb, k_mean


def moba_attend(q, kb, vb, k_mean, q_pos):
    bsz, nq = q.shape[:2]
    nb = kb.shape[1]
    q_blk = q_pos // MOBA_BLOCK
    qf = q.astype(F32)
    s_blk = jnp.einsum('bqhd,bnhd->bhqn', qf, k_mean)
    fully_past = jnp.arange(nb)[None, :] < q_blk[:, None]
    s_blk = jnp.where(fully_past[None, None], s_blk, -jnp.inf)
    n_sel = min(MOBA_TOPK, nb)
    _, top_idx = lax.top_k(s_blk, n_sel)
    own = jnp.broadcast_to(q_blk[None, None, :, None], (bsz, N_HEADS, nq, 1)).astype(top_idx.dtype)
    idx = jnp.concatenate([top_idx, own], axis=-1)
    slot_ok = jnp.concatenate([jnp.arange(n_sel)[None, :] < q_blk[:, None],
                               jnp.ones((nq, 1), dtype=bool)], axis=-1)
    bi = jnp.arange(bsz)[:, None, None, None]
    hi = jnp.arange(N_HEADS)[None, :, None, None]
    kg = kb[bi, idx, :, hi, :]
    vg = vb[bi, idx, :, hi, :]
    key_pos = idx[..., None] * MOBA_BLOCK + jnp.arange(MOBA_BLOCK)
    mask = slot_ok[None, None, :, :, None] & (key_pos <= q_pos[None, None, :, None, None])
    s = jnp.einsum('bqhd,bhqnkd->bhqnk', q, kg, preferred_element_type=F32) * (HEAD_DIM ** -0.5)
    s = jnp.where(mask, s, -jnp.inf)
    p = jax.nn.softmax(s.reshape(bsz, N_HEADS, nq, -1), axis=-1).reshape(s.shape)
    o = jnp.einsum('bhqnk,bhqnkd->bqhd', p.astype(vg.dtype), vg, preferred_element_type=F32)
    return o.astype(q.dtype)


def prompt_attend(q, k, v, pos):
    bsz, L = q.shape[:2]
    kb, vb, km = to_blocks(k, v)
    nc = L // Q_CHUNK
    qc = q.reshape(bsz, nc, Q_CHUNK, N_HEADS, HEAD_DIM).swapaxes(0, 1)
    pc = pos.reshape(nc, Q_CHUNK)
    o = lax.map(lambda a: moba_attend(a[0], kb, vb, km, a[1]), (qc, pc))
    return o.swapaxes(0, 1).reshape(bsz, L, N_HEADS, HEAD_DIM)


def s5_scan(u, h0, a_re, a_im, log_dt, b_re, b_im, c_re, c_im, d):
    bsz, L = u.shape[:2]
    uf = u.astype(F32).reshape(bsz, L, SSM_GROUPS, SSM_GROUP)
    lam = lax.complex(a_re.astype(F32), a_im.astype(F32))
    dt = jnp.exp(log_dt.astype(F32))[:, None]
    a_bar = jnp.exp(lam * dt)
    b_c = lax.complex(b_re.astype(F32), b_im.astype(F32))
    b_bar = ((a_bar - 1.0) / lam)[..., None] * b_c
    c_c = lax.complex(c_re.astype(F32), c_im.astype(F32))
    bu = jnp.einsum('gpc,blgc->blgp', b_bar, uf.astype(jnp.complex64))
    bu = bu.at[:, 0].add(a_bar[None] * h0)
    a_seq = jnp.broadcast_to(a_bar, bu.shape)

    def combine(e1, e2):
        return (e2[0] * e1[0], e2[0] * e1[1] + e2[1])

    _, h = lax.associative_scan(combine, (a_seq, bu), axis=1)
    y = jnp.einsum('gcp,blgp->blgc', c_c, h).real + d.astype(F32).reshape(SSM_GROUPS, SSM_GROUP) * uf
    return y.reshape(bsz, L, SSM_WIDTH).astype(u.dtype), h[:, -1]


def trunk_layer(x, pos, attend, h0, lp):
    bsz, L = x.shape[:2]
    h = rmsnorm(x, lp['norm_mix'])
    proj = h @ lp['w_in']
    cuts = [ATTN_WIDTH, 2 * ATTN_WIDTH, 3 * ATTN_WIDTH, 3 * ATTN_WIDTH + SSM_WIDTH,
            3 * ATTN_WIDTH + SSM_WIDTH + D_MODEL]
    q, k, v, u, g_attn, g_ssm = jnp.split(proj, cuts, axis=-1)
    q = rotary(q.reshape(bsz, L, N_HEADS, HEAD_DIM), pos)
    k = rotary(k.reshape(bsz, L, N_HEADS, HEAD_DIM), pos)
    v = v.reshape(bsz, L, N_HEADS, HEAD_DIM)
    o = attend(q, k, v, pos)
    attn_out = o.reshape(bsz, L, ATTN_WIDTH) @ lp['w_attn_proj']
    y, h_last = s5_scan(u, h0, lp['a_re'], lp['a_im'], lp['log_dt'], lp['b_re'], lp['b_im'],
                        lp['c_re'], lp['c_im'], lp['d'])
    z = jax.nn.gelu(y)
    z = z * jax.nn.sigmoid(z @ lp['w_glu'] + lp['b_glu'])
    ssm_out = z @ lp['w_ssm_proj']
    merged = jax.nn.sigmoid(g_attn) * attn_out + jax.nn.sigmoid(g_ssm) * ssm_out
    x = x + merged @ lp['w_out']
    hf = rmsnorm(x, lp['norm_ffn'])
    a, g = jnp.split(hf @ lp['w_ffn_in'], 2, axis=-1)
    x = x + (jax.nn.silu(a) * g) @ lp['w_ffn_out']
    return x, k, v, h_last


def setup_inputs(seed: int = 0) -> dict:
    key = jax.random.key(seed)
    ks = jax.random.split(key, 32)
    n_pages = PAST_LEN // PAGE_SIZE
    n_used = DEC_BATCH * n_pages
    n_phys = n_used + (n_used + 3) // 4

    def nrm(k, shape, scale):
        return jax.random.normal(k, shape, F32) * scale

    n_idx = jnp.arange(SSM_STATE, dtype=F32)
    return {
        'x_prompt': nrm(ks[0], (BATCH, SEQ, D_MODEL), 1.0),
        'x_sample': nrm(ks[1], (DEC_BATCH, DEC_SEQ, D_MODEL), 1.0),
        'cache_k': nrm(ks[2], (DEPTH, n_phys, PAGE_SIZE, N_HEADS, HEAD_DIM), 1.0),
        'cache_v': nrm(ks[3], (DEPTH, n_phys, PAGE_SIZE, N_HEADS, HEAD_DIM), 1.0),
        'state_ssm_re': nrm(ks[4], (DEPTH, DEC_BATCH, SSM_GROUPS, SSM_STATE), 0.5),
        'state_ssm_im': nrm(ks[5], (DEPTH, DEC_BATCH, SSM_GROUPS, SSM_STATE), 0.5),
        'page_table': jax.random.permutation(ks[6], n_phys)[:n_used].reshape(DEC_BATCH, n_pages).astype(jnp.int32),
        'norm_mix': 1.0 + nrm(ks[7], (DEPTH, D_MODEL), 0.02),
        'w_in': nrm(ks[8], (DEPTH, D_MODEL, IN_WIDTH), D_MODEL ** -0.5),
        'w_attn_proj': nrm(ks[9], (DEPTH, ATTN_WIDTH, D_MODEL), ATTN_WIDTH ** -0.5),
        'ssm_a_re': -0.5 + nrm(ks[10], (DEPTH, SSM_GROUPS, SSM_STATE), 0.01),
        'ssm_a_im': math.pi * n_idx + nrm(ks[11], (DEPTH, SSM_GROUPS, SSM_STATE), 0.01),
        'ssm_log_dt': jax.random.uniform(ks[12], (DEPTH, SSM_GROUPS), F32, math.log(DT_MIN), math.log(DT_MAX)),
        'ssm_b_re': nrm(ks[13], (DEPTH, SSM_GROUPS, SSM_STATE, SSM_GROUP), (2 * SSM_GROUP) ** -0.5),
        'ssm_b_im': nrm(ks[14], (DEPTH, SSM_GROUPS, SSM_STATE, SSM_GROUP), (2 * SSM_GROUP) ** -0.5),
        'ssm_c_re': nrm(ks[15], (DEPTH, SSM_GROUPS, SSM_GROUP, SSM_STATE), (2 * SSM_STATE) ** -0.5),
        'ssm_c_im': nrm(ks[16], (DEPTH, SSM_GROUPS, SSM_GROUP, SSM_STATE), (2 * SSM_STATE) ** -0.5),
        'ssm_d': nrm(ks[17], (DEPTH, SSM_WIDTH), 1.0),
        'w_glu': nrm(ks[18], (DEPTH, SSM_WIDTH, SSM_WIDTH), SSM_WIDTH ** -0.5),
        'b_glu': nrm(ks[19], (DEPTH, SSM_WIDTH), 0.01),
        'w_ssm_proj': nrm(ks[20], (DEPTH, SSM_WIDTH, D_MODEL), SSM_WIDTH ** -0.5),
        'w_out': nrm(ks[21], (DEPTH, D_MODEL, D_MODEL), D_MODEL ** -0.5),
        'norm_ffn': 1.0 + nrm(ks[22], (DEPTH, D_MODEL), 0.02),
        'w_ffn_in': nrm(ks[23], (DEPTH, D_MODEL, 2 * FFN_HIDDEN), D_MODEL ** -0.5),
        'w_ffn_out': nrm(ks[24], (DEPTH, FFN_HIDDEN, D_MODEL), FFN_HIDDEN ** -0.5),
        'norm_final': 1.0 + nrm(ks[25], (D_MODEL,), 0.02),
    }


def reference(x_prompt, x_sample, cache_k, cache_v, state_ssm_re, state_ssm_im, page_table,
              norm_mix, w_in, w_attn_proj, ssm_a_re, ssm_a_im, ssm_log_dt, ssm_b_re, ssm_b_im,
              ssm_c_re, ssm_c_im, ssm_d, w_glu, b_glu, w_ssm_proj, w_out, norm_ffn,
              w_ffn_in, w_ffn_out, norm_final):
    n_prompt, seq = x_prompt.shape[:2]
    n_seq, dec_seq = x_sample.shape[:2]
    pos_p = jnp.arange(seq, dtype=jnp.int32)
    pos_s = PAST_LEN + jnp.arange(dec_seq, dtype=jnp.int32)
    h0_prompt = jnp.zeros((n_prompt, SSM_GROUPS, SSM_STATE), jnp.complex64)

    xp, xs = x_prompt, x_sample
    kp_l, vp_l, rp_l, ip_l, ks_l, vs_l, rs_l, is_l = [], [], [], [], [], [], [], []
    for l in range(DEPTH):
        lp = dict(norm_mix=norm_mix[l], w_in=w_in[l], w_attn_proj=w_attn_proj[l],
                  a_re=ssm_a_re[l], a_im=ssm_a_im[l], log_dt=ssm_log_dt[l],
                  b_re=ssm_b_re[l], b_im=ssm_b_im[l], c_re=ssm_c_re[l], c_im=ssm_c_im[l],
                  d=ssm_d[l], w_glu=w_glu[l], b_glu=b_glu[l], w_ssm_proj=w_ssm_proj[l],
                  w_out=w_out[l], norm_ffn=norm_ffn[l], w_ffn_in=w_ffn_in[l], w_ffn_out=w_ffn_out[l])

        xp, kp, vp, hp = trunk_layer(xp, pos_p, prompt_attend, h0_prompt, lp)

        past_k = cache_k[l][page_table].reshape(n_seq, -1, N_HEADS, HEAD_DIM)
        past_v = cache_v[l][page_table].reshape(n_seq, -1, N_HEADS, HEAD_DIM)

        def sample_attend(q, k, v, pos, past_k=past_k, past_v=past_v):
            k_all = jnp.concatenate([past_k.astype(k.dtype), k], axis=1)
            v_all = jnp.concatenate([past_v.astype(v.dtype), v], axis=1)
            kb, vb, km = to_blocks(k_all, v_all)
            return moba_attend(q, kb, vb, km, pos)

        h0_s = lax.complex(state_ssm_re[l].astype(F32), state_ssm_im[l].astype(F32))
        xs, ks_new, vs_new, hs = trunk_layer(xs, pos_s, sample_attend, h0_s, lp)

        kp_l.append(kp); vp_l.append(vp); rp_l.append(hp.real); ip_l.append(hp.imag)
        ks_l.append(ks_new); vs_l.append(vs_new); rs_l.append(hs.real); is_l.append(hs.imag)

    y_prompt = rmsnorm(xp, norm_final)
    y_sample = rmsnorm(xs, norm_final)
    new_k_prompt = jnp.stack(kp_l)
    new_v_prompt = jnp.stack(vp_l)
    new_ssm_re_prompt = jnp.stack(rp_l)
    new_ssm_im_prompt = jnp.stack(ip_l)
    new_k_sample = jnp.stack(ks_l)
    new_v_sample = jnp.stack(vs_l)
    new_ssm_re_sample = jnp.stack(rs_l)
    new_ssm_im_sample = jnp.stack(is_l)
    return (y_prompt, y_sample, new_k_prompt, new_v_prompt, new_ssm_re_prompt, new_ssm_im_prompt,
            new_k_sample, new_v_sample, new_ssm_re_sample, new_ssm_im_sample)
```

```python
import numpy as np
import concourse.bass as bass
import concourse.mybir as mybir
from concourse.bass_utils import run_bass_kernel_spmd

F32 = mybir.dt.float32
BF16 = mybir.dt.bfloat16
I32 = mybir.dt.int32
AF = mybir.ActivationFunctionType
ALU = mybir.AluOpType
AX = mybir.AxisListType

D = 1024
NH = 8
HD = 64
NT_OWN = 16
NT_PREV = 16
EPS = 1e-6
BIG = 30000.0

ENGS = ('pe', 'act', 'dve', 'pool', 'sp')
BLK = {'pe': 'tensor', 'act': 'scalar', 'dve': 'vector', 'pool': 'gpsimd', 'sp': 'sync'}


class Op:
    __slots__ = ('eng', 'fn', 'tok', 'waits', 'dma')

    def __init__(self, eng, fn, tok, waits, dma):
        self.eng, self.fn, self.tok, self.waits, self.dma = eng, fn, tok, waits, dma


class Prog:
    def __init__(self, nc, sems):
        self.nc = nc
        self.sems = sems
        self.ops = []
        self.last_w = {}
        self.readers = {}
        self.cnt = {}
        self.waited = {e: {} for e in ENGS}
        self.emitted = 0

    def add(self, eng, fn, reads=(), writes=(), dma=None):
        deps = set()
        for r in reads:
            if r in self.last_w:
                deps.add(self.last_w[r])
        for w in writes:
            if w in self.last_w:
                deps.add(self.last_w[w])
            deps |= self.readers.get(w, set())
        key = ('d:' + dma) if dma else eng
        inc = 16 if dma else 1
        self.cnt[key] = self.cnt.get(key, 0) + inc
        tok = (key, self.cnt[key])
        waits = {}
        for d in deps:
            dk, dv = self.ops[d].tok
            if dk == 'pe' and eng == 'pe' and not dma:
                continue
            waits[dk] = max(waits.get(dk, 0), dv)
        final = []
        for k, v in waits.items():
            if self.waited[eng].get(k, 0) < v:
                final.append((k, v))
                self.waited[eng][k] = v
        oid = len(self.ops)
        self.ops.append(Op(eng, fn, tok, final, dma))
        for r in reads:
            self.readers.setdefault(r, set()).add(oid)
        for w in writes:
            self.last_w[w] = oid
            self.readers[w] = set()
        return oid

    def barrier(self):
        for e in ENGS:
            final = []
            for k, v in self.cnt.items():
                if self.waited[e].get(k, 0) < v:
                    final.append((k, v))
                    self.waited[e][k] = v
            if final:
                self.ops.append(Op(e, None, None, final, None))

    def emit(self):
        ops = self.ops[self.emitted:]
        self.emitted = len(self.ops)
        with self.nc.Block() as block:
            for eng in ENGS:
                my = [o for o in ops if o.eng == eng]
                if not my:
                    continue

                def body(e, my=my):
                    for o in my:
                        for (k, v) in o.waits:
                            e.wait_ge(self.sems[k], v)
                        if o.fn is None:
                            continue
                        ins = o.fn(e)
                        ins.then_inc(self.sems[o.tok[0]], 16 if o.dma else 1)

                getattr(block, BLK[eng])(body)


def bcast(ap, shape):
    return ap.broadcast_to(shape)


from contextlib import ExitStack
import os


def build():
    nc = bass.Bass("TRN2", target_bir_lowering=False)
    DBG = os.environ.get('KDBG')

    def din(name, shape, dt=F32):
        return nc.dram_tensor(name, list(shape), dt, kind="ExternalInput").ap()

    def dout(name, shape, dt=F32):
        return nc.dram_tensor(name, list(shape), dt, kind="ExternalOutput").ap()

    xo = din("xo", [2048, D])
    xp = din("xp", [2048, D])
    xs = din("xs", [4, D])
    w_in = din("w_in", [D, 4096])
    gmix = din("gmix", [128, 8])
    ropec = din("ropec", [128, 33, 32])
    ropes = din("ropes", [128, 33, 64])
    ident_d = din("ident", [128, 128])
    mask4_d = din("mask4", [128, 4, 512])
    pbias_d = din("pbias", [128, 8, 16])

    arB = din("arB", [128, 256]); aiB = din("aiB", [128, 256]); ldtB = din("ldtB", [128, 256])
    brB = din("brB", [128, 256]); biB = din("biB", [128, 256])
    arD = din("arD", [128, 16]); aiD = din("aiD", [128, 16]); ldtD = din("ldtD", [128, 16])
    crD = din("crD", [128, 256]); ciD = din("ciD", [128, 256])
    kv0_d = din("kv0", [128, 16]); kv1_d = din("kv1", [128, 16]); kvj_d = din("kvj", [128, 256])
    pmB_d = din("pmB", [128, 2]); pmD_d = din("pmD", [128, 2])
    w_attn_d = din("w_attn", [512, 1024]); w_glu_d = din("w_glu", [512, 512]); w_ssm_d = din("w_ssm", [512, 1024])
    w_out_d = din("w_out", [1024, 1024]); w_fin_d = din("w_ffn_in", [1024, 5632]); w_fout_d = din("w_ffn_out", [2816, 1024])
    bgl_d = din("bgl", [128, 4]); dvec_d = din("dvec", [128, 4]); gffn_d = din("gffn", [128, 8]); gfin_d = din("gfin", [128, 1024])
    arA = din("arA", [128, 32]); aiA = din("aiA", [128, 32]); ldtA = din("ldtA", [128, 32])
    b1A = din("b1A", [128, 512]); b2A = din("b2A", [128, 512]); cA = din("cA", [128, 512])
    sgnA_d = din("sgnA", [128, 1]); bmask_d = din("bmask", [128, 128])
    x1_scr = nc.dram_tensor("x1_scr", [2052, 1024], F32, kind="Internal").ap()
    yo = dout("yo", [2048, 1024]); yso = dout("yso", [4, 1024])
    ckh = din("cache_k", [2560 * 128 * 8, 64]); cvh = din("cache_v", [2560 * 128 * 8, 64])
    ptrep_d = din("ptrep", [128, 256], I32); ptE_d = din("ptE", [8, 128], I32); ptO_d = din("ptO", [8, 128], I32)
    piota_d = din("piota", [128, 1]); hb_d = din("hb", [128, 8]); zsel_d = din("zsel", [128, 255]); rsel_d = din("rsel", [4, 128])
    eye8_d = din("eye8", [8, 8]); esel_d = din("esel", [4, 4])
    hfr = dout("hfr", [128, 16]); hfi = dout("hfi", [128, 16])
    hsr = dout("hsr", [128, 64]); hsi = dout("hsi", [128, 64])
    h0r_d = din("h0r", [128, 64]); h0i_d = din("h0i", [128, 64])
    ko = dout("ko", [2048, 512])
    vo = dout("vo", [2048, 512])
    kso = dout("kso", [4, 512])
    vso = dout("vso", [4, 512])
    ut_scr = nc.dram_tensor("ut_scr", [128, 4, 4096], BF16, kind="Internal").ap()
    ot_scr = nc.dram_tensor("ot_scr", [128, 4, 2048], BF16, kind="Internal").ap()

    top = ExitStack()
    with top:
        def mk(es):
            def sb(name, shape, dt):
                return es.enter_context(nc.sbuf_tensor("s_" + name, list(shape), dt))

            def ps(name, shape, dt):
                return es.enter_context(nc.psum_tensor("p_" + name, list(shape), dt))
            return sb, ps

        sb, ps = mk(top)
        semnames = ['pe', 'act', 'dve', 'pool', 'd:x0', 'd:x1', 'd:w', 'd:c', 'd:ko0', 'd:ko1', 'd:vo0', 'd:vo1',
                    'd:us0', 'd:us1', 'd:dbg', 'd:og0', 'd:og1', 'd:gk', 'd:wf0', 'd:wf1', 'd:wf2', 'd:wfo'] + [f'd:pg{i}' for i in range(8)]
        sems = {k: top.enter_context(nc.semaphore(k.replace(':', '_'))) for k in semnames}
        P = Prog(nc, sems)

        def dbg(name, ap, shape, dt, toks):
            d = nc.dram_tensor("dbg_" + name, list(shape), dt, kind="ExternalOutput").ap()
            P.add('sp', lambda e: e.dma_start(out=d, in_=ap), reads=toks, dma='dbg')

        ident = sb("identb", [128, 128], BF16)
        identf = sb("identf", [128, 128], F32)
        qs_f = sb("qs_f", [4, 8, 64], F32)
        ks_f = sb("ks_f", [4, 8, 64], F32)
        vs_f = sb("vs_f", [4, 8, 64], F32)
        usT = sb("usT", [128, 4, 4], BF16)
        OTs = sb("OTs", [128, 4, 4], BF16)

        P.add('sp', lambda e: e.dma_start(out=identf[:], in_=ident_d[:, :]), writes=['identf'], dma='c')
        P.barrier()
        P.add('dve', lambda e: e.tensor_copy(out=ident[:], in_=identf[:]), reads=['identf'], writes=['ident'])

        s12 = ExitStack()
        with s12:
            sb12, _ = mk(s12)
            KaT = sb12("KaT", [128, 8, 4096], BF16)
            Vaug = sb12("Vaug", [128, 32, 8, 65], BF16)
            Qaug = sb12("Qaug", [128, 16, 8, 80], BF16)

            s1 = ExitStack()
            with s1:
                sb1, ps1 = mk(s1)
                w1 = sb1("w1", [128, 8, 2048], BF16)
                gmx = sb1("gmx", [128, 8], F32)
                rc = sb1("rc", [128, 33, 32], F32)
                rs = sb1("rs", [128, 33, 64], F32)
                xt = [sb1(f"xt{i}", [128, D], F32) for i in range(2)]
                sqj = sb1("sqj", [128, D], BF16)
                ssq = [sb1(f"ssq{i}", [128, 1], F32) for i in range(2)]
                rstd = [sb1(f"rstd{i}", [128, 1], F32) for i in range(2)]
                xn = [sb1(f"xn{i}", [128, D], BF16) for i in range(2)]
                hT = [sb1(f"hT{i}", [128, 8, 128], BF16) for i in range(2)]
                kf = [sb1(f"kf{i}", [128, 8, 64], F32) for i in range(2)]
                vf = [sb1(f"vf{i}", [128, 8, 64], F32) for i in range(2)]
                ust = [sb1(f"ust{i}", [128, 4, 128], BF16) for i in range(2)]
                t1 = sb1("t1", [128, 8, 64], F32)
                t2 = sb1("t2", [128, 8, 64], F32)
                kb = sb1("kb", [128, 8, 80], BF16)
                psT = ps1("psT", [128, 8, 128], BF16)
                psQ = ps1("psQ", [128, 512], F32)
                psK = ps1("psK", [128, 512], F32)
                psV = ps1("psV", [128, 512], F32)
                psU = ps1("psU", [128, 4, 128], F32)
                psKT = ps1("psKT", [128, 8, 128], BF16)

                P.add('sp', lambda e: e.dma_start(out=gmx[:], in_=gmix[:, :]), writes=['gmx'], dma='c')
                P.add('sp', lambda e: e.dma_start(out=rc[:], in_=ropec[:, :, :]), writes=['rc'], dma='c')
                P.add('sp', lambda e: e.dma_start(out=rs[:], in_=ropes[:, :, :]), writes=['rs'], dma='c')
                for kc in range(8):
                    for half in range(2):
                        P.add('pool', lambda e, kc=kc, half=half: e.dma_start(
                            out=w1[:, kc, half * 1024:(half + 1) * 1024],
                            in_=w_in[kc * 128:(kc + 1) * 128, half * 1024:(half + 1) * 1024]),
                            writes=[('w1', kc, half)], dma='w')
                P.add('pool', lambda e: e.memset(Vaug[:, :, :, 64:65], 1.0), writes=['Vones'])
                P.add('pool', lambda e: e.memset(Qaug[:, :, :, 0:16], 0.0), writes=['Qz'])
                P.barrier()

                def rotary(src_ps, nt, t, out_ap, ta, tb, tag):
                    x3 = src_ps[0:nt, :].rearrange("p (h d) -> p h d", h=8)
                    x4 = src_ps[0:nt, :].rearrange("p (h two d) -> p h two d", h=8, two=2)
                    cT = bcast(rc[0:nt, t:t + 1, :].unsqueeze(2), [nt, 8, 2, 32])
                    P.add('dve', lambda e: e.tensor_tensor(out=ta[0:nt].rearrange("p h (two d) -> p h two d", two=2),
                                                           in0=x4, in1=cT, op=ALU.mult),
                          reads=[tag, 'rc'], writes=['rot_ta'])
                    P.add('dve', lambda e: e.tensor_tensor(out=tb[0:nt, :, 0:32], in0=x3[:, :, 32:64],
                                                           in1=bcast(rs[0:nt, t:t + 1, 0:32], [nt, 8, 32]), op=ALU.mult),
                          reads=[tag, 'rs'], writes=['rot_tb0'])
                    P.add('dve', lambda e: e.tensor_tensor(out=tb[0:nt, :, 32:64], in0=x3[:, :, 0:32],
                                                           in1=bcast(rs[0:nt, t:t + 1, 32:64], [nt, 8, 32]), op=ALU.mult),
                          reads=[tag, 'rs'], writes=['rot_tb1'])
                    return ['rot_ta', 'rot_tb0', 'rot_tb1']

                def phase1_tile(t):
                    sample = (t == 32)
                    own = 16 <= t < 32
                    nt = 4 if sample else 128
                    if sample:
                        src, r0 = xs, 0
                    elif own:
                        src, r0 = xo, (t - 16) * 128
                    else:
                        src, r0 = xp, t * 128
                    s = t % 2
                    P.add('sp', lambda e: e.dma_start(out=xt[s][0:nt, :], in_=src[r0:r0 + nt, :]),
                          writes=[('xt', s)], dma=f'x{s}')
                    P.add('act', lambda e: e.activation(out=sqj[0:nt, :], in_=xt[s][0:nt, :], func=AF.Square,
                                                        accum_out=ssq[s][0:nt, :]),
                          reads=[('xt', s)], writes=['sqj', ('ssq', s)])
                    P.add('dve', lambda e: e.tensor_scalar(out=rstd[s][0:nt, :], in0=ssq[s][0:nt, :], scalar1=1.0 / D,
                                                           scalar2=EPS, op0=ALU.mult, op1=ALU.add),
                          reads=[('ssq', s)], writes=[('rstd', s)])
                    P.add('act', lambda e: e.activation(out=rstd[s][0:nt, :], in_=rstd[s][0:nt, :], func=AF.Sqrt),
                          reads=[('rstd', s)], writes=[('rstd', s)])
                    P.add('dve', lambda e: e.reciprocal(out=rstd[s][0:nt, :], in_=rstd[s][0:nt, :]),
                          reads=[('rstd', s)], writes=[('rstd', s)])
                    P.add('dve', lambda e: e.tensor_scalar(out=xn[s][0:nt, :], in0=xt[s][0:nt, :],
                                                           scalar1=rstd[s][0:nt, 0:1], scalar2=None, op0=ALU.mult),
                          reads=[('xt', s), ('rstd', s)], writes=[('xn', s)])
                    for kc in range(8):
                        P.add('pe', lambda e, kc=kc: e.transpose(out=psT[:, kc, 0:nt],
                                                                 in_=xn[s][0:nt, kc * 128:(kc + 1) * 128],
                                                                 identity=ident[0:nt, 0:nt]),
                              reads=[('xn', s), 'ident'], writes=['psT'])
                    P.add('dve', lambda e: e.tensor_tensor(out=hT[s][:, :, 0:nt], in0=psT[:, :, 0:nt],
                                                           in1=bcast(gmx[:].unsqueeze(2), [128, 8, nt]), op=ALU.mult),
                          reads=['psT', 'gmx'], writes=[('hT', s)])
                    projs = (('K', psK, 512), ('V', psV, 1024)) + ((('Q', psQ, 0),) if (own or sample) else ())
                    for name, pst, c0 in projs:
                        for kc in range(8):
                            P.add('pe', lambda e, kc=kc, pst=pst, c0=c0: e.matmul(
                                pst[0:nt, :], lhsT=hT[s][:, kc, 0:nt], rhs=w1[:, kc, c0:c0 + 512],
                                start=(kc == 0), stop=(kc == 7)),
                                reads=[('hT', s), ('w1', kc, c0 // 1024)], writes=['ps' + name])
                    for c in range(4):
                        for kc in range(8):
                            P.add('pe', lambda e, kc=kc, c=c: e.matmul(
                                psU[:, c, 0:nt], lhsT=w1[:, kc, 1536 + c * 128:1536 + (c + 1) * 128],
                                rhs=hT[s][:, kc, 0:nt], start=(kc == 0), stop=(kc == 7)),
                                reads=[('hT', s), ('w1', kc, 1)], writes=['psU'])
                    if sample:
                        P.add('act', lambda e: e.activation(out=usT[:], in_=psU[:, :, 0:4], func=AF.Copy),
                              reads=['psU'], writes=['usT'])
                    else:
                        P.add('act', lambda e: e.activation(out=ust[s][:], in_=psU[:], func=AF.Copy),
                              reads=['psU'], writes=[('ust', s)])
                        P.add('sp', lambda e: e.dma_start(out=ut_scr[:, :, t * 128:(t + 1) * 128], in_=ust[s][:]),
                              reads=[('ust', s)], writes=[('utscr', t)], dma=f'us{s}')
                    kout = ks_f if sample else kf[s]
                    toks = rotary(psK, nt, t, None, t1, t2, 'psK')
                    P.add('dve', lambda e: e.tensor_tensor(out=kout[0:nt], in0=t1[0:nt], in1=t2[0:nt], op=ALU.add),
                          reads=toks, writes=[('kf', s)])
                    if not sample:
                        P.add('pool', lambda e: e.memset(kb[:, :, 0:16], 0.0), writes=['kb'])
                        n_blk = t // 2
                        P.add('pool', lambda e: e.memset(kb[:, :, n_blk:n_blk + 1], 1.0), writes=['kb'])
                        P.add('act', lambda e: e.activation(out=kb[:, :, 16:80], in_=kf[s][:], func=AF.Copy),
                              reads=[('kf', s)], writes=['kb'])
                        for h in range(8):
                            P.add('pe', lambda e, h=h: e.transpose(out=psKT[0:80, h, :], in_=kb[:, h, :], identity=ident[:]),
                                  reads=['kb', 'ident'], writes=['psKT'])
                        P.add('act', lambda e: e.activation(out=KaT[0:80, :, t * 128:(t + 1) * 128], in_=psKT[0:80],
                                                            func=AF.Copy),
                              reads=['psKT'], writes=[('KaT', t)])
                    vout = vs_f if sample else vf[s]
                    P.add('act', lambda e: e.activation(out=vout[0:nt], in_=psV[0:nt, :].rearrange("p (h d) -> p h d", h=8),
                                                        func=AF.Copy),
                          reads=['psV'], writes=[('vf', s)])
                    if not sample:
                        P.add('pool', lambda e: e.tensor_copy(out=Vaug[:, t, :, 0:64], in_=vf[s][:]),
                              reads=[('vf', s)], writes=[('Vaug', t)])
                    if own or sample:
                        toksq = rotary(psQ, nt, t, None, t1, t2, 'psQ')
                        if sample:
                            P.add('dve', lambda e: e.tensor_tensor(out=qs_f[:], in0=t1[0:4], in1=t2[0:4], op=ALU.add),
                                  reads=toksq, writes=['qs_f'])
                        else:
                            ti = t - 16
                            P.add('dve', lambda e: e.tensor_tensor(out=Qaug[:, ti, :, 16:80], in0=t1[:], in1=t2[:], op=ALU.add),
                                  reads=toksq, writes=[('Qaug', ti)])
                    if own:
                        P.add('sp', lambda e: e.dma_start(out=ko[r0:r0 + 128, :], in_=kf[s][:].rearrange("p h d -> p (h d)")),
                              reads=[('kf', s)], dma=f'ko{s}')
                        P.add('sp', lambda e: e.dma_start(out=vo[r0:r0 + 128, :], in_=vf[s][:].rearrange("p h d -> p (h d)")),
                              reads=[('vf', s)], dma=f'vo{s}')
                    if sample:
                        P.add('sp', lambda e: e.dma_start(out=kso[:, :], in_=ks_f[:].rearrange("p h d -> p (h d)")),
                              reads=[('kf', s)], dma=f'ko{s}')
                        P.add('sp', lambda e: e.dma_start(out=vso[:, :], in_=vs_f[:].rearrange("p h d -> p (h d)")),
                              reads=[('vf', s)], dma=f'vo{s}')

                tiles = [int(v) for v in DBG.split(',')] if (DBG and DBG != 'all') else range(33)
                for t in tiles:
                    phase1_tile(t)
                P.barrier()
                P.emit()

            s2 = ExitStack()
            with s2:
                sb2, ps2 = mk(s2)
                mask4f = sb2("mask4f", [128, 4, 512], F32)
                mask4 = sb2("mask4b", [128, 4, 512], BF16)
                pbias = sb2("pbias_sb", [128, 8, 16], F32)
                kms = sb2("kms", [128, 8, 16], F32)
                kmT = sb2("kmT", [128, 8, 16], BF16)
                QaT = [sb2(f"QaT{i}", [128, 8, 512], BF16) for i in range(2)]
                s_m = sb2("s_m", [128, 8, 16], F32)
                top8 = sb2("top8", [128, 8, 8], F32)
                sel = sb2("sel", [128, 8, 16], F32)
                val = sb2("val", [128, 8, 16], F32)
                PT = [sb2(f"PT{i}", [128, 512], BF16) for i in range(3)]
                osb = [sb2(f"osb{i}", [128, 512], F32) for i in range(2)]
                rsum = sb2("rsum", [128, 512], F32)
                onesf = sb2("onesf", [128, 64], F32)
                ostg = [sb2(f"ostg{i}", [128, 512], BF16) for i in range(2)]
                psA = [ps2(f"psA{i}", [128, 512], F32) for i in range(2)]
                psO = [ps2(f"psO{i}", [128, 512], F32) for i in range(2)]
                psQT = ps2("psQT", [128, 8, 128], BF16)
                psS = ps2("psS", [128, 8, 16], F32)
                psBT = ps2("psBT", [16, 8, 128], BF16)
                psR = ps2("psR", [128, 512], F32)

                P.add('sp', lambda e: e.dma_start(out=mask4f[:], in_=mask4_d[:, :, :]), writes=['mask4f'], dma='c')
                P.add('sp', lambda e: e.dma_start(out=pbias[:], in_=pbias_d[:, :, :]), writes=['pbias'], dma='c')
                P.add('pool', lambda e: e.memset(onesf[:], 1.0), writes=['onesf'])
                P.barrier()
                P.add('pool', lambda e: e.tensor_copy(out=mask4[:], in_=mask4f[:]), reads=['mask4f'], writes=['mask4'])
                P.add('dve', lambda e: e.tensor_reduce(out=kms[0:80], in_=KaT[0:80].rearrange("p h (n k) -> p h n k", k=256),
                                                       axis=AX.X, op=ALU.add),
                      reads=[('KaT', t) for t in range(32)], writes=['kms'])
                P.add('dve', lambda e: e.tensor_copy(out=kmT[0:80], in_=kms[0:80]), reads=['kms'], writes=['kmT'])

                def qtok(ti):
                    return ('QaT', (ti // 4) % 2, ti % 4)

                def prep_qtile(ti, qa):
                    a = ti % 4
                    m = ti // 2
                    for h in range(8):
                        P.add('pe', lambda e, h=h: e.transpose(out=psQT[0:80, h, :], in_=Qaug[:, ti, h, :], identity=ident[:]),
                              reads=[('Qaug', ti), 'ident'], writes=['psQT'])
                    P.add('act', lambda e: e.activation(out=qa[0:80, :, a * 128:(a + 1) * 128], in_=psQT[0:80], func=AF.Copy),
                          reads=['psQT'], writes=[qtok(ti)])
                    for h in range(8):
                        P.add('pe', lambda e, h=h: e.matmul(psS[:, h, :], lhsT=qa[0:80, h, a * 128:(a + 1) * 128],
                                                            rhs=kmT[0:80, h, :], start=True, stop=True),
                              reads=[qtok(ti), 'kmT'], writes=['psS'])
                    P.add('dve', lambda e: e.tensor_tensor(out=s_m[:], in0=psS[:], in1=bcast(pbias[:, m:m + 1, :], [128, 8, 16]),
                                                           op=ALU.add),
                          reads=['psS', 'pbias'], writes=['s_m'])
                    for h in range(8):
                        P.add('dve', lambda e, h=h: e.max(out=top8[:, h, :], in_=s_m[:, h, :]),
                              reads=['s_m'], writes=[('top8', h)])
                    for h in range(8):
                        P.add('dve', lambda e, h=h: e.tensor_scalar(out=sel[:, h, :], in0=s_m[:, h, :], scalar1=top8[:, h, 2:3],
                                                                    scalar2=None, op0=ALU.is_ge),
                              reads=['s_m', ('top8', h)], writes=[('sel', h)])
                    P.add('dve', lambda e: e.tensor_scalar(out=val[:], in0=s_m[:], scalar1=-0.5 * BIG, scalar2=None, op0=ALU.is_gt),
                          reads=['s_m'], writes=['val'])
                    P.add('dve', lambda e: e.tensor_tensor(out=sel[:], in0=sel[:], in1=val[:], op=ALU.mult),
                          reads=[('sel', h) for h in range(8)] + ['val'], writes=[('sel', h) for h in range(8)])
                    P.add('dve', lambda e: e.tensor_scalar(out=Qaug[:, ti, :, 0:16], in0=sel[:], scalar1=-1.0, scalar2=BIG,
                                                           op0=ALU.add, op1=ALU.mult),
                          reads=[('sel', h) for h in range(8)], writes=[('Qb', ti)])
                    P.add('dve', lambda e: e.memset(Qaug[:, ti, :, 8 + m:9 + m], 0.0), reads=[('Qb', ti)], writes=[('Qb', ti)])
                    for h in range(8):
                        P.add('pe', lambda e, h=h: e.transpose(out=psBT[0:16, h, :], in_=Qaug[:, ti, h, 0:16], identity=ident[:]),
                              reads=[('Qb', ti), 'ident'], writes=['psBT'])
                    P.add('act', lambda e: e.activation(out=qa[0:16, :, a * 128:(a + 1) * 128], in_=psBT[0:16], func=AF.Copy),
                          reads=['psBT'], writes=[qtok(ti)])

                qgs = [int(v) for v in os.environ.get('KQG', '0,1,2,3').split(',')]
                step = 0
                def attn_group(qg):
                    nonlocal step
                    qa = QaT[qg % 2]
                    for ti in range(4 * qg, 4 * qg + 4):
                        prep_qtile(ti, qa)
                    qtoks = [qtok(ti) for ti in range(4 * qg, 4 * qg + 4)]
                    nkt = 20 + 4 * qg
                    for h in range(8):
                        po = psO[h % 2]
                        ob = osb[h % 2]

                        def qk(kt, h=h):
                            pa = psA[kt % 2]
                            P.add('pe', lambda e: e.matmul(pa[:], lhsT=KaT[0:80, h, kt * 128:(kt + 1) * 128], rhs=qa[0:80, h, :],
                                                           start=True, stop=True),
                                  reads=qtoks + [('KaT', kt)], writes=[('psA', kt % 2)])

                        def rest(kt, h=h, po=po):
                            nonlocal step
                            pa = psA[kt % 2]
                            pt = PT[step % 3]
                            pk = ('PT', step % 3)
                            step += 1
                            P.add('act', lambda e: e.activation(out=pt[:], in_=pa[:], func=AF.Exp, scale=0.125),
                                  reads=[('psA', kt % 2)], writes=[pk])
                            j = kt - (nkt - 4)
                            if j >= 0:
                                P.add('pool', lambda e: e.tensor_tensor(out=pt[:], in0=pt[:], in1=mask4[:, j, :], op=ALU.mult),
                                      reads=[pk, 'mask4'], writes=[pk])
                            P.add('pe', lambda e: e.matmul(po[0:65, :], lhsT=Vaug[:, kt, h, 0:65], rhs=pt[:],
                                                           start=(kt == 0), stop=(kt == nkt - 1)),
                                  reads=[pk, ('Vaug', kt), 'Vones'], writes=[('psO', h % 2)])

                        qk(0)
                        for kt in range(nkt):
                            if kt + 1 < nkt:
                                qk(kt + 1)
                            rest(kt)
                        P.add('act', lambda e, po=po, ob=ob: e.activation(out=ob[0:64, :], in_=po[0:64, :], func=AF.Copy),
                              reads=[('psO', h % 2)], writes=[('osb', h % 2)])
                        P.add('dve', lambda e, po=po: e.reciprocal(out=rsum[64:65, :], in_=po[64:65, :]),
                              reads=[('psO', h % 2)], writes=['rsum'])
                        P.add('pe', lambda e: e.matmul(psR[0:64, :], lhsT=onesf[64:65, 0:64], rhs=rsum[64:65, :], start=True, stop=True),
                              reads=['rsum', 'onesf'], writes=['psR'])
                        pofs = (h % 2) * 64
                        og = ostg[h % 2]
                        P.add('dve', lambda e, ob=ob, og=og: e.tensor_tensor(
                            out=og[0:64, :], in0=ob[0:64, :], in1=psR[0:64, :], op=ALU.mult),
                            reads=[('osb', h % 2), 'psR'], writes=[('ostg', h % 2)])
                        P.add('sp', lambda e, og=og, h=h, pofs=pofs, qg=qg: e.dma_start(
                            out=ot_scr[pofs:pofs + 64, h // 2, qg * 512:(qg + 1) * 512], in_=og[0:64, :]),
                            reads=[('ostg', h % 2)], writes=[('OT', qg, h)], dma=f'og{h % 2}')
                for qg in qgs:
                    attn_group(qg)
                if DBG:
                    P.barrier()
                    dOT = nc.dram_tensor("dbg_OT", [128, 4, 2048], BF16, kind="ExternalOutput").ap()
                    P.add('sp', lambda e: e.dma_start(out=dOT, in_=ot_scr), dma='dbg')
                    dbg('Qaug', Qaug[:], [128, 16, 8, 80], BF16, [('Qb', ti) for ti in range(16)])
                P.barrier()
                P.emit()

        TWO_PI = 6.283185307179586
        PI_LO = 3.1415925
        HALF_PI = 1.5707963267948966

        def D_(fn):
            P.add('dve', fn, reads=['g'], writes=['g'])

        def A_(fn):
            P.add('act', fn, reads=['g'], writes=['g'])

        def rr(x, tf, ti):
            D_(lambda e: e.tensor_scalar(out=tf, in0=x, scalar1=1.0 / TWO_PI, scalar2=None, op0=ALU.mult))
            D_(lambda e: e.tensor_copy(out=ti, in_=tf))
            D_(lambda e: e.tensor_copy(out=tf, in_=ti))
            D_(lambda e: e.scalar_tensor_tensor(out=x, in0=tf, scalar=-TWO_PI, in1=x, op0=ALU.mult, op1=ALU.add))
            D_(lambda e: e.tensor_scalar(out=tf, in0=x, scalar1=PI_LO, scalar2=None, op0=ALU.is_gt))
            D_(lambda e: e.scalar_tensor_tensor(out=x, in0=tf, scalar=-TWO_PI, in1=x, op0=ALU.mult, op1=ALU.add))
            D_(lambda e: e.tensor_scalar(out=tf, in0=x, scalar1=-PI_LO, scalar2=None, op0=ALU.is_lt))
            D_(lambda e: e.scalar_tensor_tensor(out=x, in0=tf, scalar=TWO_PI, in1=x, op0=ALU.mult, op1=ALU.add))
            D_(lambda e: e.tensor_scalar(out=x, in0=x, scalar1=-PI_LO, scalar2=PI_LO, op0=ALU.max, op1=ALU.min))

        def powtab(sbx, tagn, AR, AI, LDT, F, kv, K, coef):
            n = F * K
            al = sbx(tagn + "al", [128, F], F32); th = sbx(tagn + "th", [128, F], F32)
            tfs = sbx(tagn + "tfs", [128, F], F32); tis = sbx(tagn + "tis", [128, F], I32)
            PR = sbx(tagn + "PR", [128, n], F32); PI = sbx(tagn + "PI", [128, n], F32)
            ang = sbx(tagn + "ang", [128, n], F32); tf = sbx(tagn + "tf", [128, n], F32); ti = sbx(tagn + "ti", [128, n], I32)
            A_(lambda e: e.activation(out=tfs[:], in_=LDT, func=AF.Exp))
            D_(lambda e: e.tensor_tensor(out=al[:], in0=AR, in1=tfs[:], op=ALU.mult))
            D_(lambda e: e.tensor_tensor(out=th[:], in0=AI, in1=tfs[:], op=ALU.mult))
            rr(th[:], tfs[:], tis[:])
            v3 = lambda t: t[:].rearrange("q (f k) -> q f k", k=K)
            thb = bcast(th[:].unsqueeze(2), [128, F, K]); alb = bcast(al[:].unsqueeze(2), [128, F, K])
            kvb = bcast(kv.unsqueeze(1), [128, F, K])
            D_(lambda e: e.tensor_tensor(out=v3(ang), in0=thb, in1=kvb, op=ALU.mult))
            rr(ang[:], tf[:], ti[:])
            A_(lambda e: e.activation(out=PI[:], in_=ang[:], func=AF.Sin))
            D_(lambda e: e.tensor_scalar(out=ang[:], in0=ang[:], scalar1=HALF_PI, scalar2=None, op0=ALU.add))
            rr(ang[:], tf[:], ti[:])
            A_(lambda e: e.activation(out=PR[:], in_=ang[:], func=AF.Sin))
            D_(lambda e: e.tensor_tensor(out=v3(tf), in0=alb, in1=kvb, op=ALU.mult))
            A_(lambda e: e.activation(out=tf[:], in_=tf[:], func=AF.Exp))
            D_(lambda e: e.tensor_tensor(out=PR[:], in0=PR[:], in1=tf[:], op=ALU.mult))
            D_(lambda e: e.tensor_tensor(out=PI[:], in0=PI[:], in1=tf[:], op=ALU.mult))
            res = dict(PR=PR, PI=PI, al=al, th=th, tf=tf, ang=ang, ti=ti, tfs=tfs, tis=tis)
            if coef is not None:
                k1 = coef
                cr = sbx(tagn + "cr", [128, F], F32); ci = sbx(tagn + "ci", [128, F], F32)
                nr = sbx(tagn + "nr", [128, F], F32); den = sbx(tagn + "den", [128, F], F32)
                p1r = v3(PR)[:, :, k1]; p1i = v3(PI)[:, :, k1]
                D_(lambda e: e.tensor_scalar(out=nr[:], in0=p1r, scalar1=-1.0, scalar2=None, op0=ALU.add))
                D_(lambda e: e.tensor_tensor(out=den[:], in0=AR, in1=AR, op=ALU.mult))
                D_(lambda e: e.tensor_tensor(out=tfs[:], in0=AI, in1=AI, op=ALU.mult))
                D_(lambda e: e.tensor_tensor(out=den[:], in0=den[:], in1=tfs[:], op=ALU.add))
                D_(lambda e: e.reciprocal(out=den[:], in_=den[:]))
                D_(lambda e: e.tensor_tensor(out=cr[:], in0=nr[:], in1=AR, op=ALU.mult))
                D_(lambda e: e.tensor_tensor(out=tfs[:], in0=p1i, in1=AI, op=ALU.mult))
                D_(lambda e: e.tensor_tensor(out=cr[:], in0=cr[:], in1=tfs[:], op=ALU.add))
                D_(lambda e: e.tensor_tensor(out=cr[:], in0=cr[:], in1=den[:], op=ALU.mult))
                D_(lambda e: e.tensor_tensor(out=ci[:], in0=p1i, in1=AR, op=ALU.mult))
                D_(lambda e: e.tensor_tensor(out=tfs[:], in0=nr[:], in1=AI, op=ALU.mult))
                D_(lambda e: e.tensor_tensor(out=ci[:], in0=ci[:], in1=tfs[:], op=ALU.subtract))
                D_(lambda e: e.tensor_tensor(out=ci[:], in0=ci[:], in1=den[:], op=ALU.mult))
                res['cr'] = cr; res['ci'] = ci
            return res

        def cmul(outr, outi, ar_, ai_, br_, bi_, tmp):
            D_(lambda e: e.tensor_tensor(out=outr, in0=ar_, in1=br_, op=ALU.mult))
            D_(lambda e: e.tensor_tensor(out=tmp, in0=ai_, in1=bi_, op=ALU.mult))
            D_(lambda e: e.tensor_tensor(out=outr, in0=outr, in1=tmp, op=ALU.subtract))
            D_(lambda e: e.tensor_tensor(out=outi, in0=ar_, in1=bi_, op=ALU.mult))
            D_(lambda e: e.tensor_tensor(out=tmp, in0=ai_, in1=br_, op=ALU.mult))
            D_(lambda e: e.tensor_tensor(out=outi, in0=outi, in1=tmp, op=ALU.add))

        sS = ExitStack()
        with sS:
            sbS, psS_ = mk(sS)
            F32R = mybir.dt.float32r
            ck_rows = ckh.rearrange("(r h) d -> r (h d)", h=8)
            ptrep = sbS("ptrep", [128, 256], I32); idxP = sbS("idxP", [128, 256], I32)
            piota = sbS("piota", [128, 1], F32); hb = sbS("hb", [128, 8], F32)
            zsel = sbS("zsel", [128, 255], F32); rsel = sbS("rsel", [4, 128], F32)
            eye8 = sbS("eye8", [8, 8], F32); esel = sbS("esel", [4, 4], F32)
            ptE = sbS("ptE", [8, 128], I32); ptO = sbS("ptO", [8, 128], I32)
            ptEf = sbS("ptEf", [8, 128], F32); ptOf = sbS("ptOf", [8, 128], F32)
            ones1 = sbS("ones1", [128, 128], F32)
            pgb = [sbS(f"pgb{i}", [128, 512], F32) for i in range(8)]
            for dst, srcd in ((ptrep, ptrep_d), (piota, piota_d), (hb, hb_d), (zsel, zsel_d), (rsel, rsel_d), (eye8, eye8_d),
                              (esel, esel_d), (ptE, ptE_d), (ptO, ptO_d)):
                P.add('sp', lambda e, dst=dst, srcd=srcd: e.dma_start(out=dst[:], in_=srcd), writes=['g'], dma='c')
            P.add('pool', lambda e: e.memset(ones1[:], 1.0), writes=['g'])
            P.barrier()
            D_(lambda e: e.tensor_scalar(out=idxP[:], in0=ptrep[:], scalar1=128.0, scalar2=piota[:, 0:1], op0=ALU.mult, op1=ALU.add))
            D_(lambda e: e.tensor_copy(out=ptEf[:], in_=ptE[:]))
            D_(lambda e: e.tensor_copy(out=ptOf[:], in_=ptO[:]))
            psKS = psS_("psKS", [128, 512], F32)
            psQR = psS_("psQR", [128, 512], F32)
            psM = psS_("psM", [128, 512], F32)
            for i in range(256):
                sl = i % 8
                P.add('pool', lambda e, i=i, sl=sl: e.indirect_dma_start(
                    out=pgb[sl][:], out_offset=None, in_=ck_rows,
                    in_offset=bass.IndirectOffsetOnAxis(ap=idxP[:, i:i + 1], axis=0)),
                    reads=['g'], writes=[('pgb', sl)], dma=f'pg{sl}')
                m = (i // 64) * 32 + (i % 64) // 2
                P.add('pe', lambda e, i=i, sl=sl, m=m: e.matmul(psKS[:], lhsT=zsel[:, 127 - m:255 - m], rhs=pgb[sl][:],
                                                              start=(i == 0), stop=(i == 255)),
                      reads=[('pgb', sl), 'g'], writes=['psKS'])
            qrep = sbS("qrep", [128, 512], F32); prodS = sbS("prodS", [128, 512], F32); sbl = sbS("sbl", [128, 8], F32)
            P.add('pe', lambda e: e.matmul(psQR[:], lhsT=rsel[:], rhs=qs_f[:].rearrange("s h d -> s (h d)"), start=True, stop=True),
                  reads=['qs_f', 'g'], writes=['psQR'])
            P.add('act', lambda e: e.activation(out=qrep[:], in_=psQR[:], func=AF.Copy), reads=['psQR'], writes=['g'])
            D_(lambda e: e.tensor_tensor(out=prodS[:], in0=qrep[:], in1=psKS[:], op=ALU.mult))
            P.add('dve', lambda e: e.tensor_reduce(out=sbl[:], in_=prodS[:].rearrange("q (h d) -> q h d", d=64), axis=AX.X, op=ALU.add),
                  reads=['g', 'psKS'], writes=['g'])
            P.add('pe', lambda e: e.transpose(out=psM[0:8, 0:128], in_=sbl[:], identity=identf[:]), reads=['g', 'identf'], writes=['psM'])
            sbT = sbS("sbT", [8, 128], F32); top8s = sbS("top8s", [8, 4, 8], F32)
            selr = sbS("selr", [8, 3, 128], F32); tmpS = sbS("tmpS", [8, 3, 128], F32)
            PG = sbS("PG", [8, 2, 12], F32); rhsE = sbS("rhsE", [8, 8, 24], F32)
            P.add('act', lambda e: e.activation(out=sbT[:], in_=psM[0:8, 0:128], func=AF.Copy), reads=['psM'], writes=['g'])
            for s_ in range(4):
                D_(lambda e, s_=s_: e.max(out=top8s[:, s_, :], in_=sbT[:, s_ * 32:(s_ + 1) * 32]))
            for r_ in range(3):
                for s_ in range(4):
                    D_(lambda e, s_=s_, r_=r_: e.tensor_scalar(out=selr[:, r_, s_ * 32:(s_ + 1) * 32], in0=sbT[:, s_ * 32:(s_ + 1) * 32],
                                                               scalar1=top8s[:, s_, r_:r_ + 1], scalar2=None, op0=ALU.is_equal))
            for e_, ptf in enumerate((ptEf, ptOf)):
                D_(lambda e, ptf=ptf: e.tensor_tensor(out=tmpS[:], in0=selr[:], in1=bcast(ptf[:].unsqueeze(1), [8, 3, 128]), op=ALU.mult))
                D_(lambda e, e_=e_: e.tensor_reduce(out=PG[:, e_, :], in_=tmpS[:].rearrange("h r (s n) -> h (r s) n", n=32),
                                                    axis=AX.X, op=ALU.add))
            D_(lambda e: e.tensor_tensor(out=rhsE[:], in0=bcast(PG[:].rearrange("h e x -> h (e x)").unsqueeze(1), [8, 8, 24]),
                                         in1=bcast(eye8[:].unsqueeze(2), [8, 8, 24]), op=ALU.mult))
            P.add('pe', lambda e: e.matmul(psM[:, 0:192], lhsT=ones1[0:8, :], rhs=rhsE[:].rearrange("k h x -> k (h x)"), start=True, stop=True),
                  reads=['g'], writes=['psM'])
            idxG = sbS("idxG", [128, 8, 24], I32)
            P.add('dve', lambda e: e.scalar_tensor_tensor(out=idxG[:], in0=psM[:, 0:192].rearrange("q (h x) -> q h x", h=8), scalar=1024.0,
                                                          in1=bcast(hb[:].unsqueeze(2), [128, 8, 24]), op0=ALU.mult, op1=ALU.add),
                  reads=['psM', 'g'], writes=['g'])
            Ksel = sbS("Ksel", [128, 4, 8, 6, 64], F32); Vsel = sbS("Vsel", [128, 4, 8, 6, 64], F32)
            for h in range(8):
                for e_ in range(2):
                    for r_ in range(3):
                        for s_ in range(4):
                            x = e_ * 12 + r_ * 4 + s_
                            for srcd, dstt in ((ckh, Ksel), (cvh, Vsel)):
                                P.add('pool', lambda e, h=h, x=x, s_=s_, j=r_ * 2 + e_, srcd=srcd, dstt=dstt: e.indirect_dma_start(
                                    out=dstt[:, s_, h, j, :], out_offset=None, in_=srcd,
                                    in_offset=bass.IndirectOffsetOnAxis(ap=idxG[:, h, x:x + 1], axis=0)),
                                    reads=['g'], writes=[('KVsel', h, x, id(dstt))], dma='gk')
            P.barrier()
            qbc = sbS("qbc", [128, 4, 512], F32); sc = sbS("sc", [128, 4, 8, 6], F32); Wn = sbS("Wn", [128, 4, 512], F32)
            Wd = sbS("Wd", [128, 4, 512], F32); psum_j = sbS("psumj", [128, 4, 8], F32)
            big = sbS("bigS", [128, 8, 6, 64], F32)
            for s_ in range(4):
                P.add('pe', lambda e, s_=s_: e.matmul(psQR[:], lhsT=bcast(esel[:, s_:s_ + 1], [4, 128]), rhs=qs_f[:].rearrange("s h d -> s (h d)"),
                                                      start=True, stop=True), reads=['qs_f', 'g'], writes=['psQR'])
                P.add('act', lambda e, s_=s_: e.activation(out=qbc[:, s_, :], in_=psQR[:], func=AF.Copy), reads=['psQR'], writes=['g'])
            for s_ in range(4):
                D_(lambda e, s_=s_: e.tensor_tensor(out=big[:], in0=Ksel[:, s_],
                                                    in1=bcast(qbc[:, s_, :].rearrange("q (h d) -> q h d", d=64).unsqueeze(2), [128, 8, 6, 64]),
                                                    op=ALU.mult))
                D_(lambda e, s_=s_: e.tensor_reduce(out=sc[:, s_], in_=big[:], axis=AX.X, op=ALU.add))
            A_(lambda e: e.activation(out=sc[:], in_=sc[:], func=AF.Exp, scale=0.125))
            D_(lambda e: e.tensor_reduce(out=psum_j[:], in_=sc[:], axis=AX.X, op=ALU.add))
            for s_ in range(4):
                D_(lambda e, s_=s_: e.tensor_tensor(out=big[:], in0=Vsel[:, s_], in1=bcast(sc[:, s_].unsqueeze(3), [128, 8, 6, 64]), op=ALU.mult))
                D_(lambda e, s_=s_: e.tensor_reduce(out=Wn[:, s_, :].rearrange("q (h d) -> q h d", d=64),
                                                    in_=big[:].rearrange("q h j d -> q h d j"), axis=AX.X, op=ALU.add))
            D_(lambda e: e.tensor_copy(out=Wd[:].rearrange("q s (h d) -> q s h d", d=64), in_=bcast(psum_j[:].unsqueeze(3), [128, 4, 8, 64])))
            ssf = sbS("ssf", [4, 8], F32); prs = sbS("prs", [4, 8, 64], F32); Wsn = sbS("Wsn", [4, 512], F32); Wsd = sbS("Wsd", [4, 512], F32)
            D_(lambda e: e.tensor_tensor(out=prs[:], in0=qs_f[:], in1=ks_f[:], op=ALU.mult))
            D_(lambda e: e.tensor_reduce(out=ssf[:], in_=prs[:], axis=AX.X, op=ALU.add))
            A_(lambda e: e.activation(out=ssf[:], in_=ssf[:], func=AF.Exp, scale=0.125))
            D_(lambda e: e.tensor_tensor(out=Wsn[:].rearrange("s (h d) -> s h d", d=64), in0=vs_f[:], in1=bcast(ssf[:].unsqueeze(2), [4, 8, 64]),
                                         op=ALU.mult))
            D_(lambda e: e.tensor_copy(out=Wsd[:].rearrange("s (h d) -> s h d", d=64), in_=bcast(ssf[:].unsqueeze(2), [4, 8, 64])))
            for wi, (Wbig, Wself) in enumerate(((Wn, Wsn), (Wd, Wsd))):
                for kc in range(4):
                    for s_ in range(4):
                        col = wi * 16 + kc * 4 + s_
                        P.add('pe', lambda e, Wbig=Wbig, kc=kc, s_=s_, col=col: e.matmul(
                            psM[:, col:col + 1], lhsT=Wbig[:, s_, kc * 128:(kc + 1) * 128], rhs=ones1[:, 0:1], start=True, stop=False),
                            reads=['g'], writes=['psM'])
                        P.add('pe', lambda e, Wself=Wself, kc=kc, s_=s_, col=col: e.matmul(
                            psM[:, col:col + 1], lhsT=Wself[:, kc * 128:(kc + 1) * 128], rhs=esel[:, s_:s_ + 1], start=False, stop=True),
                            reads=['g'], writes=['psM'])
            rden = sbS("rden", [128, 16], F32)
            P.add('dve', lambda e: e.reciprocal(out=rden[:], in_=psM[:, 16:32]), reads=['psM'], writes=['g'])
            P.add('dve', lambda e: e.tensor_tensor(out=OTs[:].rearrange("q k s -> q (k s)"), in0=psM[:, 0:16], in1=rden[:], op=ALU.mult),
                  reads=['psM', 'g'], writes=['OTs'])
            P.barrier()
            P.emit()

        s3 = ExitStack()
        with s3:
            sb3, ps3 = mk(s3)
            Yc = sb3("Yc", [128, 4, 16, 128], BF16)
            Ycs = sb3("Ycs", [4, 4, 128], BF16)
            s3a = ExitStack()
            with s3a:
                sb3a, ps3a = mk(s3a)
                Ssr = sb3a("Ssr", [128, 16, 4], F32)
                Ssi = sb3a("Ssi", [128, 16, 4], F32)
                SR = sb3a("SR", [128, 16, 256], F32)
                SI = sb3a("SI", [128, 16, 256], F32)
                kv0 = sb3a("kv0s", [128, 16], F32); kv1 = sb3a("kv1s", [128, 16], F32); kvj = sb3a("kvjs", [128, 256], F32)
                pmB = sb3a("pmBs", [128, 2], F32); pmD = sb3a("pmDs", [128, 2], F32)
                for i_, (dst, srcd) in enumerate(((kv0, kv0_d), (kv1, kv1_d), (kvj, kvj_d), (pmB, pmB_d), (pmD, pmD_d))):
                    P.add('sp', lambda e, dst=dst, srcd=srcd: e.dma_start(out=dst[:], in_=srcd), writes=['g'], dma='c')
                sB = ExitStack()
                with sB:
                    sbB, psB = mk(sB)
                    VBre = sbB("VBre", [128, 4, 16, 128], BF16)
                    VBim = sbB("VBim", [128, 4, 16, 128], BF16)
                    sg = ExitStack()
                    with sg:
                        sbg, _ = mk(sg)
                        tA = {}
                        for nm, srcd in (('ar', arB), ('ai', aiB), ('ldt', ldtB), ('br', brB), ('bi', biB)):
                            tA[nm] = sbg("gB" + nm, [128, 256], F32)
                            P.add('sp', lambda e, t=tA[nm], srcd=srcd: e.dma_start(out=t[:], in_=srcd), writes=['g'], dma='c')
                        P.barrier()
                        for hf in range(2):
                            sh = ExitStack()
                            with sh:
                                sbh, _ = mk(sh)
                                fs = slice(hf * 128, (hf + 1) * 128)
                                r = powtab(sbh, f"pB{hf}", tA['ar'][:, fs], tA['ai'][:, fs], tA['ldt'][:, fs], 128, kv0[:], 16, coef=1)
                                Zr = sbh(f"Zr{hf}", [128, 2048], F32); Zi = sbh(f"Zi{hf}", [128, 2048], F32)
                                v3 = lambda t: t[:].rearrange("q (f k) -> q f k", k=16)
                                crb = bcast(r['cr'][:].unsqueeze(2), [128, 128, 16]); cib = bcast(r['ci'][:].unsqueeze(2), [128, 128, 16])
                                cmul(v3(Zr), v3(Zi), v3(r['PR']), v3(r['PI']), crb, cib, v3(r['tf']))
                                brb = bcast(tA['br'][:, fs].unsqueeze(2), [128, 128, 16]); bib = bcast(tA['bi'][:, fs].unsqueeze(2), [128, 128, 16])
                                Xr, Xi = r['PR'], r['PI']
                                cmul(v3(Xr), v3(Xi), v3(Zr), v3(Zi), brb, bib, v3(r['tf']))
                                for gw in range(2):
                                    for Xs, VB in ((Xr, VBre), (Xi, VBim)):
                                        D_(lambda e, gw=gw, Xs=Xs, VB=VB, hf=hf: e.tensor_scalar(
                                            out=VB[:, 2 * hf:2 * hf + 2, :, gw * 64:(gw + 1) * 64],
                                            in0=Xs[:].rearrange("q (c p d) -> q c d p", c=2, p=64, d=16),
                                            scalar1=pmB[:, gw:gw + 1], scalar2=None, op0=ALU.mult))
                                P.barrier()
                                P.emit()
                        P.barrier()
                        P.emit()
                    UT = sbB("UT", [128, 4, 4096], BF16)
                    for c in range(4):
                        P.add('sp', lambda e, c=c: e.dma_start(out=UT[:, c, :], in_=ut_scr[:, c, :]),
                              reads=[('utscr', t) for t in range(32)], writes=['UT'], dma='c')
                    P.barrier()
                    psS2 = [psB(f"psSB{i}", [128, 256], F32) for i in range(4)]
                    for w in range(16):
                        kc, r0 = w // 4, 32 * (w % 4)
                        for ri, (VB, SX) in enumerate(((VBre, SR), (VBim, SI))):
                            pst = psS2[(2 * w + ri) % 4]
                            for dl in range(16):
                                rhs = UT[r0:r0 + 32, kc, :].rearrange("q (j s) -> q j s", s=16)[:, :, 15 - dl]
                                P.add('pe', lambda e, pst=pst, VB=VB, dl=dl, rhs=rhs, r0=r0, kc=kc: e.matmul(
                                    pst[:], lhsT=VB[r0:r0 + 32, kc, dl, :], rhs=rhs, start=(dl == 0), stop=(dl == 15),
                                    tile_position=(r0, 0)),
                                    reads=['g', 'UT'], writes=[('psSB', (2 * w + ri) % 4)])
                            P.add('act', lambda e, pst=pst, SX=SX, w=w: e.activation(out=SX[:, w, :], in_=pst[:], func=AF.Copy),
                                  reads=[('psSB', (2 * w + ri) % 4)], writes=[('S', ri, w)])
                    for w in range(16):
                        kc, r0 = w // 4, 32 * (w % 4)
                        for ri, (VB, SX) in enumerate(((VBre, Ssr), (VBim, Ssi))):
                            pst = psS2[(2 * w + ri) % 4]
                            P.add('pe', lambda e, pst=pst, VB=VB, r0=r0, kc=kc: e.matmul(
                                pst[:, 0:4], lhsT=VB[r0:r0 + 32, kc, 0, :], rhs=usT[r0:r0 + 32, kc, :], start=True, stop=True,
                                tile_position=(r0, 0)),
                                reads=['g', 'usT'], writes=[('psSB', (2 * w + ri) % 4)])
                            P.add('act', lambda e, pst=pst, SX=SX, w=w: e.activation(out=SX[:, w, :], in_=pst[:, 0:4], func=AF.Copy),
                                  reads=[('psSB', (2 * w + ri) % 4)], writes=[('Ss', ri)])
                    P.barrier()
                    P.emit()
                sC = ExitStack()
                with sC:
                    sbC, psC = mk(sC)
                    Hre = sbC("Hre", [128, 16, 128], BF16)
                    Him = sbC("Him", [128, 16, 128], BF16)
                    WDre = sbC("WDre", [128, 16, 512], BF16)
                    WDim = sbC("WDim", [128, 16, 512], BF16)
                    cosT = sbC("cosT", [128, 4096], F32); sinT = sbC("sinT", [128, 4096], F32)
                    H0re = sbC("H0re", [128, 16, 4], BF16); H0im = sbC("H0im", [128, 16, 4], BF16)
                    M16 = sbC("M16", [128, 16], F32)
                    tD = {}
                    for nm, srcd, n_ in (('ar', arD, 16), ('ai', aiD, 16), ('ldt', ldtD, 16), ('cr', crD, 256), ('ci', ciD, 256)):
                        tD[nm] = sbC("gD" + nm, [128, n_], F32)
                        P.add('sp', lambda e, t=tD[nm], srcd=srcd: e.dma_start(out=t[:], in_=srcd), writes=['g'], dma='c')
                    P.barrier()
                    sg2 = ExitStack()
                    with sg2:
                        sbg2, _ = mk(sg2)
                        r = powtab(sbg2, "pD", tD['ar'][:], tD['ai'][:], tD['ldt'][:], 16, kv1[:], 16, coef=None)
                        Wr = sbg2("Wr", [128, 4096], F32); Wi = sbg2("Wi", [128, 4096], F32); Wt = sbg2("Wt", [128, 4096], F32)
                        v4 = lambda t: t[:].rearrange("q (w k c) -> q w k c", w=16, k=16, c=16)
                        prb = bcast(r['PR'][:].rearrange("q (w k) -> q w k", k=16).unsqueeze(3), [128, 16, 16, 16])
                        pib = bcast(r['PI'][:].rearrange("q (w k) -> q w k", k=16).unsqueeze(3), [128, 16, 16, 16])
                        crb = bcast(tD['cr'][:].rearrange("q (w c) -> q w c", c=16).unsqueeze(2), [128, 16, 16, 16])
                        cib = bcast(tD['ci'][:].rearrange("q (w c) -> q w c", c=16).unsqueeze(2), [128, 16, 16, 16])
                        cmul(v4(Wr), v4(Wi), crb, cib, prb, pib, v4(Wt))
                        for gw in range(2):
                            D_(lambda e, gw=gw: e.tensor_scalar(
                                out=WDre[:].rearrange("q w (g x) -> q w g x", g=2)[:, :, gw, :],
                                in0=Wr[:].rearrange("q (w x) -> q w x", w=16), scalar1=pmD[:, gw:gw + 1], scalar2=None, op0=ALU.mult))
                            D_(lambda e, gw=gw: e.tensor_scalar(
                                out=WDim[:].rearrange("q w (g x) -> q w g x", g=2)[:, :, gw, :],
                                in0=Wi[:].rearrange("q (w x) -> q w x", w=16), scalar1=pmD[:, gw:gw + 1], scalar2=-1.0,
                                op0=ALU.mult, op1=ALU.mult))
                        h0r = sbg2("h0r_s", [128, 64], F32); h0i = sbg2("h0i_s", [128, 64], F32)
                        hnr = sbg2("hnr", [128, 64], F32); hni = sbg2("hni", [128, 64], F32); htm = sbg2("htm", [128, 64], F32)
                        P.add('sp', lambda e: e.dma_start(out=h0r[:], in_=h0r_d), writes=['g'], dma='c')
                        P.add('sp', lambda e: e.dma_start(out=h0i[:], in_=h0i_d), writes=['g'], dma='c')
                        P.barrier()
                        w3 = lambda t: t[:].rearrange("q (w s) -> q w s", s=4)
                        a1r = bcast(r['PR'][:].rearrange("q (w k) -> q w k", k=16)[:, :, 0:1], [128, 16, 4])
                        a1i = bcast(r['PI'][:].rearrange("q (w k) -> q w k", k=16)[:, :, 0:1], [128, 16, 4])
                        cmul(w3(hnr), w3(hni), a1r, a1i, w3(h0r), w3(h0i), w3(htm))
                        D_(lambda e: e.tensor_tensor(out=w3(hnr), in0=w3(hnr), in1=Ssr[:], op=ALU.add))
                        D_(lambda e: e.tensor_tensor(out=w3(hni), in0=w3(hni), in1=Ssi[:], op=ALU.add))
                        P.add('sp', lambda e: e.dma_start(out=hsr, in_=hnr[:]), reads=['g'], dma='c')
                        P.add('sp', lambda e: e.dma_start(out=hsi, in_=hni[:]), reads=['g'], dma='c')
                        D_(lambda e: e.tensor_copy(out=H0re[:], in_=w3(h0r)))
                        D_(lambda e: e.tensor_copy(out=H0im[:], in_=w3(h0i)))
                        TH = sbg2("TH", [128, 16], F32)
                        D_(lambda e: e.tensor_scalar(out=TH[:], in0=r['th'][:], scalar1=16.0, scalar2=None, op0=ALU.mult))
                        rr(TH[:], r['tfs'][:], r['tis'][:])
                        A16 = sbg2("A16", [128, 16], F32)
                        D_(lambda e: e.tensor_scalar(out=A16[:], in0=r['al'][:], scalar1=16.0, scalar2=None, op0=ALU.mult))
                        A_(lambda e: e.activation(out=M16[:], in_=A16[:], func=AF.Exp))
                        angJ = Wt
                        tfJ = Wr
                        tiJ = sbg2("tiJ", [128, 4096], I32)
                        D_(lambda e: e.tensor_tensor(out=angJ[:].rearrange("q (w j) -> q w j", j=256),
                                                     in0=bcast(TH[:].unsqueeze(2), [128, 16, 256]),
                                                     in1=bcast(kvj[:].unsqueeze(1), [128, 16, 256]), op=ALU.mult))
                        rr(angJ[:], tfJ[:], tiJ[:])
                        A_(lambda e: e.activation(out=sinT[:], in_=angJ[:], func=AF.Sin))
                        D_(lambda e: e.tensor_scalar(out=angJ[:], in0=angJ[:], scalar1=HALF_PI, scalar2=None, op0=ALU.add))
                        rr(angJ[:], tfJ[:], tiJ[:])
                        A_(lambda e: e.activation(out=cosT[:], in_=angJ[:], func=AF.Sin))
                        P.barrier()
                        P.emit()
                    RR = sbC("RR", [128, 4096], F32); RI = sbC("RI", [128, 4096], F32); Tt = sbC("Tt", [128, 4096], F32)
                    SRf = SR[:].rearrange("q w j -> q (w j)"); SIf = SI[:].rearrange("q w j -> q (w j)")
                    D_(lambda e: e.tensor_tensor(out=RR[:], in0=cosT[:], in1=SRf, op=ALU.mult))
                    D_(lambda e: e.tensor_tensor(out=Tt[:], in0=sinT[:], in1=SIf, op=ALU.mult))
                    D_(lambda e: e.tensor_tensor(out=RR[:], in0=RR[:], in1=Tt[:], op=ALU.add))
                    D_(lambda e: e.tensor_tensor(out=RI[:], in0=cosT[:], in1=SIf, op=ALU.mult))
                    D_(lambda e: e.tensor_tensor(out=Tt[:], in0=sinT[:], in1=SRf, op=ALU.mult))
                    D_(lambda e: e.tensor_tensor(out=RI[:], in0=RI[:], in1=Tt[:], op=ALU.subtract))
                    GR = SR; GI = SI
                    for w in range(16):
                        for Rx, Gx in ((RR, GR), (RI, GI)):
                            D_(lambda e, w=w, Rx=Rx, Gx=Gx: e.tensor_tensor_scan(
                                out=Gx[:, w, :], data0=bcast(M16[:, w:w + 1], [128, 256]), data1=Rx[:, w * 256:(w + 1) * 256],
                                initial=0.0, op0=ALU.mult, op1=ALU.add))
                    c3 = cosT[:].rearrange("q (w j) -> q w j", j=256); s3_ = sinT[:].rearrange("q (w j) -> q w j", j=256)
                    HRf = RR[:].rearrange("q (w j) -> q w j", j=256); HIf = RI[:].rearrange("q (w j) -> q w j", j=256)
                    T3 = Tt[:].rearrange("q (w j) -> q w j", j=256)
                    sl = slice(127, 256)
                    D_(lambda e: e.tensor_tensor(out=HRf[:, :, sl], in0=c3[:, :, sl], in1=GR[:, :, sl], op=ALU.mult))
                    D_(lambda e: e.tensor_tensor(out=T3[:, :, sl], in0=s3_[:, :, sl], in1=GI[:, :, sl], op=ALU.mult))
                    D_(lambda e: e.tensor_tensor(out=HRf[:, :, sl], in0=HRf[:, :, sl], in1=T3[:, :, sl], op=ALU.subtract))
                    D_(lambda e: e.tensor_tensor(out=HIf[:, :, sl], in0=c3[:, :, sl], in1=GI[:, :, sl], op=ALU.mult))
                    D_(lambda e: e.tensor_tensor(out=T3[:, :, sl], in0=s3_[:, :, sl], in1=GR[:, :, sl], op=ALU.mult))
                    D_(lambda e: e.tensor_tensor(out=HIf[:, :, sl], in0=HIf[:, :, sl], in1=T3[:, :, sl], op=ALU.add))
                    D_(lambda e: e.tensor_copy(out=Hre[:], in_=HRf[:, :, 127:255]))
                    D_(lambda e: e.tensor_copy(out=Him[:], in_=HIf[:, :, 127:255]))
                    hfR = sbC("hfR", [128, 16], F32); hfI = sbC("hfI", [128, 16], F32)
                    D_(lambda e: e.tensor_copy(out=hfR[:], in_=HRf[:, :, 255]))
                    D_(lambda e: e.tensor_copy(out=hfI[:], in_=HIf[:, :, 255]))
                    P.add('sp', lambda e: e.dma_start(out=hfr, in_=hfR[:]), reads=['g'], dma='c')
                    P.add('sp', lambda e: e.dma_start(out=hfi, in_=hfI[:]), reads=['g'], dma='c')
                    psD = [psC(f"psD{i}", [128, 512], F32) for i in range(2)]
                    for w in range(16):
                        pd = psD[w % 2]
                        P.add('pe', lambda e, pd=pd, w=w: e.matmul(pd[:], lhsT=Hre[:, w, :], rhs=WDre[:, w, :], start=True, stop=False),
                              reads=['g'], writes=[('psD', w % 2)])
                        P.add('pe', lambda e, pd=pd, w=w: e.matmul(pd[:], lhsT=Him[:, w, :], rhs=WDim[:, w, :], start=False, stop=True),
                              reads=['g'], writes=[('psD', w % 2)])
                        P.add('act', lambda e, pd=pd, w=w: e.activation(
                            out=Yc[:, w // 4, :, (w % 4) * 32:(w % 4 + 1) * 32].rearrange("j t (g c) -> j t g c", g=2),
                            in_=pd[:].rearrange("j (g t c) -> j t g c", g=2, t=16), func=AF.Copy),
                              reads=[('psD', w % 2)], writes=['Yc'])
                    for w in range(16):
                        pd = psD[w % 2]
                        P.add('pe', lambda e, pd=pd, w=w: e.matmul(pd[0:4, :], lhsT=H0re[:, w, :], rhs=WDre[:, w, :], start=True, stop=False),
                              reads=['g'], writes=[('psD', w % 2)])
                        P.add('pe', lambda e, pd=pd, w=w: e.matmul(pd[0:4, :], lhsT=H0im[:, w, :], rhs=WDim[:, w, :], start=False, stop=True),
                              reads=['g'], writes=[('psD', w % 2)])
                        P.add('act', lambda e, pd=pd, w=w: e.activation(
                            out=Ycs[:, w // 4, (w % 4) * 32:(w % 4 + 1) * 32].rearrange("s (g c) -> s g c", g=2),
                            in_=pd[0:4, :].rearrange("s (g t c) -> s g t c", g=2, t=16)[:, :, 0, :], func=AF.Copy),
                              reads=[('psD', w % 2)], writes=['Ycs'])
                    P.barrier()
                    P.emit()

            def norm_transpose(xt_ap, nt, gain, hT_out, bufs, tg_, perm_a=None, hT_full=None):
                ssq_, rstd_, xn_, sqj_, psT_ = bufs
                P.add('act', lambda e: e.activation(out=sqj_[0:nt, :], in_=xt_ap, func=AF.Square, accum_out=ssq_[0:nt, :]),
                      reads=[tg_], writes=['n_sqj', 'n_ssq'])
                P.add('dve', lambda e: e.tensor_scalar(out=rstd_[0:nt, :], in0=ssq_[0:nt, :], scalar1=1.0 / D, scalar2=EPS,
                                                       op0=ALU.mult, op1=ALU.add), reads=['n_ssq'], writes=['n_rstd'])
                P.add('act', lambda e: e.activation(out=rstd_[0:nt, :], in_=rstd_[0:nt, :], func=AF.Sqrt),
                      reads=['n_rstd'], writes=['n_rstd'])
                P.add('dve', lambda e: e.reciprocal(out=rstd_[0:nt, :], in_=rstd_[0:nt, :]), reads=['n_rstd'], writes=['n_rstd'])
                P.add('dve', lambda e: e.tensor_scalar(out=xn_[0:nt, :], in0=xt_ap, scalar1=rstd_[0:nt, 0:1], scalar2=None,
                                                       op0=ALU.mult), reads=[tg_, 'n_rstd'], writes=['n_xn'])
                for kc in range(8):
                    P.add('pe', lambda e, kc=kc: e.transpose(out=psT_[:, kc, 0:nt], in_=xn_[0:nt, kc * 128:(kc + 1) * 128],
                                                             identity=ident[0:nt, 0:nt]),
                          reads=['n_xn', 'ident'], writes=['n_psT'])
                if perm_a is None:
                    P.add('dve', lambda e: e.tensor_tensor(out=hT_out, in0=psT_[:, :, 0:nt],
                                                           in1=bcast(gain[:].unsqueeze(2), [128, 8, nt]), op=ALU.mult),
                          reads=['n_psT', 'gains'], writes=['hTg'])
                else:
                    a_ = perm_a
                    P.add('dve', lambda e: e.tensor_tensor(
                        out=hT_full[:].rearrange("q k (s j) -> q k s j", s=16)[:, :, :, 8 * a_:8 * a_ + 8],
                        in0=psT_[:, :, 0:128].rearrange("q k (j s) -> q k s j", s=16),
                        in1=bcast(gain[:].unsqueeze(2).unsqueeze(3), [128, 8, 16, 8]), op=ALU.mult),
                        reads=['n_psT', 'gains'], writes=['hTg'])

            s3b = ExitStack()
            with s3b:
                sbb, psb = mk(s3b)
                wg = sbb("wg", [128, 8, 2048], BF16)
                wat = sbb("wat", [128, 4, 1024], BF16); wgl = sbb("wgl", [128, 4, 512], BF16)
                wss = sbb("wss", [128, 4, 1024], BF16); wou = sbb("wou", [128, 8, 1024], BF16)
                Gw = sbb("Gw", [128, 4, 16, 128], BF16)
                UTs = sbb("UTs", [128, 4, 2048], BF16)
                dvec = sbb("dvec", [128, 4], F32); bgl = sbb("bgls", [128, 4], F32); gmx2 = sbb("gmx2", [128, 8], F32)
                bm = sbb("bm", [128, 128], F32)
                pb = [psb(f"pb{i}", [128, 512], F32) for i in range(7)]
                psT2 = psb("psT2", [128, 8, 128], BF16)
                for kc in range(8):
                    for half in range(2):
                        P.add('pool', lambda e, kc=kc, half=half: e.dma_start(
                            out=wg[:, kc, half * 1024:(half + 1) * 1024],
                            in_=w_in[kc * 128:(kc + 1) * 128, 2048 + half * 1024:2048 + (half + 1) * 1024]), writes=['g'], dma='w')
                    P.add('pool', lambda e, kc=kc: e.dma_start(out=wou[:, kc, :], in_=w_out_d[kc * 128:(kc + 1) * 128, :]),
                          writes=['g'], dma='w')
                for kc in range(4):
                    P.add('pool', lambda e, kc=kc: e.dma_start(out=wat[:, kc, :], in_=w_attn_d[kc * 128:(kc + 1) * 128, :]), writes=['g'], dma='w')
                    P.add('pool', lambda e, kc=kc: e.dma_start(out=wss[:, kc, :], in_=w_ssm_d[kc * 128:(kc + 1) * 128, :]), writes=['g'], dma='w')
                    P.add('pool', lambda e, kc=kc: e.dma_start(out=wgl[:, kc, :], in_=w_glu_d[kc * 128:(kc + 1) * 128, :]), writes=['g'], dma='w')
                for dst, srcd in ((dvec, dvec_d), (bgl, bgl_d), (gmx2, gmix), (bm, bmask_d)):
                    P.add('sp', lambda e, dst=dst, srcd=srcd: e.dma_start(out=dst[:], in_=srcd), writes=['g'], dma='c')
                sga = ExitStack()
                with sga:
                    sbga, _ = mk(sga)
                    UTo = sbga("UTo", [128, 4, 2048], BF16)
                    for kc in range(4):
                        P.add('sp', lambda e, kc=kc: e.dma_start(out=UTo[:, kc, :], in_=ut_scr[:, kc, 2048:4096]), writes=['g'], dma='c')
                    tA = {}
                    for nm, srcd, n_ in (('ar', arA, 32), ('ai', aiA, 32), ('ldt', ldtA, 32), ('b1', b1A, 512), ('b2', b2A, 512),
                                         ('c', cA, 512), ('sgn', sgnA_d, 1)):
                        tA[nm] = sbga("gA" + nm, [128, n_], F32)
                        P.add('sp', lambda e, t=tA[nm], srcd=srcd: e.dma_start(out=t[:], in_=srcd), writes=['g'], dma='c')
                    P.barrier()
                    kvA = sbga("kvA", [128, 16], F32)
                    P.add('sp', lambda e: e.dma_start(out=kvA[:], in_=kv0_d), writes=['g'], dma='c')
                    P.barrier()
                    for kc in range(4):
                        P.add('pool', lambda e, kc=kc: e.tensor_copy(
                            out=UTs[:, kc, :].rearrange("q (g s j) -> q g s j", g=4, s=16),
                            in_=UTo[:, kc, :].rearrange("q (g j s) -> q g s j", g=4, s=16)), reads=['g'], writes=['g'])
                    r = powtab(sbga, "pA", tA['ar'][:], tA['ai'][:], tA['ldt'][:], 32, kvA[:], 16, coef=1)
                    Zr = sbga("ZrA", [128, 512], F32); Zi = sbga("ZiA", [128, 512], F32)
                    v3 = lambda t: t[:].rearrange("q (f k) -> q f k", k=16)
                    crb = bcast(r['cr'][:].unsqueeze(2), [128, 32, 16]); cib = bcast(r['ci'][:].unsqueeze(2), [128, 32, 16])
                    cmul(v3(Zr), v3(Zi), v3(r['PR']), v3(r['PI']), crb, cib, v3(r['tf']))
                    D_(lambda e: e.tensor_scalar(out=Zi[:], in0=Zi[:], scalar1=tA['sgn'][:, 0:1], scalar2=None, op0=ALU.mult))
                    Cc = sbga("Cc", [128, 512], F32); Xc = sbga("Xc", [128, 512], F32); Xt = sbga("Xt", [128, 512], F32)
                    D_(lambda e: e.tensor_scalar(out=Cc[:], in0=tA['c'][:], scalar1=tA['sgn'][:, 0:1], scalar2=-1.0,
                                                 op0=ALU.mult, op1=ALU.mult))
                    g3 = lambda t: t[:].rearrange("q (g c) -> q g c", c=16)
                    psGw = pb[0][:].rearrange("q (k n) -> q k n", k=4)
                    for dl in range(16):
                        D_(lambda e, dl=dl: e.tensor_tensor(out=g3(Xc), in0=g3(tA['b1']), in1=bcast(v3(Zr)[:, :, dl:dl + 1], [128, 32, 16]), op=ALU.mult))
                        D_(lambda e, dl=dl: e.tensor_tensor(out=g3(Xt), in0=g3(tA['b2']), in1=bcast(v3(Zi)[:, :, dl:dl + 1], [128, 32, 16]), op=ALU.mult))
                        D_(lambda e: e.tensor_tensor(out=Xc[:], in0=Xc[:], in1=Xt[:], op=ALU.add))
                        for kc in range(4):
                            P.add('pe', lambda e, kc=kc: e.matmul(psGw[:, kc, :], lhsT=Xc[:, kc * 128:(kc + 1) * 128],
                                                                  rhs=Cc[:, kc * 128:(kc + 1) * 128], start=True, stop=True),
                                  reads=['g'], writes=['psGw'])
                        P.add('dve', lambda e, dl=dl: e.tensor_tensor(out=Gw[:, :, dl, :], in0=psGw, in1=bcast(bm[:].unsqueeze(1), [128, 4, 128]),
                                                                      op=ALU.mult), reads=['psGw', 'g'], writes=['g'])
                    P.barrier()
                    P.emit()

                hTg = sbb("hTg", [128, 8, 512], BF16)
                OTg = sbb("OTg", [128, 4, 512], BF16)
                zT = sbb("zT", [128, 4, 512], BF16); z2T = sbb("z2T", [128, 4, 512], BF16)
                mT = sbb("mT", [128, 8, 512], BF16)
                xres = [sbb(f"xres{i}", [128, D], F32) for i in range(4)]
                x1t = [sbb(f"x1t{i}", [128, D], F32) for i in range(2)]
                yt = sbb("ytmp", [128, 512], F32); y2 = sbb("ytmp2", [128, 512], F32); sg1 = sbb("sg1", [128, 512], F32)
                sg2_ = sbb("sg2", [128, 512], F32)
                nb = (sbb("n_ssq", [128, 1], F32), sbb("n_rstd", [128, 1], F32), sbb("n_xn", [128, D], BF16),
                      sbb("n_sqj", [128, D], BF16), psT2)
                P.add('pool', lambda e: e.memset(OTg[:], 0.0), writes=['OTg'])
                OTp = sbb("OTp", [128, 4, 512], BF16)

                def phase3b_group(tg):
                    sample = (tg == 4)
                    N = 4 if sample else 512
                    tiles = [(xs, 0, 4)] if sample else [(xo, tg * 512 + a * 128, 128) for a in range(4)]
                    for a, (src, r0, nt) in enumerate(tiles):
                        P.add('sp', lambda e, a=a, src=src, r0=r0, nt=nt: e.dma_start(out=xres[a][0:nt, :], in_=src[r0:r0 + nt, :]),
                              writes=[('xres', a)], dma=f'x{a % 2}')
                        norm_transpose(xres[a][0:nt, :], nt, gmx2, hTg[:, :, a * 128:a * 128 + nt], nb, ('xres', a),
                                       perm_a=(None if sample else a), hT_full=hTg)
                    if sample:
                        P.add('pool', lambda e: e.tensor_copy(out=OTg[:, :, 0:4], in_=OTs[:]), reads=['OTs'], writes=['OTg'])
                    if not sample:
                        P.add('sp', lambda e: e.dma_start(out=OTg[:], in_=ot_scr[:, :, tg * 512:(tg + 1) * 512]),
                              reads=[('OT', tg, h) for h in range(8)], writes=['OTg'], dma='c')
                        P.add('pool', lambda e: e.tensor_copy(out=OTp[:].rearrange("q k (s j) -> q k s j", s=16),
                                                              in_=OTg[:].rearrange("q k (j s) -> q k s j", s=16)),
                              reads=['OTg'], writes=['OTp'])
                    else:
                        P.add('pool', lambda e: e.tensor_copy(out=OTp[:, :, 0:4], in_=OTg[:, :, 0:4]), reads=['OTg'], writes=['OTp'])
                    for kc in range(4):
                        if sample:
                            P.add('pe', lambda e, kc=kc: e.matmul(pb[kc][:, 0:4], lhsT=Gw[:, kc, 0, :], rhs=usT[:, kc, :],
                                                                  start=True, stop=False), reads=['g', 'usT'], writes=[('pb', kc)])
                            P.add('pe', lambda e, kc=kc: e.matmul(
                                pb[kc][:, 0:4], lhsT=Ycs[:, kc, :], rhs=ident[0:4, 0:4], start=False, stop=True),
                                reads=['Ycs', 'ident'], writes=[('pb', kc)])
                        else:
                            for dl in range(16):
                                P.add('pe', lambda e, kc=kc, dl=dl: e.matmul(
                                    pb[kc][:, dl * 32:512], lhsT=Gw[:, kc, dl, :], rhs=UTs[:, kc, tg * 512:tg * 512 + (16 - dl) * 32],
                                    start=(dl == 0), stop=False),
                                    reads=['g'], writes=[('pb', kc)])
                            for tl in range(16):
                                P.add('pe', lambda e, kc=kc, tl=tl: e.matmul(
                                    pb[kc][:, tl * 32:(tl + 1) * 32], lhsT=Yc[:, kc, tl, :], rhs=ident[:, tg * 32:(tg + 1) * 32], start=False, stop=(tl == 15)),
                                    reads=['Yc', 'ident'], writes=[('pb', kc)])
                    for kc in range(4):
                        uin = usT[:, kc, :] if sample else UTs[:, kc, tg * 512:(tg + 1) * 512]
                        P.add('dve', lambda e, kc=kc, uin=uin: e.scalar_tensor_tensor(
                            out=yt[:, 0:N], in0=uin, scalar=dvec[:, kc:kc + 1], in1=pb[kc][:, 0:N], op0=ALU.mult, op1=ALU.add),
                            reads=[('pb', kc), 'g'], writes=['yt'])
                        P.add('dve', lambda e: e.tensor_tensor(out=y2[:, 0:N], in0=yt[:, 0:N], in1=yt[:, 0:N], op=ALU.mult),
                              reads=['yt'], writes=['y2'])
                        P.add('dve', lambda e: e.tensor_scalar(out=y2[:, 0:N], in0=y2[:, 0:N], scalar1=0.044715, scalar2=1.0,
                                                               op0=ALU.mult, op1=ALU.add), reads=['y2'], writes=['y2'])
                        P.add('dve', lambda e: e.tensor_tensor(out=y2[:, 0:N], in0=y2[:, 0:N], in1=yt[:, 0:N], op=ALU.mult),
                              reads=['y2', 'yt'], writes=['y2'])
                        P.add('act', lambda e: e.activation(out=y2[:, 0:N], in_=y2[:, 0:N], func=AF.Sigmoid, scale=1.5957691216057308),
                              reads=['y2'], writes=['y2'])
                        P.add('dve', lambda e, kc=kc: e.tensor_tensor(out=zT[:, kc, 0:N], in0=y2[:, 0:N], in1=yt[:, 0:N], op=ALU.mult),
                              reads=['y2', 'yt'], writes=[('zT', kc)])
                    for m in range(4):
                        for kc in range(4):
                            P.add('pe', lambda e, m=m, kc=kc: e.matmul(pb[m][:, 0:N], lhsT=wgl[:, kc, m * 128:(m + 1) * 128],
                                                                       rhs=zT[:, kc, 0:N], start=(kc == 0), stop=(kc == 3)),
                                  reads=['g'] + [('zT', k) for k in range(4)], writes=[('pb', m)])
                        P.add('act', lambda e, m=m: e.activation(out=sg1[:, 0:N], in_=pb[m][:, 0:N], func=AF.Sigmoid, bias=bgl[:, m:m + 1]),
                              reads=[('pb', m), 'g'], writes=['sg1'])
                        P.add('dve', lambda e, m=m: e.tensor_tensor(out=z2T[:, m, 0:N], in0=zT[:, m, 0:N], in1=sg1[:, 0:N], op=ALU.mult),
                              reads=['sg1', ('zT', m)], writes=[('z2T', m)])
                    for m in range(8):
                        ms = slice(m * 128, (m + 1) * 128)
                        for kc in range(4):
                            P.add('pe', lambda e, kc=kc, ms=ms: e.matmul(pb[0][:, 0:N], lhsT=wat[:, kc, ms], rhs=OTp[:, kc, 0:N],
                                                                         start=(kc == 0), stop=(kc == 3)),
                                  reads=['g', 'OTp'], writes=[('pb', 0)])
                        for kc in range(4):
                            P.add('pe', lambda e, kc=kc, ms=ms: e.matmul(pb[1][:, 0:N], lhsT=wss[:, kc, ms], rhs=z2T[:, kc, 0:N],
                                                                         start=(kc == 0), stop=(kc == 3)),
                                  reads=['g'] + [('z2T', k) for k in range(4)], writes=[('pb', 1)])
                        for gi in range(2):
                            for kc in range(8):
                                P.add('pe', lambda e, kc=kc, m=m, gi=gi: e.matmul(
                                    pb[2 + gi][:, 0:N], lhsT=wg[:, kc, gi * 1024 + m * 128:gi * 1024 + (m + 1) * 128], rhs=hTg[:, kc, 0:N],
                                    start=(kc == 0), stop=(kc == 7)), reads=['g', 'hTg'], writes=[('pb', 2 + gi)])
                        P.add('act', lambda e: e.activation(out=sg1[:, 0:N], in_=pb[2][:, 0:N], func=AF.Sigmoid),
                              reads=[('pb', 2)], writes=['sg1'])
                        P.add('act', lambda e: e.activation(out=sg2_[:, 0:N], in_=pb[3][:, 0:N], func=AF.Sigmoid),
                              reads=[('pb', 3)], writes=['sg2'])
                        P.add('dve', lambda e: e.tensor_tensor(out=sg1[:, 0:N], in0=sg1[:, 0:N], in1=pb[0][:, 0:N], op=ALU.mult),
                              reads=['sg1', ('pb', 0)], writes=['sg1'])
                        P.add('dve', lambda e: e.tensor_tensor(out=sg2_[:, 0:N], in0=sg2_[:, 0:N], in1=pb[1][:, 0:N], op=ALU.mult),
                              reads=['sg2', ('pb', 1)], writes=['sg2'])
                        if sample:
                            P.add('dve', lambda e, m=m: e.tensor_tensor(out=mT[:, m, 0:N], in0=sg1[:, 0:N], in1=sg2_[:, 0:N], op=ALU.add),
                                  reads=['sg1', 'sg2'], writes=[('mT', m)])
                        else:
                            P.add('dve', lambda e, m=m: e.tensor_tensor(
                                out=mT[:, m, :].rearrange("q (j s) -> q s j", s=16), in0=sg1[:].rearrange("q (s j) -> q s j", s=16),
                                in1=sg2_[:].rearrange("q (s j) -> q s j", s=16), op=ALU.add),
                                reads=['sg1', 'sg2'], writes=[('mT', m)])
                    for a, (src, r0, nt) in enumerate(tiles):
                        xo_ = x1t[a % 2]
                        for half in range(2):
                            pX = pb[4 + half]
                            for kc in range(8):
                                P.add('pe', lambda e, kc=kc, pX=pX, a=a, nt=nt, half=half: e.matmul(
                                    pX[0:nt, :], lhsT=mT[:, kc, a * 128:a * 128 + nt], rhs=wou[:, kc, half * 512:(half + 1) * 512],
                                    start=(kc == 0), stop=(kc == 7)), reads=['g'] + [('mT', k) for k in range(8)], writes=[('pb', 4 + half)])
                            P.add('dve', lambda e, pX=pX, a=a, nt=nt, half=half, xo_=xo_: e.tensor_tensor(
                                out=xo_[0:nt, half * 512:(half + 1) * 512], in0=pX[0:nt, :], in1=xres[a][0:nt, half * 512:(half + 1) * 512],
                                op=ALU.add), reads=[('pb', 4 + half), ('xres', a)], writes=[('x1t', a % 2)])
                        rr0 = 2048 if sample else tg * 512 + a * 128
                        P.add('sp', lambda e, xo_=xo_, rr0=rr0, nt=nt: e.dma_start(out=x1_scr[rr0:rr0 + nt, :], in_=xo_[0:nt, :]),
                              reads=[('x1t', a % 2)], writes=[('x1scr', rr0)], dma=f'us{a % 2}')

                groups = [int(v) for v in os.environ.get('KTG', '0,1,2,3,4').split(',')]
                for tg in groups:
                    phase3b_group(tg)
                if DBG:
                    P.barrier()
                    dX1 = nc.dram_tensor("dbg_x1", [2052, 1024], F32, kind="ExternalOutput").ap()
                    P.add('sp', lambda e: e.dma_start(out=dX1, in_=x1_scr), dma='dbg')
                P.barrier()
                P.emit()

        s4 = ExitStack()
        with s4:
            sb4, ps4 = mk(s4)
            wfi = sb4("wfi", [128, 8, 5632], BF16)
            wfo = sb4("wfo", [128, 22, 1024], BF16)
            gff = sb4("gff", [128, 8], F32); gfin = sb4("gfin_s", [128, D], F32)
            hfT = sb4("hfT", [128, 8, 512], BF16); actT = sb4("actT", [128, 22, 512], BF16)
            xr = [sb4(f"xr{i}", [128, D], F32) for i in range(4)]
            x2 = [sb4(f"x2{i}", [128, D], F32) for i in range(2)]
            sgA = [sb4(f"sgA{i}", [128, 512], F32) for i in range(2)]
            fss = sb4("fss", [128, 1], F32); frs = sb4("frs", [128, 1], F32)
            pAb = [ps4(f"pAb{i}", [128, 512], F32) for i in range(2)]
            pGb = [ps4(f"pGb{i}", [128, 512], F32) for i in range(2)]
            pXb = [ps4(f"pXb{i}", [128, 512], F32) for i in range(2)]
            psT4 = ps4("psT4", [128, 8, 128], BF16)
            nb4 = (sb4("n4_ssq", [128, 1], F32), sb4("n4_rstd", [128, 1], F32), sb4("n4_xn", [128, D], BF16),
                   sb4("n4_sqj", [128, D], BF16), psT4)
            P.add('sp', lambda e: e.dma_start(out=gff[:], in_=gffn_d), writes=['g'], dma='c')
            P.add('sp', lambda e: e.dma_start(out=gfin[:], in_=gfin_d), writes=['g'], dma='c')
            P.barrier()
            for ci_, (c0, c1) in enumerate(((0, 2048), (2048, 4096), (4096, 5632))):
                for kc in range(8):
                    P.add('pool', lambda e, kc=kc, c0=c0, c1=c1: e.dma_start(out=wfi[:, kc, c0:c1], in_=w_fin_d[kc * 128:(kc + 1) * 128, c0:c1]),
                          writes=[('wfi', ci_, kc)], dma=f'wf{ci_}')
            for kc in range(22):
                P.add('pool', lambda e, kc=kc: e.dma_start(out=wfo[:, kc, :], in_=w_fout_d[kc * 128:(kc + 1) * 128, :]),
                      writes=[('wfo', kc)], dma='wfo')
            wfi_t = [[('wfi', c_, k_) for k_ in range(8)] for c_ in range(3)]
            wfo_t = [('wfo', k_) for k_ in range(22)]

            def phase4_group(tg):
                sample = (tg == 4)
                N = 4 if sample else 512
                tiles = [(2048, 4)] if sample else [(tg * 512 + a * 128, 128) for a in range(4)]
                for a, (r0, nt) in enumerate(tiles):
                    P.add('sp', lambda e, a=a, r0=r0, nt=nt: e.dma_start(out=xr[a][0:nt, :], in_=x1_scr[r0:r0 + nt, :]),
                          reads=[('x1scr', r0)], writes=[('xr', a)], dma=f'x{a % 2}')
                    norm_transpose(xr[a][0:nt, :], nt, gff, hfT[:, :, a * 128:a * 128 + nt], nb4, ('xr', a))
                for m in range(22):
                    pa, pg = pAb[m % 2], pGb[m % 2]
                    for kc in range(8):
                        P.add('pe', lambda e, kc=kc, m=m, pa=pa: e.matmul(pa[:, 0:N], lhsT=wfi[:, kc, m * 128:(m + 1) * 128], rhs=hfT[:, kc, 0:N],
                                                                          start=(kc == 0), stop=(kc == 7)), reads=['hTg'] + wfi_t[(m * 128) // 2048], writes=[('pAb', m % 2)])
                    for kc in range(8):
                        P.add('pe', lambda e, kc=kc, m=m, pg=pg: e.matmul(pg[:, 0:N], lhsT=wfi[:, kc, 2816 + m * 128:2816 + (m + 1) * 128],
                                                                          rhs=hfT[:, kc, 0:N], start=(kc == 0), stop=(kc == 7)),
                              reads=['hTg'] + wfi_t[(2816 + m * 128) // 2048] + wfi_t[(2816 + m * 128 + 127) // 2048], writes=[('pGb', m % 2)])
                    sg = sgA[m % 2]
                    P.add('act', lambda e, pa=pa, sg=sg: e.activation(out=sg[:, 0:N], in_=pa[:, 0:N], func=AF.Silu),
                          reads=[('pAb', m % 2)], writes=[('sgA', m % 2)])
                    P.add('dve', lambda e, pg=pg, sg=sg, m=m: e.tensor_tensor(out=actT[:, m, 0:N], in0=sg[:, 0:N], in1=pg[:, 0:N], op=ALU.mult),
                          reads=[('sgA', m % 2), ('pGb', m % 2)], writes=[('actT', m)])
                for a, (r0, nt) in enumerate(tiles):
                    xx = x2[a % 2]
                    yy = xx
                    for half in range(2):
                        pX = pXb[half]
                        for kc in range(22):
                            P.add('pe', lambda e, kc=kc, pX=pX, a=a, nt=nt, half=half: e.matmul(
                                pX[0:nt, :], lhsT=actT[:, kc, a * 128:a * 128 + nt], rhs=wfo[:, kc, half * 512:(half + 1) * 512],
                                start=(kc == 0), stop=(kc == 21)), reads=wfo_t + [('actT', k) for k in range(22)], writes=[('pXb', half)])
                        P.add('dve', lambda e, pX=pX, a=a, nt=nt, half=half, xx=xx: e.tensor_tensor(
                            out=xx[0:nt, half * 512:(half + 1) * 512], in0=pX[0:nt, :], in1=xr[a][0:nt, half * 512:(half + 1) * 512], op=ALU.add),
                            reads=[('pXb', half), ('xr', a)], writes=[('x2', a % 2)])
                    P.add('act', lambda e, xx=xx, nt=nt: e.activation(out=nb4[3][0:nt, :], in_=xx[0:nt, :], func=AF.Square, accum_out=fss[0:nt, :]),
                          reads=[('x2', a % 2)], writes=['n_sqj', 'fss'])
                    P.add('dve', lambda e, nt=nt: e.tensor_scalar(out=frs[0:nt, :], in0=fss[0:nt, :], scalar1=1.0 / D, scalar2=EPS,
                                                                  op0=ALU.mult, op1=ALU.add), reads=['fss'], writes=['frs'])
                    P.add('act', lambda e, nt=nt: e.activation(out=frs[0:nt, :], in_=frs[0:nt, :], func=AF.Sqrt), reads=['frs'], writes=['frs'])
                    P.add('dve', lambda e, nt=nt: e.reciprocal(out=frs[0:nt, :], in_=frs[0:nt, :]), reads=['frs'], writes=['frs'])
                    P.add('dve', lambda e, xx=xx, yy=yy, nt=nt: e.scalar_tensor_tensor(
                        out=yy[0:nt, :], in0=xx[0:nt, :], scalar=frs[0:nt, 0:1], in1=gfin[0:nt, :], op0=ALU.mult, op1=ALU.mult),
                        reads=[('x2', a % 2), 'frs', 'g'], writes=[('x2', a % 2)])
                    dst = yso if sample else yo
                    d0 = 0 if sample else r0
                    P.add('sp', lambda e, yy=yy, dst=dst, d0=d0, nt=nt: e.dma_start(out=dst[d0:d0 + nt, :], in_=yy[0:nt, :]),
                          reads=[('x2', a % 2)], dma=f'ko{a % 2}')

            for tg in groups:
                phase4_group(tg)
            P.barrier()
            P.emit()
    return nc


def _host_consts(p):
    half = HD // 2
    inv = np.power(np.float32(10000.0), -2.0 * np.arange(half, dtype=np.float32) / HD).astype(np.float32)
    pos = np.concatenate([np.arange(2048), 2048 * p + np.arange(2048), np.full(128, 8192)]).astype(np.float32)
    ang = pos[:, None] * inv[None, :]
    cos = np.cos(ang).astype(np.float32)
    sin = np.sin(ang).astype(np.float32)
    sF = np.concatenate([-sin, sin], axis=1)
    cF = np.ascontiguousarray(cos.reshape(33, 128, 32).transpose(1, 0, 2))
    sF = np.ascontiguousarray(sF.reshape(33, 128, 64).transpose(1, 0, 2))
    k = np.arange(128)[:, None, None, None]
    j = np.arange(4)[None, :, None, None]
    a = np.arange(4)[None, None, :, None]
    q = np.arange(128)[None, None, None, :]
    m4 = ((j < a) | ((j == a) & (k <= q))).astype(np.float32).reshape(128, 4, 512)
    pb = np.full((8, 16), -BIG, np.float32)
    for m in range(8):
        if p == 1:
            pb[m, 0:8] = 0.0
        pb[m, 8:8 + m] = 0.0
    pb = np.ascontiguousarray(np.broadcast_to(pb[None], (128, 8, 16)))
    return cF, sF, np.ascontiguousarray(m4), pb


def _s5_layouts(a_re, a_im, log_dt, b_re, b_im, c_re, c_im):
    c_ = np.ascontiguousarray
    out = {}
    def lay_gp(a):
        t = a.reshape(4, 8, 64)
        t = np.broadcast_to(t[:, :, None, :], (4, 8, 16, 64))
        return c_(t.transpose(1, 2, 0, 3).reshape(128, 256))
    out["arB"] = lay_gp(a_re); out["aiB"] = lay_gp(a_im)
    out["ldtB"] = lay_gp(np.broadcast_to(log_dt[:, None], (32, 64)))
    def lay_b(bb):
        t = bb.reshape(4, 8, 64, 16)
        return c_(t.transpose(1, 3, 0, 2).reshape(128, 256))
    out["brB"] = lay_b(b_re); out["biB"] = lay_b(b_im)
    def lay_w(a):
        t = a.reshape(16, 2, 64)
        return c_(t.transpose(1, 2, 0).reshape(128, 16))
    out["arD"] = lay_w(a_re); out["aiD"] = lay_w(a_im)
    out["ldtD"] = lay_w(np.broadcast_to(log_dt[:, None], (32, 64)))
    def lay_c(cc):
        t = cc.reshape(16, 2, 16, 64)
        return c_(t.transpose(1, 3, 0, 2).reshape(128, 256))
    out["crD"] = lay_c(c_re); out["ciD"] = lay_c(c_im)
    t2 = lambda a: c_(np.concatenate([a.T, a.T], axis=0))
    out["arA"] = t2(a_re); out["aiA"] = t2(a_im); out["ldtA"] = t2(np.broadcast_to(log_dt[:, None], (32, 64)))
    bt = lambda x: x.transpose(1, 0, 2).reshape(64, 512)
    out["b1A"] = c_(np.concatenate([bt(b_re), bt(b_im)], axis=0))
    out["b2A"] = c_(np.concatenate([bt(b_im), bt(b_re)], axis=0))
    ct = lambda x: x.transpose(2, 0, 1).reshape(64, 512)
    out["cA"] = c_(np.concatenate([ct(c_re), ct(c_im)], axis=0))
    out["kv0"] = c_(np.broadcast_to(np.arange(16, dtype=np.float32)[None], (128, 16)))
    out["kv1"] = c_(np.broadcast_to(np.arange(1, 17, dtype=np.float32)[None], (128, 16)))
    out["kvj"] = c_(np.broadcast_to(np.arange(1, 257, dtype=np.float32)[None], (128, 256)))
    r = np.arange(128)
    out["pmB"] = c_(np.stack([((r // 16) % 2 == 0), ((r // 16) % 2 == 1)], axis=1).astype(np.float32))
    out["pmD"] = c_(np.stack([(r // 64 == 0), (r // 64 == 1)], axis=1).astype(np.float32))
    return out


_NC_CACHE = {}
_LAST = {}


def kernel(x_prompt, x_sample, cache_k, cache_v, state_ssm_re, state_ssm_im, page_table,
           norm_mix, w_in, w_attn_proj, ssm_a_re, ssm_a_im, ssm_log_dt, ssm_b_re, ssm_b_im,
           ssm_c_re, ssm_c_im, ssm_d, w_glu, b_glu, w_ssm_proj, w_out, norm_ffn,
           w_ffn_in, w_ffn_out, norm_final):
    f = lambda a: np.ascontiguousarray(np.asarray(a, dtype=np.float32))
    x_prompt = f(x_prompt)
    x_sample = f(x_sample)
    if 'nc' not in _NC_CACHE:
        _NC_CACHE['nc'] = build()
    nc = _NC_CACHE['nc']
    in_maps = []
    ident = np.eye(128, dtype=np.float32)
    gm = np.ascontiguousarray(f(norm_mix)[0].reshape(8, 128).T)
    s5 = _s5_layouts(f(ssm_a_re)[0], f(ssm_a_im)[0], f(ssm_log_dt)[0], f(ssm_b_re)[0], f(ssm_b_im)[0],
                     f(ssm_c_re)[0], f(ssm_c_im)[0])
    col = lambda v, n: np.ascontiguousarray(f(v).reshape(n, 128).T)
    r_ = np.arange(128)
    wts = {"w_attn": f(w_attn_proj)[0], "w_glu": f(w_glu)[0], "w_ssm": f(w_ssm_proj)[0], "w_out": f(w_out)[0],
           "w_ffn_in": f(w_ffn_in)[0], "w_ffn_out": f(w_ffn_out)[0],
           "bgl": col(f(b_glu)[0], 4), "dvec": col(f(ssm_d)[0], 4), "gffn": col(f(norm_ffn)[0], 8),
           "gfin": np.ascontiguousarray(np.broadcast_to(f(norm_final)[None, :], (128, 1024))),
           "sgnA": np.where(r_ < 64, -1.0, 1.0).astype(np.float32).reshape(128, 1),
           "bmask": (r_[:, None] // 16 == r_[None, :] // 16).astype(np.float32)}
    zs = np.zeros((128, 255), np.float32); zs[:, 127] = 1.0
    smp = {"cache_k": f(cache_k).reshape(2560 * 128 * 8, 64), "cache_v": f(cache_v).reshape(2560 * 128 * 8, 64),
           "piota": r_.astype(np.float32).reshape(128, 1),
           "hb": (8.0 * r_[:, None] + np.arange(8)[None, :]).astype(np.float32),
           "zsel": zs, "rsel": (np.arange(128)[None, :] // 32 == np.arange(4)[:, None]).astype(np.float32),
           "eye8": np.eye(8, dtype=np.float32), "esel": np.eye(4, dtype=np.float32)}
    for c in range(8):
        b, p = c // 2, c % 2
        cF, sF, m4, pb = _host_consts(p)
        xo = x_prompt[b, 2048 * p:2048 * (p + 1)]
        xpv = x_prompt[b, 0:2048] if p == 1 else np.zeros((2048, D), np.float32)
        in_maps.append({
            "xo": np.ascontiguousarray(xo), "xp": np.ascontiguousarray(xpv),
            "xs": np.ascontiguousarray(x_sample[4 * c:4 * c + 4, 0]), "w_in": f(w_in)[0],
            "gmix": gm, "ropec": cF, "ropes": sF, "ident": ident, "mask4": m4, "pbias": pb,
        })
        in_maps[-1].update(s5)
        in_maps[-1].update(wts)
        in_maps[-1].update(smp)
        pt4 = np.asarray(page_table)[4 * c:4 * c + 4].astype(np.int32)
        in_maps[-1]["ptrep"] = np.ascontiguousarray(np.broadcast_to(pt4.reshape(1, 256), (128, 256)))
        in_maps[-1]["ptE"] = np.ascontiguousarray(np.broadcast_to(pt4[:, 0::2].reshape(1, 128), (8, 128)))
        in_maps[-1]["ptO"] = np.ascontiguousarray(np.broadcast_to(pt4[:, 1::2].reshape(1, 128), (8, 128)))
        for nm, st in (("h0r", state_ssm_re), ("h0i", state_ssm_im)):
            t = f(st)[0, 4 * c:4 * c + 4].reshape(4, 16, 2, 64)
            in_maps[-1][nm] = np.ascontiguousarray(t.transpose(2, 3, 1, 0).reshape(128, 64))
    res = run_bass_kernel_spmd(nc, in_maps, core_ids=list(range(8)))
    R = res.results
    if os.environ.get('KDBG'):
        cc = int(os.environ.get("KCORE", "1"))
        _LAST['R'] = {k: np.asarray(v).view(np.uint16) if 'bfloat' in str(np.asarray(v).dtype) else np.asarray(v)
                      for k, v in R[cc].items()}
    new_k = np.zeros((1, 4, 4096, 8, 64), np.float32)
    new_v = np.zeros((1, 4, 4096, 8, 64), np.float32)
    new_ks = np.zeros((1, 32, 1, 8, 64), np.float32)
    new_vs = np.zeros((1, 32, 1, 8, 64), np.float32)
    for c in range(8):
        b, p = c // 2, c % 2
        new_k[0, b, 2048 * p:2048 * (p + 1)] = np.asarray(R[c]["ko"]).reshape(2048, 8, 64)
        new_v[0, b, 2048 * p:2048 * (p + 1)] = np.asarray(R[c]["vo"]).reshape(2048, 8, 64)
        new_ks[0, 4 * c:4 * c + 4, 0] = np.asarray(R[c]["kso"]).reshape(4, 8, 64)
        new_vs[0, 4 * c:4 * c + 4, 0] = np.asarray(R[c]["vso"]).reshape(4, 8, 64)
    y_prompt = np.zeros((4, 4096, 1024), np.float32)
    y_sample = np.zeros((32, 1, 1024), np.float32)
    for c in range(8):
        b, p = c // 2, c % 2
        y_prompt[b, 2048 * p:2048 * (p + 1)] = np.asarray(R[c]["yo"])
        y_sample[4 * c:4 * c + 4, 0] = np.asarray(R[c]["yso"])
    hre = np.zeros((1, 4, 32, 64), np.float32)
    him = np.zeros((1, 4, 32, 64), np.float32)
    for b in range(4):
        c = 2 * b + 1
        hre[0, b] = np.asarray(R[c]["hfr"]).reshape(2, 64, 16).transpose(2, 0, 1).reshape(32, 64)
        him[0, b] = np.asarray(R[c]["hfi"]).reshape(2, 64, 16).transpose(2, 0, 1).reshape(32, 64)
    hsr_o = np.zeros((1, 32, 32, 64), np.float32)
    hsi_o = np.zeros((1, 32, 32, 64), np.float32)
    for c in range(8):
        hsr_o[0, 4 * c:4 * c + 4] = np.asarray(R[c]["hsr"]).reshape(2, 64, 16, 4).transpose(3, 2, 0, 1).reshape(4, 32, 64)
        hsi_o[0, 4 * c:4 * c + 4] = np.asarray(R[c]["hsi"]).reshape(2, 64, 16, 4).transpose(3, 2, 0, 1).reshape(4, 32, 64)
    return (y_prompt, y_sample, new_k, new_v, hre, him, new_ks, new_vs, hsr_o, hsi_o)
```

```python
import numpy as np
import concourse.bass as bass
import concourse.mybir as mybir
from concourse.bass_utils import run_bass_kernel_spmd

F32 = mybir.dt.float32
BF16 = mybir.dt.bfloat16
I32 = mybir.dt.int32
AF = mybir.ActivationFunctionType
ALU = mybir.AluOpType
AX = mybir.AxisListType

D = 1024
NH = 8
HD = 64
NT_OWN = 16
NT_PREV = 16
EPS = 1e-6
BIG = 30000.0

ENGS = ('pe', 'act', 'dve', 'pool', 'sp')
BLK = {'pe': 'tensor', 'act': 'scalar', 'dve': 'vector', 'pool': 'gpsimd', 'sp': 'sync'}


class Op:
    __slots__ = ('eng', 'fn', 'tok', 'waits', 'dma')

    def __init__(self, eng, fn, tok, waits, dma):
        self.eng, self.fn, self.tok, self.waits, self.dma = eng, fn, tok, waits, dma


class Prog:
    def __init__(self, nc, sems):
        self.nc = nc
        self.sems = sems
        self.ops = []
        self.last_w = {}
        self.readers = {}
        self.cnt = {}
        self.waited = {e: {} for e in ENGS}
        self.emitted = 0
        self.excl = set()

    def add(self, eng, fn, reads=(), writes=(), dma=None):
        deps = set()
        for r in reads:
            if r in self.last_w:
                deps.add(self.last_w[r])
        for w in writes:
            if w in self.last_w:
                deps.add(self.last_w[w])
            deps |= self.readers.get(w, set())
        key = ('d:' + dma) if dma else eng
        inc = 16 if dma else 1
        self.cnt[key] = self.cnt.get(key, 0) + inc
        tok = (key, self.cnt[key])
        waits = {}
        for d in deps:
            dk, dv = self.ops[d].tok
            if dk == 'pe' and eng == 'pe' and not dma:
                continue
            waits[dk] = max(waits.get(dk, 0), dv)
        final = []
        for k, v in waits.items():
            if self.waited[eng].get(k, 0) < v:
                final.append((k, v))
                self.waited[eng][k] = v
        oid = len(self.ops)
        self.ops.append(Op(eng, fn, tok, final, dma))
        for r in reads:
            self.readers.setdefault(r, set()).add(oid)
        for w in writes:
            self.last_w[w] = oid
            self.readers[w] = set()
        return oid

    def barrier(self):
        for e in ENGS:
            final = []
            for k, v in self.cnt.items():
                if k in self.excl:
                    continue
                if self.waited[e].get(k, 0) < v:
                    final.append((k, v))
                    self.waited[e][k] = v
            if final:
                self.ops.append(Op(e, None, None, final, None))

    def emit(self):
        ops = self.ops[self.emitted:]
        self.emitted = len(self.ops)
        with self.nc.Block() as block:
            for eng in ENGS:
                my = [o for o in ops if o.eng == eng]
                if not my:
                    continue

                def body(e, my=my):
                    for o in my:
                        for (k, v) in o.waits:
                            e.wait_ge(self.sems[k], v)
                        if o.fn is None:
                            continue
                        ins = o.fn(e)
                        ins.then_inc(self.sems[o.tok[0]], 16 if o.dma else 1)

                getattr(block, BLK[eng])(body)


def bcast(ap, shape):
    return ap.broadcast_to(shape)


from contextlib import ExitStack
import os


def build():
    nc = bass.Bass("TRN2", target_bir_lowering=False)
    DBG = os.environ.get('KDBG')

    def din(name, shape, dt=F32):
        return nc.dram_tensor(name, list(shape), dt, kind="ExternalInput").ap()

    def dout(name, shape, dt=F32):
        return nc.dram_tensor(name, list(shape), dt, kind="ExternalOutput").ap()

    xo = din("xo", [2048, D])
    xp = din("xp", [2048, D])
    xs = din("xs", [4, D])
    w_in = din("w_in", [D, 4096])
    gmix = din("gmix", [128, 8])
    ropec = din("ropec", [128, 33, 32])
    ropes = din("ropes", [128, 33, 64])
    ident_d = din("ident", [128, 128])
    mask4_d = din("mask4", [128, 4, 512])
    pbias_d = din("pbias", [128, 8, 16])

    arB = din("arB", [128, 256]); aiB = din("aiB", [128, 256]); ldtB = din("ldtB", [128, 256])
    brB = din("brB", [128, 256]); biB = din("biB", [128, 256])
    arD = din("arD", [128, 16]); aiD = din("aiD", [128, 16]); ldtD = din("ldtD", [128, 16])
    crD = din("crD", [128, 256]); ciD = din("ciD", [128, 256])
    kv0_d = din("kv0", [128, 16]); kv1_d = din("kv1", [128, 16]); kvj_d = din("kvj", [128, 256])
    pmB_d = din("pmB", [128, 2]); pmD_d = din("pmD", [128, 2])
    w_attn_d = din("w_attn", [512, 1024]); w_glu_d = din("w_glu", [512, 512]); w_ssm_d = din("w_ssm", [512, 1024])
    w_out_d = din("w_out", [1024, 1024]); w_fin_d = din("w_ffn_in", [1024, 5632]); w_fout_d = din("w_ffn_out", [2816, 1024])
    bgl_d = din("bgl", [128, 4]); dvec_d = din("dvec", [128, 4]); gffn_d = din("gffn", [128, 8]); gfin_d = din("gfin", [128, 1024])
    arA = din("arA", [128, 32]); aiA = din("aiA", [128, 32]); ldtA = din("ldtA", [128, 32])
    b1A = din("b1A", [128, 512]); b2A = din("b2A", [128, 512]); cA = din("cA", [128, 512])
    sgnA_d = din("sgnA", [128, 1]); bmask_d = din("bmask", [128, 128])
    x1_scr = nc.dram_tensor("x1_scr", [2052, 1024], F32, kind="Internal").ap()
    yo = dout("yo", [2048, 1024]); yso = dout("yso", [4, 1024])
    ckh = din("cache_k", [2560 * 128 * 8, 64]); cvh = din("cache_v", [2560 * 128 * 8, 64])
    ptrep_d = din("ptrep", [128, 256], I32); ptE_d = din("ptE", [8, 128], I32); ptO_d = din("ptO", [8, 128], I32)
    piota_d = din("piota", [128, 1]); hb_d = din("hb", [128, 8]); zsel_d = din("zsel", [128, 255]); rsel_d = din("rsel", [4, 128])
    eye8_d = din("eye8", [8, 8]); esel_d = din("esel", [4, 4])
    hfr = dout("hfr", [128, 16]); hfi = dout("hfi", [128, 16])
    hsr = dout("hsr", [128, 64]); hsi = dout("hsi", [128, 64])
    h0r_d = din("h0r", [128, 64]); h0i_d = din("h0i", [128, 64])
    ko = dout("ko", [2048, 512])
    vo = dout("vo", [2048, 512])
    kso = dout("kso", [4, 512])
    vso = dout("vso", [4, 512])
    ut_scr = nc.dram_tensor("ut_scr", [128, 4, 4096], BF16, kind="Internal").ap()
    ot_scr = nc.dram_tensor("ot_scr", [128, 4, 2048], BF16, kind="Internal").ap()

    top = ExitStack()
    with top:
        def mk(es):
            def sb(name, shape, dt):
                return es.enter_context(nc.sbuf_tensor("s_" + name, list(shape), dt))

            def ps(name, shape, dt):
                return es.enter_context(nc.psum_tensor("p_" + name, list(shape), dt))
            return sb, ps

        sb, ps = mk(top)
        semnames = ['pe', 'act', 'dve', 'pool', 'd:x0', 'd:x1', 'd:w', 'd:c', 'd:ko0', 'd:ko1', 'd:vo0', 'd:vo1',
                    'd:us0', 'd:us1', 'd:dbg', 'd:og0', 'd:og1', 'd:gk', 'd:wf0', 'd:wf1', 'd:wf2', 'd:wfo'] + [f'd:pg{i}' for i in range(8)]
        sems = {k: top.enter_context(nc.semaphore(k.replace(':', '_'))) for k in semnames}
        P = Prog(nc, sems)

        def dbg(name, ap, shape, dt, toks):
            d = nc.dram_tensor("dbg_" + name, list(shape), dt, kind="ExternalOutput").ap()
            P.add('sp', lambda e: e.dma_start(out=d, in_=ap), reads=toks, dma='dbg')

        ident = sb("identb", [128, 128], BF16)
        identf = sb("identf", [128, 128], F32)
        qs_f = sb("qs_f", [4, 8, 64], F32)
        ks_f = sb("ks_f", [4, 8, 64], F32)
        vs_f = sb("vs_f", [4, 8, 64], F32)
        usT = sb("usT", [128, 4, 4], BF16)
        OTs = sb("OTs", [128, 4, 4], BF16)

        P.add('sp', lambda e: e.dma_start(out=identf[:], in_=ident_d[:, :]), writes=['identf'], dma='c')
        P.barrier()
        P.add('dve', lambda e: e.tensor_copy(out=ident[:], in_=identf[:]), reads=['identf'], writes=['ident'])

        s12 = ExitStack()
        with s12:
            sb12, _ = mk(s12)
            KaT = sb12("KaT", [128, 8, 4096], BF16)
            Vaug = sb12("Vaug", [128, 32, 8, 65], BF16)
            Qaug = sb12("Qaug", [128, 16, 8, 80], BF16)

            s1 = ExitStack()
            with s1:
                sb1, ps1 = mk(s1)
                w1 = sb1("w1", [128, 8, 2048], BF16)
                gmx = sb1("gmx", [128, 8], F32)
                rc = sb1("rc", [128, 33, 32], F32)
                rs = sb1("rs", [128, 33, 64], F32)
                xt = [sb1(f"xt{i}", [128, D], F32) for i in range(2)]
                sqj = sb1("sqj", [128, D], BF16)
                ssq = [sb1(f"ssq{i}", [128, 1], F32) for i in range(2)]
                rstd = [sb1(f"rstd{i}", [128, 1], F32) for i in range(2)]
                xn = [sb1(f"xn{i}", [128, D], BF16) for i in range(2)]
                hT = [sb1(f"hT{i}", [128, 8, 128], BF16) for i in range(2)]
                kf = [sb1(f"kf{i}", [128, 8, 64], F32) for i in range(2)]
                vf = [sb1(f"vf{i}", [128, 8, 64], F32) for i in range(2)]
                ust = [sb1(f"ust{i}", [128, 4, 128], BF16) for i in range(2)]
                t1 = sb1("t1", [128, 8, 64], F32)
                t2 = sb1("t2", [128, 8, 64], F32)
                kb = sb1("kb", [128, 8, 80], BF16)
                psT = ps1("psT", [128, 8, 128], BF16)
                psQ = ps1("psQ", [128, 512], F32)
                psK = ps1("psK", [128, 512], F32)
                psV = ps1("psV", [128, 512], F32)
                psU = ps1("psU", [128, 4, 128], F32)
                psKT = ps1("psKT", [128, 8, 128], BF16)

                P.add('sp', lambda e: e.dma_start(out=gmx[:], in_=gmix[:, :]), writes=['gmx'], dma='c')
                P.add('sp', lambda e: e.dma_start(out=rc[:], in_=ropec[:, :, :]), writes=['rc'], dma='c')
                P.add('sp', lambda e: e.dma_start(out=rs[:], in_=ropes[:, :, :]), writes=['rs'], dma='c')
                for kc in range(8):
                    for half in range(2):
                        P.add('pool', lambda e, kc=kc, half=half: e.dma_start(
                            out=w1[:, kc, half * 1024:(half + 1) * 1024],
                            in_=w_in[kc * 128:(kc + 1) * 128, half * 1024:(half + 1) * 1024]),
                            writes=[('w1', kc, half)], dma='w')
                P.add('pool', lambda e: e.memset(Vaug[:, :, :, 64:65], 1.0), writes=['Vones'])
                P.add('pool', lambda e: e.memset(Qaug[:, :, :, 0:16], 0.0), writes=['Qz'])
                P.barrier()

                def rotary(src_ps, nt, t, out_ap, ta, tb, tag):
                    x3 = src_ps[0:nt, :].rearrange("p (h d) -> p h d", h=8)
                    x4 = src_ps[0:nt, :].rearrange("p (h two d) -> p h two d", h=8, two=2)
                    cT = bcast(rc[0:nt, t:t + 1, :].unsqueeze(2), [nt, 8, 2, 32])
                    P.add('dve', lambda e: e.tensor_tensor(out=ta[0:nt].rearrange("p h (two d) -> p h two d", two=2),
                                                           in0=x4, in1=cT, op=ALU.mult),
                          reads=[tag, 'rc'], writes=['rot_ta'])
                    P.add('dve', lambda e: e.tensor_tensor(out=tb[0:nt, :, 0:32], in0=x3[:, :, 32:64],
                                                           in1=bcast(rs[0:nt, t:t + 1, 0:32], [nt, 8, 32]), op=ALU.mult),
                          reads=[tag, 'rs'], writes=['rot_tb0'])
                    P.add('dve', lambda e: e.tensor_tensor(out=tb[0:nt, :, 32:64], in0=x3[:, :, 0:32],
                                                           in1=bcast(rs[0:nt, t:t + 1, 32:64], [nt, 8, 32]), op=ALU.mult),
                          reads=[tag, 'rs'], writes=['rot_tb1'])
                    return ['rot_ta', 'rot_tb0', 'rot_tb1']

                def phase1_tile(t):
                    sample = (t == 32)
                    own = 16 <= t < 32
                    nt = 4 if sample else 128
                    if sample:
                        src, r0 = xs, 0
                    elif own:
                        src, r0 = xo, (t - 16) * 128
                    else:
                        src, r0 = xp, t * 128
                    s = t % 2
                    P.add('sp', lambda e: e.dma_start(out=xt[s][0:nt, :], in_=src[r0:r0 + nt, :]),
                          writes=[('xt', s)], dma=f'x{s}')
                    P.add('act', lambda e: e.activation(out=sqj[0:nt, :], in_=xt[s][0:nt, :], func=AF.Square,
                                                        accum_out=ssq[s][0:nt, :]),
                          reads=[('xt', s)], writes=['sqj', ('ssq', s)])
                    P.add('dve', lambda e: e.tensor_scalar(out=rstd[s][0:nt, :], in0=ssq[s][0:nt, :], scalar1=1.0 / D,
                                                           scalar2=EPS, op0=ALU.mult, op1=ALU.add),
                          reads=[('ssq', s)], writes=[('rstd', s)])
                    P.add('act', lambda e: e.activation(out=rstd[s][0:nt, :], in_=rstd[s][0:nt, :], func=AF.Sqrt),
                          reads=[('rstd', s)], writes=[('rstd', s)])
                    P.add('dve', lambda e: e.reciprocal(out=rstd[s][0:nt, :], in_=rstd[s][0:nt, :]),
                          reads=[('rstd', s)], writes=[('rstd', s)])
                    P.add('dve', lambda e: e.tensor_scalar(out=xn[s][0:nt, :], in0=xt[s][0:nt, :],
                                                           scalar1=rstd[s][0:nt, 0:1], scalar2=None, op0=ALU.mult),
                          reads=[('xt', s), ('rstd', s)], writes=[('xn', s)])
                    for kc in range(8):
                        P.add('pe', lambda e, kc=kc: e.transpose(out=psT[:, kc, 0:nt],
                                                                 in_=xn[s][0:nt, kc * 128:(kc + 1) * 128],
                                                                 identity=ident[0:nt, 0:nt]),
                              reads=[('xn', s), 'ident'], writes=['psT'])
                    P.add('dve', lambda e: e.tensor_tensor(out=hT[s][:, :, 0:nt], in0=psT[:, :, 0:nt],
                                                           in1=bcast(gmx[:].unsqueeze(2), [128, 8, nt]), op=ALU.mult),
                          reads=['psT', 'gmx'], writes=[('hT', s)])
                    projs = (('K', psK, 512), ('V', psV, 1024)) + ((('Q', psQ, 0),) if (own or sample) else ())
                    for name, pst, c0 in projs:
                        for kc in range(8):
                            P.add('pe', lambda e, kc=kc, pst=pst, c0=c0: e.matmul(
                                pst[0:nt, :], lhsT=hT[s][:, kc, 0:nt], rhs=w1[:, kc, c0:c0 + 512],
                                start=(kc == 0), stop=(kc == 7)),
                                reads=[('hT', s), ('w1', kc, c0 // 1024)], writes=['ps' + name])
                    for c in range(4):
                        for kc in range(8):
                            P.add('pe', lambda e, kc=kc, c=c: e.matmul(
                                psU[:, c, 0:nt], lhsT=w1[:, kc, 1536 + c * 128:1536 + (c + 1) * 128],
                                rhs=hT[s][:, kc, 0:nt], start=(kc == 0), stop=(kc == 7)),
                                reads=[('hT', s), ('w1', kc, 1)], writes=['psU'])
                    if sample:
                        P.add('act', lambda e: e.activation(out=usT[:], in_=psU[:, :, 0:4], func=AF.Copy),
                              reads=['psU'], writes=['usT'])
                    else:
                        P.add('act', lambda e: e.activation(out=ust[s][:], in_=psU[:], func=AF.Copy),
                              reads=['psU'], writes=[('ust', s)])
                        P.add('sp', lambda e: e.dma_start(out=ut_scr[:, :, t * 128:(t + 1) * 128], in_=ust[s][:]),
                              reads=[('ust', s)], writes=[('utscr', t)], dma=f'us{s}')
                    kout = ks_f if sample else kf[s]
                    toks = rotary(psK, nt, t, None, t1, t2, 'psK')
                    P.add('dve', lambda e: e.tensor_tensor(out=kout[0:nt], in0=t1[0:nt], in1=t2[0:nt], op=ALU.add),
                          reads=toks, writes=[('kf', s)])
                    if not sample:
                        P.add('pool', lambda e: e.memset(kb[:, :, 0:16], 0.0), writes=['kb'])
                        n_blk = t // 2
                        P.add('pool', lambda e: e.memset(kb[:, :, n_blk:n_blk + 1], 1.0), writes=['kb'])
                        P.add('act', lambda e: e.activation(out=kb[:, :, 16:80], in_=kf[s][:], func=AF.Copy),
                              reads=[('kf', s)], writes=['kb'])
                        for h in range(8):
                            P.add('pe', lambda e, h=h: e.transpose(out=psKT[0:80, h, :], in_=kb[:, h, :], identity=ident[:]),
                                  reads=['kb', 'ident'], writes=['psKT'])
                        P.add('act', lambda e: e.activation(out=KaT[0:80, :, t * 128:(t + 1) * 128], in_=psKT[0:80],
                                                            func=AF.Copy),
                              reads=['psKT'], writes=[('KaT', t)])
                    vout = vs_f if sample else vf[s]
                    P.add('act', lambda e: e.activation(out=vout[0:nt], in_=psV[0:nt, :].rearrange("p (h d) -> p h d", h=8),
                                                        func=AF.Copy),
                          reads=['psV'], writes=[('vf', s)])
                    if not sample:
                        P.add('pool', lambda e: e.tensor_copy(out=Vaug[:, t, :, 0:64], in_=vf[s][:]),
                              reads=[('vf', s)], writes=[('Vaug', t)])
                    if own or sample:
                        toksq = rotary(psQ, nt, t, None, t1, t2, 'psQ')
                        if sample:
                            P.add('dve', lambda e: e.tensor_tensor(out=qs_f[:], in0=t1[0:4], in1=t2[0:4], op=ALU.add),
                                  reads=toksq, writes=['qs_f'])
                        else:
                            ti = t - 16
                            P.add('dve', lambda e: e.tensor_tensor(out=Qaug[:, ti, :, 16:80], in0=t1[:], in1=t2[:], op=ALU.add),
                                  reads=toksq, writes=[('Qaug', ti)])
                    if own:
                        P.add('sp', lambda e: e.dma_start(out=ko[r0:r0 + 128, :], in_=kf[s][:].rearrange("p h d -> p (h d)")),
                              reads=[('kf', s)], dma=f'ko{s}')
                        P.add('sp', lambda e: e.dma_start(out=vo[r0:r0 + 128, :], in_=vf[s][:].rearrange("p h d -> p (h d)")),
                              reads=[('vf', s)], dma=f'vo{s}')
                    if sample:
                        P.add('sp', lambda e: e.dma_start(out=kso[:, :], in_=ks_f[:].rearrange("p h d -> p (h d)")),
                              reads=[('kf', s)], dma=f'ko{s}')
                        P.add('sp', lambda e: e.dma_start(out=vso[:, :], in_=vs_f[:].rearrange("p h d -> p (h d)")),
                              reads=[('vf', s)], dma=f'vo{s}')

                tiles = [int(v) for v in DBG.split(',')] if (DBG and DBG != 'all') else range(33)
                for t in tiles:
                    phase1_tile(t)
                P.barrier()
                P.emit()

            s2 = ExitStack()
            with s2:
                sb2, ps2 = mk(s2)
                mask4f = sb2("mask4f", [128, 4, 512], F32)
                mask4 = sb2("mask4b", [128, 4, 512], BF16)
                pbias = sb2("pbias_sb", [128, 8, 16], F32)
                kms = sb2("kms", [128, 8, 16], F32)
                kmT = sb2("kmT", [128, 8, 16], BF16)
                QaT = [sb2(f"QaT{i}", [128, 8, 512], BF16) for i in range(2)]
                s_m = sb2("s_m", [128, 8, 16], F32)
                top8 = sb2("top8", [128, 8, 8], F32)
                sel = sb2("sel", [128, 8, 16], F32)
                val = sb2("val", [128, 8, 16], F32)
                PT = [sb2(f"PT{i}", [128, 512], BF16) for i in range(3)]
                osb = [sb2(f"osb{i}", [128, 512], F32) for i in range(2)]
                rsum = sb2("rsum", [128, 512], F32)
                onesf = sb2("onesf", [128, 64], F32)
                ostg = [sb2(f"ostg{i}", [128, 512], BF16) for i in range(2)]
                psA = [ps2(f"psA{i}", [128, 512], F32) for i in range(2)]
                psO = [ps2(f"psO{i}", [128, 512], F32) for i in range(2)]
                psQT = ps2("psQT", [128, 8, 128], BF16)
                psS = ps2("psS", [128, 8, 16], F32)
                psBT = ps2("psBT", [16, 8, 128], BF16)
                psR = ps2("psR", [128, 512], F32)

                P.add('sp', lambda e: e.dma_start(out=mask4f[:], in_=mask4_d[:, :, :]), writes=['mask4f'], dma='c')
                P.add('sp', lambda e: e.dma_start(out=pbias[:], in_=pbias_d[:, :, :]), writes=['pbias'], dma='c')
                P.add('pool', lambda e: e.memset(onesf[:], 1.0), writes=['onesf'])
                P.barrier()
                P.add('pool', lambda e: e.tensor_copy(out=mask4[:], in_=mask4f[:]), reads=['mask4f'], writes=['mask4'])
                P.add('dve', lambda e: e.tensor_reduce(out=kms[0:80], in_=KaT[0:80].rearrange("p h (n k) -> p h n k", k=256),
                                                       axis=AX.X, op=ALU.add),
                      reads=[('KaT', t) for t in range(32)], writes=['kms'])
                P.add('dve', lambda e: e.tensor_copy(out=kmT[0:80], in_=kms[0:80]), reads=['kms'], writes=['kmT'])

                def qtok(ti):
                    return ('QaT', (ti // 4) % 2, ti % 4)

                def prep_qtile(ti, qa):
                    a = ti % 4
                    m = ti // 2
                    for h in range(8):
                        P.add('pe', lambda e, h=h: e.transpose(out=psQT[0:80, h, :], in_=Qaug[:, ti, h, :], identity=ident[:]),
                              reads=[('Qaug', ti), 'ident'], writes=['psQT'])
                    P.add('act', lambda e: e.activation(out=qa[0:80, :, a * 128:(a + 1) * 128], in_=psQT[0:80], func=AF.Copy),
                          reads=['psQT'], writes=[qtok(ti)])
                    for h in range(8):
                        P.add('pe', lambda e, h=h: e.matmul(psS[:, h, :], lhsT=qa[0:80, h, a * 128:(a + 1) * 128],
                                                            rhs=kmT[0:80, h, :], start=True, stop=True),
                              reads=[qtok(ti), 'kmT'], writes=['psS'])
                    P.add('dve', lambda e: e.tensor_tensor(out=s_m[:], in0=psS[:], in1=bcast(pbias[:, m:m + 1, :], [128, 8, 16]),
                                                           op=ALU.add),
                          reads=['psS', 'pbias'], writes=['s_m'])
                    for h in range(8):
                        P.add('dve', lambda e, h=h: e.max(out=top8[:, h, :], in_=s_m[:, h, :]),
                              reads=['s_m'], writes=[('top8', h)])
                    for h in range(8):
                        P.add('dve', lambda e, h=h: e.tensor_scalar(out=sel[:, h, :], in0=s_m[:, h, :], scalar1=top8[:, h, 2:3],
                                                                    scalar2=None, op0=ALU.is_ge),
                              reads=['s_m', ('top8', h)], writes=[('sel', h)])
                    P.add('dve', lambda e: e.tensor_scalar(out=val[:], in0=s_m[:], scalar1=-0.5 * BIG, scalar2=None, op0=ALU.is_gt),
                          reads=['s_m'], writes=['val'])
                    P.add('dve', lambda e: e.tensor_tensor(out=sel[:], in0=sel[:], in1=val[:], op=ALU.mult),
                          reads=[('sel', h) for h in range(8)] + ['val'], writes=[('sel', h) for h in range(8)])
                    P.add('dve', lambda e: e.tensor_scalar(out=Qaug[:, ti, :, 0:16], in0=sel[:], scalar1=-1.0, scalar2=BIG,
                                                           op0=ALU.add, op1=ALU.mult),
                          reads=[('sel', h) for h in range(8)], writes=[('Qb', ti)])
                    P.add('dve', lambda e: e.memset(Qaug[:, ti, :, 8 + m:9 + m], 0.0), reads=[('Qb', ti)], writes=[('Qb', ti)])
                    for h in range(8):
                        P.add('pe', lambda e, h=h: e.transpose(out=psBT[0:16, h, :], in_=Qaug[:, ti, h, 0:16], identity=ident[:]),
                              reads=[('Qb', ti), 'ident'], writes=['psBT'])
                    P.add('act', lambda e: e.activation(out=qa[0:16, :, a * 128:(a + 1) * 128], in_=psBT[0:16], func=AF.Copy),
                          reads=['psBT'], writes=[qtok(ti)])

                qgs = [int(v) for v in os.environ.get('KQG', '0,1,2,3').split(',')]
                step = 0
                def attn_group(qg):
                    nonlocal step
                    qa = QaT[qg % 2]
                    for ti in range(4 * qg, 4 * qg + 4):
                        prep_qtile(ti, qa)
                    qtoks = [qtok(ti) for ti in range(4 * qg, 4 * qg + 4)]
                    nkt = 20 + 4 * qg
                    for h in range(8):
                        po = psO[h % 2]
                        ob = osb[h % 2]

                        def qk(kt, h=h):
                            pa = psA[kt % 2]
                            P.add('pe', lambda e: e.matmul(pa[:], lhsT=KaT[0:80, h, kt * 128:(kt + 1) * 128], rhs=qa[0:80, h, :],
                                                           start=True, stop=True),
                                  reads=qtoks + [('KaT', kt)], writes=[('psA', kt % 2)])

                        def rest(kt, h=h, po=po):
                            nonlocal step
                            pa = psA[kt % 2]
                            pt = PT[step % 3]
                            pk = ('PT', step % 3)
                            step += 1
                            P.add('act', lambda e: e.activation(out=pt[:], in_=pa[:], func=AF.Exp, scale=0.125),
                                  reads=[('psA', kt % 2)], writes=[pk])
                            j = kt - (nkt - 4)
                            if j >= 0:
                                P.add('pool', lambda e: e.tensor_tensor(out=pt[:], in0=pt[:], in1=mask4[:, j, :], op=ALU.mult),
                                      reads=[pk, 'mask4'], writes=[pk])
                            P.add('pe', lambda e: e.matmul(po[0:65, :], lhsT=Vaug[:, kt, h, 0:65], rhs=pt[:],
                                                           start=(kt == 0), stop=(kt == nkt - 1)),
                                  reads=[pk, ('Vaug', kt), 'Vones'], writes=[('psO', h % 2)])

                        qk(0)
                        for kt in range(nkt):
                            if kt + 1 < nkt:
                                qk(kt + 1)
                            rest(kt)
                        P.add('act', lambda e, po=po, ob=ob: e.activation(out=ob[0:64, :], in_=po[0:64, :], func=AF.Copy),
                              reads=[('psO', h % 2)], writes=[('osb', h % 2)])
                        P.add('dve', lambda e, po=po: e.reciprocal(out=rsum[64:65, :], in_=po[64:65, :]),
                              reads=[('psO', h % 2)], writes=['rsum'])
                        P.add('pe', lambda e: e.matmul(psR[0:64, :], lhsT=onesf[64:65, 0:64], rhs=rsum[64:65, :], start=True, stop=True),
                              reads=['rsum', 'onesf'], writes=['psR'])
                        pofs = (h % 2) * 64
                        og = ostg[h % 2]
                        P.add('dve', lambda e, ob=ob, og=og: e.tensor_tensor(
                            out=og[0:64, :], in0=ob[0:64, :], in1=psR[0:64, :], op=ALU.mult),
                            reads=[('osb', h % 2), 'psR'], writes=[('ostg', h % 2)])
                        P.add('sp', lambda e, og=og, h=h, pofs=pofs, qg=qg: e.dma_start(
                            out=ot_scr[pofs:pofs + 64, h // 2, qg * 512:(qg + 1) * 512], in_=og[0:64, :]),
                            reads=[('ostg', h % 2)], writes=[('OT', qg, h)], dma=f'og{h % 2}')
                for qg in qgs:
                    attn_group(qg)
                if DBG:
                    P.barrier()
                    dOT = nc.dram_tensor("dbg_OT", [128, 4, 2048], BF16, kind="ExternalOutput").ap()
                    P.add('sp', lambda e: e.dma_start(out=dOT, in_=ot_scr), dma='dbg')
                    dbg('Qaug', Qaug[:], [128, 16, 8, 80], BF16, [('Qb', ti) for ti in range(16)])
                P.barrier()
                P.emit()

        TWO_PI = 6.283185307179586
        PI_LO = 3.1415925
        HALF_PI = 1.5707963267948966

        def D_(fn):
            P.add('dve', fn, reads=['g'], writes=['g'])

        def A_(fn):
            P.add('act', fn, reads=['g'], writes=['g'])

        def rr(x, tf, ti):
            D_(lambda e: e.tensor_scalar(out=tf, in0=x, scalar1=1.0 / TWO_PI, scalar2=None, op0=ALU.mult))
            D_(lambda e: e.tensor_copy(out=ti, in_=tf))
            D_(lambda e: e.tensor_copy(out=tf, in_=ti))
            D_(lambda e: e.scalar_tensor_tensor(out=x, in0=tf, scalar=-TWO_PI, in1=x, op0=ALU.mult, op1=ALU.add))
            D_(lambda e: e.tensor_scalar(out=tf, in0=x, scalar1=PI_LO, scalar2=None, op0=ALU.is_gt))
            D_(lambda e: e.scalar_tensor_tensor(out=x, in0=tf, scalar=-TWO_PI, in1=x, op0=ALU.mult, op1=ALU.add))
            D_(lambda e: e.tensor_scalar(out=tf, in0=x, scalar1=-PI_LO, scalar2=None, op0=ALU.is_lt))
            D_(lambda e: e.scalar_tensor_tensor(out=x, in0=tf, scalar=TWO_PI, in1=x, op0=ALU.mult, op1=ALU.add))
            D_(lambda e: e.tensor_scalar(out=x, in0=x, scalar1=-PI_LO, scalar2=PI_LO, op0=ALU.max, op1=ALU.min))

        def powtab(sbx, tagn, AR, AI, LDT, F, kv, K, coef):
            n = F * K
            al = sbx(tagn + "al", [128, F], F32); th = sbx(tagn + "th", [128, F], F32)
            tfs = sbx(tagn + "tfs", [128, F], F32); tis = sbx(tagn + "tis", [128, F], I32)
            PR = sbx(tagn + "PR", [128, n], F32); PI = sbx(tagn + "PI", [128, n], F32)
            ang = sbx(tagn + "ang", [128, n], F32); tf = sbx(tagn + "tf", [128, n], F32); ti = sbx(tagn + "ti", [128, n], I32)
            A_(lambda e: e.activation(out=tfs[:], in_=LDT, func=AF.Exp))
            D_(lambda e: e.tensor_tensor(out=al[:], in0=AR, in1=tfs[:], op=ALU.mult))
            D_(lambda e: e.tensor_tensor(out=th[:], in0=AI, in1=tfs[:], op=ALU.mult))
            rr(th[:], tfs[:], tis[:])
            v3 = lambda t: t[:].rearrange("q (f k) -> q f k", k=K)
            thb = bcast(th[:].unsqueeze(2), [128, F, K]); alb = bcast(al[:].unsqueeze(2), [128, F, K])
            kvb = bcast(kv.unsqueeze(1), [128, F, K])
            D_(lambda e: e.tensor_tensor(out=v3(ang), in0=thb, in1=kvb, op=ALU.mult))
            rr(ang[:], tf[:], ti[:])
            A_(lambda e: e.activation(out=PI[:], in_=ang[:], func=AF.Sin))
            D_(lambda e: e.tensor_scalar(out=ang[:], in0=ang[:], scalar1=HALF_PI, scalar2=None, op0=ALU.add))
            rr(ang[:], tf[:], ti[:])
            A_(lambda e: e.activation(out=PR[:], in_=ang[:], func=AF.Sin))
            D_(lambda e: e.tensor_tensor(out=v3(tf), in0=alb, in1=kvb, op=ALU.mult))
            A_(lambda e: e.activation(out=tf[:], in_=tf[:], func=AF.Exp))
            D_(lambda e: e.tensor_tensor(out=PR[:], in0=PR[:], in1=tf[:], op=ALU.mult))
            D_(lambda e: e.tensor_tensor(out=PI[:], in0=PI[:], in1=tf[:], op=ALU.mult))
            res = dict(PR=PR, PI=PI, al=al, th=th, tf=tf, ang=ang, ti=ti, tfs=tfs, tis=tis)
            if coef is not None:
                k1 = coef
                cr = sbx(tagn + "cr", [128, F], F32); ci = sbx(tagn + "ci", [128, F], F32)
                nr = sbx(tagn + "nr", [128, F], F32); den = sbx(tagn + "den", [128, F], F32)
                p1r = v3(PR)[:, :, k1]; p1i = v3(PI)[:, :, k1]
                D_(lambda e: e.tensor_scalar(out=nr[:], in0=p1r, scalar1=-1.0, scalar2=None, op0=ALU.add))
                D_(lambda e: e.tensor_tensor(out=den[:], in0=AR, in1=AR, op=ALU.mult))
                D_(lambda e: e.tensor_tensor(out=tfs[:], in0=AI, in1=AI, op=ALU.mult))
                D_(lambda e: e.tensor_tensor(out=den[:], in0=den[:], in1=tfs[:], op=ALU.add))
                D_(lambda e: e.reciprocal(out=den[:], in_=den[:]))
                D_(lambda e: e.tensor_tensor(out=cr[:], in0=nr[:], in1=AR, op=ALU.mult))
                D_(lambda e: e.tensor_tensor(out=tfs[:], in0=p1i, in1=AI, op=ALU.mult))
                D_(lambda e: e.tensor_tensor(out=cr[:], in0=cr[:], in1=tfs[:], op=ALU.add))
                D_(lambda e: e.tensor_tensor(out=cr[:], in0=cr[:], in1=den[:], op=ALU.mult))
                D_(lambda e: e.tensor_tensor(out=ci[:], in0=p1i, in1=AR, op=ALU.mult))
                D_(lambda e: e.tensor_tensor(out=tfs[:], in0=nr[:], in1=AI, op=ALU.mult))
                D_(lambda e: e.tensor_tensor(out=ci[:], in0=ci[:], in1=tfs[:], op=ALU.subtract))
                D_(lambda e: e.tensor_tensor(out=ci[:], in0=ci[:], in1=den[:], op=ALU.mult))
                res['cr'] = cr; res['ci'] = ci
            return res

        def cmul(outr, outi, ar_, ai_, br_, bi_, tmp):
            D_(lambda e: e.tensor_tensor(out=outr, in0=ar_, in1=br_, op=ALU.mult))
            D_(lambda e: e.tensor_tensor(out=tmp, in0=ai_, in1=bi_, op=ALU.mult))
            D_(lambda e: e.tensor_tensor(out=outr, in0=outr, in1=tmp, op=ALU.subtract))
            D_(lambda e: e.tensor_tensor(out=outi, in0=ar_, in1=bi_, op=ALU.mult))
            D_(lambda e: e.tensor_tensor(out=tmp, in0=ai_, in1=br_, op=ALU.mult))
            D_(lambda e: e.tensor_tensor(out=outi, in0=outi, in1=tmp, op=ALU.add))

        s3 = ExitStack()
        with s3:
            sb3, ps3 = mk(s3)
            Yc = sb3("Yc", [128, 4, 16, 128], BF16)
            Ycs = sb3("Ycs", [4, 4, 128], BF16)
            s3a = ExitStack()
            with s3a:
                sb3a, ps3a = mk(s3a)
                Ssr = sb3a("Ssr", [128, 16, 4], F32)
                Ssi = sb3a("Ssi", [128, 16, 4], F32)
                SR = sb3a("SR", [128, 16, 256], F32)
                SI = sb3a("SI", [128, 16, 256], F32)
                kv0 = sb3a("kv0s", [128, 16], F32); kv1 = sb3a("kv1s", [128, 16], F32); kvj = sb3a("kvjs", [128, 256], F32)
                pmB = sb3a("pmBs", [128, 2], F32); pmD = sb3a("pmDs", [128, 2], F32)
                for i_, (dst, srcd) in enumerate(((kv0, kv0_d), (kv1, kv1_d), (kvj, kvj_d), (pmB, pmB_d), (pmD, pmD_d))):
                    P.add('sp', lambda e, dst=dst, srcd=srcd: e.dma_start(out=dst[:], in_=srcd), writes=['g'], dma='c')
                sKV = ExitStack()
                with sKV:
                    sbKV, _ = mk(sKV)
                    idxG = sbKV("idxG", [128, 8, 24], I32)
                    Ksel = sbKV("Ksel", [128, 4, 8, 6, 64], BF16); Vsel = sbKV("Vsel", [128, 4, 8, 6, 64], BF16)
                    sS = ExitStack()
                    with sS:
                        sbS, psS_ = mk(sS)
                        F32R = mybir.dt.float32r
                        ck_rows = ckh.rearrange("(r h) d -> r (h d)", h=8)
                        ptrep = sbS("ptrep", [128, 256], I32); idxP = sbS("idxP", [128, 256], I32)
                        piota = sbS("piota", [128, 1], F32); hb = sbS("hb", [128, 8], F32)
                        zsel = sbS("zsel", [128, 255], F32); rsel = sbS("rsel", [4, 128], F32)
                        eye8 = sbS("eye8", [8, 8], F32); esel = sbS("esel", [4, 4], F32)
                        ptE = sbS("ptE", [8, 128], I32); ptO = sbS("ptO", [8, 128], I32)
                        ptEf = sbS("ptEf", [8, 128], F32); ptOf = sbS("ptOf", [8, 128], F32)
                        ones1 = sbS("ones1", [128, 128], F32)
                        pgb = [sbS(f"pgb{i}", [128, 512], F32) for i in range(8)]
                        for dst, srcd in ((ptrep, ptrep_d), (piota, piota_d), (hb, hb_d), (zsel, zsel_d), (rsel, rsel_d), (eye8, eye8_d),
                                          (esel, esel_d), (ptE, ptE_d), (ptO, ptO_d)):
                            P.add('sp', lambda e, dst=dst, srcd=srcd: e.dma_start(out=dst[:], in_=srcd), writes=['g'], dma='c')
                        P.add('pool', lambda e: e.memset(ones1[:], 1.0), writes=['g'])
                        P.barrier()
                        D_(lambda e: e.tensor_scalar(out=idxP[:], in0=ptrep[:], scalar1=128.0, scalar2=piota[:, 0:1], op0=ALU.mult, op1=ALU.add))
                        D_(lambda e: e.tensor_copy(out=ptEf[:], in_=ptE[:]))
                        D_(lambda e: e.tensor_copy(out=ptOf[:], in_=ptO[:]))
                        psKS = psS_("psKS", [128, 512], F32)
                        psQR = psS_("psQR", [128, 512], F32)
                        psM = psS_("psM", [128, 512], F32)
                        for i in range(256):
                            sl = i % 8
                            P.add('pool', lambda e, i=i, sl=sl: e.indirect_dma_start(
                                out=pgb[sl][:], out_offset=None, in_=ck_rows,
                                in_offset=bass.IndirectOffsetOnAxis(ap=idxP[:, i:i + 1], axis=0)),
                                reads=['g'], writes=[('pgb', sl)], dma=f'pg{sl}')
                            m = (i // 64) * 32 + (i % 64) // 2
                            P.add('pe', lambda e, i=i, sl=sl, m=m: e.matmul(psKS[:], lhsT=zsel[:, 127 - m:255 - m], rhs=pgb[sl][:],
                                                                          start=(i == 0), stop=(i == 255)),
                                  reads=[('pgb', sl), 'g'], writes=['psKS'])
                        qrep = sbS("qrep", [128, 512], F32); prodS = sbS("prodS", [128, 512], F32); sbl = sbS("sbl", [128, 8], F32)
                        P.add('pe', lambda e: e.matmul(psQR[:], lhsT=rsel[:], rhs=qs_f[:].rearrange("s h d -> s (h d)"), start=True, stop=True),
                              reads=['qs_f', 'g'], writes=['psQR'])
                        P.add('act', lambda e: e.activation(out=qrep[:], in_=psQR[:], func=AF.Copy), reads=['psQR'], writes=['g'])
                        D_(lambda e: e.tensor_tensor(out=prodS[:], in0=qrep[:], in1=psKS[:], op=ALU.mult))
                        P.add('dve', lambda e: e.tensor_reduce(out=sbl[:], in_=prodS[:].rearrange("q (h d) -> q h d", d=64), axis=AX.X, op=ALU.add),
                              reads=['g', 'psKS'], writes=['g'])
                        P.add('pe', lambda e: e.transpose(out=psM[0:8, 0:128], in_=sbl[:], identity=identf[:]), reads=['g', 'identf'], writes=['psM'])
                        sbT = sbS("sbT", [8, 128], F32); top8s = sbS("top8s", [8, 4, 8], F32)
                        selr = sbS("selr", [8, 3, 128], F32); tmpS = sbS("tmpS", [8, 3, 128], F32)
                        PG = sbS("PG", [8, 2, 12], F32); rhsE = sbS("rhsE", [8, 8, 24], F32)
                        P.add('act', lambda e: e.activation(out=sbT[:], in_=psM[0:8, 0:128], func=AF.Copy), reads=['psM'], writes=['g'])
                        for s_ in range(4):
                            D_(lambda e, s_=s_: e.max(out=top8s[:, s_, :], in_=sbT[:, s_ * 32:(s_ + 1) * 32]))
                        for r_ in range(3):
                            for s_ in range(4):
                                D_(lambda e, s_=s_, r_=r_: e.tensor_scalar(out=selr[:, r_, s_ * 32:(s_ + 1) * 32], in0=sbT[:, s_ * 32:(s_ + 1) * 32],
                                                                           scalar1=top8s[:, s_, r_:r_ + 1], scalar2=None, op0=ALU.is_equal))
                        for e_, ptf in enumerate((ptEf, ptOf)):
                            D_(lambda e, ptf=ptf: e.tensor_tensor(out=tmpS[:], in0=selr[:], in1=bcast(ptf[:].unsqueeze(1), [8, 3, 128]), op=ALU.mult))
                            D_(lambda e, e_=e_: e.tensor_reduce(out=PG[:, e_, :], in_=tmpS[:].rearrange("h r (s n) -> h (r s) n", n=32),
                                                                axis=AX.X, op=ALU.add))
                        D_(lambda e: e.tensor_tensor(out=rhsE[:], in0=bcast(PG[:].rearrange("h e x -> h (e x)").unsqueeze(1), [8, 8, 24]),
                                                     in1=bcast(eye8[:].unsqueeze(2), [8, 8, 24]), op=ALU.mult))
                        P.add('pe', lambda e: e.matmul(psM[:, 0:192], lhsT=ones1[0:8, :], rhs=rhsE[:].rearrange("k h x -> k (h x)"), start=True, stop=True),
                              reads=['g'], writes=['psM'])
                        P.add('dve', lambda e: e.scalar_tensor_tensor(out=idxG[:], in0=psM[:, 0:192].rearrange("q (h x) -> q h x", h=8), scalar=1024.0,
                                                                      in1=bcast(hb[:].unsqueeze(2), [128, 8, 24]), op0=ALU.mult, op1=ALU.add),
                              reads=['psM', 'g'], writes=['g', 'idxG'])
                        for h in range(8):
                            for e_ in range(2):
                                for r_ in range(3):
                                    for s_ in range(4):
                                        x = e_ * 12 + r_ * 4 + s_
                                        for srcd, dstt in ((ckh, Ksel), (cvh, Vsel)):
                                            P.add('pool', lambda e, h=h, x=x, s_=s_, j=r_ * 2 + e_, srcd=srcd, dstt=dstt: e.indirect_dma_start(
                                                out=dstt[:, s_, h, j, :], out_offset=None, in_=srcd,
                                                in_offset=bass.IndirectOffsetOnAxis(ap=idxG[:, h, x:x + 1], axis=0)),
                                                reads=['idxG'], writes=[('KVsel', h, x, id(dstt))], dma='gk')
                        P.excl = {'d:gk'}
                        P.barrier()
                        P.emit()
                    sB = ExitStack()
                    with sB:
                        sbB, psB = mk(sB)
                        VBre = sbB("VBre", [128, 4, 16, 128], BF16)
                        VBim = sbB("VBim", [128, 4, 16, 128], BF16)
                        sg = ExitStack()
                        with sg:
                            sbg, _ = mk(sg)
                            tA = {}
                            for nm, srcd in (('ar', arB), ('ai', aiB), ('ldt', ldtB), ('br', brB), ('bi', biB)):
                                tA[nm] = sbg("gB" + nm, [128, 256], F32)
                                P.add('sp', lambda e, t=tA[nm], srcd=srcd: e.dma_start(out=t[:], in_=srcd), writes=['g'], dma='c')
                            P.barrier()
                            for hf in range(2):
                                sh = ExitStack()
                                with sh:
                                    sbh, _ = mk(sh)
                                    fs = slice(hf * 128, (hf + 1) * 128)
                                    r = powtab(sbh, f"pB{hf}", tA['ar'][:, fs], tA['ai'][:, fs], tA['ldt'][:, fs], 128, kv0[:], 16, coef=1)
                                    Zr = sbh(f"Zr{hf}", [128, 2048], F32); Zi = sbh(f"Zi{hf}", [128, 2048], F32)
                                    v3 = lambda t: t[:].rearrange("q (f k) -> q f k", k=16)
                                    crb = bcast(r['cr'][:].unsqueeze(2), [128, 128, 16]); cib = bcast(r['ci'][:].unsqueeze(2), [128, 128, 16])
                                    cmul(v3(Zr), v3(Zi), v3(r['PR']), v3(r['PI']), crb, cib, v3(r['tf']))
                                    brb = bcast(tA['br'][:, fs].unsqueeze(2), [128, 128, 16]); bib = bcast(tA['bi'][:, fs].unsqueeze(2), [128, 128, 16])
                                    Xr, Xi = r['PR'], r['PI']
                                    cmul(v3(Xr), v3(Xi), v3(Zr), v3(Zi), brb, bib, v3(r['tf']))
                                    for gw in range(2):
                                        for Xs, VB in ((Xr, VBre), (Xi, VBim)):
                                            D_(lambda e, gw=gw, Xs=Xs, VB=VB, hf=hf: e.tensor_scalar(
                                                out=VB[:, 2 * hf:2 * hf + 2, :, gw * 64:(gw + 1) * 64],
                                                in0=Xs[:].rearrange("q (c p d) -> q c d p", c=2, p=64, d=16),
                                                scalar1=pmB[:, gw:gw + 1], scalar2=None, op0=ALU.mult))
                                    P.barrier()
                                    P.emit()
                            P.barrier()
                            P.emit()
                        UT = sbB("UT", [128, 4, 4096], BF16)
                        for c in range(4):
                            P.add('sp', lambda e, c=c: e.dma_start(out=UT[:, c, :], in_=ut_scr[:, c, :]),
                                  reads=[('utscr', t) for t in range(32)], writes=['UT'], dma='c')
                        P.barrier()
                        psS2 = [psB(f"psSB{i}", [128, 256], F32) for i in range(4)]
                        for w in range(16):
                            kc, r0 = w // 4, 32 * (w % 4)
                            for ri, (VB, SX) in enumerate(((VBre, SR), (VBim, SI))):
                                pst = psS2[(2 * w + ri) % 4]
                                for dl in range(16):
                                    rhs = UT[r0:r0 + 32, kc, :].rearrange("q (j s) -> q j s", s=16)[:, :, 15 - dl]
                                    P.add('pe', lambda e, pst=pst, VB=VB, dl=dl, rhs=rhs, r0=r0, kc=kc: e.matmul(
                                        pst[:], lhsT=VB[r0:r0 + 32, kc, dl, :], rhs=rhs, start=(dl == 0), stop=(dl == 15),
                                        tile_position=(r0, 0)),
                                        reads=['g', 'UT'], writes=[('psSB', (2 * w + ri) % 4)])
                                P.add('act', lambda e, pst=pst, SX=SX, w=w: e.activation(out=SX[:, w, :], in_=pst[:], func=AF.Copy),
                                      reads=[('psSB', (2 * w + ri) % 4)], writes=[('S', ri, w)])
                        for w in range(16):
                            kc, r0 = w // 4, 32 * (w % 4)
                            for ri, (VB, SX) in enumerate(((VBre, Ssr), (VBim, Ssi))):
                                pst = psS2[(2 * w + ri) % 4]
                                P.add('pe', lambda e, pst=pst, VB=VB, r0=r0, kc=kc: e.matmul(
                                    pst[:, 0:4], lhsT=VB[r0:r0 + 32, kc, 0, :], rhs=usT[r0:r0 + 32, kc, :], start=True, stop=True,
                                    tile_position=(r0, 0)),
                                    reads=['g', 'usT'], writes=[('psSB', (2 * w + ri) % 4)])
                                P.add('act', lambda e, pst=pst, SX=SX, w=w: e.activation(out=SX[:, w, :], in_=pst[:, 0:4], func=AF.Copy),
                                      reads=[('psSB', (2 * w + ri) % 4)], writes=[('Ss', ri)])
                        P.barrier()
                        P.emit()
                    sS5 = ExitStack()
                    with sS5:
                        sbS, psS_ = mk(sS5)
                        esel = sbS("esel5", [4, 4], F32); ones1 = sbS("ones5", [128, 128], F32)
                        psQR = psS_("psQR5", [128, 512], F32); psM = psS_("psM5", [128, 512], F32)
                        P.add('sp', lambda e: e.dma_start(out=esel[:], in_=esel_d), writes=['g'], dma='c')
                        P.add('dve', lambda e: e.memset(ones1[:], 1.0), writes=['g'])
                        P.excl = set()
                        P.barrier()
                        qbc = sbS("qbc", [128, 4, 512], F32); sc = sbS("sc", [128, 4, 8, 6], F32); Wn = sbS("Wn", [128, 4, 512], F32)
                        Wd = sbS("Wd", [128, 4, 512], F32); psum_j = sbS("psumj", [128, 4, 8], F32)
                        big = sbS("bigS", [128, 8, 6, 64], F32)
                        for s_ in range(4):
                            P.add('pe', lambda e, s_=s_: e.matmul(psQR[:], lhsT=bcast(esel[:, s_:s_ + 1], [4, 128]), rhs=qs_f[:].rearrange("s h d -> s (h d)"),
                                                                  start=True, stop=True), reads=['qs_f', 'g'], writes=['psQR'])
                            P.add('act', lambda e, s_=s_: e.activation(out=qbc[:, s_, :], in_=psQR[:], func=AF.Copy), reads=['psQR'], writes=['g'])
                        for s_ in range(4):
                            D_(lambda e, s_=s_: e.tensor_tensor(out=big[:], in0=Ksel[:, s_],
                                                                in1=bcast(qbc[:, s_, :].rearrange("q (h d) -> q h d", d=64).unsqueeze(2), [128, 8, 6, 64]),
                                                                op=ALU.mult))
                            D_(lambda e, s_=s_: e.tensor_reduce(out=sc[:, s_], in_=big[:], axis=AX.X, op=ALU.add))
                        A_(lambda e: e.activation(out=sc[:], in_=sc[:], func=AF.Exp, scale=0.125))
                        D_(lambda e: e.tensor_reduce(out=psum_j[:], in_=sc[:], axis=AX.X, op=ALU.add))
                        for s_ in range(4):
                            D_(lambda e, s_=s_: e.tensor_tensor(out=big[:], in0=Vsel[:, s_], in1=bcast(sc[:, s_].unsqueeze(3), [128, 8, 6, 64]), op=ALU.mult))
                            D_(lambda e, s_=s_: e.tensor_reduce(out=Wn[:, s_, :].rearrange("q (h d) -> q h d", d=64),
                                                                in_=big[:].rearrange("q h j d -> q h d j"), axis=AX.X, op=ALU.add))
                        D_(lambda e: e.tensor_copy(out=Wd[:].rearrange("q s (h d) -> q s h d", d=64), in_=bcast(psum_j[:].unsqueeze(3), [128, 4, 8, 64])))
                        ssf = sbS("ssf", [4, 8], F32); prs = sbS("prs", [4, 8, 64], F32); Wsn = sbS("Wsn", [4, 512], F32); Wsd = sbS("Wsd", [4, 512], F32)
                        D_(lambda e: e.tensor_tensor(out=prs[:], in0=qs_f[:], in1=ks_f[:], op=ALU.mult))
                        D_(lambda e: e.tensor_reduce(out=ssf[:], in_=prs[:], axis=AX.X, op=ALU.add))
                        A_(lambda e: e.activation(out=ssf[:], in_=ssf[:], func=AF.Exp, scale=0.125))
                        D_(lambda e: e.tensor_tensor(out=Wsn[:].rearrange("s (h d) -> s h d", d=64), in0=vs_f[:], in1=bcast(ssf[:].unsqueeze(2), [4, 8, 64]),
                                                     op=ALU.mult))
                        D_(lambda e: e.tensor_copy(out=Wsd[:].rearrange("s (h d) -> s h d", d=64), in_=bcast(ssf[:].unsqueeze(2), [4, 8, 64])))
                        for wi, (Wbig, Wself) in enumerate(((Wn, Wsn), (Wd, Wsd))):
                            for kc in range(4):
                                for s_ in range(4):
                                    col = wi * 16 + kc * 4 + s_
                                    P.add('pe', lambda e, Wbig=Wbig, kc=kc, s_=s_, col=col: e.matmul(
                                        psM[:, col:col + 1], lhsT=Wbig[:, s_, kc * 128:(kc + 1) * 128], rhs=ones1[:, 0:1], start=True, stop=False),
                                        reads=['g'], writes=['psM'])
                                    P.add('pe', lambda e, Wself=Wself, kc=kc, s_=s_, col=col: e.matmul(
                                        psM[:, col:col + 1], lhsT=Wself[:, kc * 128:(kc + 1) * 128], rhs=esel[:, s_:s_ + 1], start=False, stop=True),
                                        reads=['g'], writes=['psM'])
                        rden = sbS("rden", [128, 16], F32)
                        P.add('dve', lambda e: e.reciprocal(out=rden[:], in_=psM[:, 16:32]), reads=['psM'], writes=['g'])
                        P.add('dve', lambda e: e.tensor_tensor(out=OTs[:].rearrange("q k s -> q (k s)"), in0=psM[:, 0:16], in1=rden[:], op=ALU.mult),
                              reads=['psM', 'g'], writes=['OTs'])

                        P.barrier()
                        P.emit()
                sC = ExitStack()
                with sC:
                    sbC, psC = mk(sC)
                    Hre = sbC("Hre", [128, 16, 128], BF16)
                    Him = sbC("Him", [128, 16, 128], BF16)
                    WDre = sbC("WDre", [128, 16, 512], BF16)
                    WDim = sbC("WDim", [128, 16, 512], BF16)
                    cosT = sbC("cosT", [128, 4096], F32); sinT = sbC("sinT", [128, 4096], F32)
                    H0re = sbC("H0re", [128, 16, 4], BF16); H0im = sbC("H0im", [128, 16, 4], BF16)
                    M16 = sbC("M16", [128, 16], F32)
                    tD = {}
                    for nm, srcd, n_ in (('ar', arD, 16), ('ai', aiD, 16), ('ldt', ldtD, 16), ('cr', crD, 256), ('ci', ciD, 256)):
                        tD[nm] = sbC("gD" + nm, [128, n_], F32)
                        P.add('sp', lambda e, t=tD[nm], srcd=srcd: e.dma_start(out=t[:], in_=srcd), writes=['g'], dma='c')
                    P.barrier()
                    sg2 = ExitStack()
                    with sg2:
                        sbg2, _ = mk(sg2)
                        r = powtab(sbg2, "pD", tD['ar'][:], tD['ai'][:], tD['ldt'][:], 16, kv1[:], 16, coef=None)
                        Wr = sbg2("Wr", [128, 4096], F32); Wi = sbg2("Wi", [128, 4096], F32); Wt = sbg2("Wt", [128, 4096], F32)
                        v4 = lambda t: t[:].rearrange("q (w k c) -> q w k c", w=16, k=16, c=16)
                        prb = bcast(r['PR'][:].rearrange("q (w k) -> q w k", k=16).unsqueeze(3), [128, 16, 16, 16])
                        pib = bcast(r['PI'][:].rearrange("q (w k) -> q w k", k=16).unsqueeze(3), [128, 16, 16, 16])
                        crb = bcast(tD['cr'][:].rearrange("q (w c) -> q w c", c=16).unsqueeze(2), [128, 16, 16, 16])
                        cib = bcast(tD['ci'][:].rearrange("q (w c) -> q w c", c=16).unsqueeze(2), [128, 16, 16, 16])
                        cmul(v4(Wr), v4(Wi), crb, cib, prb, pib, v4(Wt))
                        for gw in range(2):
                            D_(lambda e, gw=gw: e.tensor_scalar(
                                out=WDre[:].rearrange("q w (g x) -> q w g x", g=2)[:, :, gw, :],
                                in0=Wr[:].rearrange("q (w x) -> q w x", w=16), scalar1=pmD[:, gw:gw + 1], scalar2=None, op0=ALU.mult))
                            D_(lambda e, gw=gw: e.tensor_scalar(
                                out=WDim[:].rearrange("q w (g x) -> q w g x", g=2)[:, :, gw, :],
                                in0=Wi[:].rearrange("q (w x) -> q w x", w=16), scalar1=pmD[:, gw:gw + 1], scalar2=-1.0,
                                op0=ALU.mult, op1=ALU.mult))
                        h0r = sbg2("h0r_s", [128, 64], F32); h0i = sbg2("h0i_s", [128, 64], F32)
                        hnr = sbg2("hnr", [128, 64], F32); hni = sbg2("hni", [128, 64], F32); htm = sbg2("htm", [128, 64], F32)
                        P.add('sp', lambda e: e.dma_start(out=h0r[:], in_=h0r_d), writes=['g'], dma='c')
                        P.add('sp', lambda e: e.dma_start(out=h0i[:], in_=h0i_d), writes=['g'], dma='c')
                        P.barrier()
                        w3 = lambda t: t[:].rearrange("q (w s) -> q w s", s=4)
                        a1r = bcast(r['PR'][:].rearrange("q (w k) -> q w k", k=16)[:, :, 0:1], [128, 16, 4])
                        a1i = bcast(r['PI'][:].rearrange("q (w k) -> q w k", k=16)[:, :, 0:1], [128, 16, 4])
                        cmul(w3(hnr), w3(hni), a1r, a1i, w3(h0r), w3(h0i), w3(htm))
                        D_(lambda e: e.tensor_tensor(out=w3(hnr), in0=w3(hnr), in1=Ssr[:], op=ALU.add))
                        D_(lambda e: e.tensor_tensor(out=w3(hni), in0=w3(hni), in1=Ssi[:], op=ALU.add))
                        P.add('sp', lambda e: e.dma_start(out=hsr, in_=hnr[:]), reads=['g'], dma='c')
                        P.add('sp', lambda e: e.dma_start(out=hsi, in_=hni[:]), reads=['g'], dma='c')
                        D_(lambda e: e.tensor_copy(out=H0re[:], in_=w3(h0r)))
                        D_(lambda e: e.tensor_copy(out=H0im[:], in_=w3(h0i)))
                        TH = sbg2("TH", [128, 16], F32)
                        D_(lambda e: e.tensor_scalar(out=TH[:], in0=r['th'][:], scalar1=16.0, scalar2=None, op0=ALU.mult))
                        rr(TH[:], r['tfs'][:], r['tis'][:])
                        A16 = sbg2("A16", [128, 16], F32)
                        D_(lambda e: e.tensor_scalar(out=A16[:], in0=r['al'][:], scalar1=16.0, scalar2=None, op0=ALU.mult))
                        A_(lambda e: e.activation(out=M16[:], in_=A16[:], func=AF.Exp))
                        angJ = Wt
                        tfJ = Wr
                        tiJ = sbg2("tiJ", [128, 4096], I32)
                        D_(lambda e: e.tensor_tensor(out=angJ[:].rearrange("q (w j) -> q w j", j=256),
                                                     in0=bcast(TH[:].unsqueeze(2), [128, 16, 256]),
                                                     in1=bcast(kvj[:].unsqueeze(1), [128, 16, 256]), op=ALU.mult))
                        rr(angJ[:], tfJ[:], tiJ[:])
                        A_(lambda e: e.activation(out=sinT[:], in_=angJ[:], func=AF.Sin))
                        D_(lambda e: e.tensor_scalar(out=angJ[:], in0=angJ[:], scalar1=HALF_PI, scalar2=None, op0=ALU.add))
                        rr(angJ[:], tfJ[:], tiJ[:])
                        A_(lambda e: e.activation(out=cosT[:], in_=angJ[:], func=AF.Sin))
                        P.barrier()
                        P.emit()
                    RR = sbC("RR", [128, 4096], F32); RI = sbC("RI", [128, 4096], F32); Tt = sbC("Tt", [128, 4096], F32)
                    SRf = SR[:].rearrange("q w j -> q (w j)"); SIf = SI[:].rearrange("q w j -> q (w j)")
                    D_(lambda e: e.tensor_tensor(out=RR[:], in0=cosT[:], in1=SRf, op=ALU.mult))
                    D_(lambda e: e.tensor_tensor(out=Tt[:], in0=sinT[:], in1=SIf, op=ALU.mult))
                    D_(lambda e: e.tensor_tensor(out=RR[:], in0=RR[:], in1=Tt[:], op=ALU.add))
                    D_(lambda e: e.tensor_tensor(out=RI[:], in0=cosT[:], in1=SIf, op=ALU.mult))
                    D_(lambda e: e.tensor_tensor(out=Tt[:], in0=sinT[:], in1=SRf, op=ALU.mult))
                    D_(lambda e: e.tensor_tensor(out=RI[:], in0=RI[:], in1=Tt[:], op=ALU.subtract))
                    GR = SR; GI = SI
                    for w in range(16):
                        for Rx, Gx in ((RR, GR), (RI, GI)):
                            D_(lambda e, w=w, Rx=Rx, Gx=Gx: e.tensor_tensor_scan(
                                out=Gx[:, w, :], data0=bcast(M16[:, w:w + 1], [128, 256]), data1=Rx[:, w * 256:(w + 1) * 256],
                                initial=0.0, op0=ALU.mult, op1=ALU.add))
                    c3 = cosT[:].rearrange("q (w j) -> q w j", j=256); s3_ = sinT[:].rearrange("q (w j) -> q w j", j=256)
                    HRf = RR[:].rearrange("q (w j) -> q w j", j=256); HIf = RI[:].rearrange("q (w j) -> q w j", j=256)
                    T3 = Tt[:].rearrange("q (w j) -> q w j", j=256)
                    sl = slice(127, 256)
                    D_(lambda e: e.tensor_tensor(out=HRf[:, :, sl], in0=c3[:, :, sl], in1=GR[:, :, sl], op=ALU.mult))
                    D_(lambda e: e.tensor_tensor(out=T3[:, :, sl], in0=s3_[:, :, sl], in1=GI[:, :, sl], op=ALU.mult))
                    D_(lambda e: e.tensor_tensor(out=HRf[:, :, sl], in0=HRf[:, :, sl], in1=T3[:, :, sl], op=ALU.subtract))
                    D_(lambda e: e.tensor_tensor(out=HIf[:, :, sl], in0=c3[:, :, sl], in1=GI[:, :, sl], op=ALU.mult))
                    D_(lambda e: e.tensor_tensor(out=T3[:, :, sl], in0=s3_[:, :, sl], in1=GR[:, :, sl], op=ALU.mult))
                    D_(lambda e: e.tensor_tensor(out=HIf[:, :, sl], in0=HIf[:, :, sl], in1=T3[:, :, sl], op=ALU.add))
                    D_(lambda e: e.tensor_copy(out=Hre[:], in_=HRf[:, :, 127:255]))
                    D_(lambda e: e.tensor_copy(out=Him[:], in_=HIf[:, :, 127:255]))
                    hfR = sbC("hfR", [128, 16], F32); hfI = sbC("hfI", [128, 16], F32)
                    D_(lambda e: e.tensor_copy(out=hfR[:], in_=HRf[:, :, 255]))
                    D_(lambda e: e.tensor_copy(out=hfI[:], in_=HIf[:, :, 255]))
                    P.add('sp', lambda e: e.dma_start(out=hfr, in_=hfR[:]), reads=['g'], dma='c')
                    P.add('sp', lambda e: e.dma_start(out=hfi, in_=hfI[:]), reads=['g'], dma='c')
                    psD = [psC(f"psD{i}", [128, 512], F32) for i in range(2)]
                    for w in range(16):
                        pd = psD[w % 2]
                        P.add('pe', lambda e, pd=pd, w=w: e.matmul(pd[:], lhsT=Hre[:, w, :], rhs=WDre[:, w, :], start=True, stop=False),
                              reads=['g'], writes=[('psD', w % 2)])
                        P.add('pe', lambda e, pd=pd, w=w: e.matmul(pd[:], lhsT=Him[:, w, :], rhs=WDim[:, w, :], start=False, stop=True),
                              reads=['g'], writes=[('psD', w % 2)])
                        P.add('act', lambda e, pd=pd, w=w: e.activation(
                            out=Yc[:, w // 4, :, (w % 4) * 32:(w % 4 + 1) * 32].rearrange("j t (g c) -> j t g c", g=2),
                            in_=pd[:].rearrange("j (g t c) -> j t g c", g=2, t=16), func=AF.Copy),
                              reads=[('psD', w % 2)], writes=['Yc'])
                    for w in range(16):
                        pd = psD[w % 2]
                        P.add('pe', lambda e, pd=pd, w=w: e.matmul(pd[0:4, :], lhsT=H0re[:, w, :], rhs=WDre[:, w, :], start=True, stop=False),
                              reads=['g'], writes=[('psD', w % 2)])
                        P.add('pe', lambda e, pd=pd, w=w: e.matmul(pd[0:4, :], lhsT=H0im[:, w, :], rhs=WDim[:, w, :], start=False, stop=True),
                              reads=['g'], writes=[('psD', w % 2)])
                        P.add('act', lambda e, pd=pd, w=w: e.activation(
                            out=Ycs[:, w // 4, (w % 4) * 32:(w % 4 + 1) * 32].rearrange("s (g c) -> s g c", g=2),
                            in_=pd[0:4, :].rearrange("s (g t c) -> s g t c", g=2, t=16)[:, :, 0, :], func=AF.Copy),
                              reads=[('psD', w % 2)], writes=['Ycs'])
                    P.barrier()
                    P.emit()

            def norm_transpose(xt_ap, nt, gain, hT_out, bufs, tg_, perm_a=None, hT_full=None):
                ssq_, rstd_, xn_, sqj_, psT_ = bufs
                P.add('act', lambda e: e.activation(out=sqj_[0:nt, :], in_=xt_ap, func=AF.Square, accum_out=ssq_[0:nt, :]),
                      reads=[tg_], writes=['n_sqj', 'n_ssq'])
                P.add('dve', lambda e: e.tensor_scalar(out=rstd_[0:nt, :], in0=ssq_[0:nt, :], scalar1=1.0 / D, scalar2=EPS,
                                                       op0=ALU.mult, op1=ALU.add), reads=['n_ssq'], writes=['n_rstd'])
                P.add('act', lambda e: e.activation(out=rstd_[0:nt, :], in_=rstd_[0:nt, :], func=AF.Sqrt),
                      reads=['n_rstd'], writes=['n_rstd'])
                P.add('dve', lambda e: e.reciprocal(out=rstd_[0:nt, :], in_=rstd_[0:nt, :]), reads=['n_rstd'], writes=['n_rstd'])
                P.add('dve', lambda e: e.tensor_scalar(out=xn_[0:nt, :], in0=xt_ap, scalar1=rstd_[0:nt, 0:1], scalar2=None,
                                                       op0=ALU.mult), reads=[tg_, 'n_rstd'], writes=['n_xn'])
                for kc in range(8):
                    P.add('pe', lambda e, kc=kc: e.transpose(out=psT_[:, kc, 0:nt], in_=xn_[0:nt, kc * 128:(kc + 1) * 128],
                                                             identity=ident[0:nt, 0:nt]),
                          reads=['n_xn', 'ident'], writes=['n_psT'])
                if perm_a is None:
                    P.add('dve', lambda e: e.tensor_tensor(out=hT_out, in0=psT_[:, :, 0:nt],
                                                           in1=bcast(gain[:].unsqueeze(2), [128, 8, nt]), op=ALU.mult),
                          reads=['n_psT', 'gains'], writes=['hTg'])
                else:
                    a_ = perm_a
                    P.add('dve', lambda e: e.tensor_tensor(
                        out=hT_full[:].rearrange("q k (s j) -> q k s j", s=16)[:, :, :, 8 * a_:8 * a_ + 8],
                        in0=psT_[:, :, 0:128].rearrange("q k (j s) -> q k s j", s=16),
                        in1=bcast(gain[:].unsqueeze(2).unsqueeze(3), [128, 8, 16, 8]), op=ALU.mult),
                        reads=['n_psT', 'gains'], writes=['hTg'])

            s3b = ExitStack()
            with s3b:
                sbb, psb = mk(s3b)
                wg = sbb("wg", [128, 8, 2048], BF16)
                wat = sbb("wat", [128, 4, 1024], BF16); wgl = sbb("wgl", [128, 4, 512], BF16)
                wss = sbb("wss", [128, 4, 1024], BF16); wou = sbb("wou", [128, 8, 1024], BF16)
                Gw = sbb("Gw", [128, 4, 16, 128], BF16)
                UTs = sbb("UTs", [128, 4, 2048], BF16)
                dvec = sbb("dvec", [128, 4], F32); bgl = sbb("bgls", [128, 4], F32); gmx2 = sbb("gmx2", [128, 8], F32)
                bm = sbb("bm", [128, 128], F32)
                pb = [psb(f"pb{i}", [128, 512], F32) for i in range(7)]
                psT2 = psb("psT2", [128, 8, 128], BF16)
                for kc in range(8):
                    for half in range(2):
                        P.add('pool', lambda e, kc=kc, half=half: e.dma_start(
                            out=wg[:, kc, half * 1024:(half + 1) * 1024],
                            in_=w_in[kc * 128:(kc + 1) * 128, 2048 + half * 1024:2048 + (half + 1) * 1024]), writes=['g'], dma='w')
                    P.add('pool', lambda e, kc=kc: e.dma_start(out=wou[:, kc, :], in_=w_out_d[kc * 128:(kc + 1) * 128, :]),
                          writes=['g'], dma='w')
                for kc in range(4):
                    P.add('pool', lambda e, kc=kc: e.dma_start(out=wat[:, kc, :], in_=w_attn_d[kc * 128:(kc + 1) * 128, :]), writes=['g'], dma='w')
                    P.add('pool', lambda e, kc=kc: e.dma_start(out=wss[:, kc, :], in_=w_ssm_d[kc * 128:(kc + 1) * 128, :]), writes=['g'], dma='w')
                    P.add('pool', lambda e, kc=kc: e.dma_start(out=wgl[:, kc, :], in_=w_glu_d[kc * 128:(kc + 1) * 128, :]), writes=['g'], dma='w')
                for dst, srcd in ((dvec, dvec_d), (bgl, bgl_d), (gmx2, gmix), (bm, bmask_d)):
                    P.add('sp', lambda e, dst=dst, srcd=srcd: e.dma_start(out=dst[:], in_=srcd), writes=['g'], dma='c')
                sga = ExitStack()
                with sga:
                    sbga, _ = mk(sga)
                    UTo = sbga("UTo", [128, 4, 2048], BF16)
                    for kc in range(4):
                        P.add('sp', lambda e, kc=kc: e.dma_start(out=UTo[:, kc, :], in_=ut_scr[:, kc, 2048:4096]), writes=['g'], dma='c')
                    tA = {}
                    for nm, srcd, n_ in (('ar', arA, 32), ('ai', aiA, 32), ('ldt', ldtA, 32), ('b1', b1A, 512), ('b2', b2A, 512),
                                         ('c', cA, 512), ('sgn', sgnA_d, 1)):
                        tA[nm] = sbga("gA" + nm, [128, n_], F32)
                        P.add('sp', lambda e, t=tA[nm], srcd=srcd: e.dma_start(out=t[:], in_=srcd), writes=['g'], dma='c')
                    P.barrier()
                    kvA = sbga("kvA", [128, 16], F32)
                    P.add('sp', lambda e: e.dma_start(out=kvA[:], in_=kv0_d), writes=['g'], dma='c')
                    P.barrier()
                    for kc in range(4):
                        P.add('pool', lambda e, kc=kc: e.tensor_copy(
                            out=UTs[:, kc, :].rearrange("q (g s j) -> q g s j", g=4, s=16),
                            in_=UTo[:, kc, :].rearrange("q (g j s) -> q g s j", g=4, s=16)), reads=['g'], writes=['g'])
                    r = powtab(sbga, "pA", tA['ar'][:], tA['ai'][:], tA['ldt'][:], 32, kvA[:], 16, coef=1)
                    Zr = sbga("ZrA", [128, 512], F32); Zi = sbga("ZiA", [128, 512], F32)
                    v3 = lambda t: t[:].rearrange("q (f k) -> q f k", k=16)
                    crb = bcast(r['cr'][:].unsqueeze(2), [128, 32, 16]); cib = bcast(r['ci'][:].unsqueeze(2), [128, 32, 16])
                    cmul(v3(Zr), v3(Zi), v3(r['PR']), v3(r['PI']), crb, cib, v3(r['tf']))
                    D_(lambda e: e.tensor_scalar(out=Zi[:], in0=Zi[:], scalar1=tA['sgn'][:, 0:1], scalar2=None, op0=ALU.mult))
                    Cc = sbga("Cc", [128, 512], F32); Xc = sbga("Xc", [128, 512], F32); Xt = sbga("Xt", [128, 512], F32)
                    D_(lambda e: e.tensor_scalar(out=Cc[:], in0=tA['c'][:], scalar1=tA['sgn'][:, 0:1], scalar2=-1.0,
                                                 op0=ALU.mult, op1=ALU.mult))
                    g3 = lambda t: t[:].rearrange("q (g c) -> q g c", c=16)
                    psGw = pb[0][:].rearrange("q (k n) -> q k n", k=4)
                    for dl in range(16):
                        D_(lambda e, dl=dl: e.tensor_tensor(out=g3(Xc), in0=g3(tA['b1']), in1=bcast(v3(Zr)[:, :, dl:dl + 1], [128, 32, 16]), op=ALU.mult))
                        D_(lambda e, dl=dl: e.tensor_tensor(out=g3(Xt), in0=g3(tA['b2']), in1=bcast(v3(Zi)[:, :, dl:dl + 1], [128, 32, 16]), op=ALU.mult))
                        D_(lambda e: e.tensor_tensor(out=Xc[:], in0=Xc[:], in1=Xt[:], op=ALU.add))
                        for kc in range(4):
                            P.add('pe', lambda e, kc=kc: e.matmul(psGw[:, kc, :], lhsT=Xc[:, kc * 128:(kc + 1) * 128],
                                                                  rhs=Cc[:, kc * 128:(kc + 1) * 128], start=True, stop=True),
                                  reads=['g'], writes=['psGw'])
                        P.add('dve', lambda e, dl=dl: e.tensor_tensor(out=Gw[:, :, dl, :], in0=psGw, in1=bcast(bm[:].unsqueeze(1), [128, 4, 128]),
                                                                      op=ALU.mult), reads=['psGw', 'g'], writes=['g'])
                    P.barrier()
                    P.emit()

                hTg = sbb("hTg", [128, 8, 512], BF16)
                OTg = sbb("OTg", [128, 4, 512], BF16)
                zT = sbb("zT", [128, 4, 512], BF16); z2T = sbb("z2T", [128, 4, 512], BF16)
                mT = sbb("mT", [128, 8, 512], BF16)
                xres = [sbb(f"xres{i}", [128, D], F32) for i in range(4)]
                x1t = [sbb(f"x1t{i}", [128, D], F32) for i in range(2)]
                yt = sbb("ytmp", [128, 512], F32); y2 = sbb("ytmp2", [128, 512], F32); sg1 = sbb("sg1", [128, 512], F32)
                sg2_ = sbb("sg2", [128, 512], F32)
                nb = (sbb("n_ssq", [128, 1], F32), sbb("n_rstd", [128, 1], F32), sbb("n_xn", [128, D], BF16),
                      sbb("n_sqj", [128, D], BF16), psT2)
                P.add('pool', lambda e: e.memset(OTg[:], 0.0), writes=['OTg'])
                OTp = sbb("OTp", [128, 4, 512], BF16)

                def phase3b_group(tg):
                    sample = (tg == 4)
                    N = 4 if sample else 512
                    tiles = [(xs, 0, 4)] if sample else [(xo, tg * 512 + a * 128, 128) for a in range(4)]
                    for a, (src, r0, nt) in enumerate(tiles):
                        P.add('sp', lambda e, a=a, src=src, r0=r0, nt=nt: e.dma_start(out=xres[a][0:nt, :], in_=src[r0:r0 + nt, :]),
                              writes=[('xres', a)], dma=f'x{a % 2}')
                        norm_transpose(xres[a][0:nt, :], nt, gmx2, hTg[:, :, a * 128:a * 128 + nt], nb, ('xres', a),
                                       perm_a=(None if sample else a), hT_full=hTg)
                    if sample:
                        P.add('pool', lambda e: e.tensor_copy(out=OTg[:, :, 0:4], in_=OTs[:]), reads=['OTs'], writes=['OTg'])
                    if not sample:
                        P.add('sp', lambda e: e.dma_start(out=OTg[:], in_=ot_scr[:, :, tg * 512:(tg + 1) * 512]),
                              reads=[('OT', tg, h) for h in range(8)], writes=['OTg'], dma='c')
                        P.add('pool', lambda e: e.tensor_copy(out=OTp[:].rearrange("q k (s j) -> q k s j", s=16),
                                                              in_=OTg[:].rearrange("q k (j s) -> q k s j", s=16)),
                              reads=['OTg'], writes=['OTp'])
                    else:
                        P.add('pool', lambda e: e.tensor_copy(out=OTp[:, :, 0:4], in_=OTg[:, :, 0:4]), reads=['OTg'], writes=['OTp'])
                    for kc in range(4):
                        if sample:
                            P.add('pe', lambda e, kc=kc: e.matmul(pb[kc][:, 0:4], lhsT=Gw[:, kc, 0, :], rhs=usT[:, kc, :],
                                                                  start=True, stop=False), reads=['g', 'usT'], writes=[('pb', kc)])
                            P.add('pe', lambda e, kc=kc: e.matmul(
                                pb[kc][:, 0:4], lhsT=Ycs[:, kc, :], rhs=ident[0:4, 0:4], start=False, stop=True),
                                reads=['Ycs', 'ident'], writes=[('pb', kc)])
                        else:
                            for dl in range(16):
                                P.add('pe', lambda e, kc=kc, dl=dl: e.matmul(
                                    pb[kc][:, dl * 32:512], lhsT=Gw[:, kc, dl, :], rhs=UTs[:, kc, tg * 512:tg * 512 + (16 - dl) * 32],
                                    start=(dl == 0), stop=False),
                                    reads=['g'], writes=[('pb', kc)])
                            for tl in range(16):
                                P.add('pe', lambda e, kc=kc, tl=tl: e.matmul(
                                    pb[kc][:, tl * 32:(tl + 1) * 32], lhsT=Yc[:, kc, tl, :], rhs=ident[:, tg * 32:(tg + 1) * 32], start=False, stop=(tl == 15)),
                                    reads=['Yc', 'ident'], writes=[('pb', kc)])
                    for kc in range(4):
                        uin = usT[:, kc, :] if sample else UTs[:, kc, tg * 512:(tg + 1) * 512]
                        P.add('dve', lambda e, kc=kc, uin=uin: e.scalar_tensor_tensor(
                            out=yt[:, 0:N], in0=uin, scalar=dvec[:, kc:kc + 1], in1=pb[kc][:, 0:N], op0=ALU.mult, op1=ALU.add),
                            reads=[('pb', kc), 'g'], writes=['yt'])
                        P.add('dve', lambda e: e.tensor_tensor(out=y2[:, 0:N], in0=yt[:, 0:N], in1=yt[:, 0:N], op=ALU.mult),
                              reads=['yt'], writes=['y2'])
                        P.add('dve', lambda e: e.tensor_scalar(out=y2[:, 0:N], in0=y2[:, 0:N], scalar1=0.044715, scalar2=1.0,
                                                               op0=ALU.mult, op1=ALU.add), reads=['y2'], writes=['y2'])
                        P.add('dve', lambda e: e.tensor_tensor(out=y2[:, 0:N], in0=y2[:, 0:N], in1=yt[:, 0:N], op=ALU.mult),
                              reads=['y2', 'yt'], writes=['y2'])
                        P.add('act', lambda e: e.activation(out=y2[:, 0:N], in_=y2[:, 0:N], func=AF.Sigmoid, scale=1.5957691216057308),
                              reads=['y2'], writes=['y2'])
                        P.add('dve', lambda e, kc=kc: e.tensor_tensor(out=zT[:, kc, 0:N], in0=y2[:, 0:N], in1=yt[:, 0:N], op=ALU.mult),
                              reads=['y2', 'yt'], writes=[('zT', kc)])
                    for m in range(4):
                        for kc in range(4):
                            P.add('pe', lambda e, m=m, kc=kc: e.matmul(pb[m][:, 0:N], lhsT=wgl[:, kc, m * 128:(m + 1) * 128],
                                                                       rhs=zT[:, kc, 0:N], start=(kc == 0), stop=(kc == 3)),
                                  reads=['g'] + [('zT', k) for k in range(4)], writes=[('pb', m)])
                        P.add('act', lambda e, m=m: e.activation(out=sg1[:, 0:N], in_=pb[m][:, 0:N], func=AF.Sigmoid, bias=bgl[:, m:m + 1]),
                              reads=[('pb', m), 'g'], writes=['sg1'])
                        P.add('dve', lambda e, m=m: e.tensor_tensor(out=z2T[:, m, 0:N], in0=zT[:, m, 0:N], in1=sg1[:, 0:N], op=ALU.mult),
                              reads=['sg1', ('zT', m)], writes=[('z2T', m)])
                    for m in range(8):
                        ms = slice(m * 128, (m + 1) * 128)
                        for kc in range(4):
                            P.add('pe', lambda e, kc=kc, ms=ms: e.matmul(pb[0][:, 0:N], lhsT=wat[:, kc, ms], rhs=OTp[:, kc, 0:N],
                                                                         start=(kc == 0), stop=(kc == 3)),
                                  reads=['g', 'OTp'], writes=[('pb', 0)])
                        for kc in range(4):
                            P.add('pe', lambda e, kc=kc, ms=ms: e.matmul(pb[1][:, 0:N], lhsT=wss[:, kc, ms], rhs=z2T[:, kc, 0:N],
                                                                         start=(kc == 0), stop=(kc == 3)),
                                  reads=['g'] + [('z2T', k) for k in range(4)], writes=[('pb', 1)])
                        for gi in range(2):
                            for kc in range(8):
                                P.add('pe', lambda e, kc=kc, m=m, gi=gi: e.matmul(
                                    pb[2 + gi][:, 0:N], lhsT=wg[:, kc, gi * 1024 + m * 128:gi * 1024 + (m + 1) * 128], rhs=hTg[:, kc, 0:N],
                                    start=(kc == 0), stop=(kc == 7)), reads=['g', 'hTg'], writes=[('pb', 2 + gi)])
                        P.add('act', lambda e: e.activation(out=sg1[:, 0:N], in_=pb[2][:, 0:N], func=AF.Sigmoid),
                              reads=[('pb', 2)], writes=['sg1'])
                        P.add('act', lambda e: e.activation(out=sg2_[:, 0:N], in_=pb[3][:, 0:N], func=AF.Sigmoid),
                              reads=[('pb', 3)], writes=['sg2'])
                        P.add('dve', lambda e: e.tensor_tensor(out=sg1[:, 0:N], in0=sg1[:, 0:N], in1=pb[0][:, 0:N], op=ALU.mult),
                              reads=['sg1', ('pb', 0)], writes=['sg1'])
                        P.add('dve', lambda e: e.tensor_tensor(out=sg2_[:, 0:N], in0=sg2_[:, 0:N], in1=pb[1][:, 0:N], op=ALU.mult),
                              reads=['sg2', ('pb', 1)], writes=['sg2'])
                        if sample:
                            P.add('dve', lambda e, m=m: e.tensor_tensor(out=mT[:, m, 0:N], in0=sg1[:, 0:N], in1=sg2_[:, 0:N], op=ALU.add),
                                  reads=['sg1', 'sg2'], writes=[('mT', m)])
                        else:
                            P.add('dve', lambda e, m=m: e.tensor_tensor(
                                out=mT[:, m, :].rearrange("q (j s) -> q s j", s=16), in0=sg1[:].rearrange("q (s j) -> q s j", s=16),
                                in1=sg2_[:].rearrange("q (s j) -> q s j", s=16), op=ALU.add),
                                reads=['sg1', 'sg2'], writes=[('mT', m)])
                    for a, (src, r0, nt) in enumerate(tiles):
                        xo_ = x1t[a % 2]
                        for half in range(2):
                            pX = pb[4 + half]
                            for kc in range(8):
                                P.add('pe', lambda e, kc=kc, pX=pX, a=a, nt=nt, half=half: e.matmul(
                                    pX[0:nt, :], lhsT=mT[:, kc, a * 128:a * 128 + nt], rhs=wou[:, kc, half * 512:(half + 1) * 512],
                                    start=(kc == 0), stop=(kc == 7)), reads=['g'] + [('mT', k) for k in range(8)], writes=[('pb', 4 + half)])
                            P.add('dve', lambda e, pX=pX, a=a, nt=nt, half=half, xo_=xo_: e.tensor_tensor(
                                out=xo_[0:nt, half * 512:(half + 1) * 512], in0=pX[0:nt, :], in1=xres[a][0:nt, half * 512:(half + 1) * 512],
                                op=ALU.add), reads=[('pb', 4 + half), ('xres', a)], writes=[('x1t', a % 2)])
                        rr0 = 2048 if sample else tg * 512 + a * 128
                        P.add('sp', lambda e, xo_=xo_, rr0=rr0, nt=nt: e.dma_start(out=x1_scr[rr0:rr0 + nt, :], in_=xo_[0:nt, :]),
                              reads=[('x1t', a % 2)], writes=[('x1scr', rr0)], dma=f'us{a % 2}')

                groups = [int(v) for v in os.environ.get('KTG', '0,1,2,3,4').split(',')]
                for tg in groups:
                    phase3b_group(tg)
                if DBG:
                    P.barrier()
                    dX1 = nc.dram_tensor("dbg_x1", [2052, 1024], F32, kind="ExternalOutput").ap()
                    P.add('sp', lambda e: e.dma_start(out=dX1, in_=x1_scr), dma='dbg')
                P.barrier()
                P.emit()

        s4 = ExitStack()
        with s4:
            sb4, ps4 = mk(s4)
            wfi = sb4("wfi", [128, 8, 5632], BF16)
            wfo = sb4("wfo", [128, 22, 1024], BF16)
            gff = sb4("gff", [128, 8], F32); gfin = sb4("gfin_s", [128, D], F32)
            hfT = sb4("hfT", [128, 8, 512], BF16); actT = sb4("actT", [128, 22, 512], BF16)
            xr = [sb4(f"xr{i}", [128, D], F32) for i in range(4)]
            x2 = [sb4(f"x2{i}", [128, D], F32) for i in range(2)]
            sgA = [sb4(f"sgA{i}", [128, 512], F32) for i in range(2)]
            fss = sb4("fss", [128, 1], F32); frs = sb4("frs", [128, 1], F32)
            pAb = [ps4(f"pAb{i}", [128, 512], F32) for i in range(2)]
            pGb = [ps4(f"pGb{i}", [128, 512], F32) for i in range(2)]
            pXb = [ps4(f"pXb{i}", [128, 512], F32) for i in range(2)]
            psT4 = ps4("psT4", [128, 8, 128], BF16)
            nb4 = (sb4("n4_ssq", [128, 1], F32), sb4("n4_rstd", [128, 1], F32), sb4("n4_xn", [128, D], BF16),
                   sb4("n4_sqj", [128, D], BF16), psT4)
            P.add('sp', lambda e: e.dma_start(out=gff[:], in_=gffn_d), writes=['g'], dma='c')
            P.add('sp', lambda e: e.dma_start(out=gfin[:], in_=gfin_d), writes=['g'], dma='c')
            P.barrier()
            for ci_, (c0, c1) in enumerate(((0, 2048), (2048, 4096), (4096, 5632))):
                for kc in range(8):
                    P.add('pool', lambda e, kc=kc, c0=c0, c1=c1: e.dma_start(out=wfi[:, kc, c0:c1], in_=w_fin_d[kc * 128:(kc + 1) * 128, c0:c1]),
                          writes=[('wfi', ci_, kc)], dma=f'wf{ci_}')
            for kc in range(22):
                P.add('pool', lambda e, kc=kc: e.dma_start(out=wfo[:, kc, :], in_=w_fout_d[kc * 128:(kc + 1) * 128, :]),
                      writes=[('wfo', kc)], dma='wfo')
            wfi_t = [[('wfi', c_, k_) for k_ in range(8)] for c_ in range(3)]
            wfo_t = [('wfo', k_) for k_ in range(22)]

            def phase4_group(tg):
                sample = (tg == 4)
                N = 4 if sample else 512
                tiles = [(2048, 4)] if sample else [(tg * 512 + a * 128, 128) for a in range(4)]
                for a, (r0, nt) in enumerate(tiles):
                    P.add('sp', lambda e, a=a, r0=r0, nt=nt: e.dma_start(out=xr[a][0:nt, :], in_=x1_scr[r0:r0 + nt, :]),
                          reads=[('x1scr', r0)], writes=[('xr', a)], dma=f'x{a % 2}')
                    norm_transpose(xr[a][0:nt, :], nt, gff, hfT[:, :, a * 128:a * 128 + nt], nb4, ('xr', a))
                for m in range(22):
                    pa, pg = pAb[m % 2], pGb[m % 2]
                    for kc in range(8):
                        P.add('pe', lambda e, kc=kc, m=m, pa=pa: e.matmul(pa[:, 0:N], lhsT=wfi[:, kc, m * 128:(m + 1) * 128], rhs=hfT[:, kc, 0:N],
                                                                          start=(kc == 0), stop=(kc == 7)), reads=['hTg'] + wfi_t[(m * 128) // 2048], writes=[('pAb', m % 2)])
                    for kc in range(8):
                        P.add('pe', lambda e, kc=kc, m=m, pg=pg: e.matmul(pg[:, 0:N], lhsT=wfi[:, kc, 2816 + m * 128:2816 + (m + 1) * 128],
                                                                          rhs=hfT[:, kc, 0:N], start=(kc == 0), stop=(kc == 7)),
                              reads=['hTg'] + wfi_t[(2816 + m * 128) // 2048] + wfi_t[(2816 + m * 128 + 127) // 2048], writes=[('pGb', m % 2)])
                    sg = sgA[m % 2]
                    P.add('act', lambda e, pa=pa, sg=sg: e.activation(out=sg[:, 0:N], in_=pa[:, 0:N], func=AF.Silu),
                          reads=[('pAb', m % 2)], writes=[('sgA', m % 2)])
                    P.add('dve', lambda e, pg=pg, sg=sg, m=m: e.tensor_tensor(out=actT[:, m, 0:N], in0=sg[:, 0:N], in1=pg[:, 0:N], op=ALU.mult),
                          reads=[('sgA', m % 2), ('pGb', m % 2)], writes=[('actT', m)])
                for a, (r0, nt) in enumerate(tiles):
                    xx = x2[a % 2]
                    yy = xx
                    for half in range(2):
                        pX = pXb[half]
                        for kc in range(22):
                            P.add('pe', lambda e, kc=kc, pX=pX, a=a, nt=nt, half=half: e.matmul(
                                pX[0:nt, :], lhsT=actT[:, kc, a * 128:a * 128 + nt], rhs=wfo[:, kc, half * 512:(half + 1) * 512],
                                start=(kc == 0), stop=(kc == 21)), reads=wfo_t + [('actT', k) for k in range(22)], writes=[('pXb', half)])
                        P.add('dve', lambda e, pX=pX, a=a, nt=nt, half=half, xx=xx: e.tensor_tensor(
                            out=xx[0:nt, half * 512:(half + 1) * 512], in0=pX[0:nt, :], in1=xr[a][0:nt, half * 512:(half + 1) * 512], op=ALU.add),
                            reads=[('pXb', half), ('xr', a)], writes=[('x2', a % 2)])
                    P.add('act', lambda e, xx=xx, nt=nt: e.activation(out=nb4[3][0:nt, :], in_=xx[0:nt, :], func=AF.Square, accum_out=fss[0:nt, :]),
                          reads=[('x2', a % 2)], writes=['n_sqj', 'fss'])
                    P.add('dve', lambda e, nt=nt: e.tensor_scalar(out=frs[0:nt, :], in0=fss[0:nt, :], scalar1=1.0 / D, scalar2=EPS,
                                                                  op0=ALU.mult, op1=ALU.add), reads=['fss'], writes=['frs'])
                    P.add('act', lambda e, nt=nt: e.activation(out=frs[0:nt, :], in_=frs[0:nt, :], func=AF.Sqrt), reads=['frs'], writes=['frs'])
                    P.add('dve', lambda e, nt=nt: e.reciprocal(out=frs[0:nt, :], in_=frs[0:nt, :]), reads=['frs'], writes=['frs'])
                    P.add('dve', lambda e, xx=xx, yy=yy, nt=nt: e.scalar_tensor_tensor(
                        out=yy[0:nt, :], in0=xx[0:nt, :], scalar=frs[0:nt, 0:1], in1=gfin[0:nt, :], op0=ALU.mult, op1=ALU.mult),
                        reads=[('x2', a % 2), 'frs', 'g'], writes=[('x2', a % 2)])
                    dst = yso if sample else yo
                    d0 = 0 if sample else r0
                    P.add('sp', lambda e, yy=yy, dst=dst, d0=d0, nt=nt: e.dma_start(out=dst[d0:d0 + nt, :], in_=yy[0:nt, :]),
                          reads=[('x2', a % 2)], dma=f'ko{a % 2}')

            for tg in groups:
                phase4_group(tg)
            P.barrier()
            P.emit()
    return nc


def _host_consts(p):
    half = HD // 2
    inv = np.power(np.float32(10000.0), -2.0 * np.arange(half, dtype=np.float32) / HD).astype(np.float32)
    pos = np.concatenate([np.arange(2048), 2048 * p + np.arange(2048), np.full(128, 8192)]).astype(np.float32)
    ang = pos[:, None] * inv[None, :]
    cos = np.cos(ang).astype(np.float32)
    sin = np.sin(ang).astype(np.float32)
    sF = np.concatenate([-sin, sin], axis=1)
    cF = np.ascontiguousarray(cos.reshape(33, 128, 32).transpose(1, 0, 2))
    sF = np.ascontiguousarray(sF.reshape(33, 128, 64).transpose(1, 0, 2))
    k = np.arange(128)[:, None, None, None]
    j = np.arange(4)[None, :, None, None]
    a = np.arange(4)[None, None, :, None]
    q = np.arange(128)[None, None, None, :]
    m4 = ((j < a) | ((j == a) & (k <= q))).astype(np.float32).reshape(128, 4, 512)
    pb = np.full((8, 16), -BIG, np.float32)
    for m in range(8):
        if p == 1:
            pb[m, 0:8] = 0.0
        pb[m, 8:8 + m] = 0.0
    pb = np.ascontiguousarray(np.broadcast_to(pb[None], (128, 8, 16)))
    return cF, sF, np.ascontiguousarray(m4), pb


def _s5_layouts(a_re, a_im, log_dt, b_re, b_im, c_re, c_im):
    c_ = np.ascontiguousarray
    out = {}
    def lay_gp(a):
        t = a.reshape(4, 8, 64)
        t = np.broadcast_to(t[:, :, None, :], (4, 8, 16, 64))
        return c_(t.transpose(1, 2, 0, 3).reshape(128, 256))
    out["arB"] = lay_gp(a_re); out["aiB"] = lay_gp(a_im)
    out["ldtB"] = lay_gp(np.broadcast_to(log_dt[:, None], (32, 64)))
    def lay_b(bb):
        t = bb.reshape(4, 8, 64, 16)
        return c_(t.transpose(1, 3, 0, 2).reshape(128, 256))
    out["brB"] = lay_b(b_re); out["biB"] = lay_b(b_im)
    def lay_w(a):
        t = a.reshape(16, 2, 64)
        return c_(t.transpose(1, 2, 0).reshape(128, 16))
    out["arD"] = lay_w(a_re); out["aiD"] = lay_w(a_im)
    out["ldtD"] = lay_w(np.broadcast_to(log_dt[:, None], (32, 64)))
    def lay_c(cc):
        t = cc.reshape(16, 2, 16, 64)
        return c_(t.transpose(1, 3, 0, 2).reshape(128, 256))
    out["crD"] = lay_c(c_re); out["ciD"] = lay_c(c_im)
    t2 = lambda a: c_(np.concatenate([a.T, a.T], axis=0))
    out["arA"] = t2(a_re); out["aiA"] = t2(a_im); out["ldtA"] = t2(np.broadcast_to(log_dt[:, None], (32, 64)))
    bt = lambda x: x.transpose(1, 0, 2).reshape(64, 512)
    out["b1A"] = c_(np.concatenate([bt(b_re), bt(b_im)], axis=0))
    out["b2A"] = c_(np.concatenate([bt(b_im), bt(b_re)], axis=0))
    ct = lambda x: x.transpose(2, 0, 1).reshape(64, 512)
    out["cA"] = c_(np.concatenate([ct(c_re), ct(c_im)], axis=0))
    out["kv0"] = c_(np.broadcast_to(np.arange(16, dtype=np.float32)[None], (128, 16)))
    out["kv1"] = c_(np.broadcast_to(np.arange(1, 17, dtype=np.float32)[None], (128, 16)))
    out["kvj"] = c_(np.broadcast_to(np.arange(1, 257, dtype=np.float32)[None], (128, 256)))
    r = np.arange(128)
    out["pmB"] = c_(np.stack([((r // 16) % 2 == 0), ((r // 16) % 2 == 1)], axis=1).astype(np.float32))
    out["pmD"] = c_(np.stack([(r // 64 == 0), (r // 64 == 1)], axis=1).astype(np.float32))
    return out


_NC_CACHE = {}
_LAST = {}


def kernel(x_prompt, x_sample, cache_k, cache_v, state_ssm_re, state_ssm_im, page_table,
           norm_mix, w_in, w_attn_proj, ssm_a_re, ssm_a_im, ssm_log_dt, ssm_b_re, ssm_b_im,
           ssm_c_re, ssm_c_im, ssm_d, w_glu, b_glu, w_ssm_proj, w_out, norm_ffn,
           w_ffn_in, w_ffn_out, norm_final):
    f = lambda a: np.ascontiguousarray(np.asarray(a, dtype=np.float32))
    x_prompt = f(x_prompt)
    x_sample = f(x_sample)
    if 'nc' not in _NC_CACHE:
        _NC_CACHE['nc'] = build()
    nc = _NC_CACHE['nc']
    in_maps = []
    ident = np.eye(128, dtype=np.float32)
    gm = np.ascontiguousarray(f(norm_mix)[0].reshape(8, 128).T)
    s5 = _s5_layouts(f(ssm_a_re)[0], f(ssm_a_im)[0], f(ssm_log_dt)[0], f(ssm_b_re)[0], f(ssm_b_im)[0],
                     f(ssm_c_re)[0], f(ssm_c_im)[0])
    col = lambda v, n: np.ascontiguousarray(f(v).reshape(n, 128).T)
    r_ = np.arange(128)
    wts = {"w_attn": f(w_attn_proj)[0], "w_glu": f(w_glu)[0], "w_ssm": f(w_ssm_proj)[0], "w_out": f(w_out)[0],
           "w_ffn_in": f(w_ffn_in)[0], "w_ffn_out": f(w_ffn_out)[0],
           "bgl": col(f(b_glu)[0], 4), "dvec": col(f(ssm_d)[0], 4), "gffn": col(f(norm_ffn)[0], 8),
           "gfin": np.ascontiguousarray(np.broadcast_to(f(norm_final)[None, :], (128, 1024))),
           "sgnA": np.where(r_ < 64, -1.0, 1.0).astype(np.float32).reshape(128, 1),
           "bmask": (r_[:, None] // 16 == r_[None, :] // 16).astype(np.float32)}
    zs = np.zeros((128, 255), np.float32); zs[:, 127] = 1.0
    smp = {"cache_k": f(cache_k).reshape(2560 * 128 * 8, 64), "cache_v": f(cache_v).reshape(2560 * 128 * 8, 64),
           "piota": r_.astype(np.float32).reshape(128, 1),
           "hb": (8.0 * r_[:, None] + np.arange(8)[None, :]).astype(np.float32),
           "zsel": zs, "rsel": (np.arange(128)[None, :] // 32 == np.arange(4)[:, None]).astype(np.float32),
           "eye8": np.eye(8, dtype=np.float32), "esel": np.eye(4, dtype=np.float32)}
    for c in range(8):
        b, p = c // 2, c % 2
        cF, sF, m4, pb = _host_consts(p)
        xo = x_prompt[b, 2048 * p:2048 * (p + 1)]
        xpv = x_prompt[b, 0:2048] if p == 1 else np.zeros((2048, D), np.float32)
        in_maps.append({
            "xo": np.ascontiguousarray(xo), "xp": np.ascontiguousarray(xpv),
            "xs": np.ascontiguousarray(x_sample[4 * c:4 * c + 4, 0]), "w_in": f(w_in)[0],
            "gmix": gm, "ropec": cF, "ropes": sF, "ident": ident, "mask4": m4, "pbias": pb,
        })
        in_maps[-1].update(s5)
        in_maps[-1].update(wts)
        in_maps[-1].update(smp)
        pt4 = np.asarray(page_table)[4 * c:4 * c + 4].astype(np.int32)
        in_maps[-1]["ptrep"] = np.ascontiguousarray(np.broadcast_to(pt4.reshape(1, 256), (128, 256)))
        in_maps[-1]["ptE"] = np.ascontiguousarray(np.broadcast_to(pt4[:, 0::2].reshape(1, 128), (8, 128)))
        in_maps[-1]["ptO"] = np.ascontiguousarray(np.broadcast_to(pt4[:, 1::2].reshape(1, 128), (8, 128)))
        for nm, st in (("h0r", state_ssm_re), ("h0i", state_ssm_im)):
            t = f(st)[0, 4 * c:4 * c + 4].reshape(4, 16, 2, 64)
            in_maps[-1][nm] = np.ascontiguousarray(t.transpose(2, 3, 1, 0).reshape(128, 64))
    res = run_bass_kernel_spmd(nc, in_maps, core_ids=list(range(8)))
    R = res.results
    if os.environ.get('KDBG'):
        cc = int(os.environ.get("KCORE", "1"))
        _LAST['R'] = {k: np.asarray(v).view(np.uint16) if 'bfloat' in str(np.asarray(v).dtype) else np.asarray(v)
                      for k, v in R[cc].items()}
    new_k = np.zeros((1, 4, 4096, 8, 64), np.float32)
    new_v = np.zeros((1, 4, 4096, 8, 64), np.float32)
    new_ks = np.zeros((1, 32, 1, 8, 64), np.float32)
    new_vs = np.zeros((1, 32, 1, 8, 64), np.float32)
    for c in range(8):
        b, p = c // 2, c % 2
        new_k[0, b, 2048 * p:2048 * (p + 1)] = np.asarray(R[c]["ko"]).reshape(2048, 8, 64)
        new_v[0, b, 2048 * p:2048 * (p + 1)] = np.asarray(R[c]["vo"]).reshape(2048, 8, 64)
        new_ks[0, 4 * c:4 * c + 4, 0] = np.asarray(R[c]["kso"]).reshape(4, 8, 64)
        new_vs[0, 4 * c:4 * c + 4, 0] = np.asarray(R[c]["vso"]).reshape(4, 8, 64)
    y_prompt = np.zeros((4, 4096, 1024), np.float32)
    y_sample = np.zeros((32, 1, 1024), np.float32)
    for c in range(8):
        b, p = c // 2, c % 2
        y_prompt[b, 2048 * p:2048 * (p + 1)] = np.asarray(R[c]["yo"])
        y_sample[4 * c:4 * c + 4, 0] = np.asarray(R[c]["yso"])
    hre = np.zeros((1, 4, 32, 64), np.float32)
    him = np.zeros((1, 4, 32, 64), np.float32)
    for b in range(4):
        c = 2 * b + 1
        hre[0, b] = np.asarray(R[c]["hfr"]).reshape(2, 64, 16).transpose(2, 0, 1).reshape(32, 64)
        him[0, b] = np.asarray(R[c]["hfi"]).reshape(2, 64, 16).transpose(2, 0, 1).reshape(32, 64)
    hsr_o = np.zeros((1, 32, 32, 64), np.float32)
    hsi_o = np.zeros((1, 32, 32, 64), np.float32)
    for c in range(8):
        hsr_o[0, 4 * c:4 * c + 4] = np.asarray(R[c]["hsr"]).reshape(2, 64, 16, 4).transpose(3, 2, 0, 1).reshape(4, 32, 64)
        hsi_o[0, 4 * c:4 * c + 4] = np.asarray(R[c]["hsi"]).reshape(2, 64, 16, 4).transpose(3, 2, 0, 1).reshape(4, 32, 64)
    return (y_prompt, y_sample, new_k, new_v, hre, him, new_ks, new_vs, hsr_o, hsi_o)
```

```python
import numpy as np
import concourse.bass as bass
import concourse.mybir as mybir
from concourse.bass_utils import run_bass_kernel_spmd

F32 = mybir.dt.float32
BF16 = mybir.dt.bfloat16
I32 = mybir.dt.int32
AF = mybir.ActivationFunctionType
ALU = mybir.AluOpType
AX = mybir.AxisListType

D = 1024
NH = 8
HD = 64
NT_OWN = 16
NT_PREV = 16
EPS = 1e-6
BIG = 30000.0

ENGS = ('pe', 'act', 'dve', 'pool', 'sp')
BLK = {'pe': 'tensor', 'act': 'scalar', 'dve': 'vector', 'pool': 'gpsimd', 'sp': 'sync'}


class Op:
    __slots__ = ('eng', 'fn', 'tok', 'waits', 'dma')

    def __init__(self, eng, fn, tok, waits, dma):
        self.eng, self.fn, self.tok, self.waits, self.dma = eng, fn, tok, waits, dma


class Prog:
    def __init__(self, nc, sems):
        self.nc = nc
        self.sems = sems
        self.ops = []
        self.last_w = {}
        self.readers = {}
        self.cnt = {}
        self.waited = {e: {} for e in ENGS}
        self.emitted = 0
        self.excl = set()

    def add(self, eng, fn, reads=(), writes=(), dma=None):
        deps = set()
        for r in reads:
            if r in self.last_w:
                deps.add(self.last_w[r])
        for w in writes:
            if w in self.last_w:
                deps.add(self.last_w[w])
            deps |= self.readers.get(w, set())
        key = ('d:' + dma) if dma else eng
        inc = 16 if dma else 1
        self.cnt[key] = self.cnt.get(key, 0) + inc
        tok = (key, self.cnt[key])
        waits = {}
        for d in deps:
            dk, dv = self.ops[d].tok
            if dk == 'pe' and eng == 'pe' and not dma:
                continue
            waits[dk] = max(waits.get(dk, 0), dv)
        final = []
        for k, v in waits.items():
            if self.waited[eng].get(k, 0) < v:
                final.append((k, v))
                self.waited[eng][k] = v
        oid = len(self.ops)
        self.ops.append(Op(eng, fn, tok, final, dma))
        for r in reads:
            self.readers.setdefault(r, set()).add(oid)
        for w in writes:
            self.last_w[w] = oid
            self.readers[w] = set()
        return oid

    def barrier(self):
        for e in ENGS:
            final = []
            for k, v in self.cnt.items():
                if k in self.excl:
                    continue
                if self.waited[e].get(k, 0) < v:
                    final.append((k, v))
                    self.waited[e][k] = v
            if final:
                self.ops.append(Op(e, None, None, final, None))

    def emit(self):
        ops = self.ops[self.emitted:]
        self.emitted = len(self.ops)
        with self.nc.Block() as block:
            for eng in ENGS:
                my = [o for o in ops if o.eng == eng]
                if not my:
                    continue

                def body(e, my=my):
                    for o in my:
                        for (k, v) in o.waits:
                            e.wait_ge(self.sems[k], v)
                        if o.fn is None:
                            continue
                        ins = o.fn(e)
                        ins.then_inc(self.sems[o.tok[0]], 16 if o.dma else 1)

                getattr(block, BLK[eng])(body)


def bcast(ap, shape):
    return ap.broadcast_to(shape)


from contextlib import ExitStack
import os


def build():
    nc = bass.Bass("TRN2", target_bir_lowering=False)
    DBG = os.environ.get('KDBG')

    def din(name, shape, dt=F32):
        return nc.dram_tensor(name, list(shape), dt, kind="ExternalInput").ap()

    def dout(name, shape, dt=F32):
        return nc.dram_tensor(name, list(shape), dt, kind="ExternalOutput").ap()

    xo = din("xo", [2048, D])
    xp = din("xp", [2048, D])
    xs = din("xs", [4, D])
    w_in = din("w_in", [D, 4096])
    gmix = din("gmix", [128, 8])
    ropec = din("ropec", [128, 33, 32])
    ropes = din("ropes", [128, 33, 64])
    ident_d = din("ident", [128, 128])
    mask4_d = din("mask4", [128, 4, 512])
    pbias_d = din("pbias", [128, 8, 16])

    arB = din("arB", [128, 256]); aiB = din("aiB", [128, 256]); ldtB = din("ldtB", [128, 256])
    brB = din("brB", [128, 256]); biB = din("biB", [128, 256])
    arD = din("arD", [128, 16]); aiD = din("aiD", [128, 16]); ldtD = din("ldtD", [128, 16])
    crD = din("crD", [128, 256]); ciD = din("ciD", [128, 256])
    kv0_d = din("kv0", [128, 16]); kv1_d = din("kv1", [128, 16]); kvj_d = din("kvj", [128, 256])
    pmB_d = din("pmB", [128, 2]); pmD_d = din("pmD", [128, 2])
    w_attn_d = din("w_attn", [512, 1024]); w_glu_d = din("w_glu", [512, 512]); w_ssm_d = din("w_ssm", [512, 1024])
    w_out_d = din("w_out", [1024, 1024]); w_fin_d = din("w_ffn_in", [1024, 5632]); w_fout_d = din("w_ffn_out", [2816, 1024])
    bgl_d = din("bgl", [128, 4]); dvec_d = din("dvec", [128, 4]); gffn_d = din("gffn", [128, 8]); gfin_d = din("gfin", [128, 1024])
    arA = din("arA", [128, 32]); aiA = din("aiA", [128, 32]); ldtA = din("ldtA", [128, 32])
    b1A = din("b1A", [128, 512]); b2A = din("b2A", [128, 512]); cA = din("cA", [128, 512])
    sgnA_d = din("sgnA", [128, 1]); bmask_d = din("bmask", [128, 128])
    x1_scr = nc.dram_tensor("x1_scr", [2052, 1024], F32, kind="Internal").ap()
    yo = dout("yo", [2048, 1024]); yso = dout("yso", [4, 1024])
    ckh = din("cache_k", [2560 * 128 * 8, 64]); cvh = din("cache_v", [2560 * 128 * 8, 64])
    ptrep_d = din("ptrep", [128, 256], I32); ptE_d = din("ptE", [8, 128], I32); ptO_d = din("ptO", [8, 128], I32)
    piota_d = din("piota", [128, 1]); hb_d = din("hb", [128, 8]); zsel_d = din("zsel", [128, 255]); rsel_d = din("rsel", [4, 128])
    eye8_d = din("eye8", [8, 8]); esel_d = din("esel", [4, 4])
    hfr = dout("hfr", [128, 16]); hfi = dout("hfi", [128, 16])
    hsr = dout("hsr", [128, 64]); hsi = dout("hsi", [128, 64])
    h0r_d = din("h0r", [128, 64]); h0i_d = din("h0i", [128, 64])
    ko = dout("ko", [2048, 512])
    vo = dout("vo", [2048, 512])
    kso = dout("kso", [4, 512])
    vso = dout("vso", [4, 512])
    ut_scr = nc.dram_tensor("ut_scr", [128, 4, 4096], BF16, kind="Internal").ap()
    ot_scr = nc.dram_tensor("ot_scr", [128, 4, 2048], BF16, kind="Internal").ap()

    top = ExitStack()
    with top:
        def mk(es):
            def sb(name, shape, dt):
                return es.enter_context(nc.sbuf_tensor("s_" + name, list(shape), dt))

            def ps(name, shape, dt):
                return es.enter_context(nc.psum_tensor("p_" + name, list(shape), dt))
            return sb, ps

        sb, ps = mk(top)
        semnames = ['pe', 'act', 'dve', 'pool', 'd:x0', 'd:x1', 'd:w', 'd:c', 'd:ko0', 'd:ko1', 'd:vo0', 'd:vo1',
                    'd:us0', 'd:us1', 'd:dbg', 'd:og0', 'd:og1', 'd:gk', 'd:wf0', 'd:wf1', 'd:wf2', 'd:wfo'] + [f'd:pg{i}' for i in range(8)]
        sems = {k: top.enter_context(nc.semaphore(k.replace(':', '_'))) for k in semnames}
        P = Prog(nc, sems)

        def dbg(name, ap, shape, dt, toks):
            d = nc.dram_tensor("dbg_" + name, list(shape), dt, kind="ExternalOutput").ap()
            P.add('sp', lambda e: e.dma_start(out=d, in_=ap), reads=toks, dma='dbg')

        ident = sb("identb", [128, 128], BF16)
        identf = sb("identf", [128, 128], F32)
        qs_f = sb("qs_f", [4, 8, 64], F32)
        ks_f = sb("ks_f", [4, 8, 64], F32)
        vs_f = sb("vs_f", [4, 8, 64], F32)
        usT = sb("usT", [128, 4, 4], BF16)
        OTs = sb("OTs", [128, 4, 4], BF16)

        P.add('sp', lambda e: e.dma_start(out=identf[:], in_=ident_d[:, :]), writes=['identf'], dma='c')
        P.barrier()
        P.add('dve', lambda e: e.tensor_copy(out=ident[:], in_=identf[:]), reads=['identf'], writes=['ident'])

        s12 = ExitStack()
        with s12:
            sb12, _ = mk(s12)
            KaT = sb12("KaT", [128, 8, 4096], BF16)
            Vaug = sb12("Vaug", [128, 32, 8, 65], BF16)
            Qaug = sb12("Qaug", [128, 16, 8, 80], BF16)

            s1 = ExitStack()
            with s1:
                sb1, ps1 = mk(s1)
                w1 = sb1("w1", [128, 8, 2048], BF16)
                gmx = sb1("gmx", [128, 8], F32)
                rc = sb1("rc", [128, 33, 32], F32)
                rs = sb1("rs", [128, 33, 64], F32)
                xt = [sb1(f"xt{i}", [128, D], F32) for i in range(2)]
                sqj = sb1("sqj", [128, D], BF16)
                ssq = [sb1(f"ssq{i}", [128, 1], F32) for i in range(2)]
                rstd = [sb1(f"rstd{i}", [128, 1], F32) for i in range(2)]
                xn = [sb1(f"xn{i}", [128, D], BF16) for i in range(2)]
                hT = [sb1(f"hT{i}", [128, 8, 128], BF16) for i in range(2)]
                kf = [sb1(f"kf{i}", [128, 8, 64], F32) for i in range(2)]
                vf = [sb1(f"vf{i}", [128, 8, 64], F32) for i in range(2)]
                ust = [sb1(f"ust{i}", [128, 4, 128], BF16) for i in range(2)]
                t1 = sb1("t1", [128, 8, 64], F32)
                t2 = sb1("t2", [128, 8, 64], F32)
                kb = sb1("kb", [128, 8, 80], BF16)
                psT = ps1("psT", [128, 8, 128], BF16)
                psQ = ps1("psQ", [128, 512], F32)
                psK = ps1("psK", [128, 512], F32)
                psV = ps1("psV", [128, 512], F32)
                psU = ps1("psU", [128, 4, 128], F32)
                psKT = ps1("psKT", [128, 8, 128], BF16)

                P.add('sp', lambda e: e.dma_start(out=gmx[:], in_=gmix[:, :]), writes=['gmx'], dma='c')
                P.add('sp', lambda e: e.dma_start(out=rc[:], in_=ropec[:, :, :]), writes=['rc'], dma='c')
                P.add('sp', lambda e: e.dma_start(out=rs[:], in_=ropes[:, :, :]), writes=['rs'], dma='c')
                for kc in range(8):
                    for half in range(2):
                        P.add('pool', lambda e, kc=kc, half=half: e.dma_start(
                            out=w1[:, kc, half * 1024:(half + 1) * 1024],
                            in_=w_in[kc * 128:(kc + 1) * 128, half * 1024:(half + 1) * 1024]),
                            writes=[('w1', kc, half)], dma='w')
                P.add('pool', lambda e: e.memset(Vaug[:, :, :, 64:65], 1.0), writes=['Vones'])
                P.add('pool', lambda e: e.memset(Qaug[:, :, :, 0:16], 0.0), writes=['Qz'])
                P.barrier()

                def rotary(src_ps, nt, t, out_ap, ta, tb, tag):
                    x3 = src_ps[0:nt, :].rearrange("p (h d) -> p h d", h=8)
                    x4 = src_ps[0:nt, :].rearrange("p (h two d) -> p h two d", h=8, two=2)
                    cT = bcast(rc[0:nt, t:t + 1, :].unsqueeze(2), [nt, 8, 2, 32])
                    P.add('dve', lambda e: e.tensor_tensor(out=ta[0:nt].rearrange("p h (two d) -> p h two d", two=2),
                                                           in0=x4, in1=cT, op=ALU.mult),
                          reads=[tag, 'rc'], writes=['rot_ta'])
                    P.add('dve', lambda e: e.tensor_tensor(out=tb[0:nt, :, 0:32], in0=x3[:, :, 32:64],
                                                           in1=bcast(rs[0:nt, t:t + 1, 0:32], [nt, 8, 32]), op=ALU.mult),
                          reads=[tag, 'rs'], writes=['rot_tb0'])
                    P.add('dve', lambda e: e.tensor_tensor(out=tb[0:nt, :, 32:64], in0=x3[:, :, 0:32],
                                                           in1=bcast(rs[0:nt, t:t + 1, 32:64], [nt, 8, 32]), op=ALU.mult),
                          reads=[tag, 'rs'], writes=['rot_tb1'])
                    return ['rot_ta', 'rot_tb0', 'rot_tb1']

                prefetched = {}

                def p1_load(t):
                    nt_ = 4 if t == 32 else 128
                    if t == 32:
                        src_, r0_ = xs, 0
                    elif t >= 16:
                        src_, r0_ = xo, (t - 16) * 128
                    else:
                        src_, r0_ = xp, t * 128
                    s_ = t % 2
                    P.add('sp', lambda e: e.dma_start(out=xt[s_][0:nt_, :], in_=src_[r0_:r0_ + nt_, :]),
                          writes=[('xt', s_)], dma=f'x{s_}')
                    prefetched[t] = True

                def phase1_tile(t):
                    sample = (t == 32)
                    own = 16 <= t < 32
                    nt = 4 if sample else 128
                    if sample:
                        src, r0 = xs, 0
                    elif own:
                        src, r0 = xo, (t - 16) * 128
                    else:
                        src, r0 = xp, t * 128
                    s = t % 2
                    if not prefetched.get(t):
                        p1_load(t)
                    if t + 1 in tile_set:
                        p1_load(t + 1)
                    P.add('act', lambda e: e.activation(out=sqj[0:nt, :], in_=xt[s][0:nt, :], func=AF.Square,
                                                        accum_out=ssq[s][0:nt, :]),
                          reads=[('xt', s)], writes=['sqj', ('ssq', s)])
                    P.add('dve', lambda e: e.tensor_scalar(out=rstd[s][0:nt, :], in0=ssq[s][0:nt, :], scalar1=1.0 / D,
                                                           scalar2=EPS, op0=ALU.mult, op1=ALU.add),
                          reads=[('ssq', s)], writes=[('rstd', s)])
                    P.add('act', lambda e: e.activation(out=rstd[s][0:nt, :], in_=rstd[s][0:nt, :], func=AF.Sqrt),
                          reads=[('rstd', s)], writes=[('rstd', s)])
                    P.add('dve', lambda e: e.reciprocal(out=rstd[s][0:nt, :], in_=rstd[s][0:nt, :]),
                          reads=[('rstd', s)], writes=[('rstd', s)])
                    P.add('dve', lambda e: e.tensor_scalar(out=xn[s][0:nt, :], in0=xt[s][0:nt, :],
                                                           scalar1=rstd[s][0:nt, 0:1], scalar2=None, op0=ALU.mult),
                          reads=[('xt', s), ('rstd', s)], writes=[('xn', s)])
                    for kc in range(8):
                        P.add('pe', lambda e, kc=kc: e.transpose(out=psT[:, kc, 0:nt],
                                                                 in_=xn[s][0:nt, kc * 128:(kc + 1) * 128],
                                                                 identity=ident[0:nt, 0:nt]),
                              reads=[('xn', s), 'ident'], writes=['psT'])
                    P.add('dve', lambda e: e.tensor_tensor(out=hT[s][:, :, 0:nt], in0=psT[:, :, 0:nt],
                                                           in1=bcast(gmx[:].unsqueeze(2), [128, 8, nt]), op=ALU.mult),
                          reads=['psT', 'gmx'], writes=[('hT', s)])
                    projs = (('K', psK, 512), ('V', psV, 1024)) + ((('Q', psQ, 0),) if (own or sample) else ())
                    for name, pst, c0 in projs:
                        for kc in range(8):
                            P.add('pe', lambda e, kc=kc, pst=pst, c0=c0: e.matmul(
                                pst[0:nt, :], lhsT=hT[s][:, kc, 0:nt], rhs=w1[:, kc, c0:c0 + 512],
                                start=(kc == 0), stop=(kc == 7)),
                                reads=[('hT', s), ('w1', kc, c0 // 1024)], writes=['ps' + name])
                    for c in range(4):
                        for kc in range(8):
                            P.add('pe', lambda e, kc=kc, c=c: e.matmul(
                                psU[:, c, 0:nt], lhsT=w1[:, kc, 1536 + c * 128:1536 + (c + 1) * 128],
                                rhs=hT[s][:, kc, 0:nt], start=(kc == 0), stop=(kc == 7)),
                                reads=[('hT', s), ('w1', kc, 1)], writes=['psU'])
                    if sample:
                        P.add('act', lambda e: e.activation(out=usT[:], in_=psU[:, :, 0:4], func=AF.Copy),
                              reads=['psU'], writes=['usT'])
                    else:
                        P.add('act', lambda e: e.activation(out=ust[s][:], in_=psU[:], func=AF.Copy),
                              reads=['psU'], writes=[('ust', s)])
                        P.add('sp', lambda e: e.dma_start(out=ut_scr[:, :, t * 128:(t + 1) * 128], in_=ust[s][:]),
                              reads=[('ust', s)], writes=[('utscr', t)], dma=f'us{s}')
                    kout = ks_f if sample else kf[s]
                    toks = rotary(psK, nt, t, None, t1, t2, 'psK')
                    P.add('dve', lambda e: e.tensor_tensor(out=kout[0:nt], in0=t1[0:nt], in1=t2[0:nt], op=ALU.add),
                          reads=toks, writes=[('kf', s)])
                    if not sample:
                        P.add('pool', lambda e: e.memset(kb[:, :, 0:16], 0.0), writes=['kb'])
                        n_blk = t // 2
                        P.add('pool', lambda e: e.memset(kb[:, :, n_blk:n_blk + 1], 1.0), writes=['kb'])
                        P.add('act', lambda e: e.activation(out=kb[:, :, 16:80], in_=kf[s][:], func=AF.Copy),
                              reads=[('kf', s)], writes=['kb'])
                        for h in range(8):
                            P.add('pe', lambda e, h=h: e.transpose(out=psKT[0:80, h, :], in_=kb[:, h, :], identity=ident[:]),
                                  reads=['kb', 'ident'], writes=['psKT'])
                        P.add('act', lambda e: e.activation(out=KaT[0:80, :, t * 128:(t + 1) * 128], in_=psKT[0:80],
                                                            func=AF.Copy),
                              reads=['psKT'], writes=[('KaT', t)])
                    vout = vs_f if sample else vf[s]
                    P.add('act', lambda e: e.activation(out=vout[0:nt], in_=psV[0:nt, :].rearrange("p (h d) -> p h d", h=8),
                                                        func=AF.Copy),
                          reads=['psV'], writes=[('vf', s)])
                    if not sample:
                        P.add('pool', lambda e: e.tensor_copy(out=Vaug[:, t, :, 0:64], in_=vf[s][:]),
                              reads=[('vf', s)], writes=[('Vaug', t)])
                    if own or sample:
                        toksq = rotary(psQ, nt, t, None, t1, t2, 'psQ')
                        if sample:
                            P.add('dve', lambda e: e.tensor_tensor(out=qs_f[:], in0=t1[0:4], in1=t2[0:4], op=ALU.add),
                                  reads=toksq, writes=['qs_f'])
                        else:
                            ti = t - 16
                            P.add('dve', lambda e: e.tensor_tensor(out=Qaug[:, ti, :, 16:80], in0=t1[:], in1=t2[:], op=ALU.add),
                                  reads=toksq, writes=[('Qaug', ti)])
                    if own:
                        P.add('sp', lambda e: e.dma_start(out=ko[r0:r0 + 128, :], in_=kf[s][:].rearrange("p h d -> p (h d)")),
                              reads=[('kf', s)], dma=f'ko{s}')
                        P.add('sp', lambda e: e.dma_start(out=vo[r0:r0 + 128, :], in_=vf[s][:].rearrange("p h d -> p (h d)")),
                              reads=[('vf', s)], dma=f'vo{s}')
                    if sample:
                        P.add('sp', lambda e: e.dma_start(out=kso[:, :], in_=ks_f[:].rearrange("p h d -> p (h d)")),
                              reads=[('kf', s)], dma=f'ko{s}')
                        P.add('sp', lambda e: e.dma_start(out=vso[:, :], in_=vs_f[:].rearrange("p h d -> p (h d)")),
                              reads=[('vf', s)], dma=f'vo{s}')

                tiles = [int(v) for v in DBG.split(',')] if (DBG and DBG != 'all') else range(33)
                tile_set = set(tiles)
                for t in tiles:
                    phase1_tile(t)
                P.barrier()
                P.emit()

            s2 = ExitStack()
            with s2:
                sb2, ps2 = mk(s2)
                mask4f = sb2("mask4f", [128, 4, 512], F32)
                mask4 = sb2("mask4b", [128, 4, 512], BF16)
                pbias = sb2("pbias_sb", [128, 8, 16], F32)
                kms = sb2("kms", [128, 8, 16], F32)
                kmT = sb2("kmT", [128, 8, 16], BF16)
                QaT = [sb2(f"QaT{i}", [128, 8, 512], BF16) for i in range(2)]
                s_m = sb2("s_m", [128, 8, 16], F32)
                top8 = sb2("top8", [128, 8, 8], F32)
                sel = sb2("sel", [128, 8, 16], F32)
                val = sb2("val", [128, 8, 16], F32)
                PT = [sb2(f"PT{i}", [128, 512], BF16) for i in range(3)]
                osb = [sb2(f"osb{i}", [128, 512], F32) for i in range(2)]
                rsum = sb2("rsum", [128, 512], F32)
                onesf = sb2("onesf", [128, 64], F32)
                ostg = [sb2(f"ostg{i}", [128, 512], BF16) for i in range(2)]
                psA = [ps2(f"psA{i}", [128, 512], F32) for i in range(2)]
                psO = [ps2(f"psO{i}", [128, 512], F32) for i in range(2)]
                psQT = ps2("psQT", [128, 8, 128], BF16)
                psS = ps2("psS", [128, 8, 16], F32)
                psBT = ps2("psBT", [16, 8, 128], BF16)
                psR = ps2("psR", [128, 512], F32)

                P.add('sp', lambda e: e.dma_start(out=mask4f[:], in_=mask4_d[:, :, :]), writes=['mask4f'], dma='c')
                P.add('sp', lambda e: e.dma_start(out=pbias[:], in_=pbias_d[:, :, :]), writes=['pbias'], dma='c')
                P.add('pool', lambda e: e.memset(onesf[:], 1.0), writes=['onesf'])
                P.barrier()
                P.add('pool', lambda e: e.tensor_copy(out=mask4[:], in_=mask4f[:]), reads=['mask4f'], writes=['mask4'])
                P.add('dve', lambda e: e.tensor_reduce(out=kms[0:80], in_=KaT[0:80].rearrange("p h (n k) -> p h n k", k=256),
                                                       axis=AX.X, op=ALU.add),
                      reads=[('KaT', t) for t in range(32)], writes=['kms'])
                P.add('dve', lambda e: e.tensor_copy(out=kmT[0:80], in_=kms[0:80]), reads=['kms'], writes=['kmT'])

                def qtok(ti):
                    return ('QaT', (ti // 4) % 2, ti % 4)

                def prep_qtile(ti, qa):
                    a = ti % 4
                    m = ti // 2
                    for h in range(8):
                        P.add('pe', lambda e, h=h: e.transpose(out=psQT[0:80, h, :], in_=Qaug[:, ti, h, :], identity=ident[:]),
                              reads=[('Qaug', ti), 'ident'], writes=['psQT'])
                    P.add('act', lambda e: e.activation(out=qa[0:80, :, a * 128:(a + 1) * 128], in_=psQT[0:80], func=AF.Copy),
                          reads=['psQT'], writes=[qtok(ti)])
                    for h in range(8):
                        P.add('pe', lambda e, h=h: e.matmul(psS[:, h, :], lhsT=qa[0:80, h, a * 128:(a + 1) * 128],
                                                            rhs=kmT[0:80, h, :], start=True, stop=True),
                              reads=[qtok(ti), 'kmT'], writes=['psS'])
                    P.add('dve', lambda e: e.tensor_tensor(out=s_m[:], in0=psS[:], in1=bcast(pbias[:, m:m + 1, :], [128, 8, 16]),
                                                           op=ALU.add),
                          reads=['psS', 'pbias'], writes=['s_m'])
                    for h in range(8):
                        P.add('dve', lambda e, h=h: e.max(out=top8[:, h, :], in_=s_m[:, h, :]),
                              reads=['s_m'], writes=[('top8', h)])
                    for h in range(8):
                        P.add('dve', lambda e, h=h: e.tensor_scalar(out=sel[:, h, :], in0=s_m[:, h, :], scalar1=top8[:, h, 2:3],
                                                                    scalar2=None, op0=ALU.is_ge),
                              reads=['s_m', ('top8', h)], writes=[('sel', h)])
                    P.add('dve', lambda e: e.tensor_scalar(out=val[:], in0=s_m[:], scalar1=-0.5 * BIG, scalar2=None, op0=ALU.is_gt),
                          reads=['s_m'], writes=['val'])
                    P.add('dve', lambda e: e.tensor_tensor(out=sel[:], in0=sel[:], in1=val[:], op=ALU.mult),
                          reads=[('sel', h) for h in range(8)] + ['val'], writes=[('sel', h) for h in range(8)])
                    P.add('dve', lambda e: e.tensor_scalar(out=Qaug[:, ti, :, 0:16], in0=sel[:], scalar1=-1.0, scalar2=BIG,
                                                           op0=ALU.add, op1=ALU.mult),
                          reads=[('sel', h) for h in range(8)], writes=[('Qb', ti)])
                    P.add('dve', lambda e: e.memset(Qaug[:, ti, :, 8 + m:9 + m], 0.0), reads=[('Qb', ti)], writes=[('Qb', ti)])
                    for h in range(8):
                        P.add('pe', lambda e, h=h: e.transpose(out=psBT[0:16, h, :], in_=Qaug[:, ti, h, 0:16], identity=ident[:]),
                              reads=[('Qb', ti), 'ident'], writes=['psBT'])
                    P.add('act', lambda e: e.activation(out=qa[0:16, :, a * 128:(a + 1) * 128], in_=psBT[0:16], func=AF.Copy),
                          reads=['psBT'], writes=[qtok(ti)])

                qgs = [int(v) for v in os.environ.get('KQG', '0,1,2,3').split(',')]
                step = 0
                def attn_group(qg):
                    nonlocal step
                    qa = QaT[qg % 2]
                    for ti in range(4 * qg, 4 * qg + 4):
                        prep_qtile(ti, qa)
                    qtoks = [qtok(ti) for ti in range(4 * qg, 4 * qg + 4)]
                    nkt = 20 + 4 * qg
                    for h in range(8):
                        po = psO[h % 2]
                        ob = osb[h % 2]

                        def qk(kt, h=h):
                            pa = psA[kt % 2]
                            P.add('pe', lambda e: e.matmul(pa[:], lhsT=KaT[0:80, h, kt * 128:(kt + 1) * 128], rhs=qa[0:80, h, :],
                                                           start=True, stop=True),
                                  reads=qtoks + [('KaT', kt)], writes=[('psA', kt % 2)])

                        def rest(kt, h=h, po=po):
                            nonlocal step
                            pa = psA[kt % 2]
                            pt = PT[step % 3]
                            pk = ('PT', step % 3)
                            step += 1
                            P.add('act', lambda e: e.activation(out=pt[:], in_=pa[:], func=AF.Exp, scale=0.125),
                                  reads=[('psA', kt % 2)], writes=[pk])
                            j = kt - (nkt - 4)
                            if j >= 0:
                                P.add('pool', lambda e: e.tensor_tensor(out=pt[:], in0=pt[:], in1=mask4[:, j, :], op=ALU.mult),
                                      reads=[pk, 'mask4'], writes=[pk])
                            P.add('pe', lambda e: e.matmul(po[0:65, :], lhsT=Vaug[:, kt, h, 0:65], rhs=pt[:],
                                                           start=(kt == 0), stop=(kt == nkt - 1)),
                                  reads=[pk, ('Vaug', kt), 'Vones'], writes=[('psO', h % 2)])

                        qk(0)
                        for kt in range(nkt):
                            if kt + 1 < nkt:
                                qk(kt + 1)
                            rest(kt)
                        P.add('act', lambda e, po=po, ob=ob: e.activation(out=ob[0:64, :], in_=po[0:64, :], func=AF.Copy),
                              reads=[('psO', h % 2)], writes=[('osb', h % 2)])
                        P.add('dve', lambda e, po=po: e.reciprocal(out=rsum[64:65, :], in_=po[64:65, :]),
                              reads=[('psO', h % 2)], writes=['rsum'])
                        P.add('pe', lambda e: e.matmul(psR[0:64, :], lhsT=onesf[64:65, 0:64], rhs=rsum[64:65, :], start=True, stop=True),
                              reads=['rsum', 'onesf'], writes=['psR'])
                        pofs = (h % 2) * 64
                        og = ostg[h % 2]
                        P.add('dve', lambda e, ob=ob, og=og: e.tensor_tensor(
                            out=og[0:64, :], in0=ob[0:64, :], in1=psR[0:64, :], op=ALU.mult),
                            reads=[('osb', h % 2), 'psR'], writes=[('ostg', h % 2)])
                        P.add('sp', lambda e, og=og, h=h, pofs=pofs, qg=qg: e.dma_start(
                            out=ot_scr[pofs:pofs + 64, h // 2, qg * 512:(qg + 1) * 512], in_=og[0:64, :]),
                            reads=[('ostg', h % 2)], writes=[('OT', qg, h)], dma=f'og{h % 2}')
                for qg in qgs:
                    attn_group(qg)
                if DBG:
                    P.barrier()
                    dOT = nc.dram_tensor("dbg_OT", [128, 4, 2048], BF16, kind="ExternalOutput").ap()
                    P.add('sp', lambda e: e.dma_start(out=dOT, in_=ot_scr), dma='dbg')
                    dbg('Qaug', Qaug[:], [128, 16, 8, 80], BF16, [('Qb', ti) for ti in range(16)])
                P.barrier()
                P.emit()

        TWO_PI = 6.283185307179586
        PI_LO = 3.1415925
        HALF_PI = 1.5707963267948966

        def D_(fn):
            P.add('dve', fn, reads=['g'], writes=['g'])

        def A_(fn):
            P.add('act', fn, reads=['g'], writes=['g'])

        def rr(x, tf, ti):
            D_(lambda e: e.tensor_scalar(out=tf, in0=x, scalar1=1.0 / TWO_PI, scalar2=None, op0=ALU.mult))
            D_(lambda e: e.tensor_copy(out=ti, in_=tf))
            D_(lambda e: e.tensor_copy(out=tf, in_=ti))
            D_(lambda e: e.scalar_tensor_tensor(out=x, in0=tf, scalar=-TWO_PI, in1=x, op0=ALU.mult, op1=ALU.add))
            D_(lambda e: e.tensor_scalar(out=tf, in0=x, scalar1=PI_LO, scalar2=None, op0=ALU.is_gt))
            D_(lambda e: e.scalar_tensor_tensor(out=x, in0=tf, scalar=-TWO_PI, in1=x, op0=ALU.mult, op1=ALU.add))
            D_(lambda e: e.tensor_scalar(out=tf, in0=x, scalar1=-PI_LO, scalar2=None, op0=ALU.is_lt))
            D_(lambda e: e.scalar_tensor_tensor(out=x, in0=tf, scalar=TWO_PI, in1=x, op0=ALU.mult, op1=ALU.add))
            D_(lambda e: e.tensor_scalar(out=x, in0=x, scalar1=-PI_LO, scalar2=PI_LO, op0=ALU.max, op1=ALU.min))

        def powtab(sbx, tagn, AR, AI, LDT, F, kv, K, coef):
            n = F * K
            al = sbx(tagn + "al", [128, F], F32); th = sbx(tagn + "th", [128, F], F32)
            tfs = sbx(tagn + "tfs", [128, F], F32); tis = sbx(tagn + "tis", [128, F], I32)
            PR = sbx(tagn + "PR", [128, n], F32); PI = sbx(tagn + "PI", [128, n], F32)
            ang = sbx(tagn + "ang", [128, n], F32); tf = sbx(tagn + "tf", [128, n], F32); ti = sbx(tagn + "ti", [128, n], I32)
            A_(lambda e: e.activation(out=tfs[:], in_=LDT, func=AF.Exp))
            D_(lambda e: e.tensor_tensor(out=al[:], in0=AR, in1=tfs[:], op=ALU.mult))
            D_(lambda e: e.tensor_tensor(out=th[:], in0=AI, in1=tfs[:], op=ALU.mult))
            rr(th[:], tfs[:], tis[:])
            v3 = lambda t: t[:].rearrange("q (f k) -> q f k", k=K)
            thb = bcast(th[:].unsqueeze(2), [128, F, K]); alb = bcast(al[:].unsqueeze(2), [128, F, K])
            kvb = bcast(kv.unsqueeze(1), [128, F, K])
            D_(lambda e: e.tensor_tensor(out=v3(ang), in0=thb, in1=kvb, op=ALU.mult))
            rr(ang[:], tf[:], ti[:])
            A_(lambda e: e.activation(out=PI[:], in_=ang[:], func=AF.Sin))
            D_(lambda e: e.tensor_scalar(out=ang[:], in0=ang[:], scalar1=HALF_PI, scalar2=None, op0=ALU.add))
            rr(ang[:], tf[:], ti[:])
            A_(lambda e: e.activation(out=PR[:], in_=ang[:], func=AF.Sin))
            D_(lambda e: e.tensor_tensor(out=v3(tf), in0=alb, in1=kvb, op=ALU.mult))
            A_(lambda e: e.activation(out=tf[:], in_=tf[:], func=AF.Exp))
            D_(lambda e: e.tensor_tensor(out=PR[:], in0=PR[:], in1=tf[:], op=ALU.mult))
            D_(lambda e: e.tensor_tensor(out=PI[:], in0=PI[:], in1=tf[:], op=ALU.mult))
            res = dict(PR=PR, PI=PI, al=al, th=th, tf=tf, ang=ang, ti=ti, tfs=tfs, tis=tis)
            if coef is not None:
                k1 = coef
                cr = sbx(tagn + "cr", [128, F], F32); ci = sbx(tagn + "ci", [128, F], F32)
                nr = sbx(tagn + "nr", [128, F], F32); den = sbx(tagn + "den", [128, F], F32)
                p1r = v3(PR)[:, :, k1]; p1i = v3(PI)[:, :, k1]
                D_(lambda e: e.tensor_scalar(out=nr[:], in0=p1r, scalar1=-1.0, scalar2=None, op0=ALU.add))
                D_(lambda e: e.tensor_tensor(out=den[:], in0=AR, in1=AR, op=ALU.mult))
                D_(lambda e: e.tensor_tensor(out=tfs[:], in0=AI, in1=AI, op=ALU.mult))
                D_(lambda e: e.tensor_tensor(out=den[:], in0=den[:], in1=tfs[:], op=ALU.add))
                D_(lambda e: e.reciprocal(out=den[:], in_=den[:]))
                D_(lambda e: e.tensor_tensor(out=cr[:], in0=nr[:], in1=AR, op=ALU.mult))
                D_(lambda e: e.tensor_tensor(out=tfs[:], in0=p1i, in1=AI, op=ALU.mult))
                D_(lambda e: e.tensor_tensor(out=cr[:], in0=cr[:], in1=tfs[:], op=ALU.add))
                D_(lambda e: e.tensor_tensor(out=cr[:], in0=cr[:], in1=den[:], op=ALU.mult))
                D_(lambda e: e.tensor_tensor(out=ci[:], in0=p1i, in1=AR, op=ALU.mult))
                D_(lambda e: e.tensor_tensor(out=tfs[:], in0=nr[:], in1=AI, op=ALU.mult))
                D_(lambda e: e.tensor_tensor(out=ci[:], in0=ci[:], in1=tfs[:], op=ALU.subtract))
                D_(lambda e: e.tensor_tensor(out=ci[:], in0=ci[:], in1=den[:], op=ALU.mult))
                res['cr'] = cr; res['ci'] = ci
            return res

        def cmul(outr, outi, ar_, ai_, br_, bi_, tmp):
            D_(lambda e: e.tensor_tensor(out=outr, in0=ar_, in1=br_, op=ALU.mult))
            D_(lambda e: e.tensor_tensor(out=tmp, in0=ai_, in1=bi_, op=ALU.mult))
            D_(lambda e: e.tensor_tensor(out=outr, in0=outr, in1=tmp, op=ALU.subtract))
            D_(lambda e: e.tensor_tensor(out=outi, in0=ar_, in1=bi_, op=ALU.mult))
            D_(lambda e: e.tensor_tensor(out=tmp, in0=ai_, in1=br_, op=ALU.mult))
            D_(lambda e: e.tensor_tensor(out=outi, in0=outi, in1=tmp, op=ALU.add))

        s3 = ExitStack()
        with s3:
            sb3, ps3 = mk(s3)
            Yc = sb3("Yc", [128, 4, 16, 128], BF16)
            Ycs = sb3("Ycs", [4, 4, 128], BF16)
            s3a = ExitStack()
            with s3a:
                sb3a, ps3a = mk(s3a)
                Ssr = sb3a("Ssr", [128, 16, 4], F32)
                Ssi = sb3a("Ssi", [128, 16, 4], F32)
                SR = sb3a("SR", [128, 16, 256], F32)
                SI = sb3a("SI", [128, 16, 256], F32)
                kv0 = sb3a("kv0s", [128, 16], F32); kv1 = sb3a("kv1s", [128, 16], F32); kvj = sb3a("kvjs", [128, 256], F32)
                pmB = sb3a("pmBs", [128, 2], F32); pmD = sb3a("pmDs", [128, 2], F32)
                for i_, (dst, srcd) in enumerate(((kv0, kv0_d), (kv1, kv1_d), (kvj, kvj_d), (pmB, pmB_d), (pmD, pmD_d))):
                    P.add('sp', lambda e, dst=dst, srcd=srcd: e.dma_start(out=dst[:], in_=srcd), writes=['g'], dma='c')
                tA = {}
                for nm, srcd in (('ar', arB), ('ai', aiB), ('ldt', ldtB), ('br', brB), ('bi', biB)):
                    tA[nm] = sb3a("gB" + nm, [128, 256], F32)
                    P.add('sp', lambda e, t=tA[nm], srcd=srcd: e.dma_start(out=t[:], in_=srcd), writes=['g'], dma='c')
                sKV = ExitStack()
                with sKV:
                    sbKV, _ = mk(sKV)
                    idxG = sbKV("idxG", [128, 8, 24], I32)
                    Ksel = sbKV("Ksel", [128, 4, 8, 6, 64], BF16); Vsel = sbKV("Vsel", [128, 4, 8, 6, 64], BF16)
                    sS = ExitStack()
                    with sS:
                        sbS, psS_ = mk(sS)
                        F32R = mybir.dt.float32r
                        ck_rows = ckh.rearrange("(r h) d -> r (h d)", h=8)
                        ptrep = sbS("ptrep", [128, 256], I32); idxP = sbS("idxP", [128, 256], I32)
                        piota = sbS("piota", [128, 1], F32); hb = sbS("hb", [128, 8], F32)
                        zsel = sbS("zsel", [128, 255], F32); rsel = sbS("rsel", [4, 128], F32)
                        eye8 = sbS("eye8", [8, 8], F32); esel = sbS("esel", [4, 4], F32)
                        ptE = sbS("ptE", [8, 128], I32); ptO = sbS("ptO", [8, 128], I32)
                        ptEf = sbS("ptEf", [8, 128], F32); ptOf = sbS("ptOf", [8, 128], F32)
                        ones1 = sbS("ones1", [128, 128], F32)
                        pgb = [sbS(f"pgb{i}", [128, 512], F32) for i in range(8)]
                        for dst, srcd in ((ptrep, ptrep_d), (piota, piota_d), (hb, hb_d), (zsel, zsel_d), (rsel, rsel_d), (eye8, eye8_d),
                                          (esel, esel_d), (ptE, ptE_d), (ptO, ptO_d)):
                            P.add('sp', lambda e, dst=dst, srcd=srcd: e.dma_start(out=dst[:], in_=srcd), writes=['g'], dma='c')
                        P.add('pool', lambda e: e.memset(ones1[:], 1.0), writes=['g'])
                        P.barrier()
                        D_(lambda e: e.tensor_scalar(out=idxP[:], in0=ptrep[:], scalar1=128.0, scalar2=piota[:, 0:1], op0=ALU.mult, op1=ALU.add))
                        D_(lambda e: e.tensor_copy(out=ptEf[:], in_=ptE[:]))
                        D_(lambda e: e.tensor_copy(out=ptOf[:], in_=ptO[:]))
                        psKS = psS_("psKS", [128, 512], F32)
                        psQR = psS_("psQR", [128, 512], F32)
                        psM = psS_("psM", [128, 512], F32)
                        for i in range(256):
                            sl = i % 8
                            P.add('pool', lambda e, i=i, sl=sl: e.indirect_dma_start(
                                out=pgb[sl][:], out_offset=None, in_=ck_rows,
                                in_offset=bass.IndirectOffsetOnAxis(ap=idxP[:, i:i + 1], axis=0)),
                                reads=['g'], writes=[('pgb', sl)], dma=f'pg{sl}')
                            m = (i // 64) * 32 + (i % 64) // 2
                            P.add('pe', lambda e, i=i, sl=sl, m=m: e.matmul(psKS[:], lhsT=zsel[:, 127 - m:255 - m], rhs=pgb[sl][:],
                                                                          start=(i == 0), stop=(i == 255)),
                                  reads=[('pgb', sl), 'g'], writes=['psKS'])
                        qrep = sbS("qrep", [128, 512], F32); prodS = sbS("prodS", [128, 512], F32); sbl = sbS("sbl", [128, 8], F32)
                        P.add('pe', lambda e: e.matmul(psQR[:], lhsT=rsel[:], rhs=qs_f[:].rearrange("s h d -> s (h d)"), start=True, stop=True),
                              reads=['qs_f', 'g'], writes=['psQR'])
                        P.add('act', lambda e: e.activation(out=qrep[:], in_=psQR[:], func=AF.Copy), reads=['psQR'], writes=['g'])
                        D_(lambda e: e.tensor_tensor(out=prodS[:], in0=qrep[:], in1=psKS[:], op=ALU.mult))
                        P.add('dve', lambda e: e.tensor_reduce(out=sbl[:], in_=prodS[:].rearrange("q (h d) -> q h d", d=64), axis=AX.X, op=ALU.add),
                              reads=['g', 'psKS'], writes=['g'])
                        P.add('pe', lambda e: e.transpose(out=psM[0:8, 0:128], in_=sbl[:], identity=identf[:]), reads=['g', 'identf'], writes=['psM'])
                        sbT = sbS("sbT", [8, 128], F32); top8s = sbS("top8s", [8, 4, 8], F32)
                        selr = sbS("selr", [8, 3, 128], F32); tmpS = sbS("tmpS", [8, 3, 128], F32)
                        PG = sbS("PG", [8, 2, 12], F32); rhsE = sbS("rhsE", [8, 8, 24], F32)
                        P.add('act', lambda e: e.activation(out=sbT[:], in_=psM[0:8, 0:128], func=AF.Copy), reads=['psM'], writes=['g'])
                        for s_ in range(4):
                            D_(lambda e, s_=s_: e.max(out=top8s[:, s_, :], in_=sbT[:, s_ * 32:(s_ + 1) * 32]))
                        for r_ in range(3):
                            for s_ in range(4):
                                D_(lambda e, s_=s_, r_=r_: e.tensor_scalar(out=selr[:, r_, s_ * 32:(s_ + 1) * 32], in0=sbT[:, s_ * 32:(s_ + 1) * 32],
                                                                           scalar1=top8s[:, s_, r_:r_ + 1], scalar2=None, op0=ALU.is_equal))
                        for e_, ptf in enumerate((ptEf, ptOf)):
                            D_(lambda e, ptf=ptf: e.tensor_tensor(out=tmpS[:], in0=selr[:], in1=bcast(ptf[:].unsqueeze(1), [8, 3, 128]), op=ALU.mult))
                            D_(lambda e, e_=e_: e.tensor_reduce(out=PG[:, e_, :], in_=tmpS[:].rearrange("h r (s n) -> h (r s) n", n=32),
                                                                axis=AX.X, op=ALU.add))
                        D_(lambda e: e.tensor_tensor(out=rhsE[:], in0=bcast(PG[:].rearrange("h e x -> h (e x)").unsqueeze(1), [8, 8, 24]),
                                                     in1=bcast(eye8[:].unsqueeze(2), [8, 8, 24]), op=ALU.mult))
                        P.add('pe', lambda e: e.matmul(psM[:, 0:192], lhsT=ones1[0:8, :], rhs=rhsE[:].rearrange("k h x -> k (h x)"), start=True, stop=True),
                              reads=['g'], writes=['psM'])
                        P.add('dve', lambda e: e.scalar_tensor_tensor(out=idxG[:], in0=psM[:, 0:192].rearrange("q (h x) -> q h x", h=8), scalar=1024.0,
                                                                      in1=bcast(hb[:].unsqueeze(2), [128, 8, 24]), op0=ALU.mult, op1=ALU.add),
                              reads=['psM', 'g'], writes=['g', 'idxG'])
                        for h in range(8):
                            for e_ in range(2):
                                for r_ in range(3):
                                    for s_ in range(4):
                                        x = e_ * 12 + r_ * 4 + s_
                                        for srcd, dstt in ((ckh, Ksel), (cvh, Vsel)):
                                            P.add('pool', lambda e, h=h, x=x, s_=s_, j=r_ * 2 + e_, srcd=srcd, dstt=dstt: e.indirect_dma_start(
                                                out=dstt[:, s_, h, j, :], out_offset=None, in_=srcd,
                                                in_offset=bass.IndirectOffsetOnAxis(ap=idxG[:, h, x:x + 1], axis=0)),
                                                reads=['idxG'], writes=[('KVsel', h, x, id(dstt))], dma='gk')
                        P.excl = {'d:gk'}
                        P.barrier()
                        P.emit()
                    sB = ExitStack()
                    with sB:
                        sbB, psB = mk(sB)
                        VBre = sbB("VBre", [128, 4, 16, 128], BF16)
                        VBim = sbB("VBim", [128, 4, 16, 128], BF16)
                        sg = ExitStack()
                        with sg:
                            sbg, _ = mk(sg)
                            for hf in range(2):
                                sh = ExitStack()
                                with sh:
                                    sbh, _ = mk(sh)
                                    fs = slice(hf * 128, (hf + 1) * 128)
                                    r = powtab(sbh, f"pB{hf}", tA['ar'][:, fs], tA['ai'][:, fs], tA['ldt'][:, fs], 128, kv0[:], 16, coef=1)
                                    Zr = sbh(f"Zr{hf}", [128, 2048], F32); Zi = sbh(f"Zi{hf}", [128, 2048], F32)
                                    v3 = lambda t: t[:].rearrange("q (f k) -> q f k", k=16)
                                    crb = bcast(r['cr'][:].unsqueeze(2), [128, 128, 16]); cib = bcast(r['ci'][:].unsqueeze(2), [128, 128, 16])
                                    cmul(v3(Zr), v3(Zi), v3(r['PR']), v3(r['PI']), crb, cib, v3(r['tf']))
                                    brb = bcast(tA['br'][:, fs].unsqueeze(2), [128, 128, 16]); bib = bcast(tA['bi'][:, fs].unsqueeze(2), [128, 128, 16])
                                    Xr, Xi = r['PR'], r['PI']
                                    cmul(v3(Xr), v3(Xi), v3(Zr), v3(Zi), brb, bib, v3(r['tf']))
                                    for gw in range(2):
                                        for Xs, VB in ((Xr, VBre), (Xi, VBim)):
                                            D_(lambda e, gw=gw, Xs=Xs, VB=VB, hf=hf: e.tensor_scalar(
                                                out=VB[:, 2 * hf:2 * hf + 2, :, gw * 64:(gw + 1) * 64],
                                                in0=Xs[:].rearrange("q (c p d) -> q c d p", c=2, p=64, d=16),
                                                scalar1=pmB[:, gw:gw + 1], scalar2=None, op0=ALU.mult))
                                    P.barrier()
                                    P.emit()
                            P.barrier()
                            P.emit()
                        UT = sbB("UT", [128, 4, 4096], BF16)
                        for c in range(4):
                            P.add('sp', lambda e, c=c: e.dma_start(out=UT[:, c, :], in_=ut_scr[:, c, :]),
                                  reads=[('utscr', t) for t in range(32)], writes=['UT'], dma='c')
                        P.barrier()
                        psS2 = [psB(f"psSB{i}", [128, 256], F32) for i in range(4)]
                        for w in range(16):
                            kc, r0 = w // 4, 32 * (w % 4)
                            for ri, (VB, SX) in enumerate(((VBre, SR), (VBim, SI))):
                                pst = psS2[(2 * w + ri) % 4]
                                for dl in range(16):
                                    rhs = UT[r0:r0 + 32, kc, :].rearrange("q (j s) -> q j s", s=16)[:, :, 15 - dl]
                                    P.add('pe', lambda e, pst=pst, VB=VB, dl=dl, rhs=rhs, r0=r0, kc=kc: e.matmul(
                                        pst[:], lhsT=VB[r0:r0 + 32, kc, dl, :], rhs=rhs, start=(dl == 0), stop=(dl == 15),
                                        tile_position=(r0, 0)),
                                        reads=['g', 'UT'], writes=[('psSB', (2 * w + ri) % 4)])
                                P.add('act', lambda e, pst=pst, SX=SX, w=w: e.activation(out=SX[:, w, :], in_=pst[:], func=AF.Copy),
                                      reads=[('psSB', (2 * w + ri) % 4)], writes=[('S', ri, w)])
                        for w in range(16):
                            kc, r0 = w // 4, 32 * (w % 4)
                            for ri, (VB, SX) in enumerate(((VBre, Ssr), (VBim, Ssi))):
                                pst = psS2[(2 * w + ri) % 4]
                                P.add('pe', lambda e, pst=pst, VB=VB, r0=r0, kc=kc: e.matmul(
                                    pst[:, 0:4], lhsT=VB[r0:r0 + 32, kc, 0, :], rhs=usT[r0:r0 + 32, kc, :], start=True, stop=True,
                                    tile_position=(r0, 0)),
                                    reads=['g', 'usT'], writes=[('psSB', (2 * w + ri) % 4)])
                                P.add('act', lambda e, pst=pst, SX=SX, w=w: e.activation(out=SX[:, w, :], in_=pst[:, 0:4], func=AF.Copy),
                                      reads=[('psSB', (2 * w + ri) % 4)], writes=[('Ss', ri)])
                        P.barrier()
                        P.emit()
                    sS5 = ExitStack()
                    with sS5:
                        sbS, psS_ = mk(sS5)
                        esel = sbS("esel5", [4, 4], F32); ones1 = sbS("ones5", [128, 128], F32)
                        psQR = psS_("psQR5", [128, 512], F32); psM = psS_("psM5", [128, 512], F32)
                        P.add('sp', lambda e: e.dma_start(out=esel[:], in_=esel_d), writes=['g'], dma='c')
                        P.add('dve', lambda e: e.memset(ones1[:], 1.0), writes=['g'])
                        P.excl = set()
                        P.barrier()
                        qbc = sbS("qbc", [128, 4, 512], F32); sc = sbS("sc", [128, 4, 8, 6], F32); Wn = sbS("Wn", [128, 4, 512], F32)
                        Wd = sbS("Wd", [128, 4, 512], F32); psum_j = sbS("psumj", [128, 4, 8], F32)
                        big = sbS("bigS", [128, 8, 6, 64], F32)
                        for s_ in range(4):
                            P.add('pe', lambda e, s_=s_: e.matmul(psQR[:], lhsT=bcast(esel[:, s_:s_ + 1], [4, 128]), rhs=qs_f[:].rearrange("s h d -> s (h d)"),
                                                                  start=True, stop=True), reads=['qs_f', 'g'], writes=['psQR'])
                            P.add('act', lambda e, s_=s_: e.activation(out=qbc[:, s_, :], in_=psQR[:], func=AF.Copy), reads=['psQR'], writes=['g'])
                        for s_ in range(4):
                            D_(lambda e, s_=s_: e.tensor_tensor(out=big[:], in0=Ksel[:, s_],
                                                                in1=bcast(qbc[:, s_, :].rearrange("q (h d) -> q h d", d=64).unsqueeze(2), [128, 8, 6, 64]),
                                                                op=ALU.mult))
                            D_(lambda e, s_=s_: e.tensor_reduce(out=sc[:, s_], in_=big[:], axis=AX.X, op=ALU.add))
                        A_(lambda e: e.activation(out=sc[:], in_=sc[:], func=AF.Exp, scale=0.125))
                        D_(lambda e: e.tensor_reduce(out=psum_j[:], in_=sc[:], axis=AX.X, op=ALU.add))
                        for s_ in range(4):
                            D_(lambda e, s_=s_: e.tensor_tensor(out=big[:], in0=Vsel[:, s_], in1=bcast(sc[:, s_].unsqueeze(3), [128, 8, 6, 64]), op=ALU.mult))
                            D_(lambda e, s_=s_: e.tensor_reduce(out=Wn[:, s_, :].rearrange("q (h d) -> q h d", d=64),
                                                                in_=big[:].rearrange("q h j d -> q h d j"), axis=AX.X, op=ALU.add))
                        D_(lambda e: e.tensor_copy(out=Wd[:].rearrange("q s (h d) -> q s h d", d=64), in_=bcast(psum_j[:].unsqueeze(3), [128, 4, 8, 64])))
                        ssf = sbS("ssf", [4, 8], F32); prs = sbS("prs", [4, 8, 64], F32); Wsn = sbS("Wsn", [4, 512], F32); Wsd = sbS("Wsd", [4, 512], F32)
                        D_(lambda e: e.tensor_tensor(out=prs[:], in0=qs_f[:], in1=ks_f[:], op=ALU.mult))
                        D_(lambda e: e.tensor_reduce(out=ssf[:], in_=prs[:], axis=AX.X, op=ALU.add))
                        A_(lambda e: e.activation(out=ssf[:], in_=ssf[:], func=AF.Exp, scale=0.125))
                        D_(lambda e: e.tensor_tensor(out=Wsn[:].rearrange("s (h d) -> s h d", d=64), in0=vs_f[:], in1=bcast(ssf[:].unsqueeze(2), [4, 8, 64]),
                                                     op=ALU.mult))
                        D_(lambda e: e.tensor_copy(out=Wsd[:].rearrange("s (h d) -> s h d", d=64), in_=bcast(ssf[:].unsqueeze(2), [4, 8, 64])))
                        for wi, (Wbig, Wself) in enumerate(((Wn, Wsn), (Wd, Wsd))):
                            for kc in range(4):
                                for s_ in range(4):
                                    col = wi * 16 + kc * 4 + s_
                                    P.add('pe', lambda e, Wbig=Wbig, kc=kc, s_=s_, col=col: e.matmul(
                                        psM[:, col:col + 1], lhsT=Wbig[:, s_, kc * 128:(kc + 1) * 128], rhs=ones1[:, 0:1], start=True, stop=False),
                                        reads=['g'], writes=['psM'])
                                    P.add('pe', lambda e, Wself=Wself, kc=kc, s_=s_, col=col: e.matmul(
                                        psM[:, col:col + 1], lhsT=Wself[:, kc * 128:(kc + 1) * 128], rhs=esel[:, s_:s_ + 1], start=False, stop=True),
                                        reads=['g'], writes=['psM'])
                        rden = sbS("rden", [128, 16], F32)
                        P.add('dve', lambda e: e.reciprocal(out=rden[:], in_=psM[:, 16:32]), reads=['psM'], writes=['g'])
                        P.add('dve', lambda e: e.tensor_tensor(out=OTs[:].rearrange("q k s -> q (k s)"), in0=psM[:, 0:16], in1=rden[:], op=ALU.mult),
                              reads=['psM', 'g'], writes=['OTs'])

                        P.barrier()
                        P.emit()
                sC = ExitStack()
                with sC:
                    sbC, psC = mk(sC)
                    Hre = sbC("Hre", [128, 16, 128], BF16)
                    Him = sbC("Him", [128, 16, 128], BF16)
                    WDre = sbC("WDre", [128, 16, 512], BF16)
                    WDim = sbC("WDim", [128, 16, 512], BF16)
                    cosT = sbC("cosT", [128, 4096], F32); sinT = sbC("sinT", [128, 4096], F32)
                    H0re = sbC("H0re", [128, 16, 4], BF16); H0im = sbC("H0im", [128, 16, 4], BF16)
                    M16 = sbC("M16", [128, 16], F32)
                    tD = {}
                    for nm, srcd, n_ in (('ar', arD, 16), ('ai', aiD, 16), ('ldt', ldtD, 16), ('cr', crD, 256), ('ci', ciD, 256)):
                        tD[nm] = sbC("gD" + nm, [128, n_], F32)
                        P.add('sp', lambda e, t=tD[nm], srcd=srcd: e.dma_start(out=t[:], in_=srcd), writes=['g'], dma='c')
                    P.barrier()
                    sg2 = ExitStack()
                    with sg2:
                        sbg2, _ = mk(sg2)
                        r = powtab(sbg2, "pD", tD['ar'][:], tD['ai'][:], tD['ldt'][:], 16, kv1[:], 16, coef=None)
                        Wr = sbg2("Wr", [128, 4096], F32); Wi = sbg2("Wi", [128, 4096], F32); Wt = sbg2("Wt", [128, 4096], F32)
                        v4 = lambda t: t[:].rearrange("q (w k c) -> q w k c", w=16, k=16, c=16)
                        prb = bcast(r['PR'][:].rearrange("q (w k) -> q w k", k=16).unsqueeze(3), [128, 16, 16, 16])
                        pib = bcast(r['PI'][:].rearrange("q (w k) -> q w k", k=16).unsqueeze(3), [128, 16, 16, 16])
                        crb = bcast(tD['cr'][:].rearrange("q (w c) -> q w c", c=16).unsqueeze(2), [128, 16, 16, 16])
                        cib = bcast(tD['ci'][:].rearrange("q (w c) -> q w c", c=16).unsqueeze(2), [128, 16, 16, 16])
                        cmul(v4(Wr), v4(Wi), crb, cib, prb, pib, v4(Wt))
                        for gw in range(2):
                            D_(lambda e, gw=gw: e.tensor_scalar(
                                out=WDre[:].rearrange("q w (g x) -> q w g x", g=2)[:, :, gw, :],
                                in0=Wr[:].rearrange("q (w x) -> q w x", w=16), scalar1=pmD[:, gw:gw + 1], scalar2=None, op0=ALU.mult))
                            D_(lambda e, gw=gw: e.tensor_scalar(
                                out=WDim[:].rearrange("q w (g x) -> q w g x", g=2)[:, :, gw, :],
                                in0=Wi[:].rearrange("q (w x) -> q w x", w=16), scalar1=pmD[:, gw:gw + 1], scalar2=-1.0,
                                op0=ALU.mult, op1=ALU.mult))
                        h0r = sbg2("h0r_s", [128, 64], F32); h0i = sbg2("h0i_s", [128, 64], F32)
                        hnr = sbg2("hnr", [128, 64], F32); hni = sbg2("hni", [128, 64], F32); htm = sbg2("htm", [128, 64], F32)
                        P.add('sp', lambda e: e.dma_start(out=h0r[:], in_=h0r_d), writes=['g'], dma='c')
                        P.add('sp', lambda e: e.dma_start(out=h0i[:], in_=h0i_d), writes=['g'], dma='c')
                        P.barrier()
                        w3 = lambda t: t[:].rearrange("q (w s) -> q w s", s=4)
                        a1r = bcast(r['PR'][:].rearrange("q (w k) -> q w k", k=16)[:, :, 0:1], [128, 16, 4])
                        a1i = bcast(r['PI'][:].rearrange("q (w k) -> q w k", k=16)[:, :, 0:1], [128, 16, 4])
                        cmul(w3(hnr), w3(hni), a1r, a1i, w3(h0r), w3(h0i), w3(htm))
                        D_(lambda e: e.tensor_tensor(out=w3(hnr), in0=w3(hnr), in1=Ssr[:], op=ALU.add))
                        D_(lambda e: e.tensor_tensor(out=w3(hni), in0=w3(hni), in1=Ssi[:], op=ALU.add))
                        P.add('sp', lambda e: e.dma_start(out=hsr, in_=hnr[:]), reads=['g'], dma='c')
                        P.add('sp', lambda e: e.dma_start(out=hsi, in_=hni[:]), reads=['g'], dma='c')
                        D_(lambda e: e.tensor_copy(out=H0re[:], in_=w3(h0r)))
                        D_(lambda e: e.tensor_copy(out=H0im[:], in_=w3(h0i)))
                        TH = sbg2("TH", [128, 16], F32)
                        D_(lambda e: e.tensor_scalar(out=TH[:], in0=r['th'][:], scalar1=16.0, scalar2=None, op0=ALU.mult))
                        rr(TH[:], r['tfs'][:], r['tis'][:])
                        A16 = sbg2("A16", [128, 16], F32)
                        D_(lambda e: e.tensor_scalar(out=A16[:], in0=r['al'][:], scalar1=16.0, scalar2=None, op0=ALU.mult))
                        A_(lambda e: e.activation(out=M16[:], in_=A16[:], func=AF.Exp))
                        angJ = Wt
                        tfJ = Wr
                        tiJ = sbg2("tiJ", [128, 4096], I32)
                        D_(lambda e: e.tensor_tensor(out=angJ[:].rearrange("q (w j) -> q w j", j=256),
                                                     in0=bcast(TH[:].unsqueeze(2), [128, 16, 256]),
                                                     in1=bcast(kvj[:].unsqueeze(1), [128, 16, 256]), op=ALU.mult))
                        rr(angJ[:], tfJ[:], tiJ[:])
                        A_(lambda e: e.activation(out=sinT[:], in_=angJ[:], func=AF.Sin))
                        D_(lambda e: e.tensor_scalar(out=angJ[:], in0=angJ[:], scalar1=HALF_PI, scalar2=None, op0=ALU.add))
                        rr(angJ[:], tfJ[:], tiJ[:])
                        A_(lambda e: e.activation(out=cosT[:], in_=angJ[:], func=AF.Sin))
                        P.barrier()
                        P.emit()
                    RR = sbC("RR", [128, 4096], F32); RI = sbC("RI", [128, 4096], F32); Tt = sbC("Tt", [128, 4096], F32)
                    SRf = SR[:].rearrange("q w j -> q (w j)"); SIf = SI[:].rearrange("q w j -> q (w j)")
                    D_(lambda e: e.tensor_tensor(out=RR[:], in0=cosT[:], in1=SRf, op=ALU.mult))
                    D_(lambda e: e.tensor_tensor(out=Tt[:], in0=sinT[:], in1=SIf, op=ALU.mult))
                    D_(lambda e: e.tensor_tensor(out=RR[:], in0=RR[:], in1=Tt[:], op=ALU.add))
                    D_(lambda e: e.tensor_tensor(out=RI[:], in0=cosT[:], in1=SIf, op=ALU.mult))
                    D_(lambda e: e.tensor_tensor(out=Tt[:], in0=sinT[:], in1=SRf, op=ALU.mult))
                    D_(lambda e: e.tensor_tensor(out=RI[:], in0=RI[:], in1=Tt[:], op=ALU.subtract))
                    GR = SR; GI = SI
                    for w in range(16):
                        for Rx, Gx in ((RR, GR), (RI, GI)):
                            D_(lambda e, w=w, Rx=Rx, Gx=Gx: e.tensor_tensor_scan(
                                out=Gx[:, w, :], data0=bcast(M16[:, w:w + 1], [128, 256]), data1=Rx[:, w * 256:(w + 1) * 256],
                                initial=0.0, op0=ALU.mult, op1=ALU.add))
                    c3 = cosT[:].rearrange("q (w j) -> q w j", j=256); s3_ = sinT[:].rearrange("q (w j) -> q w j", j=256)
                    HRf = RR[:].rearrange("q (w j) -> q w j", j=256); HIf = RI[:].rearrange("q (w j) -> q w j", j=256)
                    T3 = Tt[:].rearrange("q (w j) -> q w j", j=256)
                    sl = slice(127, 256)
                    D_(lambda e: e.tensor_tensor(out=HRf[:, :, sl], in0=c3[:, :, sl], in1=GR[:, :, sl], op=ALU.mult))
                    D_(lambda e: e.tensor_tensor(out=T3[:, :, sl], in0=s3_[:, :, sl], in1=GI[:, :, sl], op=ALU.mult))
                    D_(lambda e: e.tensor_tensor(out=HRf[:, :, sl], in0=HRf[:, :, sl], in1=T3[:, :, sl], op=ALU.subtract))
                    D_(lambda e: e.tensor_tensor(out=HIf[:, :, sl], in0=c3[:, :, sl], in1=GI[:, :, sl], op=ALU.mult))
                    D_(lambda e: e.tensor_tensor(out=T3[:, :, sl], in0=s3_[:, :, sl], in1=GR[:, :, sl], op=ALU.mult))
                    D_(lambda e: e.tensor_tensor(out=HIf[:, :, sl], in0=HIf[:, :, sl], in1=T3[:, :, sl], op=ALU.add))
                    D_(lambda e: e.tensor_copy(out=Hre[:], in_=HRf[:, :, 127:255]))
                    D_(lambda e: e.tensor_copy(out=Him[:], in_=HIf[:, :, 127:255]))
                    hfR = sbC("hfR", [128, 16], F32); hfI = sbC("hfI", [128, 16], F32)
                    D_(lambda e: e.tensor_copy(out=hfR[:], in_=HRf[:, :, 255]))
                    D_(lambda e: e.tensor_copy(out=hfI[:], in_=HIf[:, :, 255]))
                    P.add('sp', lambda e: e.dma_start(out=hfr, in_=hfR[:]), reads=['g'], dma='c')
                    P.add('sp', lambda e: e.dma_start(out=hfi, in_=hfI[:]), reads=['g'], dma='c')
                    psD = [psC(f"psD{i}", [128, 512], F32) for i in range(2)]
                    for w in range(16):
                        pd = psD[w % 2]
                        P.add('pe', lambda e, pd=pd, w=w: e.matmul(pd[:], lhsT=Hre[:, w, :], rhs=WDre[:, w, :], start=True, stop=False),
                              reads=['g'], writes=[('psD', w % 2)])
                        P.add('pe', lambda e, pd=pd, w=w: e.matmul(pd[:], lhsT=Him[:, w, :], rhs=WDim[:, w, :], start=False, stop=True),
                              reads=['g'], writes=[('psD', w % 2)])
                        P.add('act', lambda e, pd=pd, w=w: e.activation(
                            out=Yc[:, w // 4, :, (w % 4) * 32:(w % 4 + 1) * 32].rearrange("j t (g c) -> j t g c", g=2),
                            in_=pd[:].rearrange("j (g t c) -> j t g c", g=2, t=16), func=AF.Copy),
                              reads=[('psD', w % 2)], writes=['Yc'])
                    for w in range(16):
                        pd = psD[w % 2]
                        P.add('pe', lambda e, pd=pd, w=w: e.matmul(pd[0:4, :], lhsT=H0re[:, w, :], rhs=WDre[:, w, :], start=True, stop=False),
                              reads=['g'], writes=[('psD', w % 2)])
                        P.add('pe', lambda e, pd=pd, w=w: e.matmul(pd[0:4, :], lhsT=H0im[:, w, :], rhs=WDim[:, w, :], start=False, stop=True),
                              reads=['g'], writes=[('psD', w % 2)])
                        P.add('act', lambda e, pd=pd, w=w: e.activation(
                            out=Ycs[:, w // 4, (w % 4) * 32:(w % 4 + 1) * 32].rearrange("s (g c) -> s g c", g=2),
                            in_=pd[0:4, :].rearrange("s (g t c) -> s g t c", g=2, t=16)[:, :, 0, :], func=AF.Copy),
                              reads=[('psD', w % 2)], writes=['Ycs'])
                    P.barrier()
                    P.emit()

            def norm_transpose(xt_ap, nt, gain, hT_out, bufs, tg_, perm_a=None, hT_full=None):
                ssq_, rstd_, xn_, sqj_, psT_ = bufs
                P.add('act', lambda e: e.activation(out=sqj_[0:nt, :], in_=xt_ap, func=AF.Square, accum_out=ssq_[0:nt, :]),
                      reads=[tg_], writes=['n_sqj', 'n_ssq'])
                P.add('dve', lambda e: e.tensor_scalar(out=rstd_[0:nt, :], in0=ssq_[0:nt, :], scalar1=1.0 / D, scalar2=EPS,
                                                       op0=ALU.mult, op1=ALU.add), reads=['n_ssq'], writes=['n_rstd'])
                P.add('act', lambda e: e.activation(out=rstd_[0:nt, :], in_=rstd_[0:nt, :], func=AF.Sqrt),
                      reads=['n_rstd'], writes=['n_rstd'])
                P.add('dve', lambda e: e.reciprocal(out=rstd_[0:nt, :], in_=rstd_[0:nt, :]), reads=['n_rstd'], writes=['n_rstd'])
                P.add('dve', lambda e: e.tensor_scalar(out=xn_[0:nt, :], in0=xt_ap, scalar1=rstd_[0:nt, 0:1], scalar2=None,
                                                       op0=ALU.mult), reads=[tg_, 'n_rstd'], writes=['n_xn'])
                for kc in range(8):
                    P.add('pe', lambda e, kc=kc: e.transpose(out=psT_[:, kc, 0:nt], in_=xn_[0:nt, kc * 128:(kc + 1) * 128],
                                                             identity=ident[0:nt, 0:nt]),
                          reads=['n_xn', 'ident'], writes=['n_psT'])
                if perm_a is None:
                    P.add('dve', lambda e: e.tensor_tensor(out=hT_out, in0=psT_[:, :, 0:nt],
                                                           in1=bcast(gain[:].unsqueeze(2), [128, 8, nt]), op=ALU.mult),
                          reads=['n_psT', 'gains'], writes=['hTg'])
                else:
                    a_ = perm_a
                    P.add('dve', lambda e: e.tensor_tensor(
                        out=hT_full[:].rearrange("q k (s j) -> q k s j", s=16)[:, :, :, 8 * a_:8 * a_ + 8],
                        in0=psT_[:, :, 0:128].rearrange("q k (j s) -> q k s j", s=16),
                        in1=bcast(gain[:].unsqueeze(2).unsqueeze(3), [128, 8, 16, 8]), op=ALU.mult),
                        reads=['n_psT', 'gains'], writes=['hTg'])

            s3b = ExitStack()
            with s3b:
                sbb, psb = mk(s3b)
                wg = sbb("wg", [128, 8, 2048], BF16)
                wat = sbb("wat", [128, 4, 1024], BF16); wgl = sbb("wgl", [128, 4, 512], BF16)
                wss = sbb("wss", [128, 4, 1024], BF16); wou = sbb("wou", [128, 8, 1024], BF16)
                Gw = sbb("Gw", [128, 4, 16, 128], BF16)
                UTs = sbb("UTs", [128, 4, 2048], BF16)
                dvec = sbb("dvec", [128, 4], F32); bgl = sbb("bgls", [128, 4], F32); gmx2 = sbb("gmx2", [128, 8], F32)
                bm = sbb("bm", [128, 128], F32)
                pb = [psb(f"pb{i}", [128, 512], F32) for i in range(7)]
                psT2 = psb("psT2", [128, 8, 128], BF16)
                for kc in range(8):
                    for half in range(2):
                        P.add('pool', lambda e, kc=kc, half=half: e.dma_start(
                            out=wg[:, kc, half * 1024:(half + 1) * 1024],
                            in_=w_in[kc * 128:(kc + 1) * 128, 2048 + half * 1024:2048 + (half + 1) * 1024]), writes=['g'], dma='w')
                    P.add('pool', lambda e, kc=kc: e.dma_start(out=wou[:, kc, :], in_=w_out_d[kc * 128:(kc + 1) * 128, :]),
                          writes=['g'], dma='w')
                for kc in range(4):
                    P.add('pool', lambda e, kc=kc: e.dma_start(out=wat[:, kc, :], in_=w_attn_d[kc * 128:(kc + 1) * 128, :]), writes=['g'], dma='w')
                    P.add('pool', lambda e, kc=kc: e.dma_start(out=wss[:, kc, :], in_=w_ssm_d[kc * 128:(kc + 1) * 128, :]), writes=['g'], dma='w')
                    P.add('pool', lambda e, kc=kc: e.dma_start(out=wgl[:, kc, :], in_=w_glu_d[kc * 128:(kc + 1) * 128, :]), writes=['g'], dma='w')
                for dst, srcd in ((dvec, dvec_d), (bgl, bgl_d), (gmx2, gmix), (bm, bmask_d)):
                    P.add('sp', lambda e, dst=dst, srcd=srcd: e.dma_start(out=dst[:], in_=srcd), writes=['g'], dma='c')
                sga = ExitStack()
                with sga:
                    sbga, _ = mk(sga)
                    UTo = sbga("UTo", [128, 4, 2048], BF16)
                    for kc in range(4):
                        P.add('sp', lambda e, kc=kc: e.dma_start(out=UTo[:, kc, :], in_=ut_scr[:, kc, 2048:4096]), writes=['g'], dma='c')
                    tA = {}
                    for nm, srcd, n_ in (('ar', arA, 32), ('ai', aiA, 32), ('ldt', ldtA, 32), ('b1', b1A, 512), ('b2', b2A, 512),
                                         ('c', cA, 512), ('sgn', sgnA_d, 1)):
                        tA[nm] = sbga("gA" + nm, [128, n_], F32)
                        P.add('sp', lambda e, t=tA[nm], srcd=srcd: e.dma_start(out=t[:], in_=srcd), writes=['g'], dma='c')
                    P.barrier()
                    kvA = sbga("kvA", [128, 16], F32)
                    P.add('sp', lambda e: e.dma_start(out=kvA[:], in_=kv0_d), writes=['g'], dma='c')
                    P.barrier()
                    for kc in range(4):
                        P.add('pool', lambda e, kc=kc: e.tensor_copy(
                            out=UTs[:, kc, :].rearrange("q (g s j) -> q g s j", g=4, s=16),
                            in_=UTo[:, kc, :].rearrange("q (g j s) -> q g s j", g=4, s=16)), reads=['g'], writes=['g'])
                    r = powtab(sbga, "pA", tA['ar'][:], tA['ai'][:], tA['ldt'][:], 32, kvA[:], 16, coef=1)
                    Zr = sbga("ZrA", [128, 512], F32); Zi = sbga("ZiA", [128, 512], F32)
                    v3 = lambda t: t[:].rearrange("q (f k) -> q f k", k=16)
                    crb = bcast(r['cr'][:].unsqueeze(2), [128, 32, 16]); cib = bcast(r['ci'][:].unsqueeze(2), [128, 32, 16])
                    cmul(v3(Zr), v3(Zi), v3(r['PR']), v3(r['PI']), crb, cib, v3(r['tf']))
                    D_(lambda e: e.tensor_scalar(out=Zi[:], in0=Zi[:], scalar1=tA['sgn'][:, 0:1], scalar2=None, op0=ALU.mult))
                    Cc = sbga("Cc", [128, 512], F32); Xc = sbga("Xc", [128, 512], F32); Xt = sbga("Xt", [128, 512], F32)
                    D_(lambda e: e.tensor_scalar(out=Cc[:], in0=tA['c'][:], scalar1=tA['sgn'][:, 0:1], scalar2=-1.0,
                                                 op0=ALU.mult, op1=ALU.mult))
                    g3 = lambda t: t[:].rearrange("q (g c) -> q g c", c=16)
                    psGw = pb[0][:].rearrange("q (k n) -> q k n", k=4)
                    for dl in range(16):
                        D_(lambda e, dl=dl: e.tensor_tensor(out=g3(Xc), in0=g3(tA['b1']), in1=bcast(v3(Zr)[:, :, dl:dl + 1], [128, 32, 16]), op=ALU.mult))
                        D_(lambda e, dl=dl: e.tensor_tensor(out=g3(Xt), in0=g3(tA['b2']), in1=bcast(v3(Zi)[:, :, dl:dl + 1], [128, 32, 16]), op=ALU.mult))
                        D_(lambda e: e.tensor_tensor(out=Xc[:], in0=Xc[:], in1=Xt[:], op=ALU.add))
                        for kc in range(4):
                            P.add('pe', lambda e, kc=kc: e.matmul(psGw[:, kc, :], lhsT=Xc[:, kc * 128:(kc + 1) * 128],
                                                                  rhs=Cc[:, kc * 128:(kc + 1) * 128], start=True, stop=True),
                                  reads=['g'], writes=['psGw'])
                        P.add('dve', lambda e, dl=dl: e.tensor_tensor(out=Gw[:, :, dl, :], in0=psGw, in1=bcast(bm[:].unsqueeze(1), [128, 4, 128]),
                                                                      op=ALU.mult), reads=['psGw', 'g'], writes=['g'])
                    P.barrier()
                    P.emit()

                hTg = sbb("hTg", [128, 8, 512], BF16)
                OTg = sbb("OTg", [128, 4, 512], BF16)
                zT = sbb("zT", [128, 4, 512], BF16); z2T = sbb("z2T", [128, 4, 512], BF16)
                mT = sbb("mT", [128, 8, 512], BF16)
                xres = [sbb(f"xres{i}", [128, D], F32) for i in range(4)]
                x1t = [sbb(f"x1t{i}", [128, D], F32) for i in range(2)]
                yt = sbb("ytmp", [128, 512], F32); y2 = sbb("ytmp2", [128, 512], F32); sg1 = sbb("sg1", [128, 512], F32)
                sg2_ = sbb("sg2", [128, 512], F32)
                nb = (sbb("n_ssq", [128, 1], F32), sbb("n_rstd", [128, 1], F32), sbb("n_xn", [128, D], BF16),
                      sbb("n_sqj", [128, D], BF16), psT2)
                P.add('pool', lambda e: e.memset(OTg[:], 0.0), writes=['OTg'])
                OTp = sbb("OTp", [128, 4, 512], BF16)

                def phase3b_group(tg):
                    sample = (tg == 4)
                    N = 4 if sample else 512
                    tiles = [(xs, 0, 4)] if sample else [(xo, tg * 512 + a * 128, 128) for a in range(4)]
                    for a, (src, r0, nt) in enumerate(tiles):
                        P.add('sp', lambda e, a=a, src=src, r0=r0, nt=nt: e.dma_start(out=xres[a][0:nt, :], in_=src[r0:r0 + nt, :]),
                              writes=[('xres', a)], dma=f'x{a % 2}')
                        norm_transpose(xres[a][0:nt, :], nt, gmx2, hTg[:, :, a * 128:a * 128 + nt], nb, ('xres', a),
                                       perm_a=(None if sample else a), hT_full=hTg)
                    if sample:
                        P.add('pool', lambda e: e.tensor_copy(out=OTg[:, :, 0:4], in_=OTs[:]), reads=['OTs'], writes=['OTg'])
                    if not sample:
                        P.add('sp', lambda e: e.dma_start(out=OTg[:], in_=ot_scr[:, :, tg * 512:(tg + 1) * 512]),
                              reads=[('OT', tg, h) for h in range(8)], writes=['OTg'], dma='c')
                        P.add('pool', lambda e: e.tensor_copy(out=OTp[:].rearrange("q k (s j) -> q k s j", s=16),
                                                              in_=OTg[:].rearrange("q k (j s) -> q k s j", s=16)),
                              reads=['OTg'], writes=['OTp'])
                    else:
                        P.add('pool', lambda e: e.tensor_copy(out=OTp[:, :, 0:4], in_=OTg[:, :, 0:4]), reads=['OTg'], writes=['OTp'])
                    for kc in range(4):
                        if sample:
                            P.add('pe', lambda e, kc=kc: e.matmul(pb[kc][:, 0:4], lhsT=Gw[:, kc, 0, :], rhs=usT[:, kc, :],
                                                                  start=True, stop=False), reads=['g', 'usT'], writes=[('pb', kc)])
                            P.add('pe', lambda e, kc=kc: e.matmul(
                                pb[kc][:, 0:4], lhsT=Ycs[:, kc, :], rhs=ident[0:4, 0:4], start=False, stop=True),
                                reads=['Ycs', 'ident'], writes=[('pb', kc)])
                        else:
                            for dl in range(16):
                                P.add('pe', lambda e, kc=kc, dl=dl: e.matmul(
                                    pb[kc][:, dl * 32:512], lhsT=Gw[:, kc, dl, :], rhs=UTs[:, kc, tg * 512:tg * 512 + (16 - dl) * 32],
                                    start=(dl == 0), stop=False),
                                    reads=['g'], writes=[('pb', kc)])
                            for tl in range(16):
                                P.add('pe', lambda e, kc=kc, tl=tl: e.matmul(
                                    pb[kc][:, tl * 32:(tl + 1) * 32], lhsT=Yc[:, kc, tl, :], rhs=ident[:, tg * 32:(tg + 1) * 32], start=False, stop=(tl == 15)),
                                    reads=['Yc', 'ident'], writes=[('pb', kc)])
                    for kc in range(4):
                        uin = usT[:, kc, :] if sample else UTs[:, kc, tg * 512:(tg + 1) * 512]
                        P.add('dve', lambda e, kc=kc, uin=uin: e.scalar_tensor_tensor(
                            out=yt[:, 0:N], in0=uin, scalar=dvec[:, kc:kc + 1], in1=pb[kc][:, 0:N], op0=ALU.mult, op1=ALU.add),
                            reads=[('pb', kc), 'g'], writes=['yt'])
                        P.add('dve', lambda e: e.tensor_tensor(out=y2[:, 0:N], in0=yt[:, 0:N], in1=yt[:, 0:N], op=ALU.mult),
                              reads=['yt'], writes=['y2'])
                        P.add('dve', lambda e: e.tensor_scalar(out=y2[:, 0:N], in0=y2[:, 0:N], scalar1=0.044715, scalar2=1.0,
                                                               op0=ALU.mult, op1=ALU.add), reads=['y2'], writes=['y2'])
                        P.add('dve', lambda e: e.tensor_tensor(out=y2[:, 0:N], in0=y2[:, 0:N], in1=yt[:, 0:N], op=ALU.mult),
                              reads=['y2', 'yt'], writes=['y2'])
                        P.add('act', lambda e: e.activation(out=y2[:, 0:N], in_=y2[:, 0:N], func=AF.Sigmoid, scale=1.5957691216057308),
                              reads=['y2'], writes=['y2'])
                        P.add('dve', lambda e, kc=kc: e.tensor_tensor(out=zT[:, kc, 0:N], in0=y2[:, 0:N], in1=yt[:, 0:N], op=ALU.mult),
                              reads=['y2', 'yt'], writes=[('zT', kc)])
                    for m in range(4):
                        for kc in range(4):
                            P.add('pe', lambda e, m=m, kc=kc: e.matmul(pb[m][:, 0:N], lhsT=wgl[:, kc, m * 128:(m + 1) * 128],
                                                                       rhs=zT[:, kc, 0:N], start=(kc == 0), stop=(kc == 3)),
                                  reads=['g'] + [('zT', k) for k in range(4)], writes=[('pb', m)])
                        P.add('act', lambda e, m=m: e.activation(out=sg1[:, 0:N], in_=pb[m][:, 0:N], func=AF.Sigmoid, bias=bgl[:, m:m + 1]),
                              reads=[('pb', m), 'g'], writes=['sg1'])
                        P.add('dve', lambda e, m=m: e.tensor_tensor(out=z2T[:, m, 0:N], in0=zT[:, m, 0:N], in1=sg1[:, 0:N], op=ALU.mult),
                              reads=['sg1', ('zT', m)], writes=[('z2T', m)])
                    for m in range(8):
                        ms = slice(m * 128, (m + 1) * 128)
                        for kc in range(4):
                            P.add('pe', lambda e, kc=kc, ms=ms: e.matmul(pb[0][:, 0:N], lhsT=wat[:, kc, ms], rhs=OTp[:, kc, 0:N],
                                                                         start=(kc == 0), stop=(kc == 3)),
                                  reads=['g', 'OTp'], writes=[('pb', 0)])
                        for kc in range(4):
                            P.add('pe', lambda e, kc=kc, ms=ms: e.matmul(pb[1][:, 0:N], lhsT=wss[:, kc, ms], rhs=z2T[:, kc, 0:N],
                                                                         start=(kc == 0), stop=(kc == 3)),
                                  reads=['g'] + [('z2T', k) for k in range(4)], writes=[('pb', 1)])
                        for gi in range(2):
                            for kc in range(8):
                                P.add('pe', lambda e, kc=kc, m=m, gi=gi: e.matmul(
                                    pb[2 + gi][:, 0:N], lhsT=wg[:, kc, gi * 1024 + m * 128:gi * 1024 + (m + 1) * 128], rhs=hTg[:, kc, 0:N],
                                    start=(kc == 0), stop=(kc == 7)), reads=['g', 'hTg'], writes=[('pb', 2 + gi)])
                        P.add('act', lambda e: e.activation(out=sg1[:, 0:N], in_=pb[2][:, 0:N], func=AF.Sigmoid),
                              reads=[('pb', 2)], writes=['sg1'])
                        P.add('act', lambda e: e.activation(out=sg2_[:, 0:N], in_=pb[3][:, 0:N], func=AF.Sigmoid),
                              reads=[('pb', 3)], writes=['sg2'])
                        P.add('dve', lambda e: e.tensor_tensor(out=sg1[:, 0:N], in0=sg1[:, 0:N], in1=pb[0][:, 0:N], op=ALU.mult),
                              reads=['sg1', ('pb', 0)], writes=['sg1'])
                        P.add('dve', lambda e: e.tensor_tensor(out=sg2_[:, 0:N], in0=sg2_[:, 0:N], in1=pb[1][:, 0:N], op=ALU.mult),
                              reads=['sg2', ('pb', 1)], writes=['sg2'])
                        if sample:
                            P.add('dve', lambda e, m=m: e.tensor_tensor(out=mT[:, m, 0:N], in0=sg1[:, 0:N], in1=sg2_[:, 0:N], op=ALU.add),
                                  reads=['sg1', 'sg2'], writes=[('mT', m)])
                        else:
                            P.add('dve', lambda e, m=m: e.tensor_tensor(
                                out=mT[:, m, :].rearrange("q (j s) -> q s j", s=16), in0=sg1[:].rearrange("q (s j) -> q s j", s=16),
                                in1=sg2_[:].rearrange("q (s j) -> q s j", s=16), op=ALU.add),
                                reads=['sg1', 'sg2'], writes=[('mT', m)])
                    for a, (src, r0, nt) in enumerate(tiles):
                        xo_ = x1t[a % 2]
                        for half in range(2):
                            pX = pb[4 + half]
                            for kc in range(8):
                                P.add('pe', lambda e, kc=kc, pX=pX, a=a, nt=nt, half=half: e.matmul(
                                    pX[0:nt, :], lhsT=mT[:, kc, a * 128:a * 128 + nt], rhs=wou[:, kc, half * 512:(half + 1) * 512],
                                    start=(kc == 0), stop=(kc == 7)), reads=['g'] + [('mT', k) for k in range(8)], writes=[('pb', 4 + half)])
                            P.add('dve', lambda e, pX=pX, a=a, nt=nt, half=half, xo_=xo_: e.tensor_tensor(
                                out=xo_[0:nt, half * 512:(half + 1) * 512], in0=pX[0:nt, :], in1=xres[a][0:nt, half * 512:(half + 1) * 512],
                                op=ALU.add), reads=[('pb', 4 + half), ('xres', a)], writes=[('x1t', a % 2)])
                        rr0 = 2048 if sample else tg * 512 + a * 128
                        P.add('sp', lambda e, xo_=xo_, rr0=rr0, nt=nt: e.dma_start(out=x1_scr[rr0:rr0 + nt, :], in_=xo_[0:nt, :]),
                              reads=[('x1t', a % 2)], writes=[('x1scr', rr0)], dma=f'us{a % 2}')

                groups = [int(v) for v in os.environ.get('KTG', '0,1,2,3,4').split(',')]
                for tg in groups:
                    phase3b_group(tg)
                if DBG:
                    P.barrier()
                    dX1 = nc.dram_tensor("dbg_x1", [2052, 1024], F32, kind="ExternalOutput").ap()
                    P.add('sp', lambda e: e.dma_start(out=dX1, in_=x1_scr), dma='dbg')
                P.barrier()
                P.emit()

        s4 = ExitStack()
        with s4:
            sb4, ps4 = mk(s4)
            wfi = sb4("wfi", [128, 8, 5632], BF16)
            wfo = sb4("wfo", [128, 22, 1024], BF16)
            gff = sb4("gff", [128, 8], F32); gfin = sb4("gfin_s", [128, D], F32)
            hfT = sb4("hfT", [128, 8, 512], BF16); actT = sb4("actT", [128, 22, 512], BF16)
            xr = [sb4(f"xr{i}", [128, D], F32) for i in range(4)]
            x2 = [sb4(f"x2{i}", [128, D], F32) for i in range(2)]
            sgA = [sb4(f"sgA{i}", [128, 512], F32) for i in range(2)]
            fss = sb4("fss", [128, 1], F32); frs = sb4("frs", [128, 1], F32)
            pAb = [ps4(f"pAb{i}", [128, 512], F32) for i in range(2)]
            pGb = [ps4(f"pGb{i}", [128, 512], F32) for i in range(2)]
            pXb = [ps4(f"pXb{i}", [128, 512], F32) for i in range(2)]
            psT4 = ps4("psT4", [128, 8, 128], BF16)
            nb4 = (sb4("n4_ssq", [128, 1], F32), sb4("n4_rstd", [128, 1], F32), sb4("n4_xn", [128, D], BF16),
                   sb4("n4_sqj", [128, D], BF16), psT4)
            P.add('sp', lambda e: e.dma_start(out=gff[:], in_=gffn_d), writes=['g'], dma='c')
            P.add('sp', lambda e: e.dma_start(out=gfin[:], in_=gfin_d), writes=['g'], dma='c')
            P.barrier()
            for ci_, (c0, c1) in enumerate(((0, 2048), (2048, 4096), (4096, 5632))):
                for kc in range(8):
                    P.add('pool', lambda e, kc=kc, c0=c0, c1=c1: e.dma_start(out=wfi[:, kc, c0:c1], in_=w_fin_d[kc * 128:(kc + 1) * 128, c0:c1]),
                          writes=[('wfi', ci_, kc)], dma=f'wf{ci_}')
            for kc in range(22):
                P.add('pool', lambda e, kc=kc: e.dma_start(out=wfo[:, kc, :], in_=w_fout_d[kc * 128:(kc + 1) * 128, :]),
                      writes=[('wfo', kc)], dma='wfo')
            wfi_t = [[('wfi', c_, k_) for k_ in range(8)] for c_ in range(3)]
            wfo_t = [('wfo', k_) for k_ in range(22)]

            def phase4_group(tg):
                sample = (tg == 4)
                N = 4 if sample else 512
                tiles = [(2048, 4)] if sample else [(tg * 512 + a * 128, 128) for a in range(4)]
                for a, (r0, nt) in enumerate(tiles):
                    P.add('sp', lambda e, a=a, r0=r0, nt=nt: e.dma_start(out=xr[a][0:nt, :], in_=x1_scr[r0:r0 + nt, :]),
                          reads=[('x1scr', r0)], writes=[('xr', a)], dma=f'x{a % 2}')
                    norm_transpose(xr[a][0:nt, :], nt, gff, hfT[:, :, a * 128:a * 128 + nt], nb4, ('xr', a))
                for m in range(22):
                    pa, pg = pAb[m % 2], pGb[m % 2]
                    for kc in range(8):
                        P.add('pe', lambda e, kc=kc, m=m, pa=pa: e.matmul(pa[:, 0:N], lhsT=wfi[:, kc, m * 128:(m + 1) * 128], rhs=hfT[:, kc, 0:N],
                                                                          start=(kc == 0), stop=(kc == 7)), reads=['hTg'] + wfi_t[(m * 128) // 2048], writes=[('pAb', m % 2)])
                    for kc in range(8):
                        P.add('pe', lambda e, kc=kc, m=m, pg=pg: e.matmul(pg[:, 0:N], lhsT=wfi[:, kc, 2816 + m * 128:2816 + (m + 1) * 128],
                                                                          rhs=hfT[:, kc, 0:N], start=(kc == 0), stop=(kc == 7)),
                              reads=['hTg'] + wfi_t[(2816 + m * 128) // 2048] + wfi_t[(2816 + m * 128 + 127) // 2048], writes=[('pGb', m % 2)])
                    sg = sgA[m % 2]
                    P.add('act', lambda e, pa=pa, sg=sg: e.activation(out=sg[:, 0:N], in_=pa[:, 0:N], func=AF.Silu),
                          reads=[('pAb', m % 2)], writes=[('sgA', m % 2)])
                    P.add('dve', lambda e, pg=pg, sg=sg, m=m: e.tensor_tensor(out=actT[:, m, 0:N], in0=sg[:, 0:N], in1=pg[:, 0:N], op=ALU.mult),
                          reads=[('sgA', m % 2), ('pGb', m % 2)], writes=[('actT', m)])
                for a, (r0, nt) in enumerate(tiles):
                    xx = x2[a % 2]
                    yy = xx
                    for half in range(2):
                        pX = pXb[half]
                        for kc in range(22):
                            P.add('pe', lambda e, kc=kc, pX=pX, a=a, nt=nt, half=half: e.matmul(
                                pX[0:nt, :], lhsT=actT[:, kc, a * 128:a * 128 + nt], rhs=wfo[:, kc, half * 512:(half + 1) * 512],
                                start=(kc == 0), stop=(kc == 21)), reads=wfo_t + [('actT', k) for k in range(22)], writes=[('pXb', half)])
                        P.add('dve', lambda e, pX=pX, a=a, nt=nt, half=half, xx=xx: e.tensor_tensor(
                            out=xx[0:nt, half * 512:(half + 1) * 512], in0=pX[0:nt, :], in1=xr[a][0:nt, half * 512:(half + 1) * 512], op=ALU.add),
                            reads=[('pXb', half), ('xr', a)], writes=[('x2', a % 2)])
                    P.add('act', lambda e, xx=xx, nt=nt: e.activation(out=nb4[3][0:nt, :], in_=xx[0:nt, :], func=AF.Square, accum_out=fss[0:nt, :]),
                          reads=[('x2', a % 2)], writes=['n_sqj', 'fss'])
                    P.add('dve', lambda e, nt=nt: e.tensor_scalar(out=frs[0:nt, :], in0=fss[0:nt, :], scalar1=1.0 / D, scalar2=EPS,
                                                                  op0=ALU.mult, op1=ALU.add), reads=['fss'], writes=['frs'])
                    P.add('act', lambda e, nt=nt: e.activation(out=frs[0:nt, :], in_=frs[0:nt, :], func=AF.Sqrt), reads=['frs'], writes=['frs'])
                    P.add('dve', lambda e, nt=nt: e.reciprocal(out=frs[0:nt, :], in_=frs[0:nt, :]), reads=['frs'], writes=['frs'])
                    P.add('dve', lambda e, xx=xx, yy=yy, nt=nt: e.scalar_tensor_tensor(
                        out=yy[0:nt, :], in0=xx[0:nt, :], scalar=frs[0:nt, 0:1], in1=gfin[0:nt, :], op0=ALU.mult, op1=ALU.mult),
                        reads=[('x2', a % 2), 'frs', 'g'], writes=[('x2', a % 2)])
                    dst = yso if sample else yo
                    d0 = 0 if sample else r0
                    P.add('sp', lambda e, yy=yy, dst=dst, d0=d0, nt=nt: e.dma_start(out=dst[d0:d0 + nt, :], in_=yy[0:nt, :]),
                          reads=[('x2', a % 2)], dma=f'ko{a % 2}')

            for tg in groups:
                phase4_group(tg)
            P.barrier()
            P.emit()
    return nc


def _host_consts(p):
    half = HD // 2
    inv = np.power(np.float32(10000.0), -2.0 * np.arange(half, dtype=np.float32) / HD).astype(np.float32)
    pos = np.concatenate([np.arange(2048), 2048 * p + np.arange(2048), np.full(128, 8192)]).astype(np.float32)
    ang = pos[:, None] * inv[None, :]
    cos = np.cos(ang).astype(np.float32)
    sin = np.sin(ang).astype(np.float32)
    sF = np.concatenate([-sin, sin], axis=1)
    cF = np.ascontiguousarray(cos.reshape(33, 128, 32).transpose(1, 0, 2))
    sF = np.ascontiguousarray(sF.reshape(33, 128, 64).transpose(1, 0, 2))
    k = np.arange(128)[:, None, None, None]
    j = np.arange(4)[None, :, None, None]
    a = np.arange(4)[None, None, :, None]
    q = np.arange(128)[None, None, None, :]
    m4 = ((j < a) | ((j == a) & (k <= q))).astype(np.float32).reshape(128, 4, 512)
    pb = np.full((8, 16), -BIG, np.float32)
    for m in range(8):
        if p == 1:
            pb[m, 0:8] = 0.0
        pb[m, 8:8 + m] = 0.0
    pb = np.ascontiguousarray(np.broadcast_to(pb[None], (128, 8, 16)))
    return cF, sF, np.ascontiguousarray(m4), pb


def _s5_layouts(a_re, a_im, log_dt, b_re, b_im, c_re, c_im):
    c_ = np.ascontiguousarray
    out = {}
    def lay_gp(a):
        t = a.reshape(4, 8, 64)
        t = np.broadcast_to(t[:, :, None, :], (4, 8, 16, 64))
        return c_(t.transpose(1, 2, 0, 3).reshape(128, 256))
    out["arB"] = lay_gp(a_re); out["aiB"] = lay_gp(a_im)
    out["ldtB"] = lay_gp(np.broadcast_to(log_dt[:, None], (32, 64)))
    def lay_b(bb):
        t = bb.reshape(4, 8, 64, 16)
        return c_(t.transpose(1, 3, 0, 2).reshape(128, 256))
    out["brB"] = lay_b(b_re); out["biB"] = lay_b(b_im)
    def lay_w(a):
        t = a.reshape(16, 2, 64)
        return c_(t.transpose(1, 2, 0).reshape(128, 16))
    out["arD"] = lay_w(a_re); out["aiD"] = lay_w(a_im)
    out["ldtD"] = lay_w(np.broadcast_to(log_dt[:, None], (32, 64)))
    def lay_c(cc):
        t = cc.reshape(16, 2, 16, 64)
        return c_(t.transpose(1, 3, 0, 2).reshape(128, 256))
    out["crD"] = lay_c(c_re); out["ciD"] = lay_c(c_im)
    t2 = lambda a: c_(np.concatenate([a.T, a.T], axis=0))
    out["arA"] = t2(a_re); out["aiA"] = t2(a_im); out["ldtA"] = t2(np.broadcast_to(log_dt[:, None], (32, 64)))
    bt = lambda x: x.transpose(1, 0, 2).reshape(64, 512)
    out["b1A"] = c_(np.concatenate([bt(b_re), bt(b_im)], axis=0))
    out["b2A"] = c_(np.concatenate([bt(b_im), bt(b_re)], axis=0))
    ct = lambda x: x.transpose(2, 0, 1).reshape(64, 512)
    out["cA"] = c_(np.concatenate([ct(c_re), ct(c_im)], axis=0))
    out["kv0"] = c_(np.broadcast_to(np.arange(16, dtype=np.float32)[None], (128, 16)))
    out["kv1"] = c_(np.broadcast_to(np.arange(1, 17, dtype=np.float32)[None], (128, 16)))
    out["kvj"] = c_(np.broadcast_to(np.arange(1, 257, dtype=np.float32)[None], (128, 256)))
    r = np.arange(128)
    out["pmB"] = c_(np.stack([((r // 16) % 2 == 0), ((r // 16) % 2 == 1)], axis=1).astype(np.float32))
    out["pmD"] = c_(np.stack([(r // 64 == 0), (r // 64 == 1)], axis=1).astype(np.float32))
    return out


_NC_CACHE = {}
_LAST = {}


def kernel(x_prompt, x_sample, cache_k, cache_v, state_ssm_re, state_ssm_im, page_table,
           norm_mix, w_in, w_attn_proj, ssm_a_re, ssm_a_im, ssm_log_dt, ssm_b_re, ssm_b_im,
           ssm_c_re, ssm_c_im, ssm_d, w_glu, b_glu, w_ssm_proj, w_out, norm_ffn,
           w_ffn_in, w_ffn_out, norm_final):
    f = lambda a: np.ascontiguousarray(np.asarray(a, dtype=np.float32))
    x_prompt = f(x_prompt)
    x_sample = f(x_sample)
    if 'nc' not in _NC_CACHE:
        _NC_CACHE['nc'] = build()
    nc = _NC_CACHE['nc']
    in_maps = []
    ident = np.eye(128, dtype=np.float32)
    gm = np.ascontiguousarray(f(norm_mix)[0].reshape(8, 128).T)
    s5 = _s5_layouts(f(ssm_a_re)[0], f(ssm_a_im)[0], f(ssm_log_dt)[0], f(ssm_b_re)[0], f(ssm_b_im)[0],
                     f(ssm_c_re)[0], f(ssm_c_im)[0])
    col = lambda v, n: np.ascontiguousarray(f(v).reshape(n, 128).T)
    r_ = np.arange(128)
    wts = {"w_attn": f(w_attn_proj)[0], "w_glu": f(w_glu)[0], "w_ssm": f(w_ssm_proj)[0], "w_out": f(w_out)[0],
           "w_ffn_in": f(w_ffn_in)[0], "w_ffn_out": f(w_ffn_out)[0],
           "bgl": col(f(b_glu)[0], 4), "dvec": col(f(ssm_d)[0], 4), "gffn": col(f(norm_ffn)[0], 8),
           "gfin": np.ascontiguousarray(np.broadcast_to(f(norm_final)[None, :], (128, 1024))),
           "sgnA": np.where(r_ < 64, -1.0, 1.0).astype(np.float32).reshape(128, 1),
           "bmask": (r_[:, None] // 16 == r_[None, :] // 16).astype(np.float32)}
    zs = np.zeros((128, 255), np.float32); zs[:, 127] = 1.0
    smp = {"cache_k": f(cache_k).reshape(2560 * 128 * 8, 64), "cache_v": f(cache_v).reshape(2560 * 128 * 8, 64),
           "piota": r_.astype(np.float32).reshape(128, 1),
           "hb": (8.0 * r_[:, None] + np.arange(8)[None, :]).astype(np.float32),
           "zsel": zs, "rsel": (np.arange(128)[None, :] // 32 == np.arange(4)[:, None]).astype(np.float32),
           "eye8": np.eye(8, dtype=np.float32), "esel": np.eye(4, dtype=np.float32)}
    for c in range(8):
        b, p = c // 2, c % 2
        cF, sF, m4, pb = _host_consts(p)
        xo = x_prompt[b, 2048 * p:2048 * (p + 1)]
        xpv = x_prompt[b, 0:2048] if p == 1 else np.zeros((2048, D), np.float32)
        in_maps.append({
            "xo": np.ascontiguousarray(xo), "xp": np.ascontiguousarray(xpv),
            "xs": np.ascontiguousarray(x_sample[4 * c:4 * c + 4, 0]), "w_in": f(w_in)[0],
            "gmix": gm, "ropec": cF, "ropes": sF, "ident": ident, "mask4": m4, "pbias": pb,
        })
        in_maps[-1].update(s5)
        in_maps[-1].update(wts)
        in_maps[-1].update(smp)
        pt4 = np.asarray(page_table)[4 * c:4 * c + 4].astype(np.int32)
        in_maps[-1]["ptrep"] = np.ascontiguousarray(np.broadcast_to(pt4.reshape(1, 256), (128, 256)))
        in_maps[-1]["ptE"] = np.ascontiguousarray(np.broadcast_to(pt4[:, 0::2].reshape(1, 128), (8, 128)))
        in_maps[-1]["ptO"] = np.ascontiguousarray(np.broadcast_to(pt4[:, 1::2].reshape(1, 128), (8, 128)))
        for nm, st in (("h0r", state_ssm_re), ("h0i", state_ssm_im)):
            t = f(st)[0, 4 * c:4 * c + 4].reshape(4, 16, 2, 64)
            in_maps[-1][nm] = np.ascontiguousarray(t.transpose(2, 3, 1, 0).reshape(128, 64))
    res = run_bass_kernel_spmd(nc, in_maps, core_ids=list(range(8)))
    R = res.results
    if os.environ.get('KDBG'):
        cc = int(os.environ.get("KCORE", "1"))
        _LAST['R'] = {k: np.asarray(v).view(np.uint16) if 'bfloat' in str(np.asarray(v).dtype) else np.asarray(v)
                      for k, v in R[cc].items()}
    new_k = np.zeros((1, 4, 4096, 8, 64), np.float32)
    new_v = np.zeros((1, 4, 4096, 8, 64), np.float32)
    new_ks = np.zeros((1, 32, 1, 8, 64), np.float32)
    new_vs = np.zeros((1, 32, 1, 8, 64), np.float32)
    for c in range(8):
        b, p = c // 2, c % 2
        new_k[0, b, 2048 * p:2048 * (p + 1)] = np.asarray(R[c]["ko"]).reshape(2048, 8, 64)
        new_v[0, b, 2048 * p:2048 * (p + 1)] = np.asarray(R[c]["vo"]).reshape(2048, 8, 64)
        new_ks[0, 4 * c:4 * c + 4, 0] = np.asarray(R[c]["kso"]).reshape(4, 8, 64)
        new_vs[0, 4 * c:4 * c + 4, 0] = np.asarray(R[c]["vso"]).reshape(4, 8, 64)
    y_prompt = np.zeros((4, 4096, 1024), np.float32)
    y_sample = np.zeros((32, 1, 1024), np.float32)
    for c in range(8):
        b, p = c // 2, c % 2
        y_prompt[b, 2048 * p:2048 * (p + 1)] = np.asarray(R[c]["yo"])
        y_sample[4 * c:4 * c + 4, 0] = np.asarray(R[c]["yso"])
    hre = np.zeros((1, 4, 32, 64), np.float32)
    him = np.zeros((1, 4, 32, 64), np.float32)
    for b in range(4):
        c = 2 * b + 1
        hre[0, b] = np.asarray(R[c]["hfr"]).reshape(2, 64, 16).transpose(2, 0, 1).reshape(32, 64)
        him[0, b] = np.asarray(R[c]["hfi"]).reshape(2, 64, 16).transpose(2, 0, 1).reshape(32, 64)
    hsr_o = np.zeros((1, 32, 32, 64), np.float32)
    hsi_o = np.zeros((1, 32, 32, 64), np.float32)
    for c in range(8):
        hsr_o[0, 4 * c:4 * c + 4] = np.asarray(R[c]["hsr"]).reshape(2, 64, 16, 4).transpose(3, 2, 0, 1).reshape(4, 32, 64)
        hsi_o[0, 4 * c:4 * c + 4] = np.asarray(R[c]["hsi"]).reshape(2, 64, 16, 4).transpose(3, 2, 0, 1).reshape(4, 32, 64)
    return (y_prompt, y_sample, new_k, new_v, hre, him, new_ks, new_vs, hsr_o, hsi_o)
```

```python
import numpy as np
import concourse.bass as bass
import concourse.mybir as mybir
from concourse.bass_utils import run_bass_kernel_spmd

F32 = mybir.dt.float32
BF16 = mybir.dt.bfloat16
I32 = mybir.dt.int32
AF = mybir.ActivationFunctionType
ALU = mybir.AluOpType
AX = mybir.AxisListType

D = 1024
NH = 8
HD = 64
NT_OWN = 16
NT_PREV = 16
EPS = 1e-6
BIG = 30000.0

ENGS = ('pe', 'act', 'dve', 'pool', 'sp')
BLK = {'pe': 'tensor', 'act': 'scalar', 'dve': 'vector', 'pool': 'gpsimd', 'sp': 'sync'}


class Op:
    __slots__ = ('eng', 'fn', 'tok', 'waits', 'dma')

    def __init__(self, eng, fn, tok, waits, dma):
        self.eng, self.fn, self.tok, self.waits, self.dma = eng, fn, tok, waits, dma


class Prog:
    def __init__(self, nc, sems):
        self.nc = nc
        self.sems = sems
        self.ops = []
        self.last_w = {}
        self.readers = {}
        self.cnt = {}
        self.waited = {e: {} for e in ENGS}
        self.emitted = 0
        self.excl = set()

    def add(self, eng, fn, reads=(), writes=(), dma=None):
        deps = set()
        for r in reads:
            if r in self.last_w:
                deps.add(self.last_w[r])
        for w in writes:
            if w in self.last_w:
                deps.add(self.last_w[w])
            deps |= self.readers.get(w, set())
        key = ('d:' + dma) if dma else eng
        inc = 16 if dma else 1
        self.cnt[key] = self.cnt.get(key, 0) + inc
        tok = (key, self.cnt[key])
        waits = {}
        for d in deps:
            dk, dv = self.ops[d].tok
            if dk == 'pe' and eng == 'pe' and not dma:
                continue
            waits[dk] = max(waits.get(dk, 0), dv)
        final = []
        for k, v in waits.items():
            if self.waited[eng].get(k, 0) < v:
                final.append((k, v))
                self.waited[eng][k] = v
        oid = len(self.ops)
        self.ops.append(Op(eng, fn, tok, final, dma))
        for r in reads:
            self.readers.setdefault(r, set()).add(oid)
        for w in writes:
            self.last_w[w] = oid
            self.readers[w] = set()
        return oid

    def barrier(self):
        for e in ENGS:
            final = []
            for k, v in self.cnt.items():
                if k in self.excl:
                    continue
                if self.waited[e].get(k, 0) < v:
                    final.append((k, v))
                    self.waited[e][k] = v
            if final:
                self.ops.append(Op(e, None, None, final, None))

    def emit(self):
        ops = self.ops[self.emitted:]
        self.emitted = len(self.ops)
        with self.nc.Block() as block:
            for eng in ENGS:
                my = [o for o in ops if o.eng == eng]
                if not my:
                    continue

                def body(e, my=my):
                    for o in my:
                        for (k, v) in o.waits:
                            e.wait_ge(self.sems[k], v)
                        if o.fn is None:
                            continue
                        ins = o.fn(e)
                        ins.then_inc(self.sems[o.tok[0]], 16 if o.dma else 1)

                getattr(block, BLK[eng])(body)


def bcast(ap, shape):
    return ap.broadcast_to(shape)


from contextlib import ExitStack
import os


def build():
    nc = bass.Bass("TRN2", target_bir_lowering=False)
    DBG = os.environ.get('KDBG')

    def din(name, shape, dt=F32):
        return nc.dram_tensor(name, list(shape), dt, kind="ExternalInput").ap()

    def dout(name, shape, dt=F32):
        return nc.dram_tensor(name, list(shape), dt, kind="ExternalOutput").ap()

    xo = din("xo", [2048, D])
    xp = din("xp", [2048, D])
    xs = din("xs", [4, D])
    w_in = din("w_in", [D, 4096])
    gmix = din("gmix", [128, 8])
    ropec = din("ropec", [128, 33, 32])
    ropes = din("ropes", [128, 33, 64])
    ident_d = din("ident", [128, 128])
    mask4_d = din("mask4", [128, 4, 512])
    pbias_d = din("pbias", [128, 8, 16])

    arB = din("arB", [128, 256]); aiB = din("aiB", [128, 256]); ldtB = din("ldtB", [128, 256])
    brB = din("brB", [128, 256]); biB = din("biB", [128, 256])
    arD = din("arD", [128, 16]); aiD = din("aiD", [128, 16]); ldtD = din("ldtD", [128, 16])
    crD = din("crD", [128, 256]); ciD = din("ciD", [128, 256])
    kv0_d = din("kv0", [128, 16]); kv1_d = din("kv1", [128, 16]); kvj_d = din("kvj", [128, 256])
    pmB_d = din("pmB", [128, 2]); pmD_d = din("pmD", [128, 2])
    w_attn_d = din("w_attn", [512, 1024]); w_glu_d = din("w_glu", [512, 512]); w_ssm_d = din("w_ssm", [512, 1024])
    w_out_d = din("w_out", [1024, 1024]); w_fin_d = din("w_ffn_in", [1024, 5632]); w_fout_d = din("w_ffn_out", [2816, 1024])
    bgl_d = din("bgl", [128, 4]); dvec_d = din("dvec", [128, 4]); gffn_d = din("gffn", [128, 8]); gfin_d = din("gfin", [128, 1024])
    arA = din("arA", [128, 32]); aiA = din("aiA", [128, 32]); ldtA = din("ldtA", [128, 32])
    b1A = din("b1A", [128, 512]); b2A = din("b2A", [128, 512]); cA = din("cA", [128, 512])
    sgnA_d = din("sgnA", [128, 1]); bmask_d = din("bmask", [128, 128])
    x1_scr = nc.dram_tensor("x1_scr", [2052, 1024], F32, kind="Internal").ap()
    yo = dout("yo", [2048, 1024]); yso = dout("yso", [4, 1024])
    ckh = din("cache_k", [2560 * 128 * 8, 64]); cvh = din("cache_v", [2560 * 128 * 8, 64])
    ptrep_d = din("ptrep", [128, 256], I32); ptE_d = din("ptE", [8, 128], I32); ptO_d = din("ptO", [8, 128], I32)
    piota_d = din("piota", [128, 1]); hb_d = din("hb", [128, 8]); zsel_d = din("zsel", [128, 255]); rsel_d = din("rsel", [4, 128])
    eye8_d = din("eye8", [8, 8]); esel_d = din("esel", [4, 4])
    hfr = dout("hfr", [128, 16]); hfi = dout("hfi", [128, 16])
    hsr = dout("hsr", [128, 64]); hsi = dout("hsi", [128, 64])
    h0r_d = din("h0r", [128, 64]); h0i_d = din("h0i", [128, 64])
    ko = dout("ko", [2048, 512])
    vo = dout("vo", [2048, 512])
    kso = dout("kso", [4, 512])
    vso = dout("vso", [4, 512])
    ut_scr = nc.dram_tensor("ut_scr", [128, 4, 4096], BF16, kind="Internal").ap()
    ot_scr = nc.dram_tensor("ot_scr", [128, 4, 2048], BF16, kind="Internal").ap()

    top = ExitStack()
    with top:
        def mk(es):
            def sb(name, shape, dt):
                return es.enter_context(nc.sbuf_tensor("s_" + name, list(shape), dt))

            def ps(name, shape, dt):
                return es.enter_context(nc.psum_tensor("p_" + name, list(shape), dt))
            return sb, ps

        sb, ps = mk(top)
        semnames = ['pe', 'act', 'dve', 'pool', 'd:x0', 'd:x1', 'd:w', 'd:c', 'd:ko0', 'd:ko1', 'd:vo0', 'd:vo1',
                    'd:us0', 'd:us1', 'd:dbg', 'd:og0', 'd:og1', 'd:gk', 'd:wf0', 'd:wf1', 'd:wf2', 'd:wfo'] + [f'd:pg{i}' for i in range(8)]
        sems = {k: top.enter_context(nc.semaphore(k.replace(':', '_'))) for k in semnames}
        P = Prog(nc, sems)

        def dbg(name, ap, shape, dt, toks):
            d = nc.dram_tensor("dbg_" + name, list(shape), dt, kind="ExternalOutput").ap()
            P.add('sp', lambda e: e.dma_start(out=d, in_=ap), reads=toks, dma='dbg')

        ident = sb("identb", [128, 128], BF16)
        identf = sb("identf", [128, 128], F32)
        qs_f = sb("qs_f", [4, 8, 64], F32)
        ks_f = sb("ks_f", [4, 8, 64], F32)
        vs_f = sb("vs_f", [4, 8, 64], F32)
        usT = sb("usT", [128, 4, 4], BF16)
        OTs = sb("OTs", [128, 4, 4], BF16)

        P.add('sp', lambda e: e.dma_start(out=identf[:], in_=ident_d[:, :]), writes=['identf'], dma='c')
        P.barrier()
        P.add('dve', lambda e: e.tensor_copy(out=ident[:], in_=identf[:]), reads=['identf'], writes=['ident'])

        s12 = ExitStack()
        with s12:
            sb12, _ = mk(s12)
            KaT = sb12("KaT", [128, 8, 4096], BF16)
            Vaug = sb12("Vaug", [128, 32, 8, 65], BF16)
            Qaug = sb12("Qaug", [128, 16, 8, 80], BF16)

            s1 = ExitStack()
            with s1:
                sb1, ps1 = mk(s1)
                w1 = sb1("w1", [128, 8, 2048], BF16)
                gmx = sb1("gmx", [128, 8], F32)
                rc = sb1("rc", [128, 33, 32], F32)
                rs = sb1("rs", [128, 33, 64], F32)
                xt = [sb1(f"xt{i}", [128, D], F32) for i in range(2)]
                sqj = sb1("sqj", [128, D], BF16)
                ssq = [sb1(f"ssq{i}", [128, 1], F32) for i in range(2)]
                rstd = [sb1(f"rstd{i}", [128, 1], F32) for i in range(2)]
                xn = [sb1(f"xn{i}", [128, D], BF16) for i in range(2)]
                hT = [sb1(f"hT{i}", [128, 8, 128], BF16) for i in range(2)]
                kf = [sb1(f"kf{i}", [128, 8, 64], F32) for i in range(2)]
                vf = [sb1(f"vf{i}", [128, 8, 64], F32) for i in range(2)]
                ust = [sb1(f"ust{i}", [128, 4, 128], BF16) for i in range(2)]
                t1 = sb1("t1", [128, 8, 64], F32)
                t2 = sb1("t2", [128, 8, 64], F32)
                kb = sb1("kb", [128, 8, 80], BF16)
                psT = ps1("psT", [128, 8, 128], BF16)
                psQ = ps1("psQ", [128, 512], F32)
                psK = ps1("psK", [128, 512], F32)
                psV = ps1("psV", [128, 512], F32)
                psU = ps1("psU", [128, 4, 128], F32)
                psKT = ps1("psKT", [128, 8, 128], BF16)

                P.add('sp', lambda e: e.dma_start(out=gmx[:], in_=gmix[:, :]), writes=['gmx'], dma='c')
                P.add('sp', lambda e: e.dma_start(out=rc[:], in_=ropec[:, :, :]), writes=['rc'], dma='c')
                P.add('sp', lambda e: e.dma_start(out=rs[:], in_=ropes[:, :, :]), writes=['rs'], dma='c')
                for kc in range(8):
                    for half in range(2):
                        P.add('pool', lambda e, kc=kc, half=half: e.dma_start(
                            out=w1[:, kc, half * 1024:(half + 1) * 1024],
                            in_=w_in[kc * 128:(kc + 1) * 128, half * 1024:(half + 1) * 1024]),
                            writes=[('w1', kc, half)], dma='w')
                P.add('pool', lambda e: e.memset(Vaug[:, :, :, 64:65], 1.0), writes=['Vones'])
                P.add('pool', lambda e: e.memset(Qaug[:, :, :, 0:16], 0.0), writes=['Qz'])
                P.barrier()

                def rotary(src_ps, nt, t, out_ap, ta, tb, tag):
                    x3 = src_ps[0:nt, :].rearrange("p (h d) -> p h d", h=8)
                    x4 = src_ps[0:nt, :].rearrange("p (h two d) -> p h two d", h=8, two=2)
                    cT = bcast(rc[0:nt, t:t + 1, :].unsqueeze(2), [nt, 8, 2, 32])
                    P.add('dve', lambda e: e.tensor_tensor(out=ta[0:nt].rearrange("p h (two d) -> p h two d", two=2),
                                                           in0=x4, in1=cT, op=ALU.mult),
                          reads=[tag, 'rc'], writes=['rot_ta'])
                    P.add('dve', lambda e: e.tensor_tensor(out=tb[0:nt, :, 0:32], in0=x3[:, :, 32:64],
                                                           in1=bcast(rs[0:nt, t:t + 1, 0:32], [nt, 8, 32]), op=ALU.mult),
                          reads=[tag, 'rs'], writes=['rot_tb0'])
                    P.add('dve', lambda e: e.tensor_tensor(out=tb[0:nt, :, 32:64], in0=x3[:, :, 0:32],
                                                           in1=bcast(rs[0:nt, t:t + 1, 32:64], [nt, 8, 32]), op=ALU.mult),
                          reads=[tag, 'rs'], writes=['rot_tb1'])
                    return ['rot_ta', 'rot_tb0', 'rot_tb1']

                prefetched = {}

                def p1_load(t):
                    nt_ = 4 if t == 32 else 128
                    if t == 32:
                        src_, r0_ = xs, 0
                    elif t >= 16:
                        src_, r0_ = xo, (t - 16) * 128
                    else:
                        src_, r0_ = xp, t * 128
                    s_ = t % 2
                    P.add('sp', lambda e: e.dma_start(out=xt[s_][0:nt_, :], in_=src_[r0_:r0_ + nt_, :]),
                          writes=[('xt', s_)], dma=f'x{s_}')
                    prefetched[t] = True

                def phase1_tile(t):
                    sample = (t == 32)
                    own = 16 <= t < 32
                    nt = 4 if sample else 128
                    if sample:
                        src, r0 = xs, 0
                    elif own:
                        src, r0 = xo, (t - 16) * 128
                    else:
                        src, r0 = xp, t * 128
                    s = t % 2
                    if not prefetched.get(t):
                        p1_load(t)
                    if t + 1 in tile_set:
                        p1_load(t + 1)
                    P.add('act', lambda e: e.activation(out=sqj[0:nt, :], in_=xt[s][0:nt, :], func=AF.Square,
                                                        accum_out=ssq[s][0:nt, :]),
                          reads=[('xt', s)], writes=['sqj', ('ssq', s)])
                    P.add('dve', lambda e: e.tensor_scalar(out=rstd[s][0:nt, :], in0=ssq[s][0:nt, :], scalar1=1.0 / D,
                                                           scalar2=EPS, op0=ALU.mult, op1=ALU.add),
                          reads=[('ssq', s)], writes=[('rstd', s)])
                    P.add('act', lambda e: e.activation(out=rstd[s][0:nt, :], in_=rstd[s][0:nt, :], func=AF.Sqrt),
                          reads=[('rstd', s)], writes=[('rstd', s)])
                    P.add('dve', lambda e: e.reciprocal(out=rstd[s][0:nt, :], in_=rstd[s][0:nt, :]),
                          reads=[('rstd', s)], writes=[('rstd', s)])
                    P.add('dve', lambda e: e.tensor_scalar(out=xn[s][0:nt, :], in0=xt[s][0:nt, :],
                                                           scalar1=rstd[s][0:nt, 0:1], scalar2=None, op0=ALU.mult),
                          reads=[('xt', s), ('rstd', s)], writes=[('xn', s)])
                    for kc in range(8):
                        P.add('pe', lambda e, kc=kc: e.transpose(out=psT[:, kc, 0:nt],
                                                                 in_=xn[s][0:nt, kc * 128:(kc + 1) * 128],
                                                                 identity=ident[0:nt, 0:nt]),
                              reads=[('xn', s), 'ident'], writes=['psT'])
                    P.add('dve', lambda e: e.tensor_tensor(out=hT[s][:, :, 0:nt], in0=psT[:, :, 0:nt],
                                                           in1=bcast(gmx[:].unsqueeze(2), [128, 8, nt]), op=ALU.mult),
                          reads=['psT', 'gmx'], writes=[('hT', s)])
                    projs = (('K', psK, 512), ('V', psV, 1024)) + ((('Q', psQ, 0),) if (own or sample) else ())
                    for name, pst, c0 in projs:
                        for kc in range(8):
                            P.add('pe', lambda e, kc=kc, pst=pst, c0=c0: e.matmul(
                                pst[0:nt, :], lhsT=hT[s][:, kc, 0:nt], rhs=w1[:, kc, c0:c0 + 512],
                                start=(kc == 0), stop=(kc == 7)),
                                reads=[('hT', s), ('w1', kc, c0 // 1024)], writes=['ps' + name])
                    for c in range(4):
                        for kc in range(8):
                            P.add('pe', lambda e, kc=kc, c=c: e.matmul(
                                psU[:, c, 0:nt], lhsT=w1[:, kc, 1536 + c * 128:1536 + (c + 1) * 128],
                                rhs=hT[s][:, kc, 0:nt], start=(kc == 0), stop=(kc == 7)),
                                reads=[('hT', s), ('w1', kc, 1)], writes=['psU'])
                    if sample:
                        P.add('act', lambda e: e.activation(out=usT[:], in_=psU[:, :, 0:4], func=AF.Copy),
                              reads=['psU'], writes=['usT'])
                    else:
                        P.add('act', lambda e: e.activation(out=ust[s][:], in_=psU[:], func=AF.Copy),
                              reads=['psU'], writes=[('ust', s)])
                        P.add('sp', lambda e: e.dma_start(out=ut_scr[:, :, t * 128:(t + 1) * 128], in_=ust[s][:]),
                              reads=[('ust', s)], writes=[('utscr', t)], dma=f'us{s}')
                    kout = ks_f if sample else kf[s]
                    toks = rotary(psK, nt, t, None, t1, t2, 'psK')
                    P.add('dve', lambda e: e.tensor_tensor(out=kout[0:nt], in0=t1[0:nt], in1=t2[0:nt], op=ALU.add),
                          reads=toks, writes=[('kf', s)])
                    if not sample:
                        P.add('pool', lambda e: e.memset(kb[:, :, 0:16], 0.0), writes=['kb'])
                        n_blk = t // 2
                        P.add('pool', lambda e: e.memset(kb[:, :, n_blk:n_blk + 1], 1.0), writes=['kb'])
                        P.add('act', lambda e: e.activation(out=kb[:, :, 16:80], in_=kf[s][:], func=AF.Copy),
                              reads=[('kf', s)], writes=['kb'])
                        for h in range(8):
                            P.add('pe', lambda e, h=h: e.transpose(out=psKT[0:80, h, :], in_=kb[:, h, :], identity=ident[:]),
                                  reads=['kb', 'ident'], writes=['psKT'])
                        P.add('act', lambda e: e.activation(out=KaT[0:80, :, t * 128:(t + 1) * 128], in_=psKT[0:80],
                                                            func=AF.Copy),
                              reads=['psKT'], writes=[('KaT', t)])
                    vout = vs_f if sample else vf[s]
                    P.add('act', lambda e: e.activation(out=vout[0:nt], in_=psV[0:nt, :].rearrange("p (h d) -> p h d", h=8),
                                                        func=AF.Copy),
                          reads=['psV'], writes=[('vf', s)])
                    if not sample:
                        P.add('pool', lambda e: e.tensor_copy(out=Vaug[:, t, :, 0:64], in_=vf[s][:]),
                              reads=[('vf', s)], writes=[('Vaug', t)])
                    if own or sample:
                        toksq = rotary(psQ, nt, t, None, t1, t2, 'psQ')
                        if sample:
                            P.add('dve', lambda e: e.tensor_tensor(out=qs_f[:], in0=t1[0:4], in1=t2[0:4], op=ALU.add),
                                  reads=toksq, writes=['qs_f'])
                        else:
                            ti = t - 16
                            P.add('dve', lambda e: e.tensor_tensor(out=Qaug[:, ti, :, 16:80], in0=t1[:], in1=t2[:], op=ALU.add),
                                  reads=toksq, writes=[('Qaug', ti)])
                    if own:
                        P.add('sp', lambda e: e.dma_start(out=ko[r0:r0 + 128, :], in_=kf[s][:].rearrange("p h d -> p (h d)")),
                              reads=[('kf', s)], dma=f'ko{s}')
                        P.add('sp', lambda e: e.dma_start(out=vo[r0:r0 + 128, :], in_=vf[s][:].rearrange("p h d -> p (h d)")),
                              reads=[('vf', s)], dma=f'vo{s}')
                    if sample:
                        P.add('sp', lambda e: e.dma_start(out=kso[:, :], in_=ks_f[:].rearrange("p h d -> p (h d)")),
                              reads=[('kf', s)], dma=f'ko{s}')
                        P.add('sp', lambda e: e.dma_start(out=vso[:, :], in_=vs_f[:].rearrange("p h d -> p (h d)")),
                              reads=[('vf', s)], dma=f'vo{s}')

                tiles = [int(v) for v in DBG.split(',')] if (DBG and DBG != 'all') else range(33)
                tile_set = set(tiles)
                for t in tiles:
                    phase1_tile(t)
                P.barrier()
                P.emit()

            s2 = ExitStack()
            with s2:
                sb2, ps2 = mk(s2)
                mask4f = sb2("mask4f", [128, 4, 512], F32)
                mask4 = sb2("mask4b", [128, 4, 512], BF16)
                pbias = sb2("pbias_sb", [128, 8, 16], F32)
                kms = sb2("kms", [128, 8, 16], F32)
                kmT = sb2("kmT", [128, 8, 16], BF16)
                QaT = [sb2(f"QaT{i}", [128, 8, 512], BF16) for i in range(2)]
                s_m = sb2("s_m", [128, 8, 16], F32)
                top8 = sb2("top8", [128, 8, 8], F32)
                sel = sb2("sel", [128, 8, 16], F32)
                val = sb2("val", [128, 8, 16], F32)
                PT = [sb2(f"PT{i}", [128, 512], BF16) for i in range(3)]
                osb = [sb2(f"osb{i}", [128, 512], F32) for i in range(2)]
                rsum = sb2("rsum", [128, 512], F32)
                onesf = sb2("onesf", [128, 64], F32)
                ostg = [sb2(f"ostg{i}", [128, 512], BF16) for i in range(2)]
                psA = [ps2(f"psA{i}", [128, 512], F32) for i in range(2)]
                psO = [ps2(f"psO{i}", [128, 512], F32) for i in range(2)]
                psQT = ps2("psQT", [128, 8, 128], BF16)
                psS = ps2("psS", [128, 8, 16], F32)
                psBT = ps2("psBT", [16, 8, 128], BF16)
                psR = ps2("psR", [128, 512], F32)

                P.add('sp', lambda e: e.dma_start(out=mask4f[:], in_=mask4_d[:, :, :]), writes=['mask4f'], dma='c')
                P.add('sp', lambda e: e.dma_start(out=pbias[:], in_=pbias_d[:, :, :]), writes=['pbias'], dma='c')
                P.add('pool', lambda e: e.memset(onesf[:], 1.0), writes=['onesf'])
                P.barrier()
                P.add('pool', lambda e: e.tensor_copy(out=mask4[:], in_=mask4f[:]), reads=['mask4f'], writes=['mask4'])
                P.add('dve', lambda e: e.tensor_reduce(out=kms[0:80], in_=KaT[0:80].rearrange("p h (n k) -> p h n k", k=256),
                                                       axis=AX.X, op=ALU.add),
                      reads=[('KaT', t) for t in range(32)], writes=['kms'])
                P.add('dve', lambda e: e.tensor_copy(out=kmT[0:80], in_=kms[0:80]), reads=['kms'], writes=['kmT'])

                def qtok(ti):
                    return ('QaT', (ti // 4) % 2, ti % 4)

                def prep_qtile(ti, qa):
                    a = ti % 4
                    m = ti // 2
                    for h in range(8):
                        P.add('pe', lambda e, h=h: e.transpose(out=psQT[0:80, h, :], in_=Qaug[:, ti, h, :], identity=ident[:]),
                              reads=[('Qaug', ti), 'ident'], writes=['psQT'])
                    P.add('act', lambda e: e.activation(out=qa[0:80, :, a * 128:(a + 1) * 128], in_=psQT[0:80], func=AF.Copy),
                          reads=['psQT'], writes=[qtok(ti)])
                    for h in range(8):
                        P.add('pe', lambda e, h=h: e.matmul(psS[:, h, :], lhsT=qa[0:80, h, a * 128:(a + 1) * 128],
                                                            rhs=kmT[0:80, h, :], start=True, stop=True),
                              reads=[qtok(ti), 'kmT'], writes=['psS'])
                    P.add('dve', lambda e: e.tensor_tensor(out=s_m[:], in0=psS[:], in1=bcast(pbias[:, m:m + 1, :], [128, 8, 16]),
                                                           op=ALU.add),
                          reads=['psS', 'pbias'], writes=['s_m'])
                    for h in range(8):
                        P.add('dve', lambda e, h=h: e.max(out=top8[:, h, :], in_=s_m[:, h, :]),
                              reads=['s_m'], writes=[('top8', h)])
                    for h in range(8):
                        P.add('dve', lambda e, h=h: e.tensor_scalar(out=sel[:, h, :], in0=s_m[:, h, :], scalar1=top8[:, h, 2:3],
                                                                    scalar2=None, op0=ALU.is_ge),
                              reads=['s_m', ('top8', h)], writes=[('sel', h)])
                    P.add('dve', lambda e: e.tensor_scalar(out=val[:], in0=s_m[:], scalar1=-0.5 * BIG, scalar2=None, op0=ALU.is_gt),
                          reads=['s_m'], writes=['val'])
                    P.add('dve', lambda e: e.tensor_tensor(out=sel[:], in0=sel[:], in1=val[:], op=ALU.mult),
                          reads=[('sel', h) for h in range(8)] + ['val'], writes=[('sel', h) for h in range(8)])
                    P.add('dve', lambda e: e.tensor_scalar(out=Qaug[:, ti, :, 0:16], in0=sel[:], scalar1=-1.0, scalar2=BIG,
                                                           op0=ALU.add, op1=ALU.mult),
                          reads=[('sel', h) for h in range(8)], writes=[('Qb', ti)])
                    P.add('dve', lambda e: e.memset(Qaug[:, ti, :, 8 + m:9 + m], 0.0), reads=[('Qb', ti)], writes=[('Qb', ti)])
                    for h in range(8):
                        P.add('pe', lambda e, h=h: e.transpose(out=psBT[0:16, h, :], in_=Qaug[:, ti, h, 0:16], identity=ident[:]),
                              reads=[('Qb', ti), 'ident'], writes=['psBT'])
                    P.add('act', lambda e: e.activation(out=qa[0:16, :, a * 128:(a + 1) * 128], in_=psBT[0:16], func=AF.Copy),
                          reads=['psBT'], writes=[qtok(ti)])

                qgs = [int(v) for v in os.environ.get('KQG', '0,1,2,3').split(',')]
                step = 0
                def attn_group(qg):
                    nonlocal step
                    qa = QaT[qg % 2]
                    for ti in range(4 * qg, 4 * qg + 4):
                        prep_qtile(ti, qa)
                    qtoks = [qtok(ti) for ti in range(4 * qg, 4 * qg + 4)]
                    nkt = 20 + 4 * qg
                    for h in range(8):
                        po = psO[h % 2]
                        ob = osb[h % 2]

                        def qk(kt, h=h):
                            pa = psA[kt % 2]
                            P.add('pe', lambda e: e.matmul(pa[:], lhsT=KaT[0:80, h, kt * 128:(kt + 1) * 128], rhs=qa[0:80, h, :],
                                                           start=True, stop=True),
                                  reads=qtoks + [('KaT', kt)], writes=[('psA', kt % 2)])

                        def rest(kt, h=h, po=po):
                            nonlocal step
                            pa = psA[kt % 2]
                            pt = PT[step % 3]
                            pk = ('PT', step % 3)
                            step += 1
                            P.add('act', lambda e: e.activation(out=pt[:], in_=pa[:], func=AF.Exp, scale=0.125),
                                  reads=[('psA', kt % 2)], writes=[pk])
                            j = kt - (nkt - 4)
                            if j >= 0:
                                P.add('pool', lambda e: e.tensor_tensor(out=pt[:], in0=pt[:], in1=mask4[:, j, :], op=ALU.mult),
                                      reads=[pk, 'mask4'], writes=[pk])
                            P.add('pe', lambda e: e.matmul(po[0:65, :], lhsT=Vaug[:, kt, h, 0:65], rhs=pt[:],
                                                           start=(kt == 0), stop=(kt == nkt - 1)),
                                  reads=[pk, ('Vaug', kt), 'Vones'], writes=[('psO', h % 2)])

                        qk(0)
                        for kt in range(nkt):
                            if kt + 1 < nkt:
                                qk(kt + 1)
                            rest(kt)
                        P.add('act', lambda e, po=po, ob=ob: e.activation(out=ob[0:64, :], in_=po[0:64, :], func=AF.Copy),
                              reads=[('psO', h % 2)], writes=[('osb', h % 2)])
                        P.add('dve', lambda e, po=po: e.reciprocal(out=rsum[64:65, :], in_=po[64:65, :]),
                              reads=[('psO', h % 2)], writes=['rsum'])
                        P.add('pe', lambda e: e.matmul(psR[0:64, :], lhsT=onesf[64:65, 0:64], rhs=rsum[64:65, :], start=True, stop=True),
                              reads=['rsum', 'onesf'], writes=['psR'])
                        pofs = (h % 2) * 64
                        og = ostg[h % 2]
                        P.add('dve', lambda e, ob=ob, og=og: e.tensor_tensor(
                            out=og[0:64, :], in0=ob[0:64, :], in1=psR[0:64, :], op=ALU.mult),
                            reads=[('osb', h % 2), 'psR'], writes=[('ostg', h % 2)])
                        P.add('sp', lambda e, og=og, h=h, pofs=pofs, qg=qg: e.dma_start(
                            out=ot_scr[pofs:pofs + 64, h // 2, qg * 512:(qg + 1) * 512], in_=og[0:64, :]),
                            reads=[('ostg', h % 2)], writes=[('OT', qg, h)], dma=f'og{h % 2}')
                for qg in qgs:
                    attn_group(qg)
                if DBG:
                    P.barrier()
                    dOT = nc.dram_tensor("dbg_OT", [128, 4, 2048], BF16, kind="ExternalOutput").ap()
                    P.add('sp', lambda e: e.dma_start(out=dOT, in_=ot_scr), dma='dbg')
                    dbg('Qaug', Qaug[:], [128, 16, 8, 80], BF16, [('Qb', ti) for ti in range(16)])
                P.barrier()
                P.emit()

        TWO_PI = 6.283185307179586
        PI_LO = 3.1415925
        HALF_PI = 1.5707963267948966

        def D_(fn):
            P.add('dve', fn, reads=['g'], writes=['g'])

        def A_(fn):
            P.add('act', fn, reads=['g'], writes=['g'])

        def rr(x, tf, ti):
            D_(lambda e: e.tensor_scalar(out=tf, in0=x, scalar1=1.0 / TWO_PI, scalar2=None, op0=ALU.mult))
            D_(lambda e: e.tensor_copy(out=ti, in_=tf))
            D_(lambda e: e.tensor_copy(out=tf, in_=ti))
            D_(lambda e: e.scalar_tensor_tensor(out=x, in0=tf, scalar=-TWO_PI, in1=x, op0=ALU.mult, op1=ALU.add))
            D_(lambda e: e.tensor_scalar(out=tf, in0=x, scalar1=PI_LO, scalar2=None, op0=ALU.is_gt))
            D_(lambda e: e.scalar_tensor_tensor(out=x, in0=tf, scalar=-TWO_PI, in1=x, op0=ALU.mult, op1=ALU.add))
            D_(lambda e: e.tensor_scalar(out=tf, in0=x, scalar1=-PI_LO, scalar2=None, op0=ALU.is_lt))
            D_(lambda e: e.scalar_tensor_tensor(out=x, in0=tf, scalar=TWO_PI, in1=x, op0=ALU.mult, op1=ALU.add))
            D_(lambda e: e.tensor_scalar(out=x, in0=x, scalar1=-PI_LO, scalar2=PI_LO, op0=ALU.max, op1=ALU.min))

        def powtab(sbx, tagn, AR, AI, LDT, F, kv, K, coef):
            n = F * K
            al = sbx(tagn + "al", [128, F], F32); th = sbx(tagn + "th", [128, F], F32)
            tfs = sbx(tagn + "tfs", [128, F], F32); tis = sbx(tagn + "tis", [128, F], I32)
            PR = sbx(tagn + "PR", [128, n], F32); PI = sbx(tagn + "PI", [128, n], F32)
            ang = sbx(tagn + "ang", [128, n], F32); tf = sbx(tagn + "tf", [128, n], F32); ti = sbx(tagn + "ti", [128, n], I32)
            A_(lambda e: e.activation(out=tfs[:], in_=LDT, func=AF.Exp))
            D_(lambda e: e.tensor_tensor(out=al[:], in0=AR, in1=tfs[:], op=ALU.mult))
            D_(lambda e: e.tensor_tensor(out=th[:], in0=AI, in1=tfs[:], op=ALU.mult))
            rr(th[:], tfs[:], tis[:])
            v3 = lambda t: t[:].rearrange("q (f k) -> q f k", k=K)
            thb = bcast(th[:].unsqueeze(2), [128, F, K]); alb = bcast(al[:].unsqueeze(2), [128, F, K])
            kvb = bcast(kv.unsqueeze(1), [128, F, K])
            D_(lambda e: e.tensor_tensor(out=v3(ang), in0=thb, in1=kvb, op=ALU.mult))
            rr(ang[:], tf[:], ti[:])
            A_(lambda e: e.activation(out=PI[:], in_=ang[:], func=AF.Sin))
            D_(lambda e: e.tensor_scalar(out=ang[:], in0=ang[:], scalar1=HALF_PI, scalar2=None, op0=ALU.add))
            rr(ang[:], tf[:], ti[:])
            A_(lambda e: e.activation(out=PR[:], in_=ang[:], func=AF.Sin))
            D_(lambda e: e.tensor_tensor(out=v3(tf), in0=alb, in1=kvb, op=ALU.mult))
            A_(lambda e: e.activation(out=tf[:], in_=tf[:], func=AF.Exp))
            D_(lambda e: e.tensor_tensor(out=PR[:], in0=PR[:], in1=tf[:], op=ALU.mult))
            D_(lambda e: e.tensor_tensor(out=PI[:], in0=PI[:], in1=tf[:], op=ALU.mult))
            res = dict(PR=PR, PI=PI, al=al, th=th, tf=tf, ang=ang, ti=ti, tfs=tfs, tis=tis)
            if coef is not None:
                k1 = coef
                cr = sbx(tagn + "cr", [128, F], F32); ci = sbx(tagn + "ci", [128, F], F32)
                nr = sbx(tagn + "nr", [128, F], F32); den = sbx(tagn + "den", [128, F], F32)
                p1r = v3(PR)[:, :, k1]; p1i = v3(PI)[:, :, k1]
                D_(lambda e: e.tensor_scalar(out=nr[:], in0=p1r, scalar1=-1.0, scalar2=None, op0=ALU.add))
                D_(lambda e: e.tensor_tensor(out=den[:], in0=AR, in1=AR, op=ALU.mult))
                D_(lambda e: e.tensor_tensor(out=tfs[:], in0=AI, in1=AI, op=ALU.mult))
                D_(lambda e: e.tensor_tensor(out=den[:], in0=den[:], in1=tfs[:], op=ALU.add))
                D_(lambda e: e.reciprocal(out=den[:], in_=den[:]))
                D_(lambda e: e.tensor_tensor(out=cr[:], in0=nr[:], in1=AR, op=ALU.mult))
                D_(lambda e: e.tensor_tensor(out=tfs[:], in0=p1i, in1=AI, op=ALU.mult))
                D_(lambda e: e.tensor_tensor(out=cr[:], in0=cr[:], in1=tfs[:], op=ALU.add))
                D_(lambda e: e.tensor_tensor(out=cr[:], in0=cr[:], in1=den[:], op=ALU.mult))
                D_(lambda e: e.tensor_tensor(out=ci[:], in0=p1i, in1=AR, op=ALU.mult))
                D_(lambda e: e.tensor_tensor(out=tfs[:], in0=nr[:], in1=AI, op=ALU.mult))
                D_(lambda e: e.tensor_tensor(out=ci[:], in0=ci[:], in1=tfs[:], op=ALU.subtract))
                D_(lambda e: e.tensor_tensor(out=ci[:], in0=ci[:], in1=den[:], op=ALU.mult))
                res['cr'] = cr; res['ci'] = ci
            return res

        def cmul(outr, outi, ar_, ai_, br_, bi_, tmp):
            D_(lambda e: e.tensor_tensor(out=outr, in0=ar_, in1=br_, op=ALU.mult))
            D_(lambda e: e.tensor_tensor(out=tmp, in0=ai_, in1=bi_, op=ALU.mult))
            D_(lambda e: e.tensor_tensor(out=outr, in0=outr, in1=tmp, op=ALU.subtract))
            D_(lambda e: e.tensor_tensor(out=outi, in0=ar_, in1=bi_, op=ALU.mult))
            D_(lambda e: e.tensor_tensor(out=tmp, in0=ai_, in1=br_, op=ALU.mult))
            D_(lambda e: e.tensor_tensor(out=outi, in0=outi, in1=tmp, op=ALU.add))

        s3 = ExitStack()
        with s3:
            sb3, ps3 = mk(s3)
            Yc = sb3("Yc", [128, 4, 16, 128], BF16)
            Ycs = sb3("Ycs", [4, 4, 128], BF16)
            s3a = ExitStack()
            with s3a:
                sb3a, ps3a = mk(s3a)
                Ssr = sb3a("Ssr", [128, 16, 4], F32)
                Ssi = sb3a("Ssi", [128, 16, 4], F32)
                SR = sb3a("SR", [128, 16, 256], F32)
                SI = sb3a("SI", [128, 16, 256], F32)
                kv0 = sb3a("kv0s", [128, 16], F32); kv1 = sb3a("kv1s", [128, 16], F32); kvj = sb3a("kvjs", [128, 256], F32)
                pmB = sb3a("pmBs", [128, 2], F32); pmD = sb3a("pmDs", [128, 2], F32)
                for i_, (dst, srcd) in enumerate(((kv0, kv0_d), (kv1, kv1_d), (kvj, kvj_d), (pmB, pmB_d), (pmD, pmD_d))):
                    P.add('sp', lambda e, dst=dst, srcd=srcd: e.dma_start(out=dst[:], in_=srcd), writes=['g'], dma='c')
                tA = {}
                for nm, srcd in (('ar', arB), ('ai', aiB), ('ldt', ldtB), ('br', brB), ('bi', biB)):
                    tA[nm] = sb3a("gB" + nm, [128, 256], F32)
                    P.add('sp', lambda e, t=tA[nm], srcd=srcd: e.dma_start(out=t[:], in_=srcd), writes=['g'], dma='c')
                sKV = ExitStack()
                with sKV:
                    sbKV, _ = mk(sKV)
                    idxG = sbKV("idxG", [128, 8, 24], I32)
                    Ksel = sbKV("Ksel", [128, 4, 8, 6, 64], BF16); Vsel = sbKV("Vsel", [128, 4, 8, 6, 64], BF16)
                    sS = ExitStack()
                    with sS:
                        sbS, psS_ = mk(sS)
                        F32R = mybir.dt.float32r
                        ck_rows = ckh.rearrange("(r h) d -> r (h d)", h=8)
                        ptrep = sbS("ptrep", [128, 256], I32); idxP = sbS("idxP", [128, 256], I32)
                        piota = sbS("piota", [128, 1], F32); hb = sbS("hb", [128, 8], F32)
                        zsel = sbS("zsel", [128, 255], F32); rsel = sbS("rsel", [4, 128], F32)
                        eye8 = sbS("eye8", [8, 8], F32); esel = sbS("esel", [4, 4], F32)
                        ptE = sbS("ptE", [8, 128], I32); ptO = sbS("ptO", [8, 128], I32)
                        ptEf = sbS("ptEf", [8, 128], F32); ptOf = sbS("ptOf", [8, 128], F32)
                        ones1 = sbS("ones1", [128, 128], F32)
                        pgb = [sbS(f"pgb{i}", [128, 512], F32) for i in range(8)]
                        for dst, srcd in ((ptrep, ptrep_d), (piota, piota_d), (hb, hb_d), (zsel, zsel_d), (rsel, rsel_d), (eye8, eye8_d),
                                          (esel, esel_d), (ptE, ptE_d), (ptO, ptO_d)):
                            P.add('sp', lambda e, dst=dst, srcd=srcd: e.dma_start(out=dst[:], in_=srcd), writes=['g'], dma='c')
                        P.add('pool', lambda e: e.memset(ones1[:], 1.0), writes=['g'])
                        P.barrier()
                        D_(lambda e: e.tensor_scalar(out=idxP[:], in0=ptrep[:], scalar1=128.0, scalar2=piota[:, 0:1], op0=ALU.mult, op1=ALU.add))
                        D_(lambda e: e.tensor_copy(out=ptEf[:], in_=ptE[:]))
                        D_(lambda e: e.tensor_copy(out=ptOf[:], in_=ptO[:]))
                        psKS = psS_("psKS", [128, 512], F32)
                        psQR = psS_("psQR", [128, 512], F32)
                        psM = psS_("psM", [128, 512], F32)
                        for i in range(256):
                            sl = i % 8
                            P.add('pool', lambda e, i=i, sl=sl: e.indirect_dma_start(
                                out=pgb[sl][:], out_offset=None, in_=ck_rows,
                                in_offset=bass.IndirectOffsetOnAxis(ap=idxP[:, i:i + 1], axis=0)),
                                reads=['g'], writes=[('pgb', sl)], dma=f'pg{sl}')
                            m = (i // 64) * 32 + (i % 64) // 2
                            P.add('pe', lambda e, i=i, sl=sl, m=m: e.matmul(psKS[:], lhsT=zsel[:, 127 - m:255 - m], rhs=pgb[sl][:],
                                                                          start=(i == 0), stop=(i == 255)),
                                  reads=[('pgb', sl), 'g'], writes=['psKS'])
                        qrep = sbS("qrep", [128, 512], F32); prodS = sbS("prodS", [128, 512], F32); sbl = sbS("sbl", [128, 8], F32)
                        P.add('pe', lambda e: e.matmul(psQR[:], lhsT=rsel[:], rhs=qs_f[:].rearrange("s h d -> s (h d)"), start=True, stop=True),
                              reads=['qs_f', 'g'], writes=['psQR'])
                        P.add('act', lambda e: e.activation(out=qrep[:], in_=psQR[:], func=AF.Copy), reads=['psQR'], writes=['g'])
                        D_(lambda e: e.tensor_tensor(out=prodS[:], in0=qrep[:], in1=psKS[:], op=ALU.mult))
                        P.add('dve', lambda e: e.tensor_reduce(out=sbl[:], in_=prodS[:].rearrange("q (h d) -> q h d", d=64), axis=AX.X, op=ALU.add),
                              reads=['g', 'psKS'], writes=['g'])
                        P.add('pe', lambda e: e.transpose(out=psM[0:8, 0:128], in_=sbl[:], identity=identf[:]), reads=['g', 'identf'], writes=['psM'])
                        sbT = sbS("sbT", [8, 128], F32); top8s = sbS("top8s", [8, 4, 8], F32)
                        selr = sbS("selr", [8, 3, 128], F32); tmpS = sbS("tmpS", [8, 3, 128], F32)
                        PG = sbS("PG", [8, 2, 12], F32); rhsE = sbS("rhsE", [8, 8, 24], F32)
                        P.add('act', lambda e: e.activation(out=sbT[:], in_=psM[0:8, 0:128], func=AF.Copy), reads=['psM'], writes=['g'])
                        for s_ in range(4):
                            D_(lambda e, s_=s_: e.max(out=top8s[:, s_, :], in_=sbT[:, s_ * 32:(s_ + 1) * 32]))
                        for r_ in range(3):
                            for s_ in range(4):
                                D_(lambda e, s_=s_, r_=r_: e.tensor_scalar(out=selr[:, r_, s_ * 32:(s_ + 1) * 32], in0=sbT[:, s_ * 32:(s_ + 1) * 32],
                                                                           scalar1=top8s[:, s_, r_:r_ + 1], scalar2=None, op0=ALU.is_equal))
                        for e_, ptf in enumerate((ptEf, ptOf)):
                            D_(lambda e, ptf=ptf: e.tensor_tensor(out=tmpS[:], in0=selr[:], in1=bcast(ptf[:].unsqueeze(1), [8, 3, 128]), op=ALU.mult))
                            D_(lambda e, e_=e_: e.tensor_reduce(out=PG[:, e_, :], in_=tmpS[:].rearrange("h r (s n) -> h (r s) n", n=32),
                                                                axis=AX.X, op=ALU.add))
                        D_(lambda e: e.tensor_tensor(out=rhsE[:], in0=bcast(PG[:].rearrange("h e x -> h (e x)").unsqueeze(1), [8, 8, 24]),
                                                     in1=bcast(eye8[:].unsqueeze(2), [8, 8, 24]), op=ALU.mult))
                        P.add('pe', lambda e: e.matmul(psM[:, 0:192], lhsT=ones1[0:8, :], rhs=rhsE[:].rearrange("k h x -> k (h x)"), start=True, stop=True),
                              reads=['g'], writes=['psM'])
                        P.add('dve', lambda e: e.scalar_tensor_tensor(out=idxG[:], in0=psM[:, 0:192].rearrange("q (h x) -> q h x", h=8), scalar=1024.0,
                                                                      in1=bcast(hb[:].unsqueeze(2), [128, 8, 24]), op0=ALU.mult, op1=ALU.add),
                              reads=['psM', 'g'], writes=['g', 'idxG'])
                        P.excl = {'d:gk'}
                        P.barrier()
                        P.emit()
                    for h in range(0, 4):
                        for e_ in range(2):
                            for r_ in range(3):
                                for s_ in range(4):
                                    x = e_ * 12 + r_ * 4 + s_
                                    for srcd, dstt in ((ckh, Ksel), (cvh, Vsel)):
                                        P.add('pool', lambda e, h=h, x=x, s_=s_, j=r_ * 2 + e_, srcd=srcd, dstt=dstt: e.indirect_dma_start(
                                            out=dstt[:, s_, h, j, :], out_offset=None, in_=srcd,
                                            in_offset=bass.IndirectOffsetOnAxis(ap=idxG[:, h, x:x + 1], axis=0)),
                                            reads=['idxG'], writes=[('KVsel', h, x, id(dstt))], dma='gk')
                    sB = ExitStack()
                    with sB:
                        sbB, psB = mk(sB)
                        VBre = sbB("VBre", [128, 4, 16, 128], BF16)
                        VBim = sbB("VBim", [128, 4, 16, 128], BF16)
                        sg = ExitStack()
                        with sg:
                            sbg, _ = mk(sg)
                            _cache = {}

                            def sbh(name, shape, dt):
                                if name not in _cache:
                                    _cache[name] = sbg(name, shape, dt)
                                return _cache[name]
                            for hf in range(2):
                                if True:
                                    fs = slice(hf * 128, (hf + 1) * 128)
                                    r = powtab(sbh, "pBg", tA['ar'][:, fs], tA['ai'][:, fs], tA['ldt'][:, fs], 128, kv0[:], 16, coef=1)
                                    Zr = sbh("ZrBg", [128, 2048], F32); Zi = sbh("ZiBg", [128, 2048], F32)
                                    v3 = lambda t: t[:].rearrange("q (f k) -> q f k", k=16)
                                    crb = bcast(r['cr'][:].unsqueeze(2), [128, 128, 16]); cib = bcast(r['ci'][:].unsqueeze(2), [128, 128, 16])
                                    cmul(v3(Zr), v3(Zi), v3(r['PR']), v3(r['PI']), crb, cib, v3(r['tf']))
                                    brb = bcast(tA['br'][:, fs].unsqueeze(2), [128, 128, 16]); bib = bcast(tA['bi'][:, fs].unsqueeze(2), [128, 128, 16])
                                    Xr, Xi = r['PR'], r['PI']
                                    cmul(v3(Xr), v3(Xi), v3(Zr), v3(Zi), brb, bib, v3(r['tf']))
                                    for gw in range(2):
                                        for Xs, VB in ((Xr, VBre), (Xi, VBim)):
                                            D_(lambda e, gw=gw, Xs=Xs, VB=VB, hf=hf: e.tensor_scalar(
                                                out=VB[:, 2 * hf:2 * hf + 2, :, gw * 64:(gw + 1) * 64],
                                                in0=Xs[:].rearrange("q (c p d) -> q c d p", c=2, p=64, d=16),
                                                scalar1=pmB[:, gw:gw + 1], scalar2=None, op0=ALU.mult))
                            P.barrier()
                            P.emit()
                        for h in range(4, 8):
                            for e_ in range(2):
                                for r_ in range(3):
                                    for s_ in range(4):
                                        x = e_ * 12 + r_ * 4 + s_
                                        for srcd, dstt in ((ckh, Ksel), (cvh, Vsel)):
                                            P.add('pool', lambda e, h=h, x=x, s_=s_, j=r_ * 2 + e_, srcd=srcd, dstt=dstt: e.indirect_dma_start(
                                                out=dstt[:, s_, h, j, :], out_offset=None, in_=srcd,
                                                in_offset=bass.IndirectOffsetOnAxis(ap=idxG[:, h, x:x + 1], axis=0)),
                                                reads=['idxG'], writes=[('KVsel', h, x, id(dstt))], dma='gk')
                        UT = sbB("UT", [128, 4, 4096], BF16)
                        for c in range(4):
                            P.add('sp', lambda e, c=c: e.dma_start(out=UT[:, c, :], in_=ut_scr[:, c, :]),
                                  reads=[('utscr', t) for t in range(32)], writes=['UT'], dma='c')
                        P.barrier()
                        psS2 = [psB(f"psSB{i}", [128, 256], F32) for i in range(4)]
                        for w in range(16):
                            kc, r0 = w // 4, 32 * (w % 4)
                            for ri, (VB, SX) in enumerate(((VBre, SR), (VBim, SI))):
                                pst = psS2[(2 * w + ri) % 4]
                                for dl in range(16):
                                    rhs = UT[r0:r0 + 32, kc, :].rearrange("q (j s) -> q j s", s=16)[:, :, 15 - dl]
                                    P.add('pe', lambda e, pst=pst, VB=VB, dl=dl, rhs=rhs, r0=r0, kc=kc: e.matmul(
                                        pst[:], lhsT=VB[r0:r0 + 32, kc, dl, :], rhs=rhs, start=(dl == 0), stop=(dl == 15),
                                        tile_position=(r0, 0)),
                                        reads=['g', 'UT'], writes=[('psSB', (2 * w + ri) % 4)])
                                P.add('act', lambda e, pst=pst, SX=SX, w=w: e.activation(out=SX[:, w, :], in_=pst[:], func=AF.Copy),
                                      reads=[('psSB', (2 * w + ri) % 4)], writes=[('S', ri, w)])
                        for w in range(16):
                            kc, r0 = w // 4, 32 * (w % 4)
                            for ri, (VB, SX) in enumerate(((VBre, Ssr), (VBim, Ssi))):
                                pst = psS2[(2 * w + ri) % 4]
                                P.add('pe', lambda e, pst=pst, VB=VB, r0=r0, kc=kc: e.matmul(
                                    pst[:, 0:4], lhsT=VB[r0:r0 + 32, kc, 0, :], rhs=usT[r0:r0 + 32, kc, :], start=True, stop=True,
                                    tile_position=(r0, 0)),
                                    reads=['g', 'usT'], writes=[('psSB', (2 * w + ri) % 4)])
                                P.add('act', lambda e, pst=pst, SX=SX, w=w: e.activation(out=SX[:, w, :], in_=pst[:, 0:4], func=AF.Copy),
                                      reads=[('psSB', (2 * w + ri) % 4)], writes=[('Ss', ri)])
                        P.barrier()
                        P.emit()
                    sS5 = ExitStack()
                    with sS5:
                        sbS, psS_ = mk(sS5)
                        esel = sbS("esel5", [4, 4], F32); ones1 = sbS("ones5", [128, 128], F32)
                        psQR = psS_("psQR5", [128, 512], F32); psM = psS_("psM5", [128, 512], F32)
                        P.add('sp', lambda e: e.dma_start(out=esel[:], in_=esel_d), writes=['g'], dma='c')
                        P.add('dve', lambda e: e.memset(ones1[:], 1.0), writes=['g'])
                        P.excl = set()
                        P.barrier()
                        qbc = sbS("qbc", [128, 4, 512], F32); sc = sbS("sc", [128, 4, 8, 6], F32); Wn = sbS("Wn", [128, 4, 512], F32)
                        Wd = sbS("Wd", [128, 4, 512], F32); psum_j = sbS("psumj", [128, 4, 8], F32)
                        big = sbS("bigS", [128, 8, 6, 64], F32)
                        for s_ in range(4):
                            P.add('pe', lambda e, s_=s_: e.matmul(psQR[:], lhsT=bcast(esel[:, s_:s_ + 1], [4, 128]), rhs=qs_f[:].rearrange("s h d -> s (h d)"),
                                                                  start=True, stop=True), reads=['qs_f', 'g'], writes=['psQR'])
                            P.add('act', lambda e, s_=s_: e.activation(out=qbc[:, s_, :], in_=psQR[:], func=AF.Copy), reads=['psQR'], writes=['g'])
                        for s_ in range(4):
                            D_(lambda e, s_=s_: e.tensor_tensor(out=big[:], in0=Ksel[:, s_],
                                                                in1=bcast(qbc[:, s_, :].rearrange("q (h d) -> q h d", d=64).unsqueeze(2), [128, 8, 6, 64]),
                                                                op=ALU.mult))
                            D_(lambda e, s_=s_: e.tensor_reduce(out=sc[:, s_], in_=big[:], axis=AX.X, op=ALU.add))
                        A_(lambda e: e.activation(out=sc[:], in_=sc[:], func=AF.Exp, scale=0.125))
                        D_(lambda e: e.tensor_reduce(out=psum_j[:], in_=sc[:], axis=AX.X, op=ALU.add))
                        for s_ in range(4):
                            D_(lambda e, s_=s_: e.tensor_tensor(out=big[:], in0=Vsel[:, s_], in1=bcast(sc[:, s_].unsqueeze(3), [128, 8, 6, 64]), op=ALU.mult))
                            D_(lambda e, s_=s_: e.tensor_reduce(out=Wn[:, s_, :].rearrange("q (h d) -> q h d", d=64),
                                                                in_=big[:].rearrange("q h j d -> q h d j"), axis=AX.X, op=ALU.add))
                        D_(lambda e: e.tensor_copy(out=Wd[:].rearrange("q s (h d) -> q s h d", d=64), in_=bcast(psum_j[:].unsqueeze(3), [128, 4, 8, 64])))
                        ssf = sbS("ssf", [4, 8], F32); prs = sbS("prs", [4, 8, 64], F32); Wsn = sbS("Wsn", [4, 512], F32); Wsd = sbS("Wsd", [4, 512], F32)
                        D_(lambda e: e.tensor_tensor(out=prs[:], in0=qs_f[:], in1=ks_f[:], op=ALU.mult))
                        D_(lambda e: e.tensor_reduce(out=ssf[:], in_=prs[:], axis=AX.X, op=ALU.add))
                        A_(lambda e: e.activation(out=ssf[:], in_=ssf[:], func=AF.Exp, scale=0.125))
                        D_(lambda e: e.tensor_tensor(out=Wsn[:].rearrange("s (h d) -> s h d", d=64), in0=vs_f[:], in1=bcast(ssf[:].unsqueeze(2), [4, 8, 64]),
                                                     op=ALU.mult))
                        D_(lambda e: e.tensor_copy(out=Wsd[:].rearrange("s (h d) -> s h d", d=64), in_=bcast(ssf[:].unsqueeze(2), [4, 8, 64])))
                        for wi, (Wbig, Wself) in enumerate(((Wn, Wsn), (Wd, Wsd))):
                            for kc in range(4):
                                for s_ in range(4):
                                    col = wi * 16 + kc * 4 + s_
                                    P.add('pe', lambda e, Wbig=Wbig, kc=kc, s_=s_, col=col: e.matmul(
                                        psM[:, col:col + 1], lhsT=Wbig[:, s_, kc * 128:(kc + 1) * 128], rhs=ones1[:, 0:1], start=True, stop=False),
                                        reads=['g'], writes=['psM'])
                                    P.add('pe', lambda e, Wself=Wself, kc=kc, s_=s_, col=col: e.matmul(
                                        psM[:, col:col + 1], lhsT=Wself[:, kc * 128:(kc + 1) * 128], rhs=esel[:, s_:s_ + 1], start=False, stop=True),
                                        reads=['g'], writes=['psM'])
                        rden = sbS("rden", [128, 16], F32)
                        P.add('dve', lambda e: e.reciprocal(out=rden[:], in_=psM[:, 16:32]), reads=['psM'], writes=['g'])
                        P.add('dve', lambda e: e.tensor_tensor(out=OTs[:].rearrange("q k s -> q (k s)"), in0=psM[:, 0:16], in1=rden[:], op=ALU.mult),
                              reads=['psM', 'g'], writes=['OTs'])

                        P.barrier()
                        P.emit()
                sC = ExitStack()
                with sC:
                    sbC, psC = mk(sC)
                    Hre = sbC("Hre", [128, 16, 128], BF16)
                    Him = sbC("Him", [128, 16, 128], BF16)
                    WDre = sbC("WDre", [128, 16, 512], BF16)
                    WDim = sbC("WDim", [128, 16, 512], BF16)
                    cosT = sbC("cosT", [128, 4096], F32); sinT = sbC("sinT", [128, 4096], F32)
                    H0re = sbC("H0re", [128, 16, 4], BF16); H0im = sbC("H0im", [128, 16, 4], BF16)
                    M16 = sbC("M16", [128, 16], F32)
                    tD = {}
                    for nm, srcd, n_ in (('ar', arD, 16), ('ai', aiD, 16), ('ldt', ldtD, 16), ('cr', crD, 256), ('ci', ciD, 256)):
                        tD[nm] = sbC("gD" + nm, [128, n_], F32)
                        P.add('sp', lambda e, t=tD[nm], srcd=srcd: e.dma_start(out=t[:], in_=srcd), writes=['g'], dma='c')
                    P.barrier()
                    sg2 = ExitStack()
                    with sg2:
                        sbg2, _ = mk(sg2)
                        r = powtab(sbg2, "pD", tD['ar'][:], tD['ai'][:], tD['ldt'][:], 16, kv1[:], 16, coef=None)
                        Wr = sbg2("Wr", [128, 4096], F32); Wi = sbg2("Wi", [128, 4096], F32); Wt = sbg2("Wt", [128, 4096], F32)
                        v4 = lambda t: t[:].rearrange("q (w k c) -> q w k c", w=16, k=16, c=16)
                        prb = bcast(r['PR'][:].rearrange("q (w k) -> q w k", k=16).unsqueeze(3), [128, 16, 16, 16])
                        pib = bcast(r['PI'][:].rearrange("q (w k) -> q w k", k=16).unsqueeze(3), [128, 16, 16, 16])
                        crb = bcast(tD['cr'][:].rearrange("q (w c) -> q w c", c=16).unsqueeze(2), [128, 16, 16, 16])
                        cib = bcast(tD['ci'][:].rearrange("q (w c) -> q w c", c=16).unsqueeze(2), [128, 16, 16, 16])
                        cmul(v4(Wr), v4(Wi), crb, cib, prb, pib, v4(Wt))
                        for gw in range(2):
                            D_(lambda e, gw=gw: e.tensor_scalar(
                                out=WDre[:].rearrange("q w (g x) -> q w g x", g=2)[:, :, gw, :],
                                in0=Wr[:].rearrange("q (w x) -> q w x", w=16), scalar1=pmD[:, gw:gw + 1], scalar2=None, op0=ALU.mult))
                            D_(lambda e, gw=gw: e.tensor_scalar(
                                out=WDim[:].rearrange("q w (g x) -> q w g x", g=2)[:, :, gw, :],
                                in0=Wi[:].rearrange("q (w x) -> q w x", w=16), scalar1=pmD[:, gw:gw + 1], scalar2=-1.0,
                                op0=ALU.mult, op1=ALU.mult))
                        h0r = sbg2("h0r_s", [128, 64], F32); h0i = sbg2("h0i_s", [128, 64], F32)
                        hnr = sbg2("hnr", [128, 64], F32); hni = sbg2("hni", [128, 64], F32); htm = sbg2("htm", [128, 64], F32)
                        P.add('sp', lambda e: e.dma_start(out=h0r[:], in_=h0r_d), writes=['g'], dma='c')
                        P.add('sp', lambda e: e.dma_start(out=h0i[:], in_=h0i_d), writes=['g'], dma='c')
                        P.barrier()
                        w3 = lambda t: t[:].rearrange("q (w s) -> q w s", s=4)
                        a1r = bcast(r['PR'][:].rearrange("q (w k) -> q w k", k=16)[:, :, 0:1], [128, 16, 4])
                        a1i = bcast(r['PI'][:].rearrange("q (w k) -> q w k", k=16)[:, :, 0:1], [128, 16, 4])
                        cmul(w3(hnr), w3(hni), a1r, a1i, w3(h0r), w3(h0i), w3(htm))
                        D_(lambda e: e.tensor_tensor(out=w3(hnr), in0=w3(hnr), in1=Ssr[:], op=ALU.add))
                        D_(lambda e: e.tensor_tensor(out=w3(hni), in0=w3(hni), in1=Ssi[:], op=ALU.add))
                        P.add('sp', lambda e: e.dma_start(out=hsr, in_=hnr[:]), reads=['g'], dma='c')
                        P.add('sp', lambda e: e.dma_start(out=hsi, in_=hni[:]), reads=['g'], dma='c')
                        D_(lambda e: e.tensor_copy(out=H0re[:], in_=w3(h0r)))
                        D_(lambda e: e.tensor_copy(out=H0im[:], in_=w3(h0i)))
                        TH = sbg2("TH", [128, 16], F32)
                        D_(lambda e: e.tensor_scalar(out=TH[:], in0=r['th'][:], scalar1=16.0, scalar2=None, op0=ALU.mult))
                        rr(TH[:], r['tfs'][:], r['tis'][:])
                        A16 = sbg2("A16", [128, 16], F32)
                        D_(lambda e: e.tensor_scalar(out=A16[:], in0=r['al'][:], scalar1=16.0, scalar2=None, op0=ALU.mult))
                        A_(lambda e: e.activation(out=M16[:], in_=A16[:], func=AF.Exp))
                        angJ = Wt
                        tfJ = Wr
                        tiJ = sbg2("tiJ", [128, 4096], I32)
                        D_(lambda e: e.tensor_tensor(out=angJ[:].rearrange("q (w j) -> q w j", j=256),
                                                     in0=bcast(TH[:].unsqueeze(2), [128, 16, 256]),
                                                     in1=bcast(kvj[:].unsqueeze(1), [128, 16, 256]), op=ALU.mult))
                        rr(angJ[:], tfJ[:], tiJ[:])
                        A_(lambda e: e.activation(out=sinT[:], in_=angJ[:], func=AF.Sin))
                        D_(lambda e: e.tensor_scalar(out=angJ[:], in0=angJ[:], scalar1=HALF_PI, scalar2=None, op0=ALU.add))
                        rr(angJ[:], tfJ[:], tiJ[:])
                        A_(lambda e: e.activation(out=cosT[:], in_=angJ[:], func=AF.Sin))
                        P.barrier()
                        P.emit()
                    RR = sbC("RR", [128, 4096], F32); RI = sbC("RI", [128, 4096], F32); Tt = sbC("Tt", [128, 4096], F32)
                    SRf = SR[:].rearrange("q w j -> q (w j)"); SIf = SI[:].rearrange("q w j -> q (w j)")
                    D_(lambda e: e.tensor_tensor(out=RR[:], in0=cosT[:], in1=SRf, op=ALU.mult))
                    D_(lambda e: e.tensor_tensor(out=Tt[:], in0=sinT[:], in1=SIf, op=ALU.mult))
                    D_(lambda e: e.tensor_tensor(out=RR[:], in0=RR[:], in1=Tt[:], op=ALU.add))
                    D_(lambda e: e.tensor_tensor(out=RI[:], in0=cosT[:], in1=SIf, op=ALU.mult))
                    D_(lambda e: e.tensor_tensor(out=Tt[:], in0=sinT[:], in1=SRf, op=ALU.mult))
                    D_(lambda e: e.tensor_tensor(out=RI[:], in0=RI[:], in1=Tt[:], op=ALU.subtract))
                    GR = SR; GI = SI
                    for w in range(16):
                        for Rx, Gx in ((RR, GR), (RI, GI)):
                            D_(lambda e, w=w, Rx=Rx, Gx=Gx: e.tensor_tensor_scan(
                                out=Gx[:, w, :], data0=bcast(M16[:, w:w + 1], [128, 256]), data1=Rx[:, w * 256:(w + 1) * 256],
                                initial=0.0, op0=ALU.mult, op1=ALU.add))
                    c3 = cosT[:].rearrange("q (w j) -> q w j", j=256); s3_ = sinT[:].rearrange("q (w j) -> q w j", j=256)
                    HRf = RR[:].rearrange("q (w j) -> q w j", j=256); HIf = RI[:].rearrange("q (w j) -> q w j", j=256)
                    T3 = Tt[:].rearrange("q (w j) -> q w j", j=256)
                    sl = slice(127, 256)
                    D_(lambda e: e.tensor_tensor(out=HRf[:, :, sl], in0=c3[:, :, sl], in1=GR[:, :, sl], op=ALU.mult))
                    D_(lambda e: e.tensor_tensor(out=T3[:, :, sl], in0=s3_[:, :, sl], in1=GI[:, :, sl], op=ALU.mult))
                    D_(lambda e: e.tensor_tensor(out=HRf[:, :, sl], in0=HRf[:, :, sl], in1=T3[:, :, sl], op=ALU.subtract))
                    D_(lambda e: e.tensor_tensor(out=HIf[:, :, sl], in0=c3[:, :, sl], in1=GI[:, :, sl], op=ALU.mult))
                    D_(lambda e: e.tensor_tensor(out=T3[:, :, sl], in0=s3_[:, :, sl], in1=GR[:, :, sl], op=ALU.mult))
                    D_(lambda e: e.tensor_tensor(out=HIf[:, :, sl], in0=HIf[:, :, sl], in1=T3[:, :, sl], op=ALU.add))
                    D_(lambda e: e.tensor_copy(out=Hre[:], in_=HRf[:, :, 127:255]))
                    D_(lambda e: e.tensor_copy(out=Him[:], in_=HIf[:, :, 127:255]))
                    hfR = sbC("hfR", [128, 16], F32); hfI = sbC("hfI", [128, 16], F32)
                    D_(lambda e: e.tensor_copy(out=hfR[:], in_=HRf[:, :, 255]))
                    D_(lambda e: e.tensor_copy(out=hfI[:], in_=HIf[:, :, 255]))
                    P.add('sp', lambda e: e.dma_start(out=hfr, in_=hfR[:]), reads=['g'], dma='c')
                    P.add('sp', lambda e: e.dma_start(out=hfi, in_=hfI[:]), reads=['g'], dma='c')
                    psD = [psC(f"psD{i}", [128, 512], F32) for i in range(2)]
                    for w in range(16):
                        pd = psD[w % 2]
                        P.add('pe', lambda e, pd=pd, w=w: e.matmul(pd[:], lhsT=Hre[:, w, :], rhs=WDre[:, w, :], start=True, stop=False),
                              reads=['g'], writes=[('psD', w % 2)])
                        P.add('pe', lambda e, pd=pd, w=w: e.matmul(pd[:], lhsT=Him[:, w, :], rhs=WDim[:, w, :], start=False, stop=True),
                              reads=['g'], writes=[('psD', w % 2)])
                        P.add('act', lambda e, pd=pd, w=w: e.activation(
                            out=Yc[:, w // 4, :, (w % 4) * 32:(w % 4 + 1) * 32].rearrange("j t (g c) -> j t g c", g=2),
                            in_=pd[:].rearrange("j (g t c) -> j t g c", g=2, t=16), func=AF.Copy),
                              reads=[('psD', w % 2)], writes=['Yc'])
                    for w in range(16):
                        pd = psD[w % 2]
                        P.add('pe', lambda e, pd=pd, w=w: e.matmul(pd[0:4, :], lhsT=H0re[:, w, :], rhs=WDre[:, w, :], start=True, stop=False),
                              reads=['g'], writes=[('psD', w % 2)])
                        P.add('pe', lambda e, pd=pd, w=w: e.matmul(pd[0:4, :], lhsT=H0im[:, w, :], rhs=WDim[:, w, :], start=False, stop=True),
                              reads=['g'], writes=[('psD', w % 2)])
                        P.add('act', lambda e, pd=pd, w=w: e.activation(
                            out=Ycs[:, w // 4, (w % 4) * 32:(w % 4 + 1) * 32].rearrange("s (g c) -> s g c", g=2),
                            in_=pd[0:4, :].rearrange("s (g t c) -> s g t c", g=2, t=16)[:, :, 0, :], func=AF.Copy),
                              reads=[('psD', w % 2)], writes=['Ycs'])
                    P.barrier()
                    P.emit()

            def norm_transpose(xt_ap, nt, gain, hT_out, bufs, tg_, perm_a=None, hT_full=None):
                ssq_, rstd_, xn_, sqj_, psT_ = bufs
                P.add('act', lambda e: e.activation(out=sqj_[0:nt, :], in_=xt_ap, func=AF.Square, accum_out=ssq_[0:nt, :]),
                      reads=[tg_], writes=['n_sqj', 'n_ssq'])
                P.add('dve', lambda e: e.tensor_scalar(out=rstd_[0:nt, :], in0=ssq_[0:nt, :], scalar1=1.0 / D, scalar2=EPS,
                                                       op0=ALU.mult, op1=ALU.add), reads=['n_ssq'], writes=['n_rstd'])
                P.add('act', lambda e: e.activation(out=rstd_[0:nt, :], in_=rstd_[0:nt, :], func=AF.Sqrt),
                      reads=['n_rstd'], writes=['n_rstd'])
                P.add('dve', lambda e: e.reciprocal(out=rstd_[0:nt, :], in_=rstd_[0:nt, :]), reads=['n_rstd'], writes=['n_rstd'])
                P.add('dve', lambda e: e.tensor_scalar(out=xn_[0:nt, :], in0=xt_ap, scalar1=rstd_[0:nt, 0:1], scalar2=None,
                                                       op0=ALU.mult), reads=[tg_, 'n_rstd'], writes=['n_xn'])
                for kc in range(8):
                    P.add('pe', lambda e, kc=kc: e.transpose(out=psT_[:, kc, 0:nt], in_=xn_[0:nt, kc * 128:(kc + 1) * 128],
                                                             identity=ident[0:nt, 0:nt]),
                          reads=['n_xn', 'ident'], writes=['n_psT'])
                if perm_a is None:
                    P.add('dve', lambda e: e.tensor_tensor(out=hT_out, in0=psT_[:, :, 0:nt],
                                                           in1=bcast(gain[:].unsqueeze(2), [128, 8, nt]), op=ALU.mult),
                          reads=['n_psT', 'gains'], writes=['hTg'])
                else:
                    a_ = perm_a
                    P.add('dve', lambda e: e.tensor_tensor(
                        out=hT_full[:].rearrange("q k (s j) -> q k s j", s=16)[:, :, :, 8 * a_:8 * a_ + 8],
                        in0=psT_[:, :, 0:128].rearrange("q k (j s) -> q k s j", s=16),
                        in1=bcast(gain[:].unsqueeze(2).unsqueeze(3), [128, 8, 16, 8]), op=ALU.mult),
                        reads=['n_psT', 'gains'], writes=['hTg'])

            s3b = ExitStack()
            with s3b:
                sbb, psb = mk(s3b)
                wg = sbb("wg", [128, 8, 2048], BF16)
                wat = sbb("wat", [128, 4, 1024], BF16); wgl = sbb("wgl", [128, 4, 512], BF16)
                wss = sbb("wss", [128, 4, 1024], BF16); wou = sbb("wou", [128, 8, 1024], BF16)
                Gw = sbb("Gw", [128, 4, 16, 128], BF16)
                UTs = sbb("UTs", [128, 4, 2048], BF16)
                dvec = sbb("dvec", [128, 4], F32); bgl = sbb("bgls", [128, 4], F32); gmx2 = sbb("gmx2", [128, 8], F32)
                bm = sbb("bm", [128, 128], F32)
                pb = [psb(f"pb{i}", [128, 512], F32) for i in range(7)]
                psT2 = psb("psT2", [128, 8, 128], BF16)
                for kc in range(8):
                    for half in range(2):
                        P.add('pool', lambda e, kc=kc, half=half: e.dma_start(
                            out=wg[:, kc, half * 1024:(half + 1) * 1024],
                            in_=w_in[kc * 128:(kc + 1) * 128, 2048 + half * 1024:2048 + (half + 1) * 1024]), writes=['g'], dma='w')
                    P.add('pool', lambda e, kc=kc: e.dma_start(out=wou[:, kc, :], in_=w_out_d[kc * 128:(kc + 1) * 128, :]),
                          writes=['g'], dma='w')
                for kc in range(4):
                    P.add('pool', lambda e, kc=kc: e.dma_start(out=wat[:, kc, :], in_=w_attn_d[kc * 128:(kc + 1) * 128, :]), writes=['g'], dma='w')
                    P.add('pool', lambda e, kc=kc: e.dma_start(out=wss[:, kc, :], in_=w_ssm_d[kc * 128:(kc + 1) * 128, :]), writes=['g'], dma='w')
                    P.add('pool', lambda e, kc=kc: e.dma_start(out=wgl[:, kc, :], in_=w_glu_d[kc * 128:(kc + 1) * 128, :]), writes=['g'], dma='w')
                for dst, srcd in ((dvec, dvec_d), (bgl, bgl_d), (gmx2, gmix), (bm, bmask_d)):
                    P.add('sp', lambda e, dst=dst, srcd=srcd: e.dma_start(out=dst[:], in_=srcd), writes=['g'], dma='c')
                sga = ExitStack()
                with sga:
                    sbga, _ = mk(sga)
                    UTo = sbga("UTo", [128, 4, 2048], BF16)
                    for kc in range(4):
                        P.add('sp', lambda e, kc=kc: e.dma_start(out=UTo[:, kc, :], in_=ut_scr[:, kc, 2048:4096]), writes=['g'], dma='c')
                    tA = {}
                    for nm, srcd, n_ in (('ar', arA, 32), ('ai', aiA, 32), ('ldt', ldtA, 32), ('b1', b1A, 512), ('b2', b2A, 512),
                                         ('c', cA, 512), ('sgn', sgnA_d, 1)):
                        tA[nm] = sbga("gA" + nm, [128, n_], F32)
                        P.add('sp', lambda e, t=tA[nm], srcd=srcd: e.dma_start(out=t[:], in_=srcd), writes=['g'], dma='c')
                    P.barrier()
                    kvA = sbga("kvA", [128, 16], F32)
                    P.add('sp', lambda e: e.dma_start(out=kvA[:], in_=kv0_d), writes=['g'], dma='c')
                    P.barrier()
                    for kc in range(4):
                        P.add('pool', lambda e, kc=kc: e.tensor_copy(
                            out=UTs[:, kc, :].rearrange("q (g s j) -> q g s j", g=4, s=16),
                            in_=UTo[:, kc, :].rearrange("q (g j s) -> q g s j", g=4, s=16)), reads=['g'], writes=['g'])
                    r = powtab(sbga, "pA", tA['ar'][:], tA['ai'][:], tA['ldt'][:], 32, kvA[:], 16, coef=1)
                    Zr = sbga("ZrA", [128, 512], F32); Zi = sbga("ZiA", [128, 512], F32)
                    v3 = lambda t: t[:].rearrange("q (f k) -> q f k", k=16)
                    crb = bcast(r['cr'][:].unsqueeze(2), [128, 32, 16]); cib = bcast(r['ci'][:].unsqueeze(2), [128, 32, 16])
                    cmul(v3(Zr), v3(Zi), v3(r['PR']), v3(r['PI']), crb, cib, v3(r['tf']))
                    D_(lambda e: e.tensor_scalar(out=Zi[:], in0=Zi[:], scalar1=tA['sgn'][:, 0:1], scalar2=None, op0=ALU.mult))
                    Cc = sbga("Cc", [128, 512], F32); Xc = sbga("Xc", [128, 512], F32); Xt = sbga("Xt", [128, 512], F32)
                    D_(lambda e: e.tensor_scalar(out=Cc[:], in0=tA['c'][:], scalar1=tA['sgn'][:, 0:1], scalar2=-1.0,
                                                 op0=ALU.mult, op1=ALU.mult))
                    g3 = lambda t: t[:].rearrange("q (g c) -> q g c", c=16)
                    psGw = pb[0][:].rearrange("q (k n) -> q k n", k=4)
                    for dl in range(16):
                        D_(lambda e, dl=dl: e.tensor_tensor(out=g3(Xc), in0=g3(tA['b1']), in1=bcast(v3(Zr)[:, :, dl:dl + 1], [128, 32, 16]), op=ALU.mult))
                        D_(lambda e, dl=dl: e.tensor_tensor(out=g3(Xt), in0=g3(tA['b2']), in1=bcast(v3(Zi)[:, :, dl:dl + 1], [128, 32, 16]), op=ALU.mult))
                        D_(lambda e: e.tensor_tensor(out=Xc[:], in0=Xc[:], in1=Xt[:], op=ALU.add))
                        for kc in range(4):
                            P.add('pe', lambda e, kc=kc: e.matmul(psGw[:, kc, :], lhsT=Xc[:, kc * 128:(kc + 1) * 128],
                                                                  rhs=Cc[:, kc * 128:(kc + 1) * 128], start=True, stop=True),
                                  reads=['g'], writes=['psGw'])
                        P.add('dve', lambda e, dl=dl: e.tensor_tensor(out=Gw[:, :, dl, :], in0=psGw, in1=bcast(bm[:].unsqueeze(1), [128, 4, 128]),
                                                                      op=ALU.mult), reads=['psGw', 'g'], writes=['g'])
                    P.barrier()
                    P.emit()

                hTg = sbb("hTg", [128, 8, 512], BF16)
                OTg = sbb("OTg", [128, 4, 512], BF16)
                zT = sbb("zT", [128, 4, 512], BF16); z2T = sbb("z2T", [128, 4, 512], BF16)
                mT = sbb("mT", [128, 8, 512], BF16)
                xres = [sbb(f"xres{i}", [128, D], F32) for i in range(4)]
                x1t = [sbb(f"x1t{i}", [128, D], F32) for i in range(2)]
                yt = sbb("ytmp", [128, 512], F32); y2 = sbb("ytmp2", [128, 512], F32); sg1 = sbb("sg1", [128, 512], F32)
                sg2_ = sbb("sg2", [128, 512], F32)
                nb = (sbb("n_ssq", [128, 1], F32), sbb("n_rstd", [128, 1], F32), sbb("n_xn", [128, D], BF16),
                      sbb("n_sqj", [128, D], BF16), psT2)
                P.add('pool', lambda e: e.memset(OTg[:], 0.0), writes=['OTg'])
                OTp = sbb("OTp", [128, 4, 512], BF16)

                def phase3b_group(tg):
                    sample = (tg == 4)
                    N = 4 if sample else 512
                    tiles = [(xs, 0, 4)] if sample else [(xo, tg * 512 + a * 128, 128) for a in range(4)]
                    for a, (src, r0, nt) in enumerate(tiles):
                        P.add('sp', lambda e, a=a, src=src, r0=r0, nt=nt: e.dma_start(out=xres[a][0:nt, :], in_=src[r0:r0 + nt, :]),
                              writes=[('xres', a)], dma=f'x{a % 2}')
                        norm_transpose(xres[a][0:nt, :], nt, gmx2, hTg[:, :, a * 128:a * 128 + nt], nb, ('xres', a),
                                       perm_a=(None if sample else a), hT_full=hTg)
                    if sample:
                        P.add('pool', lambda e: e.tensor_copy(out=OTg[:, :, 0:4], in_=OTs[:]), reads=['OTs'], writes=['OTg'])
                    if not sample:
                        P.add('sp', lambda e: e.dma_start(out=OTg[:], in_=ot_scr[:, :, tg * 512:(tg + 1) * 512]),
                              reads=[('OT', tg, h) for h in range(8)], writes=['OTg'], dma='c')
                        P.add('pool', lambda e: e.tensor_copy(out=OTp[:].rearrange("q k (s j) -> q k s j", s=16),
                                                              in_=OTg[:].rearrange("q k (j s) -> q k s j", s=16)),
                              reads=['OTg'], writes=['OTp'])
                    else:
                        P.add('pool', lambda e: e.tensor_copy(out=OTp[:, :, 0:4], in_=OTg[:, :, 0:4]), reads=['OTg'], writes=['OTp'])
                    for kc in range(4):
                        if sample:
                            P.add('pe', lambda e, kc=kc: e.matmul(pb[kc][:, 0:4], lhsT=Gw[:, kc, 0, :], rhs=usT[:, kc, :],
                                                                  start=True, stop=False), reads=['g', 'usT'], writes=[('pb', kc)])
                            P.add('pe', lambda e, kc=kc: e.matmul(
                                pb[kc][:, 0:4], lhsT=Ycs[:, kc, :], rhs=ident[0:4, 0:4], start=False, stop=True),
                                reads=['Ycs', 'ident'], writes=[('pb', kc)])
                        else:
                            for dl in range(16):
                                P.add('pe', lambda e, kc=kc, dl=dl: e.matmul(
                                    pb[kc][:, dl * 32:512], lhsT=Gw[:, kc, dl, :], rhs=UTs[:, kc, tg * 512:tg * 512 + (16 - dl) * 32],
                                    start=(dl == 0), stop=False),
                                    reads=['g'], writes=[('pb', kc)])
                            for tl in range(16):
                                P.add('pe', lambda e, kc=kc, tl=tl: e.matmul(
                                    pb[kc][:, tl * 32:(tl + 1) * 32], lhsT=Yc[:, kc, tl, :], rhs=ident[:, tg * 32:(tg + 1) * 32], start=False, stop=(tl == 15)),
                                    reads=['Yc', 'ident'], writes=[('pb', kc)])
                    for kc in range(4):
                        uin = usT[:, kc, :] if sample else UTs[:, kc, tg * 512:(tg + 1) * 512]
                        P.add('dve', lambda e, kc=kc, uin=uin: e.scalar_tensor_tensor(
                            out=yt[:, 0:N], in0=uin, scalar=dvec[:, kc:kc + 1], in1=pb[kc][:, 0:N], op0=ALU.mult, op1=ALU.add),
                            reads=[('pb', kc), 'g'], writes=['yt'])
                        P.add('dve', lambda e: e.tensor_tensor(out=y2[:, 0:N], in0=yt[:, 0:N], in1=yt[:, 0:N], op=ALU.mult),
                              reads=['yt'], writes=['y2'])
                        P.add('dve', lambda e: e.tensor_scalar(out=y2[:, 0:N], in0=y2[:, 0:N], scalar1=0.044715, scalar2=1.0,
                                                               op0=ALU.mult, op1=ALU.add), reads=['y2'], writes=['y2'])
                        P.add('dve', lambda e: e.tensor_tensor(out=y2[:, 0:N], in0=y2[:, 0:N], in1=yt[:, 0:N], op=ALU.mult),
                              reads=['y2', 'yt'], writes=['y2'])
                        P.add('act', lambda e: e.activation(out=y2[:, 0:N], in_=y2[:, 0:N], func=AF.Sigmoid, scale=1.5957691216057308),
                              reads=['y2'], writes=['y2'])
                        P.add('dve', lambda e, kc=kc: e.tensor_tensor(out=zT[:, kc, 0:N], in0=y2[:, 0:N], in1=yt[:, 0:N], op=ALU.mult),
                              reads=['y2', 'yt'], writes=[('zT', kc)])
                    for m in range(4):
                        for kc in range(4):
                            P.add('pe', lambda e, m=m, kc=kc: e.matmul(pb[m][:, 0:N], lhsT=wgl[:, kc, m * 128:(m + 1) * 128],
                                                                       rhs=zT[:, kc, 0:N], start=(kc == 0), stop=(kc == 3)),
                                  reads=['g'] + [('zT', k) for k in range(4)], writes=[('pb', m)])
                        P.add('act', lambda e, m=m: e.activation(out=sg1[:, 0:N], in_=pb[m][:, 0:N], func=AF.Sigmoid, bias=bgl[:, m:m + 1]),
                              reads=[('pb', m), 'g'], writes=['sg1'])
                        P.add('dve', lambda e, m=m: e.tensor_tensor(out=z2T[:, m, 0:N], in0=zT[:, m, 0:N], in1=sg1[:, 0:N], op=ALU.mult),
                              reads=['sg1', ('zT', m)], writes=[('z2T', m)])
                    for m in range(8):
                        ms = slice(m * 128, (m + 1) * 128)
                        for gi in range(2):
                            for kc in range(8):
                                P.add('pe', lambda e, kc=kc, m=m, gi=gi: e.matmul(
                                    pb[2 + gi][:, 0:N], lhsT=wg[:, kc, gi * 1024 + m * 128:gi * 1024 + (m + 1) * 128], rhs=hTg[:, kc, 0:N],
                                    start=(kc == 0), stop=(kc == 7)), reads=['g', 'hTg'], writes=[('pb', 2 + gi)])
                        for kc in range(4):
                            P.add('pe', lambda e, kc=kc, ms=ms: e.matmul(pb[0][:, 0:N], lhsT=wat[:, kc, ms], rhs=OTp[:, kc, 0:N],
                                                                         start=(kc == 0), stop=(kc == 3)),
                                  reads=['g', 'OTp'], writes=[('pb', 0)])
                        for kc in range(4):
                            P.add('pe', lambda e, kc=kc, ms=ms: e.matmul(pb[1][:, 0:N], lhsT=wss[:, kc, ms], rhs=z2T[:, kc, 0:N],
                                                                         start=(kc == 0), stop=(kc == 3)),
                                  reads=['g'] + [('z2T', k) for k in range(4)], writes=[('pb', 1)])
                        P.add('act', lambda e: e.activation(out=sg1[:, 0:N], in_=pb[2][:, 0:N], func=AF.Sigmoid),
                              reads=[('pb', 2)], writes=['sg1'])
                        P.add('act', lambda e: e.activation(out=sg2_[:, 0:N], in_=pb[3][:, 0:N], func=AF.Sigmoid),
                              reads=[('pb', 3)], writes=['sg2'])
                        P.add('dve', lambda e: e.tensor_tensor(out=sg1[:, 0:N], in0=sg1[:, 0:N], in1=pb[0][:, 0:N], op=ALU.mult),
                              reads=['sg1', ('pb', 0)], writes=['sg1'])
                        P.add('dve', lambda e: e.tensor_tensor(out=sg2_[:, 0:N], in0=sg2_[:, 0:N], in1=pb[1][:, 0:N], op=ALU.mult),
                              reads=['sg2', ('pb', 1)], writes=['sg2'])
                        if sample:
                            P.add('dve', lambda e, m=m: e.tensor_tensor(out=mT[:, m, 0:N], in0=sg1[:, 0:N], in1=sg2_[:, 0:N], op=ALU.add),
                                  reads=['sg1', 'sg2'], writes=[('mT', m)])
                        else:
                            P.add('dve', lambda e, m=m: e.tensor_tensor(
                                out=mT[:, m, :].rearrange("q (j s) -> q s j", s=16), in0=sg1[:].rearrange("q (s j) -> q s j", s=16),
                                in1=sg2_[:].rearrange("q (s j) -> q s j", s=16), op=ALU.add),
                                reads=['sg1', 'sg2'], writes=[('mT', m)])
                    for a, (src, r0, nt) in enumerate(tiles):
                        xo_ = x1t[a % 2]
                        for half in range(2):
                            pX = pb[4 + half]
                            for kc in range(8):
                                P.add('pe', lambda e, kc=kc, pX=pX, a=a, nt=nt, half=half: e.matmul(
                                    pX[0:nt, :], lhsT=mT[:, kc, a * 128:a * 128 + nt], rhs=wou[:, kc, half * 512:(half + 1) * 512],
                                    start=(kc == 0), stop=(kc == 7)), reads=['g'] + [('mT', k) for k in range(8)], writes=[('pb', 4 + half)])
                            P.add('dve', lambda e, pX=pX, a=a, nt=nt, half=half, xo_=xo_: e.tensor_tensor(
                                out=xo_[0:nt, half * 512:(half + 1) * 512], in0=pX[0:nt, :], in1=xres[a][0:nt, half * 512:(half + 1) * 512],
                                op=ALU.add), reads=[('pb', 4 + half), ('xres', a)], writes=[('x1t', a % 2)])
                        rr0 = 2048 if sample else tg * 512 + a * 128
                        P.add('sp', lambda e, xo_=xo_, rr0=rr0, nt=nt: e.dma_start(out=x1_scr[rr0:rr0 + nt, :], in_=xo_[0:nt, :]),
                              reads=[('x1t', a % 2)], writes=[('x1scr', rr0)], dma=f'us{a % 2}')

                groups = [int(v) for v in os.environ.get('KTG', '0,1,2,3,4').split(',')]
                for tg in groups:
                    phase3b_group(tg)
                if DBG:
                    P.barrier()
                    dX1 = nc.dram_tensor("dbg_x1", [2052, 1024], F32, kind="ExternalOutput").ap()
                    P.add('sp', lambda e: e.dma_start(out=dX1, in_=x1_scr), dma='dbg')
                P.barrier()
                P.emit()

        s4 = ExitStack()
        with s4:
            sb4, ps4 = mk(s4)
            wfi = sb4("wfi", [128, 8, 5632], BF16)
            wfo = sb4("wfo", [128, 22, 1024], BF16)
            gff = sb4("gff", [128, 8], F32); gfin = sb4("gfin_s", [128, D], F32)
            hfT = sb4("hfT", [128, 8, 512], BF16); actT = sb4("actT", [128, 22, 512], BF16)
            xr = [sb4(f"xr{i}", [128, D], F32) for i in range(4)]
            x2 = [sb4(f"x2{i}", [128, D], F32) for i in range(2)]
            sgA = [sb4(f"sgA{i}", [128, 512], F32) for i in range(2)]
            fss = sb4("fss", [128, 1], F32); frs = sb4("frs", [128, 1], F32)
            pAb = [ps4(f"pAb{i}", [128, 512], F32) for i in range(2)]
            pGb = [ps4(f"pGb{i}", [128, 512], F32) for i in range(2)]
            pXb = [ps4(f"pXb{i}", [128, 512], F32) for i in range(2)]
            psT4 = ps4("psT4", [128, 8, 128], BF16)
            nb4 = (sb4("n4_ssq", [128, 1], F32), sb4("n4_rstd", [128, 1], F32), sb4("n4_xn", [128, D], BF16),
                   sb4("n4_sqj", [128, D], BF16), psT4)
            P.add('sp', lambda e: e.dma_start(out=gff[:], in_=gffn_d), writes=['g'], dma='c')
            P.add('sp', lambda e: e.dma_start(out=gfin[:], in_=gfin_d), writes=['g'], dma='c')
            P.barrier()
            for ci_, (c0, c1) in enumerate(((0, 2048), (2048, 4096), (4096, 5632))):
                for kc in range(8):
                    P.add('pool', lambda e, kc=kc, c0=c0, c1=c1: e.dma_start(out=wfi[:, kc, c0:c1], in_=w_fin_d[kc * 128:(kc + 1) * 128, c0:c1]),
                          writes=[('wfi', ci_, kc)], dma=f'wf{ci_}')
            for kc in range(22):
                P.add('pool', lambda e, kc=kc: e.dma_start(out=wfo[:, kc, :], in_=w_fout_d[kc * 128:(kc + 1) * 128, :]),
                      writes=[('wfo', kc)], dma='wfo')
            wfi_t = [[('wfi', c_, k_) for k_ in range(8)] for c_ in range(3)]
            wfo_t = [('wfo', k_) for k_ in range(22)]

            def phase4_group(tg):
                sample = (tg == 4)
                N = 4 if sample else 512
                tiles = [(2048, 4)] if sample else [(tg * 512 + a * 128, 128) for a in range(4)]
                for a, (r0, nt) in enumerate(tiles):
                    P.add('sp', lambda e, a=a, r0=r0, nt=nt: e.dma_start(out=xr[a][0:nt, :], in_=x1_scr[r0:r0 + nt, :]),
                          reads=[('x1scr', r0)], writes=[('xr', a)], dma=f'x{a % 2}')
                    norm_transpose(xr[a][0:nt, :], nt, gff, hfT[:, :, a * 128:a * 128 + nt], nb4, ('xr', a))
                for m in range(22):
                    pa, pg = pAb[m % 2], pGb[m % 2]
                    for kc in range(8):
                        P.add('pe', lambda e, kc=kc, m=m, pa=pa: e.matmul(pa[:, 0:N], lhsT=wfi[:, kc, m * 128:(m + 1) * 128], rhs=hfT[:, kc, 0:N],
                                                                          start=(kc == 0), stop=(kc == 7)), reads=['hTg'] + wfi_t[(m * 128) // 2048], writes=[('pAb', m % 2)])
                    for kc in range(8):
                        P.add('pe', lambda e, kc=kc, m=m, pg=pg: e.matmul(pg[:, 0:N], lhsT=wfi[:, kc, 2816 + m * 128:2816 + (m + 1) * 128],
                                                                          rhs=hfT[:, kc, 0:N], start=(kc == 0), stop=(kc == 7)),
                              reads=['hTg'] + wfi_t[(2816 + m * 128) // 2048] + wfi_t[(2816 + m * 128 + 127) // 2048], writes=[('pGb', m % 2)])
                    sg = sgA[m % 2]
                    P.add('act', lambda e, pa=pa, sg=sg: e.activation(out=sg[:, 0:N], in_=pa[:, 0:N], func=AF.Silu),
                          reads=[('pAb', m % 2)], writes=[('sgA', m % 2)])
                    P.add('dve', lambda e, pg=pg, sg=sg, m=m: e.tensor_tensor(out=actT[:, m, 0:N], in0=sg[:, 0:N], in1=pg[:, 0:N], op=ALU.mult),
                          reads=[('sgA', m % 2), ('pGb', m % 2)], writes=[('actT', m)])
                for a, (r0, nt) in enumerate(tiles):
                    xx = x2[a % 2]
                    yy = xx
                    for half in range(2):
                        pX = pXb[half]
                        for kc in range(22):
                            P.add('pe', lambda e, kc=kc, pX=pX, a=a, nt=nt, half=half: e.matmul(
                                pX[0:nt, :], lhsT=actT[:, kc, a * 128:a * 128 + nt], rhs=wfo[:, kc, half * 512:(half + 1) * 512],
                                start=(kc == 0), stop=(kc == 21)), reads=wfo_t + [('actT', k) for k in range(22)], writes=[('pXb', half)])
                        P.add('dve', lambda e, pX=pX, a=a, nt=nt, half=half, xx=xx: e.tensor_tensor(
                            out=xx[0:nt, half * 512:(half + 1) * 512], in0=pX[0:nt, :], in1=xr[a][0:nt, half * 512:(half + 1) * 512], op=ALU.add),
                            reads=[('pXb', half), ('xr', a)], writes=[('x2', a % 2)])
                    P.add('act', lambda e, xx=xx, nt=nt: e.activation(out=nb4[3][0:nt, :], in_=xx[0:nt, :], func=AF.Square, accum_out=fss[0:nt, :]),
                          reads=[('x2', a % 2)], writes=['n_sqj', 'fss'])
                    P.add('dve', lambda e, nt=nt: e.tensor_scalar(out=frs[0:nt, :], in0=fss[0:nt, :], scalar1=1.0 / D, scalar2=EPS,
                                                                  op0=ALU.mult, op1=ALU.add), reads=['fss'], writes=['frs'])
                    P.add('act', lambda e, nt=nt: e.activation(out=frs[0:nt, :], in_=frs[0:nt, :], func=AF.Sqrt), reads=['frs'], writes=['frs'])
                    P.add('dve', lambda e, nt=nt: e.reciprocal(out=frs[0:nt, :], in_=frs[0:nt, :]), reads=['frs'], writes=['frs'])
                    P.add('dve', lambda e, xx=xx, yy=yy, nt=nt: e.scalar_tensor_tensor(
                        out=yy[0:nt, :], in0=xx[0:nt, :], scalar=frs[0:nt, 0:1], in1=gfin[0:nt, :], op0=ALU.mult, op1=ALU.mult),
                        reads=[('x2', a % 2), 'frs', 'g'], writes=[('x2', a % 2)])
                    dst = yso if sample else yo
                    d0 = 0 if sample else r0
                    P.add('sp', lambda e, yy=yy, dst=dst, d0=d0, nt=nt: e.dma_start(out=dst[d0:d0 + nt, :], in_=yy[0:nt, :]),
                          reads=[('x2', a % 2)], dma=f'ko{a % 2}')

            for tg in groups:
                phase4_group(tg)
            P.barrier()
            P.emit()
    return nc


def _host_consts(p):
    half = HD // 2
    inv = np.power(np.float32(10000.0), -2.0 * np.arange(half, dtype=np.float32) / HD).astype(np.float32)
    pos = np.concatenate([np.arange(2048), 2048 * p + np.arange(2048), np.full(128, 8192)]).astype(np.float32)
    ang = pos[:, None] * inv[None, :]
    cos = np.cos(ang).astype(np.float32)
    sin = np.sin(ang).astype(np.float32)
    sF = np.concatenate([-sin, sin], axis=1)
    cF = np.ascontiguousarray(cos.reshape(33, 128, 32).transpose(1, 0, 2))
    sF = np.ascontiguousarray(sF.reshape(33, 128, 64).transpose(1, 0, 2))
    k = np.arange(128)[:, None, None, None]
    j = np.arange(4)[None, :, None, None]
    a = np.arange(4)[None, None, :, None]
    q = np.arange(128)[None, None, None, :]
    m4 = ((j < a) | ((j == a) & (k <= q))).astype(np.float32).reshape(128, 4, 512)
    pb = np.full((8, 16), -BIG, np.float32)
    for m in range(8):
        if p == 1:
            pb[m, 0:8] = 0.0
        pb[m, 8:8 + m] = 0.0
    pb = np.ascontiguousarray(np.broadcast_to(pb[None], (128, 8, 16)))
    return cF, sF, np.ascontiguousarray(m4), pb


def _s5_layouts(a_re, a_im, log_dt, b_re, b_im, c_re, c_im):
    c_ = np.ascontiguousarray
    out = {}
    def lay_gp(a):
        t = a.reshape(4, 8, 64)
        t = np.broadcast_to(t[:, :, None, :], (4, 8, 16, 64))
        return c_(t.transpose(1, 2, 0, 3).reshape(128, 256))
    out["arB"] = lay_gp(a_re); out["aiB"] = lay_gp(a_im)
    out["ldtB"] = lay_gp(np.broadcast_to(log_dt[:, None], (32, 64)))
    def lay_b(bb):
        t = bb.reshape(4, 8, 64, 16)
        return c_(t.transpose(1, 3, 0, 2).reshape(128, 256))
    out["brB"] = lay_b(b_re); out["biB"] = lay_b(b_im)
    def lay_w(a):
        t = a.reshape(16, 2, 64)
        return c_(t.transpose(1, 2, 0).reshape(128, 16))
    out["arD"] = lay_w(a_re); out["aiD"] = lay_w(a_im)
    out["ldtD"] = lay_w(np.broadcast_to(log_dt[:, None], (32, 64)))
    def lay_c(cc):
        t = cc.reshape(16, 2, 16, 64)
        return c_(t.transpose(1, 3, 0, 2).reshape(128, 256))
    out["crD"] = lay_c(c_re); out["ciD"] = lay_c(c_im)
    t2 = lambda a: c_(np.concatenate([a.T, a.T], axis=0))
    out["arA"] = t2(a_re); out["aiA"] = t2(a_im); out["ldtA"] = t2(np.broadcast_to(log_dt[:, None], (32, 64)))
    bt = lambda x: x.transpose(1, 0, 2).reshape(64, 512)
    out["b1A"] = c_(np.concatenate([bt(b_re), bt(b_im)], axis=0))
    out["b2A"] = c_(np.concatenate([bt(b_im), bt(b_re)], axis=0))
    ct = lambda x: x.transpose(2, 0, 1).reshape(64, 512)
    out["cA"] = c_(np.concatenate([ct(c_re), ct(c_im)], axis=0))
    out["kv0"] = c_(np.broadcast_to(np.arange(16, dtype=np.float32)[None], (128, 16)))
    out["kv1"] = c_(np.broadcast_to(np.arange(1, 17, dtype=np.float32)[None], (128, 16)))
    out["kvj"] = c_(np.broadcast_to(np.arange(1, 257, dtype=np.float32)[None], (128, 256)))
    r = np.arange(128)
    out["pmB"] = c_(np.stack([((r // 16) % 2 == 0), ((r // 16) % 2 == 1)], axis=1).astype(np.float32))
    out["pmD"] = c_(np.stack([(r // 64 == 0), (r // 64 == 1)], axis=1).astype(np.float32))
    return out


_NC_CACHE = {}
_LAST = {}


def kernel(x_prompt, x_sample, cache_k, cache_v, state_ssm_re, state_ssm_im, page_table,
           norm_mix, w_in, w_attn_proj, ssm_a_re, ssm_a_im, ssm_log_dt, ssm_b_re, ssm_b_im,
           ssm_c_re, ssm_c_im, ssm_d, w_glu, b_glu, w_ssm_proj, w_out, norm_ffn,
           w_ffn_in, w_ffn_out, norm_final):
    f = lambda a: np.ascontiguousarray(np.asarray(a, dtype=np.float32))
    x_prompt = f(x_prompt)
    x_sample = f(x_sample)
    if 'nc' not in _NC_CACHE:
        _NC_CACHE['nc'] = build()
    nc = _NC_CACHE['nc']
    in_maps = []
    ident = np.eye(128, dtype=np.float32)
    gm = np.ascontiguousarray(f(norm_mix)[0].reshape(8, 128).T)
    s5 = _s5_layouts(f(ssm_a_re)[0], f(ssm_a_im)[0], f(ssm_log_dt)[0], f(ssm_b_re)[0], f(ssm_b_im)[0],
                     f(ssm_c_re)[0], f(ssm_c_im)[0])
    col = lambda v, n: np.ascontiguousarray(f(v).reshape(n, 128).T)
    r_ = np.arange(128)
    wts = {"w_attn": f(w_attn_proj)[0], "w_glu": f(w_glu)[0], "w_ssm": f(w_ssm_proj)[0], "w_out": f(w_out)[0],
           "w_ffn_in": f(w_ffn_in)[0], "w_ffn_out": f(w_ffn_out)[0],
           "bgl": col(f(b_glu)[0], 4), "dvec": col(f(ssm_d)[0], 4), "gffn": col(f(norm_ffn)[0], 8),
           "gfin": np.ascontiguousarray(np.broadcast_to(f(norm_final)[None, :], (128, 1024))),
           "sgnA": np.where(r_ < 64, -1.0, 1.0).astype(np.float32).reshape(128, 1),
           "bmask": (r_[:, None] // 16 == r_[None, :] // 16).astype(np.float32)}
    zs = np.zeros((128, 255), np.float32); zs[:, 127] = 1.0
    smp = {"cache_k": f(cache_k).reshape(2560 * 128 * 8, 64), "cache_v": f(cache_v).reshape(2560 * 128 * 8, 64),
           "piota": r_.astype(np.float32).reshape(128, 1),
           "hb": (8.0 * r_[:, None] + np.arange(8)[None, :]).astype(np.float32),
           "zsel": zs, "rsel": (np.arange(128)[None, :] // 32 == np.arange(4)[:, None]).astype(np.float32),
           "eye8": np.eye(8, dtype=np.float32), "esel": np.eye(4, dtype=np.float32)}
    for c in range(8):
        b, p = c // 2, c % 2
        cF, sF, m4, pb = _host_consts(p)
        xo = x_prompt[b, 2048 * p:2048 * (p + 1)]
        xpv = x_prompt[b, 0:2048] if p == 1 else np.zeros((2048, D), np.float32)
        in_maps.append({
            "xo": np.ascontiguousarray(xo), "xp": np.ascontiguousarray(xpv),
            "xs": np.ascontiguousarray(x_sample[4 * c:4 * c + 4, 0]), "w_in": f(w_in)[0],
            "gmix": gm, "ropec": cF, "ropes": sF, "ident": ident, "mask4": m4, "pbias": pb,
        })
        in_maps[-1].update(s5)
        in_maps[-1].update(wts)
        in_maps[-1].update(smp)
        pt4 = np.asarray(page_table)[4 * c:4 * c + 4].astype(np.int32)
        in_maps[-1]["ptrep"] = np.ascontiguousarray(np.broadcast_to(pt4.reshape(1, 256), (128, 256)))
        in_maps[-1]["ptE"] = np.ascontiguousarray(np.broadcast_to(pt4[:, 0::2].reshape(1, 128), (8, 128)))
        in_maps[-1]["ptO"] = np.ascontiguousarray(np.broadcast_to(pt4[:, 1::2].reshape(1, 128), (8, 128)))
        for nm, st in (("h0r", state_ssm_re), ("h0i", state_ssm_im)):
            t = f(st)[0, 4 * c:4 * c + 4].reshape(4, 16, 2, 64)
            in_maps[-1][nm] = np.ascontiguousarray(t.transpose(2, 3, 1, 0).reshape(128, 64))
    res = run_bass_kernel_spmd(nc, in_maps, core_ids=list(range(8)))
    R = res.results
    if os.environ.get('KDBG'):
        cc = int(os.environ.get("KCORE", "1"))
        _LAST['R'] = {k: np.asarray(v).view(np.uint16) if 'bfloat' in str(np.asarray(v).dtype) else np.asarray(v)
                      for k, v in R[cc].items()}
    new_k = np.zeros((1, 4, 4096, 8, 64), np.float32)
    new_v = np.zeros((1, 4, 4096, 8, 64), np.float32)
    new_ks = np.zeros((1, 32, 1, 8, 64), np.float32)
    new_vs = np.zeros((1, 32, 1, 8, 64), np.float32)
    for c in range(8):
        b, p = c // 2, c % 2
        new_k[0, b, 2048 * p:2048 * (p + 1)] = np.asarray(R[c]["ko"]).reshape(2048, 8, 64)
        new_v[0, b, 2048 * p:2048 * (p + 1)] = np.asarray(R[c]["vo"]).reshape(2048, 8, 64)
        new_ks[0, 4 * c:4 * c + 4, 0] = np.asarray(R[c]["kso"]).reshape(4, 8, 64)
        new_vs[0, 4 * c:4 * c + 4, 0] = np.asarray(R[c]["vso"]).reshape(4, 8, 64)
    y_prompt = np.zeros((4, 4096, 1024), np.float32)
    y_sample = np.zeros((32, 1, 1024), np.float32)
    for c in range(8):
        b, p = c // 2, c % 2
        y_prompt[b, 2048 * p:2048 * (p + 1)] = np.asarray(R[c]["yo"])
        y_sample[4 * c:4 * c + 4, 0] = np.asarray(R[c]["yso"])
    hre = np.zeros((1, 4, 32, 64), np.float32)
    him = np.zeros((1, 4, 32, 64), np.float32)
    for b in range(4):
        c = 2 * b + 1
        hre[0, b] = np.asarray(R[c]["hfr"]).reshape(2, 64, 16).transpose(2, 0, 1).reshape(32, 64)
        him[0, b] = np.asarray(R[c]["hfi"]).reshape(2, 64, 16).transpose(2, 0, 1).reshape(32, 64)
    hsr_o = np.zeros((1, 32, 32, 64), np.float32)
    hsi_o = np.zeros((1, 32, 32, 64), np.float32)
    for c in range(8):
        hsr_o[0, 4 * c:4 * c + 4] = np.asarray(R[c]["hsr"]).reshape(2, 64, 16, 4).transpose(3, 2, 0, 1).reshape(4, 32, 64)
        hsi_o[0, 4 * c:4 * c + 4] = np.asarray(R[c]["hsi"]).reshape(2, 64, 16, 4).transpose(3, 2, 0, 1).reshape(4, 32, 64)
    return (y_prompt, y_sample, new_k, new_v, hre, him, new_ks, new_vs, hsr_o, hsi_o)
```
